# Optimizing a Trainium2 kernel written in Bass

```python
import math
import jax
import jax.numpy as jnp
from jax import lax
import numpy as np

D_MODEL = 1024
BATCH = 8
SEQ = 4096
DEPTH = 2

GRID_W = 64
BLOCK_Q = 128
EPS = 1e-6
N_BRANCH = 4

A_HEADS = 4
A_DH = 128
A_WIDTH = A_HEADS * A_DH
A_CHUNK = 128
A_CONV = 3
B_HEADS = 8
B_KV = 2
B_DH = 64
B_WIDTH = B_HEADS * B_DH
ROPE_THETA = 10000.0
C_HEADS = 8
C_DH = 64
C_WIDTH = C_HEADS * C_DH
C_WIN_R = 8
C_WIN_C = 16
D_HEADS = 4
D_DH = 64
D_DV = 2 * D_DH
D_WIDTH = D_HEADS * D_DV
D_FF = ((8 * D_MODEL + 3 * 256 - 1) // (3 * 256)) * 256

SPLIT_SIZES = (
    A_WIDTH, A_WIDTH, A_WIDTH, A_WIDTH, 4 * A_HEADS,
    B_WIDTH, B_KV * B_DH, B_KV * B_DH,
    C_WIDTH, C_WIDTH, C_WIDTH,
    2 * D_HEADS * D_DH, 2 * D_HEADS * D_DH, D_WIDTH,
    N_BRANCH * D_MODEL,
)
D_IN = sum(SPLIT_SIZES)

kernel_name = "hybrid_mlstm_gqa_natten_diffattn_encoder"

F32 = jnp.float32


def rms_norm(x, g):
    xf = x.astype(F32)
    y = xf * lax.rsqrt(jnp.mean(xf * xf, axis=-1, keepdims=True) + EPS)
    return (y * g.astype(F32)).astype(x.dtype)


def centred_dwconv(x, w):
    k = w.shape[0]
    p = k // 2
    s = x.shape[1]
    xp = jnp.pad(x, ((0, 0), (p, p), (0, 0)))
    return sum(xp[:, j:j + s, :] * w[j] for j in range(k))


def sweep_query_blocks(fn, q):
    bsz, s = q.shape[:2]
    nb = s // BLOCK_Q
    qb = jnp.moveaxis(q.reshape((bsz, nb, BLOCK_Q) + q.shape[2:]), 1, 0)
    out = lax.map(lambda a: fn(a[0], a[1]), (jnp.arange(nb), qb))
    return jnp.moveaxis(out, 0, 1).reshape((bsz, s) + out.shape[3:])


def mlstm_scan(q, k, v, li, lf):
    bsz, nh, s, d = q.shape
    nc = s // A_CHUNK

    def chunks(t):
        t = t.reshape(t.shape[:2] + (nc, A_CHUNK) + t.shape[3:])
        return jnp.moveaxis(t, 2, 0)

    qc, kc, vc, lic = chunks(q), chunks(k), chunks(v), chunks(li)
    bc = jnp.cumsum(chunks(lf), axis=-1)
    lower = jnp.tril(jnp.ones((A_CHUNK, A_CHUNK), dtype=bool))

    def step(carry, xs):
        c_mat, n_vec, m = carry
        qt, kt, vt, it, bt = xs
        dmat = jnp.where(lower, bt[..., :, None] - bt[..., None, :] + it[..., None, :], -jnp.inf)
        inter = bt + m[..., None]
        mt = jnp.maximum(inter, dmat.max(-1))
        w_inter = jnp.exp(inter - mt)
        sqk = jnp.einsum('bhtd,bhsd->bhts', qt, kt) * jnp.exp(dmat - mt[..., None])
        num = (w_inter[..., None] * jnp.einsum('bhtk,bhkv->bhtv', qt, c_mat)
               + jnp.einsum('bhts,bhsv->bhtv', sqk, vt))
        den = w_inter * jnp.einsum('bhtk,bhk->bht', qt, n_vec) + sqk.sum(-1)
        ht = num / jnp.maximum(jnp.abs(den), jnp.exp(-mt))[..., None]
        bl = bt[..., -1]
        g = bl[..., None] - bt + it
        m_new = jnp.maximum(bl + m, g.max(-1))
        wc = jnp.exp(bl + m - m_new)
        ws = jnp.exp(g - m_new[..., None])
        c_new = wc[..., None, None] * c_mat + jnp.einsum('bhs,bhsk,bhsv->bhkv', ws, kt, vt)
        n_new = wc[..., None] * n_vec + jnp.einsum('bhs,bhsk->bhk', ws, kt)
        return (c_new, n_new, m_new), ht

    init = (jnp.zeros((bsz, nh, d, d), F32), jnp.zeros((bsz, nh, d), F32), jnp.zeros((bsz, nh), F32))
    _, h = lax.scan(step, init, (qc, kc, vc, lic, bc))
    return jnp.moveaxis(h, 0, 2).reshape(bsz, nh, s, d)


def mlstm_branch(q, k, v, o, gates, conv_w, gate_bias, norm_g):
    bsz, s, _ = q.shape
    dt = q.dtype
    qk = jax.nn.silu(centred_dwconv(jnp.concatenate([q, k], axis=-1), conv_w))
    q, k = jnp.split(qk, 2, axis=-1)

    def heads(t):
        return t.reshape(bsz, s, A_HEADS, A_DH).transpose(0, 2, 1, 3).astype(F32)

    qh, kh, vh = heads(q), heads(k) * (A_DH ** -0.5), heads(v)
    g = (gates.astype(F32) + gate_bias.astype(F32)).reshape(bsz, s, 4, A_HEADS).transpose(2, 0, 3, 1)
    i_fwd, f_fwd, i_bwd, f_bwd = g[0], g[1], g[2], g[3]
    h_fwd = mlstm_scan(qh, kh, vh, i_fwd, jax.nn.log_sigmoid(f_fwd))
    flip = lambda t: jnp.flip(t, axis=2)
    h_bwd = flip(mlstm_scan(flip(qh), flip(kh), flip(vh), flip(i_bwd), flip(jax.nn.log_sigmoid(f_bwd))))
    h = h_fwd + h_bwd
    mu = jnp.mean(h, axis=-1, keepdims=True)
    var = jnp.mean(jnp.square(h - mu), axis=-1, keepdims=True)
    h = (h - mu) * lax.rsqrt(var + EPS)
    h = h.transpose(0, 2, 1, 3).reshape(bsz, s, A_WIDTH) * norm_g.astype(F32)
    return (h * jax.nn.sigmoid(o.astype(F32))).astype(dt)


def axial_rope_angles(s):
    t = jnp.arange(s)
    row = (t // GRID_W).astype(F32)
    col = (t % GRID_W).astype(F32)
    n_freq = B_DH // 4
    inv = ROPE_THETA ** (-jnp.arange(n_freq, dtype=F32) / n_freq)
    return row[:, None] * inv, col[:, None] * inv


def rotate_half_rope(x, ang):
    x1, x2 = jnp.split(x, 2, axis=-1)
    c = jnp.cos(ang)[None, :, None, :]
    sn = jnp.sin(ang)[None, :, None, :]
    return jnp.concatenate([x1 * c - x2 * sn, x2 * c + x1 * sn], axis=-1)


def axial_rope(x, ang_r, ang_c):
    xr, xc = jnp.split(x, 2, axis=-1)
    return jnp.concatenate([rotate_half_rope(xr, ang_r), rotate_half_rope(xc, ang_c)], axis=-1)


def gqa_branch(q, k, v, qn_g, kn_g):
    bsz, s, _ = q.shape
    dt = q.dtype
    q = rms_norm(q.reshape(bsz, s, B_HEADS, B_DH), qn_g).astype(F32)
    k = rms_norm(k.reshape(bsz, s, B_KV, B_DH), kn_g).astype(F32)
    v = v.reshape(bsz, s, B_KV, B_DH).astype(F32)
    ang_r, ang_c = axial_rope_angles(s)
    q = axial_rope(q, ang_r, ang_c).reshape(bsz, s, B_KV, B_HEADS // B_KV, B_DH) * (B_DH ** -0.5)
    k = axial_rope(k, ang_r, ang_c)

    def block(_, qb):
        sc = jnp.einsum('bqgrd,bkgd->bgrqk', qb, k)
        p = jax.nn.softmax(sc, axis=-1)
        return jnp.einsum('bgrqk,bkgd->bqgrd', p, v)

    o = sweep_query_blocks(block, q)
    return o.reshape(bsz, s, B_WIDTH).astype(dt)


def natten_indices(s):
    rows = s // GRID_W
    wr = min(C_WIN_R, rows)
    t = jnp.arange(s)
    r = t // GRID_W
    c = t % GRID_W
    rs = jnp.clip(r - wr // 2, 0, rows - wr)
    cs = jnp.clip(c - C_WIN_C // 2, 0, GRID_W - C_WIN_C)
    kr = rs[:, None, None] + jnp.arange(wr)[None, :, None]
    kc = cs[:, None, None] + jnp.arange(C_WIN_C)[None, None, :]
    shape = (s, wr, C_WIN_C)
    idx = jnp.broadcast_to(kr * GRID_W + kc, shape).reshape(s, wr * C_WIN_C)
    off_r = jnp.broadcast_to(kr - r[:, None, None] + (C_WIN_R - 1), shape).reshape(s, wr * C_WIN_C)
    off_c = jnp.broadcast_to(kc - c[:, None, None] + (C_WIN_C - 1), shape).reshape(s, wr * C_WIN_C)
    return idx, off_r, off_c


def natten_branch(q, k, v, rpb):
    bsz, s, _ = q.shape
    dt = q.dtype
    rows = s // GRID_W
    idx, off_r, off_c = natten_indices(s)
    nk = idx.shape[-1]
    bias = rpb.astype(F32)[:, off_r, off_c]
    q = q.reshape(bsz, rows, GRID_W, C_HEADS, C_DH).astype(F32) * (C_DH ** -0.5)
    k = k.reshape(bsz, s, C_HEADS, C_DH).astype(F32)
    v = v.reshape(bsz, s, C_HEADS, C_DH).astype(F32)
    idx_rows = idx.reshape(rows, GRID_W, nk)
    bias_rows = jnp.moveaxis(bias.reshape(C_HEADS, rows, GRID_W, nk), 1, 0)

    def row_block(args):
        qr, ir, br = args
        kg = jnp.take(k, ir, axis=1)
        vg = jnp.take(v, ir, axis=1)
        sc = jnp.einsum('bqhd,bqnhd->bhqn', qr, kg) + br[None]
        p = jax.nn.softmax(sc, axis=-1)
        return jnp.einsum('bhqn,bqnhd->bqhd', p, vg)

    o = lax.map(row_block, (jnp.moveaxis(q, 1, 0), idx_rows, bias_rows))
    return jnp.moveaxis(o, 0, 1).reshape(bsz, s, C_WIDTH).astype(dt)


def diff_branch(q, k, v, lq1, lk1, lq2, lk2, subln_g, lambda_init):
    bsz, s, _ = q.shape
    dt = q.dtype
    q = q.reshape(bsz, s, 2, D_HEADS, D_DH).astype(F32) * (D_DH ** -0.5)
    k = k.reshape(bsz, s, 2, D_HEADS, D_DH).astype(F32)
    v = v.reshape(bsz, s, D_HEADS, D_DV).astype(F32)
    lam = (jnp.exp(jnp.sum(lq1.astype(F32) * lk1.astype(F32)))
           - jnp.exp(jnp.sum(lq2.astype(F32) * lk2.astype(F32))) + lambda_init)
    slopes = 2.0 ** (-8.0 * jnp.arange(1, D_HEADS + 1, dtype=F32) / D_HEADS)
    kpos = jnp.arange(s, dtype=F32)

    def block(i, qb):
        qpos = (i * BLOCK_Q + jnp.arange(BLOCK_Q)).astype(F32)
        alibi = -slopes[:, None, None] * jnp.abs(qpos[:, None] - kpos[None, :])
        sc = jnp.einsum('bqchd,bkchd->bchqk', qb, k) + alibi[None, None]
        p = jax.nn.softmax(sc, axis=-1)
        a = p[:, 0] - lam * p[:, 1]
        return jnp.einsum('bhqk,bkhv->bqhv', a, v)

    o = sweep_query_blocks(block, q)
    o = rms_norm(o, subln_g) * (1.0 - lambda_init)
    return o.reshape(bsz, s, D_WIDTH).astype(dt)


def hybrid_layer(x, layer_idx, norm1_g, w_in, a_conv_w, a_gate_bias, a_norm_g, b_qnorm_g, b_knorm_g,
                 c_rpb, d_lq1, d_lk1, d_lq2, d_lk2, d_subln_g, w_up_a, w_up_b, w_up_c, w_up_d,
                 w_out, norm2_g, w_ffn_gate, w_ffn_up, w_ffn_down):
    bsz, s, _ = x.shape
    dt = x.dtype
    h = rms_norm(x, norm1_g)
    proj = jnp.einsum('bsd,de->bse', h, w_in)
    points = [sum(SPLIT_SIZES[:i + 1]) for i in range(len(SPLIT_SIZES) - 1)]
    (aq, ak, av, ao, ag, bq, bk, bv, cq, ck, cv, dq, dk, dv, gl) = jnp.split(proj, points, axis=-1)
    y_a = mlstm_branch(aq, ak, av, ao, ag, a_conv_w, a_gate_bias, a_norm_g)
    y_b = gqa_branch(bq, bk, bv, b_qnorm_g, b_knorm_g)
    y_c = natten_branch(cq, ck, cv, c_rpb)
    lambda_init = 0.8 - 0.6 * math.exp(-0.3 * layer_idx)
    y_d = diff_branch(dq, dk, dv, d_lq1, d_lk1, d_lq2, d_lk2, d_subln_g, lambda_init)
    g = jax.nn.sigmoid(gl.astype(F32)).reshape(bsz, s, N_BRANCH, D_MODEL)
    merged = (g[:, :, 0] * (y_a @ w_up_a) + g[:, :, 1] * (y_b @ w_up_b)
              + g[:, :, 2] * (y_c @ w_up_c) + g[:, :, 3] * (y_d @ w_up_d))
    x = x + merged.astype(dt) @ w_out
    h2 = rms_norm(x, norm2_g)
    ffn = (jax.nn.silu(h2 @ w_ffn_gate) * (h2 @ w_ffn_up)) @ w_ffn_down
    return x + ffn


def setup_inputs(seed: int = 0) -> dict:
    key = jax.random.key(seed)
    ks = jax.random.split(key, 26)
    L = DEPTH
    nrm = lambda k, shape, scale: jax.random.normal(k, shape, F32) * scale
    gain = lambda k, shape: 1.0 + 0.02 * jax.random.normal(k, shape, F32)
    fb = jnp.linspace(3.0, 6.0, A_HEADS, dtype=F32)
    zh = jnp.zeros((A_HEADS,), F32)
    gate_offset = jnp.concatenate([zh, fb, zh, fb])
    return {
        "x": jax.random.normal(ks[0], (BATCH, SEQ, D_MODEL), F32),
        "norm1_g": gain(ks[1], (L, D_MODEL)),
        "w_in": nrm(ks[2], (L, D_MODEL, D_IN), D_MODEL ** -0.5),
        "a_conv_w": nrm(ks[3], (L, A_CONV, 2 * A_WIDTH), A_CONV ** -0.5),
        "a_gate_bias": gate_offset[None] + nrm(ks[4], (L, 4 * A_HEADS), 0.1),
        "a_norm_g": gain(ks[5], (L, A_WIDTH)),
        "b_qnorm_g": gain(ks[6], (L, B_DH)),
        "b_knorm_g": gain(ks[7], (L, B_DH)),
        "c_rpb": nrm(ks[8], (L, C_HEADS, 2 * C_WIN_R - 1, 2 * C_WIN_C - 1), 0.02),
        "d_lambda_q1": nrm(ks[9], (L, D_DH), 0.1),
        "d_lambda_k1": nrm(ks[10], (L, D_DH), 0.1),
        "d_lambda_q2": nrm(ks[11], (L, D_DH), 0.1),
        "d_lambda_k2": nrm(ks[12], (L, D_DH), 0.1),
        "d_subln_g": gain(ks[13], (L, D_DV)),
        "w_up_a": nrm(ks[14], (L, A_WIDTH, D_MODEL), A_WIDTH ** -0.5),
        "w_up_b": nrm(ks[15], (L, B_WIDTH, D_MODEL), B_WIDTH ** -0.5),
        "w_up_c": nrm(ks[16], (L, C_WIDTH, D_MODEL), C_WIDTH ** -0.5),
        "w_up_d": nrm(ks[17], (L, D_WIDTH, D_MODEL), D_WIDTH ** -0.5),
        "w_out": nrm(ks[18], (L, D_MODEL, D_MODEL), D_MODEL ** -0.5),
        "norm2_g": gain(ks[19], (L, D_MODEL)),
        "w_ffn_gate": nrm(ks[20], (L, D_MODEL, D_FF), D_MODEL ** -0.5),
        "w_ffn_up": nrm(ks[21], (L, D_MODEL, D_FF), D_MODEL ** -0.5),
        "w_ffn_down": nrm(ks[22], (L, D_FF, D_MODEL), D_FF ** -0.5),
        "final_g": gain(ks[23], (D_MODEL,)),
    }


def reference(x, norm1_g, w_in, a_conv_w, a_gate_bias, a_norm_g, b_qnorm_g, b_knorm_g, c_rpb,
              d_lambda_q1, d_lambda_k1, d_lambda_q2, d_lambda_k2, d_subln_g,
              w_up_a, w_up_b, w_up_c, w_up_d, w_out, norm2_g, w_ffn_gate, w_ffn_up, w_ffn_down, final_g):
    for l in range(DEPTH):
        x = hybrid_layer(x, l, norm1_g[l], w_in[l], a_conv_w[l], a_gate_bias[l], a_norm_g[l],
                         b_qnorm_g[l], b_knorm_g[l], c_rpb[l],
                         d_lambda_q1[l], d_lambda_k1[l], d_lambda_q2[l], d_lambda_k2[l], d_subln_g[l],
                         w_up_a[l], w_up_b[l], w_up_c[l], w_up_d[l], w_out[l],
                         norm2_g[l], w_ffn_gate[l], w_ffn_up[l], w_ffn_down[l])
    return rms_norm(x, final_g)
```

```python
import math
import numpy as np
import ml_dtypes
import concourse.bass as bass
import concourse.mybir as mybir
from concourse.bass_utils import run_bass_kernel_spmd

F32 = mybir.dt.float32
BF16 = mybir.dt.bfloat16
AF = mybir.ActivationFunctionType
ALU = mybir.AluOpType
AX = mybir.AxisListType

D = 1024
SEQ = 4096
DEPTH = 2
GRID_W = 64
EPS = 1e-6
D_IN = 10000
D_FF = 2816
NEG = -30000.0

ENGS = ("pe", "act", "dve", "pool", "sp")
DMA_RING = 8


class Op:
    __slots__ = ("eng", "fn", "idx", "dma", "waits", "val", "need_inc", "clock", "ring", "semval")

    def __init__(self, eng, fn, idx, dma):
        self.eng = eng
        self.fn = fn
        self.idx = idx
        self.dma = dma
        self.waits = []
        self.need_inc = False
        self.ring = None
        self.semval = None


class Prog:
    def __init__(self, nc, same_engine_sync=True):
        self.nc = nc
        self.ops = {e: [] for e in ENGS}
        self.writer = {}
        self.readers = {}
        self.know = {e: {} for e in ENGS}
        self.dma_count = {e: 0 for e in ENGS}
        self.dma_ops = {e: [] for e in ENGS}
        self.same_engine_sync = same_engine_sync
        self.barrier_deps = {e: [] for e in ENGS}

    def add(self, eng, fn, reads=(), writes=(), dma=False):
        lst = self.ops[eng]
        op = Op(eng, fn, len(lst), dma)
        deps = []
        for k in reads:
            w = self.writer.get(k)
            if w is not None:
                deps.append(w)
        for k in writes:
            w = self.writer.get(k)
            if w is not None:
                deps.append(w)
            deps.extend(self.readers.get(k, ()))
        if self.barrier_deps[eng]:
            deps.extend(self.barrier_deps[eng])
            self.barrier_deps[eng] = []
        if dma:
            n = self.dma_count[eng]
            op.ring = n % DMA_RING
            if n >= DMA_RING:
                deps.append(self.dma_ops[eng][n - DMA_RING])
            self.dma_count[eng] = n + 1
            self.dma_ops[eng].append(op)
        know = self.know[eng]
        for d in deps:
            if d is op:
                continue
            if d.dma:
                key = ("dma", d.eng, d.ring)
                val = d.val
            else:
                if d.eng == eng and (eng == "pe" or not self.same_engine_sync):
                    continue
                key = d.eng
                val = d.idx
            if know.get(key, -1) >= val:
                continue
            op.waits.append(d)
            d.need_inc = True
            for k2, v2 in d.clock.items():
                if know.get(k2, -1) < v2:
                    know[k2] = v2
        if dma:
            op.val = (self.dma_count[eng] - 1) // DMA_RING
            ck = ("dma", eng, op.ring)
        else:
            op.val = op.idx
            ck = eng
        op.clock = dict(know)
        op.clock[ck] = op.val
        lst.append(op)
        for k in writes:
            self.writer[k] = op
            self.readers[k] = []
        for k in reads:
            if k not in writes:
                self.readers.setdefault(k, []).append(op)
        return op

    def wait_all(self, eng, keys):
        return self.add(eng, lambda e: None, reads=list(keys))

    def barrier(self):
        lasts = []
        for e in ENGS:
            if self.ops[e]:
                lasts.append(self.ops[e][-1])
            lasts.extend(self.dma_ops[e][-DMA_RING:])
        for e in ENGS:
            self.barrier_deps[e] = list(lasts)

    def emit(self):
        nc = self.nc
        from contextlib import ExitStack
        with ExitStack() as es:
            csem = {e: es.enter_context(nc.semaphore("c_" + e)) for e in ENGS}
            dsem = {(e, r): es.enter_context(nc.semaphore("d_%s_%d" % (e, r)))
                    for e in ENGS for r in range(DMA_RING) if self.dma_count[e] > 0}
            for e in ENGS:
                cnt = 0
                for op in self.ops[e]:
                    if not op.dma and op.need_inc:
                        cnt += 1
                        op.semval = cnt
            block = es.enter_context(nc.Block())

            def run(e, eng):
                for op in self.ops[e]:
                    for d in op.waits:
                        if d.dma:
                            eng.wait_ge(dsem[(d.eng, d.ring)], 16 * (d.val + 1))
                        else:
                            eng.wait_ge(csem[d.eng], d.semval)
                    ins = op.fn(eng)
                    if ins is None:
                        continue
                    if op.dma:
                        ins.then_inc(dsem[(e, op.ring)], 16)
                    elif op.need_inc:
                        ins.then_inc(csem[e], 1)

            @block.sync
            def _(eng):
                run("sp", eng)

            @block.scalar
            def _(eng):
                run("act", eng)

            @block.vector
            def _(eng):
                run("dve", eng)

            @block.gpsimd
            def _(eng):
                run("pool", eng)

            @block.tensor
            def _(eng):
                run("pe", eng)


class Pipe:
    def __init__(self, offs):
        self.offs = offs
        self.items = []

    def push(self, *fns):
        self.items.append(fns)

    def flush(self):
        n = len(self.items)
        m = max(self.offs)
        for t in range(n + m):
            for j, o in enumerate(self.offs):
                i = t - o
                if 0 <= i < n and self.items[i][j] is not None:
                    self.items[i][j]()
        self.items = []


SB_BASE = 16640
SB_LIMIT = 229376


class Arena:
    def __init__(self, nc):
        self.nc = nc
        self.off = SB_BASE
        self.n = 0

    def alloc(self, name, shape, dt):
        esz = 4 if dt == F32 else 2
        nbytes = int(np.prod(shape[1:])) * esz
        nbytes = (nbytes + 63) // 64 * 64
        assert self.off + nbytes <= SB_LIMIT, "SBUF overflow at %s: %d" % (name, self.off + nbytes)
        self.n += 1
        t = self.nc.alloc_sbuf_tensor_at("%s_%d" % (name, self.n), list(shape), dt, offset=self.off)
        self.off += nbytes
        return t

    def mark(self):
        return self.off

    def reset(self, to=SB_BASE):
        self.off = to


def bc_ap(ap, dims):
    return bass.AP(tensor=ap.tensor, offset=ap.offset, ap=[list(ap.ap[0])] + [list(d) for d in dims])


def build_program(S=SEQ, depth=DEPTH, debug=False, phases="ABCDEFGZ"):
    NT = S // 128
    NB = S // 512
    rows = S // GRID_W
    nc = bass.Bass("TRN2", target_bir_lowering=False)
    p = Prog(nc)
    sb = Arena(nc)
    L = depth

    def din(name, shape, dt=F32):
        return nc.dram_tensor(name, list(shape), dt, kind="ExternalInput")

    def dscr(name, shape, dt=BF16):
        return nc.dram_tensor(name, list(shape), dt, kind=("ExternalOutput" if debug else "Internal"))

    x_d = din("x", [S, D])
    w_in_d = din("w_in", [L, D, D_IN])
    w_up_d = [din("w_up_%s" % c, [L, 512, D]) for c in "abcd"]
    w_out_d = din("w_out", [L, D, D])
    w_fg_d = din("w_ffn_gate", [L, D, D_FF])
    w_fu_d = din("w_ffn_up", [L, D, D_FF])
    w_fd_d = din("w_ffn_down", [L, D_FF, D])
    n1g_d = din("norm1_g_r", [L, 128, D])
    n2g_d = din("norm2_g_r", [L, 128, D])
    fg_d = din("final_g_r", [128, D])
    convw_d = din("conv_w_r", [L, 128, 8, 3])
    gbias_d = din("gbias_r", [L, 128, 16])
    ang_d = din("anorm_g_r", [L, 128, 512])
    bqg_d = din("bqg_r", [L, 128, 64])
    bkg_d = din("bkg_r", [L, 128, 64])
    dl_d = din("dl_r", [L, 128, 4, 64])
    dsub_d = din("dsub_r", [L, 128, 128])
    natb_d = din("natb", [L, 5, 128, 5, 8, 128])
    ident_d = din("ident", [128, 128], BF16)
    cos_d = din("cos64", [S, 64])
    sin_d = din("sin64", [S, 64])
    tz_d = din("tz", [128, 2 * S - 128])
    maskf_d = din("maskf", [128, 128])
    maskb_d = din("maskb", [128, 128])
    ones_d = din("ones", [128, 128])
    sel_d = din("sel", [65, 64])
    out_d = nc.dram_tensor("out", [S, D], F32, kind="ExternalOutput")

    xres_d = dscr("xres", [S, D], F32)
    hT_d = dscr("hT_s", [D, S])
    aqk_d = dscr("aqk_s", [1024, S])
    av1_d = dscr("av1_s", [S, 4, 129])
    ao_d = dscr("ao_s", [S, 512])
    ag_d = dscr("ag_s", [S, 16], F32)
    bqT_d = dscr("bqT_s", [512, S])
    bkT_d = dscr("bkT_s", [128, S])
    bv1_d = dscr("bv1_s", [S, 2, 65])
    cqT_d = dscr("cqT_s", [512, S])
    ckT_d = dscr("ckT_s", [512, S])
    cv1_d = dscr("cv1_s", [S, 8, 65])
    dqT_d = dscr("dqT_s", [512, S])
    dkT_d = dscr("dkT_s", [512, S])
    dv1_d = dscr("dv1_s", [S, 4, 129])
    yT_d = dscr("yT_s", [2048, S])

    psb = [nc.alloc_psum_tensor("psb%d" % i, [128, 512], F32) for i in range(8)]
    psb16 = [b.bitcast(BF16) for b in psb]

    def PK(i):
        return ("ps", i)

    def dma(eng, out, in_, reads=(), writes=(), **kw):
        return p.add(eng, lambda e: e.dma_start(out=out, in_=in_, **kw), reads=reads, writes=writes, dma=True)

    def new_phase():
        p.barrier()
        sb.reset()

    cnt = [0]

    def uid():
        cnt[0] += 1
        return cnt[0]

    ident = sb.alloc("ident", [128, 128], BF16)
    dma("sp", ident[:], ident_d.ap(), writes=["ident"])
    base_mark = sb.mark()

    def phase_reset():
        p.barrier()
        sb.reset(base_mark)

    evac_rr = [0]

    def evac_engine():
        evac_rr[0] += 1
        return "act" if evac_rr[0] % 2 else "dve"

    def copy_op(eng, out, in_, reads, writes, scale=None):
        if eng == "act":
            if scale is None:
                return p.add("act", lambda e: e.copy(out=out, in_=in_), reads=reads, writes=writes)
            return p.add("act", lambda e: e.mul(out=out, in_=in_, mul=scale), reads=reads, writes=writes)
        else:
            if scale is None:
                return p.add(eng, lambda e: e.tensor_copy(out=out, in_=in_), reads=reads, writes=writes)
            return p.add(eng, lambda e: e.tensor_scalar(out=out, in0=in_, scalar1=scale, scalar2=None, op0=ALU.mult),
                         reads=reads, writes=writes)

    def rstd_ops(v, key, n_scale, eps=EPS):
        p.add("dve", lambda e: e.tensor_scalar(out=v, in0=v, scalar1=n_scale, scalar2=eps, op0=ALU.mult, op1=ALU.add),
              reads=[key], writes=[key])
        p.add("act", lambda e: e.activation(out=v, in_=v, func=AF.Ln), reads=[key], writes=[key])
        p.add("act", lambda e: e.activation(out=v, in_=v, func=AF.Exp, scale=-0.5), reads=[key], writes=[key])

    stage_rr = [0]

    def load_cast(dst_ap_fn, src_ap_fn, nchunk, ncols, stage_f, tag, cols_per=512):
        for c0 in range(0, ncols, cols_per):
            c1 = min(ncols, c0 + cols_per)
            i = stage_rr[0]
            stage_rr[0] += 1
            st = stage_f[i % len(stage_f)]
            sk = ("stage", i % len(stage_f))
            dma("sp", st[:, 0:nchunk, 0:c1 - c0], src_ap_fn(c0, c1), writes=[sk])
            eng = "pool" if i % 2 == 0 else "dve"
            p.add(eng, lambda e, st=st, c0=c0, c1=c1: e.tensor_copy(out=dst_ap_fn(c0, c1), in_=st[:, 0:nchunk, 0:c1 - c0]),
                  reads=[sk], writes=[("w", tag, c0)])

    def norm_tile(xt, xk, gt, gk, sqj, ss, ssk, hb, hbk, ptr_i, dst_ap, dst_key):
        p.add("dve", lambda e: e.memset(ss, 0.0), writes=[ssk])
        p.add("act", lambda e: e.activation(out=sqj, in_=xt, func=AF.Square, accum_out=ss), reads=[xk], writes=[ssk, "sqj"])
        rstd_ops(ss, ssk, 1.0 / D)
        p.add("dve", lambda e: e.scalar_tensor_tensor(out=hb, in0=xt, scalar=ss, in1=gt, op0=ALU.mult, op1=ALU.mult),
              reads=[xk, ssk, gk], writes=[hbk])
        for c in range(8):
            p.add("pe", lambda e, c=c: e.transpose(out=psb16[ptr_i][:, c * 128:(c + 1) * 128], in_=hb[:, c * 128:(c + 1) * 128],
                                                   identity=ident[:]), reads=[hbk], writes=[PK(ptr_i)])
        copy_op("act", dst_ap, psb16[ptr_i][:, 0:1024].rearrange("p (c t) -> p c t", c=8), [PK(ptr_i)], [dst_key])

    for l in range(L):
        xsrc_d = x_d if l == 0 else xres_d
        lambda_init = 0.8 - 0.6 * math.exp(-0.3 * l)

        def _phA(l=l, xsrc_d=xsrc_d, lambda_init=lambda_init):
            phase_reset()
            hT = sb.alloc("hT", [128, 8, S], BF16)
            g1 = sb.alloc("g1", [128, D], F32)
            dma("sp", g1[:], n1g_d[l], writes=["g1"])
            xb = [sb.alloc("xb", [128, D], F32) for _ in range(2)]
            sqj = sb.alloc("sqj", [128, D], F32)
            ssb = [sb.alloc("ss", [128, 1], F32) for _ in range(2)]
            hbb = [sb.alloc("hb", [128, D], BF16) for _ in range(2)]
            mA = sb.mark()
            for t in range(NT):
                xt = xb[t % 2]
                dma("sp", xt[:], xsrc_d[t * 128:(t + 1) * 128, :], writes=[("xb", t % 2)])
                norm_tile(xt[:], ("xb", t % 2), g1[:], "g1", sqj[:], ssb[t % 2][:], ("ss", t % 2), hbb[t % 2][:], ("hb", t % 2),
                          6 + t % 2, hT[:, :, t * 128:(t + 1) * 128], ("hT", t))
            p.barrier()
            dma("pool", hT_d.ap().rearrange("(c p) s -> p c s", p=128), hT[:], writes=["hT_d"])

            wf = [sb.alloc("wf", [128, 8, 512], F32) for _ in range(2)]
            wb = [sb.alloc("wb", [128, 8, 512], BF16) for _ in range(2)]
            stgf = [sb.alloc("stgf", [128, S + 2], F32) for _ in range(2)]
            stgb = [sb.alloc("stgb", [128, S], BF16) for _ in range(2)]
            cw = sb.alloc("cw", [128, 8, 3], F32)
            ctmp = sb.alloc("ctmp", [128, S], F32)
            dma("sp", cw[:], convw_d[l], writes=["cw"])
            for i in range(2):
                p.add("pool", lambda e, i=i: e.memset(stgf[i][:], 0.0), writes=[("stgf", i, b) for b in range(NB)])
            groups = [("aq", 0, aqk_d, 0, None), ("ak", 512, aqk_d, 512, None),
                      ("cq", 2832, cqT_d, 0, 0.125), ("ck", 3344, ckT_d, 0, None),
                      ("dq", 4368, dqT_d, 0, 0.125), ("dk", 4880, dkT_d, 0, None)]
            cidx = 0
            bank = 0
            for gi, (gname, col0, dst_d, drow0, scale) in enumerate(groups):
                wfi, wbi = wf[gi % 2], wb[gi % 2]
                dma("sp", wfi[:], w_in_d[l, :, col0:col0 + 512].rearrange("(c p) n -> p c n", p=128), writes=[("wf", gi % 2)])
                p.add("pool", lambda e, wfi=wfi, wbi=wbi: e.tensor_copy(out=wbi[:], in_=wfi[:]),
                      reads=[("wf", gi % 2)], writes=[("wb", gi % 2)])
                is_a = gname in ("aq", "ak")
                for cc in range(4):
                    si = cidx % 2
                    cidx += 1
                    for b in range(NB):
                        bk = bank % 6
                        bank += 1
                        for k in range(8):
                            p.add("pe", lambda e, k=k, cc=cc, b=b, bk=bk, wbi=wbi: e.matmul(
                                psb[bk][:], lhsT=wbi[:, k, cc * 128:(cc + 1) * 128], rhs=hT[:, k, b * 512:(b + 1) * 512],
                                start=(k == 0), stop=(k == 7)), reads=[("wb", gi % 2)], writes=[PK(bk)])
                        if is_a:
                            copy_op(evac_engine(), stgf[si][:, 1 + b * 512:1 + (b + 1) * 512], psb[bk][:], [PK(bk)], [("stgf", si, b)])
                        else:
                            copy_op(evac_engine(), stgb[si][:, b * 512:(b + 1) * 512], psb[bk][:], [PK(bk)], [("stgb", si, b)], scale=scale)
                    if is_a:
                        ch = (col0 // 128) + cc
                        sf = stgf[si]
                        p.add("dve", lambda e, sf=sf, ch=ch: e.tensor_scalar(out=ctmp[:], in0=sf[:, 1:S + 1], scalar1=cw[:, ch, 1:2],
                                                                             scalar2=None, op0=ALU.mult),
                              reads=[("stgf", si, b_) for b_ in range(NB)] + ["cw"], writes=["ctmp"])
                        p.add("dve", lambda e, sf=sf, ch=ch: e.scalar_tensor_tensor(out=ctmp[:], in0=sf[:, 0:S], scalar=cw[:, ch, 0:1],
                                                                                    in1=ctmp[:], op0=ALU.mult, op1=ALU.add),
                              reads=[("stgf", si, b_) for b_ in range(NB)] + ["cw"], writes=["ctmp"])
                        p.add("dve", lambda e, sf=sf, ch=ch: e.scalar_tensor_tensor(out=ctmp[:], in0=sf[:, 2:S + 2], scalar=cw[:, ch, 2:3],
                                                                                    in1=ctmp[:], op0=ALU.mult, op1=ALU.add),
                              reads=[("stgf", si, b_) for b_ in range(NB)] + ["cw"], writes=["ctmp"])
                        p.add("act", lambda e, si=si: e.activation(out=stgb[si][:], in_=ctmp[:], func=AF.Silu),
                              reads=["ctmp"], writes=[("stgb", si, b_) for b_ in range(NB)])
                    r0 = drow0 + cc * 128
                    dma("pool", dst_d[r0:r0 + 128, :], stgb[si][:], reads=[("stgb", si, b_) for b_ in range(NB)], writes=[(gname, cc)])

            p.barrier()
            sb.reset(mA)
            wf = [sb.alloc("wf", [128, 8, 512], F32) for _ in range(2)]
            wb = [sb.alloc("wb", [128, 8, 512], BF16) for _ in range(2)]
            cos_s = sb.alloc("cos", [128, NT, 64], F32)
            sin_s = sb.alloc("sin", [128, NT, 64], F32)
            dma("sp", cos_s[:], cos_d.ap().rearrange("(t p) f -> p t f", p=128), writes=["cos"])
            dma("sp", sin_s[:], sin_d.ap().rearrange("(t p) f -> p t f", p=128), writes=["sin"])
            gb = sb.alloc("gb", [128, 16], F32)
            dma("sp", gb[:], gbias_d[l], writes=["gb"])
            bqg = sb.alloc("bqg", [128, 64], F32)
            bkg = sb.alloc("bkg", [128, 64], F32)
            dma("sp", bqg[:], bqg_d[l], writes=["bqg"])
            dma("sp", bkg[:], bkg_d[l], writes=["bkg"])
            p.add("dve", lambda e: e.tensor_scalar(out=bqg[:], in0=bqg[:], scalar1=0.125, scalar2=None, op0=ALU.mult),
                  reads=["bqg"], writes=["bqg"])
            st129 = [sb.alloc("st129", [128, 4, 129], BF16) for _ in range(2)]
            st65 = [sb.alloc("st65", [128, 8, 65], BF16) for _ in range(2)]
            stb = [sb.alloc("stb", [128, 512], BF16) for _ in range(2)]
            stg16 = [sb.alloc("stg16", [128, 16], F32) for _ in range(2)]
            for i in range(2):
                p.add("pool", lambda e, i=i: e.memset(st129[i][:], 1.0), writes=[("st129", i)])
                p.add("pool", lambda e, i=i: e.memset(st65[i][:], 1.0), writes=[("st65", i)])
            qf = sb.alloc("qf", [128, 512], F32)
            qn = sb.alloc("qn", [128, 512], F32)
            t1 = sb.alloc("t1", [128, 512], F32)
            t2 = sb.alloc("t2", [128, 512], F32)
            ssh = sb.alloc("ssh", [128, 8], F32)
            qrb = [sb.alloc("qrb", [128, 512], BF16) for _ in range(2)]
            bqT_s = sb.alloc("bqT_s", [128, 4, S], BF16)
            bkT_s = sb.alloc("bkT_s", [128, S], BF16)

            def rope_norm(ps_ap, psk, nh, gtile, t, outb, outk):
                W = nh * 64
                qf_, qn_, t1_, t2_ = qf[:, 0:W], qn[:, 0:W], t1[:, 0:W], t2[:, 0:W]
                p.add("act", lambda e: e.copy(out=qf_, in_=ps_ap), reads=[psk], writes=["qf"])
                p.add("dve", lambda e: e.tensor_tensor(out=t1_, in0=qf_, in1=qf_, op=ALU.mult), reads=["qf"], writes=["t1"])
                p.add("dve", lambda e: e.tensor_reduce(out=ssh[:, 0:nh], in_=t1_.rearrange("p (h d) -> p h d", h=nh), axis=AX.X, op=ALU.add),
                      reads=["t1"], writes=["ssh"])
                rstd_ops(ssh[:, 0:nh], "ssh", 1.0 / 64)
                p.add("dve", lambda e: e.tensor_tensor(out=qn_.rearrange("p (h d) -> p h d", h=nh), in0=qf_.rearrange("p (h d) -> p h d", h=nh),
                                                       in1=bc_ap(ssh[:, 0:nh], [[1, nh], [0, 64]]), op=ALU.mult),
                      reads=["qf", "ssh"], writes=["qn"])
                p.add("pool", lambda e: e.tensor_tensor(out=qn_.rearrange("p (h d) -> p h d", h=nh), in0=qn_.rearrange("p (h d) -> p h d", h=nh),
                                                        in1=bc_ap(gtile[:], [[0, nh], [1, 64]]), op=ALU.mult),
                      reads=["qn", "bqg", "bkg"], writes=["qn"])
                cos_b = bc_ap(cos_s[:, t, :], [[0, nh], [1, 64]])
                p.add("dve", lambda e: e.tensor_tensor(out=t1_.rearrange("p (h d) -> p h d", h=nh), in0=qn_.rearrange("p (h d) -> p h d", h=nh),
                                                       in1=cos_b, op=ALU.mult), reads=["qn", "cos"], writes=["t1"])
                sin_lo = bc_ap(sin_s[:, t, 0:16], [[0, nh], [32, 2], [1, 16]])
                sin_hi = bc_ap(sin_s[:, t, 16:32], [[0, nh], [32, 2], [1, 16]])
                x4 = qn_.rearrange("p (h r f) -> p h r f", h=nh, r=2)
                o4 = t2_.rearrange("p (h r f) -> p h r f", h=nh, r=2)
                p.add("pool", lambda e: e.tensor_tensor(out=o4[:, :, :, 0:16], in0=x4[:, :, :, 16:32], in1=sin_lo, op=ALU.mult),
                      reads=["qn", "sin"], writes=["t2"])
                p.add("pool", lambda e: e.tensor_tensor(out=o4[:, :, :, 16:32], in0=x4[:, :, :, 0:16], in1=sin_hi, op=ALU.mult),
                      reads=["qn", "sin"], writes=["t2"])
                p.add("dve", lambda e: e.tensor_tensor(out=outb, in0=t1_, in1=t2_, op=ALU.add), reads=["t1", "t2"], writes=[outk])

            tm_groups = [("av", 1024, 512), ("ao", 1536, 512), ("ag", 2048, 16), ("bq", 2064, 512),
                         ("bkv", 2576, 256), ("cv", 3856, 512), ("dv", 5392, 512)]
            for gi, (gname, col0, ncol) in enumerate(tm_groups):
                wfi, wbi = wf[gi % 2], wb[gi % 2]
                dma("sp", wfi[:, :, 0:ncol], w_in_d[l, :, col0:col0 + ncol].rearrange("(c p) n -> p c n", p=128),
                    writes=[("wf", gi % 2)])
                p.add("pool", lambda e, wfi=wfi, wbi=wbi, ncol=ncol: e.tensor_copy(out=wbi[:, :, 0:ncol], in_=wfi[:, :, 0:ncol]),
                      reads=[("wf", gi % 2)], writes=[("wb", gi % 2)])
                for t in range(NT):
                    bk = t % 4
                    si = t % 2
                    for k in range(8):
                        p.add("pe", lambda e, k=k, t=t, bk=bk, wbi=wbi, ncol=ncol: e.matmul(
                            psb[bk][:, 0:ncol], lhsT=hT[:, k, t * 128:(t + 1) * 128], rhs=wbi[:, k, 0:ncol],
                            start=(k == 0), stop=(k == 7)), reads=[("wb", gi % 2)], writes=[PK(bk)])
                    tok = slice(t * 128, (t + 1) * 128)
                    if gname == "av" or gname == "dv":
                        dst = av1_d if gname == "av" else dv1_d
                        copy_op(evac_engine(), st129[si][:, :, 0:128], psb[bk][:, 0:512].rearrange("p (h d) -> p h d", h=4),
                                [PK(bk)], [("st129", si)])
                        dma("pool", dst[tok], st129[si][:], reads=[("st129", si)], writes=[(gname, t)])
                    elif gname == "cv":
                        copy_op(evac_engine(), st65[si][:, :, 0:64], psb[bk][:, 0:512].rearrange("p (h d) -> p h d", h=8),
                                [PK(bk)], [("st65", si)])
                        dma("pool", cv1_d[tok], st65[si][:], reads=[("st65", si)], writes=[(gname, t)])
                    elif gname == "ao":
                        p.add("act", lambda e, bk=bk, si=si: e.activation(out=stb[si][:], in_=psb[bk][:], func=AF.Sigmoid),
                              reads=[PK(bk)], writes=[("stb", si)])
                        dma("pool", ao_d[tok, :], stb[si][:], reads=[("stb", si)], writes=[(gname, t)])
                    elif gname == "ag":
                        p.add("dve", lambda e, bk=bk, si=si: e.tensor_tensor(out=stg16[si][:], in0=psb[bk][:, 0:16], in1=gb[:], op=ALU.add),
                              reads=[PK(bk), "gb"], writes=[("stg16", si)])
                        dma("pool", ag_d[tok, :], stg16[si][:], reads=[("stg16", si)], writes=[(gname, t)])
                    elif gname == "bq":
                        rope_norm(psb[bk][:, 0:512], PK(bk), 8, bqg, t, qrb[si][:], ("qrb", si))
                        for c in range(4):
                            p.add("pe", lambda e, c=c, si=si: e.transpose(out=psb16[6 + si][:, c * 128:(c + 1) * 128],
                                                                          in_=qrb[si][:, c * 128:(c + 1) * 128], identity=ident[:]),
                                  reads=[("qrb", si)], writes=[PK(6 + si)])
                        copy_op("act", bqT_s[:, :, t * 128:(t + 1) * 128], psb16[6 + si][:, 0:512].rearrange("p (c t) -> p c t", c=4),
                                [PK(6 + si)], [("bqT_s", t)])
                    elif gname == "bkv":
                        rope_norm(psb[bk][:, 0:128], PK(bk), 2, bkg, t, qrb[si][:, 0:128], ("qrb", si))
                        p.add("pe", lambda e, si=si: e.transpose(out=psb16[6 + si][:, 0:128], in_=qrb[si][:, 0:128], identity=ident[:]),
                              reads=[("qrb", si)], writes=[PK(6 + si)])
                        copy_op("act", bkT_s[:, t * 128:(t + 1) * 128], psb16[6 + si][:, 0:128], [PK(6 + si)], [("bkT_s", t)])
                        copy_op("dve", st65[si][:, 0:2, 0:64], psb[bk][:, 128:256].rearrange("p (h d) -> p h d", h=2),
                                [PK(bk)], [("st65", si)])
                        dma("pool", bv1_d[tok], st65[si][:, 0:2, :], reads=[("st65", si)], writes=[("bv", t)])
                if gname == "bq":
                    p.barrier()
                    dma("pool", bqT_d.ap().rearrange("(c p) s -> p c s", p=128), bqT_s[:], writes=["bqT_d"])
                if gname == "bkv":
                    p.barrier()
                    dma("pool", bkT_d.ap(), bkT_s[:], writes=["bkT_d"])

        if "A" in phases:
            _phA()

        def _phB(l=l, xsrc_d=xsrc_d, lambda_init=lambda_init):
            phase_reset()
            qT = sb.alloc("qT", [128, 4, S], BF16)
            kT2 = sb.alloc("kT2", [128, 2, S], BF16)
            v1 = sb.alloc("v1", [128, NT, 2, 65], BF16)
            sel = sb.alloc("sel", [65, 64], F32)
            dma("sp", qT[:], bqT_d.ap().rearrange("(c p) s -> p c s", p=128), writes=["qT"])
            for g in range(2):
                dma("sp", kT2[0:64, g, :], bkT_d[g * 64:(g + 1) * 64, :], writes=[("kT2", g, 0)])
                dma("sp", kT2[64:128, g, :], bkT_d[g * 64:(g + 1) * 64, :], writes=[("kT2", g, 1)])
            dma("sp", v1[:], bv1_d.ap().rearrange("(t p) g e -> p t g e", p=128), writes=["v1"])
            dma("sp", sel[:], sel_d.ap(), writes=["sel"])
            pT = [sb.alloc("pT", [128, 512], BF16) for _ in range(3)]
            osb = [sb.alloc("osb", [65, 512], F32) for _ in range(2)]
            rb = [sb.alloc("rb", [64, 512], F32) for _ in range(2)]
            ybs = [sb.alloc("ybs", [64, 512], BF16) for _ in range(2)]
            p.barrier()
            pipe = Pipe([0, 1, 2])
            it = 0
            for h in range(8):
                g = h // 4
                j = h // 2
                pr = slice((h % 2) * 64, (h % 2) * 64 + 64)
                for qb in range(NB):
                    ob = 3 + (it // NT) % 2
                    for kt in range(NT):
                        sbk = it % 3
                        it += 1

                        def s0(sbk=sbk, g=g, j=j, pr=pr, kt=kt, qb=qb):
                            p.add("pe", lambda e: e.matmul(psb[sbk][:], lhsT=kT2[pr, g, kt * 128:(kt + 1) * 128],
                                                           rhs=qT[pr, j, qb * 512:(qb + 1) * 512], start=True, stop=True),
                                  writes=[PK(sbk)])

                        def s1(sbk=sbk):
                            p.add("act", lambda e: e.activation(out=pT[sbk][:], in_=psb[sbk][:], func=AF.Exp),
                                  reads=[PK(sbk)], writes=[("pT", sbk)])

                        def s2(sbk=sbk, ob=ob, kt=kt, g=g, h=h, qb=qb):
                            p.add("pe", lambda e: e.matmul(psb[ob][0:65, :], lhsT=v1[:, kt, g, :], rhs=pT[sbk][:],
                                                           start=(kt == 0), stop=(kt == NT - 1)),
                                  reads=[("pT", sbk)], writes=[PK(ob)])
                            if kt == NT - 1:
                                oi = ob - 3
                                p.add("act", lambda e: e.copy(out=osb[oi][:], in_=psb[ob][0:65, :]), reads=[PK(ob)], writes=[("osb", oi)])
                                p.add("pe", lambda e: e.matmul(psb[5][0:64, :], lhsT=sel[:], rhs=osb[oi][:], start=True, stop=True),
                                      reads=[("osb", oi), "sel"], writes=[PK(5)])
                                p.add("dve", lambda e: e.reciprocal(out=rb[oi][:], in_=psb[5][0:64, :]), reads=[PK(5)], writes=[("rb", oi)])
                                p.add("dve", lambda e: e.tensor_tensor(out=ybs[oi][:], in0=osb[oi][0:64, :], in1=rb[oi][:], op=ALU.mult),
                                      reads=[("osb", oi), ("rb", oi)], writes=[("ybs", oi)])
                                r0 = 512 + h * 64
                                dma("pool", yT_d[r0:r0 + 64, qb * 512:(qb + 1) * 512], ybs[oi][:], reads=[("ybs", oi)],
                                    writes=[("yb", h, qb)])
                        pipe.push(s0, s1, s2)
            pipe.flush()

        if "B" in phases:
            _phB()

        def _phC(l=l, xsrc_d=xsrc_d, lambda_init=lambda_init):
            phase_reset()
            qT = sb.alloc("qT", [128, 4, S], BF16)
            kT = sb.alloc("kT", [128, 4, S], BF16)
            v1 = sb.alloc("v1", [128, NT, 8, 65], BF16)
            sel = sb.alloc("sel", [65, 64], F32)
            nb_int = sb.alloc("nb_int", [128, 5, 8, 128], F32)
            nb_edge = sb.alloc("nb_edge", [128, 5, 8, 128], F32)
            dma("sp", qT[:], cqT_d.ap().rearrange("(c p) s -> p c s", p=128), writes=["qT"])
            dma("sp", kT[:], ckT_d.ap().rearrange("(c p) s -> p c s", p=128), writes=["kT"])
            dma("sp", v1[:], cv1_d.ap().rearrange("(t p) g e -> p t g e", p=128), writes=["v1"])
            dma("sp", sel[:], sel_d.ap(), writes=["sel"])
            dma("sp", nb_int[:], natb_d[l, 2], writes=["nb_int"])
            pT = [sb.alloc("pT", [128, 512], BF16) for _ in range(3)]
            tmpf = [sb.alloc("tmpf", [128, 512], F32) for _ in range(3)]
            osb = [sb.alloc("osb", [65, 512], F32) for _ in range(2)]
            rb = [sb.alloc("rb", [64, 512], F32) for _ in range(2)]
            ybs = [sb.alloc("ybs", [64, 512], BF16) for _ in range(2)]
            p.barrier()
            pipe = Pipe([0, 1, 2])
            it = 0
            oit = 0
            for i in range(NT):
                pat = 0 if i == 0 else 1 if i == 1 else 3 if i == NT - 2 else 4 if i == NT - 1 else 2
                kb0 = min(max(i - 2, 0), NT - 5)
                if pat != 2:
                    dma("sp", nb_edge[:], natb_d[l, pat], writes=["nb_edge"])
                nbt = nb_int if pat == 2 else nb_edge
                nbk = "nb_int" if pat == 2 else "nb_edge"
                for hg in range(2):
                    ob = 3 + oit % 2
                    oit += 1
                    for kk in range(5):
                        kt = kb0 + kk
                        sbk = it % 3
                        sbanks = [(0, 1), (2, 6)][it % 2]
                        it += 1

                        def s0(sbanks=sbanks, hg=hg, kt=kt, i=i):
                            for hh in range(4):
                                h = hg * 4 + hh
                                pr = slice((h % 2) * 64, (h % 2) * 64 + 64)
                                bkx = sbanks[h % 2]
                                p.add("pe", lambda e, hh=hh, h=h, pr=pr, bkx=bkx: e.matmul(
                                    psb[bkx][:, (hh // 2) * 128:(hh // 2 + 1) * 128], lhsT=kT[pr, h // 2, kt * 128:(kt + 1) * 128],
                                    rhs=qT[pr, h // 2, i * 128:(i + 1) * 128], start=True, stop=True), writes=[PK(bkx)])

                        def s1(sbk=sbk, sbanks=sbanks, kk=kk, hg=hg, nbt=nbt, nbk=nbk):
                            for par in range(2):
                                bkx = sbanks[par]
                                p.add("dve", lambda e, par=par, bkx=bkx: e.tensor_tensor(
                                    out=tmpf[sbk][:, par * 256:(par + 1) * 256].rearrange("p (h q) -> p h q", h=2),
                                    in0=psb[bkx][:, 0:256].rearrange("p (h q) -> p h q", h=2),
                                    in1=bc_ap(nbt[:, kk, hg * 4 + par, :], [[256, 2], [1, 128]]), op=ALU.add),
                                    reads=[PK(bkx), nbk], writes=[("tmpf", sbk)])
                            p.add("act", lambda e: e.activation(out=pT[sbk][:], in_=tmpf[sbk][:], func=AF.Exp),
                                  reads=[("tmpf", sbk)], writes=[("pT", sbk)])

                        def s2(sbk=sbk, ob=ob, kk=kk, kt=kt, hg=hg, i=i):
                            for hh in range(4):
                                h = hg * 4 + hh
                                p.add("pe", lambda e, hh=hh, h=h: e.matmul(
                                    psb[ob][0:65, hh * 128:(hh + 1) * 128], lhsT=v1[:, kt, h, :],
                                    rhs=pT[sbk][:, (hh % 2) * 256 + (hh // 2) * 128:(hh % 2) * 256 + (hh // 2 + 1) * 128],
                                    start=(kk == 0 and hh == 0), stop=(kk == 4), skip_group_check=True), reads=[("pT", sbk)], writes=[PK(ob)])
                            if kk == 4:
                                oi = ob - 3
                                p.add("act", lambda e: e.copy(out=osb[oi][:], in_=psb[ob][0:65, :]), reads=[PK(ob)], writes=[("osb", oi)])
                                p.add("pe", lambda e: e.matmul(psb[5][0:64, :], lhsT=sel[:], rhs=osb[oi][:], start=True, stop=True),
                                      reads=[("osb", oi), "sel"], writes=[PK(5)])
                                p.add("dve", lambda e: e.reciprocal(out=rb[oi][:], in_=psb[5][0:64, :]), reads=[PK(5)], writes=[("rb", oi)])
                                p.add("dve", lambda e: e.tensor_tensor(out=ybs[oi][:], in0=osb[oi][0:64, :], in1=rb[oi][:], op=ALU.mult),
                                      reads=[("osb", oi), ("rb", oi)], writes=[("ybs", oi)])
                                for hh in range(4):
                                    r0 = 1024 + (hg * 4 + hh) * 64
                                    dma("pool", yT_d[r0:r0 + 64, i * 128:(i + 1) * 128], ybs[oi][:, hh * 128:(hh + 1) * 128],
                                        reads=[("ybs", oi)], writes=[("yc", hg * 4 + hh, i)])
                        pipe.push(s0, s1, s2)
                if pat != 2:
                    pipe.flush()
            pipe.flush()

        if "C" in phases:
            _phC()

        def _phD(l=l, xsrc_d=xsrc_d, lambda_init=lambda_init):
            phase_reset()
            qT = sb.alloc("qT", [128, 4, S], BF16)
            kT = sb.alloc("kT", [128, 4, S], BF16)
            v1 = sb.alloc("v1", [128, NT, 4, 129], BF16)
            tz = sb.alloc("tz", [128, 2 * S - 128], F32)
            dlr = sb.alloc("dlr", [128, 4, 64], F32)
            dsub = sb.alloc("dsub", [128, 128], F32)
            dma("sp", qT[:], dqT_d.ap().rearrange("(c p) s -> p c s", p=128), writes=["qT"])
            dma("sp", kT[:], dkT_d.ap().rearrange("(c p) s -> p c s", p=128), writes=["kT"])
            dma("sp", v1[:], dv1_d.ap().rearrange("(t p) g e -> p t g e", p=128), writes=["v1"])
            dma("sp", tz[:], tz_d.ap(), writes=["tz"])
            dma("sp", dlr[:], dl_d[l], writes=["dlr"])
            dma("sp", dsub[:], dsub_d[l], writes=["dsub"])
            lt = sb.alloc("lt", [128, 2, 64], F32)
            ls = sb.alloc("ls", [128, 2], F32)
            nlam = sb.alloc("nlam", [128, 1], F32)
            p.add("dve", lambda e: e.tensor_tensor(out=lt[:, 0, :], in0=dlr[:, 0, :], in1=dlr[:, 1, :], op=ALU.mult), reads=["dlr"], writes=["lt"])
            p.add("dve", lambda e: e.tensor_tensor(out=lt[:, 1, :], in0=dlr[:, 2, :], in1=dlr[:, 3, :], op=ALU.mult), reads=["dlr"], writes=["lt"])
            p.add("dve", lambda e: e.tensor_reduce(out=ls[:], in_=lt[:], axis=AX.X, op=ALU.add), reads=["lt"], writes=["ls"])
            p.add("act", lambda e: e.activation(out=ls[:], in_=ls[:], func=AF.Exp), reads=["ls"], writes=["ls"])
            p.add("dve", lambda e: e.tensor_tensor(out=nlam[:], in0=ls[:, 1:2], in1=ls[:, 0:1], op=ALU.subtract), reads=["ls"], writes=["nlam"])
            p.add("dve", lambda e: e.tensor_scalar(out=nlam[:], in0=nlam[:], scalar1=-lambda_init, scalar2=None, op0=ALU.add),
                  reads=["nlam"], writes=["nlam"])
            p.add("dve", lambda e: e.tensor_scalar(out=dsub[:], in0=dsub[:], scalar1=1.0 - lambda_init, scalar2=None, op0=ALU.mult),
                  reads=["dsub"], writes=["dsub"])
            pT = [sb.alloc("pT", [128, 512], BF16) for _ in range(3)]
            tmpf = [sb.alloc("tmpf", [128, 512], F32) for _ in range(3)]
            r1 = sb.alloc("r1", [128, 1], F32)
            r2 = sb.alloc("r2", [128, 1], F32)
            of = sb.alloc("of", [128, 128], F32)
            osq = sb.alloc("osq", [128, 128], F32)
            oss = sb.alloc("oss", [128, 1], F32)
            yb = [sb.alloc("yb", [128, 128], BF16) for _ in range(2)]
            yds = [sb.alloc("yds", [128, 512], BF16) for _ in range(2)]
            p.barrier()
            regs = {}
            ri = 0
            for c in range(2):
                for qq in range(4):
                    regs[(c, qq)] = (3 + ri // 3, (ri % 3) * 160)
                    ri += 1
            pipe = Pipe([0, 1, 2])
            it = 0
            ep = 0
            for h in range(4):
                slope = 2.0 ** (-8.0 * (h + 1) / 4)
                for qb in range(NB):
                    for kt in range(NT):
                        for c in range(2):
                            f0 = c * 256 + h * 64
                            j = f0 // 128
                            pr = slice(f0 % 128, f0 % 128 + 64)
                            sbk = it % 3
                            it += 1
                            off = qb * 512 - kt * 128 + (NT - 1) * 128

                            def s0(sbk=sbk, j=j, pr=pr, kt=kt, qb=qb):
                                p.add("pe", lambda e: e.matmul(psb[sbk][:], lhsT=kT[pr, j, kt * 128:(kt + 1) * 128],
                                                               rhs=qT[pr, j, qb * 512:(qb + 1) * 512], start=True, stop=True),
                                      writes=[PK(sbk)])

                            def s1(sbk=sbk, off=off, slope=slope):
                                p.add("dve", lambda e: e.scalar_tensor_tensor(out=tmpf[sbk][:], in0=tz[:, off:off + 512], scalar=slope,
                                                                              in1=psb[sbk][:], op0=ALU.mult, op1=ALU.add),
                                      reads=[PK(sbk)], writes=[("tmpf", sbk)])
                                p.add("act", lambda e: e.activation(out=pT[sbk][:], in_=tmpf[sbk][:], func=AF.Exp),
                                      reads=[("tmpf", sbk)], writes=[("pT", sbk)])

                            def s2(sbk=sbk, c=c, kt=kt, h=h, qb=qb):
                                nonlocal ep
                                for qq in range(4):
                                    bkk, co = regs[(c, qq)]
                                    p.add("pe", lambda e, qq=qq, bkk=bkk, co=co: e.matmul(
                                        psb[bkk][:, co:co + 129], lhsT=pT[sbk][:, qq * 128:(qq + 1) * 128], rhs=v1[:, kt, h, :],
                                        start=(kt == 0 and co == 0), stop=(kt == NT - 1), skip_group_check=True),
                                        reads=[("pT", sbk)], writes=[PK(bkk)])
                                if kt == NT - 1 and c == 1:
                                    ydi = ep % 2
                                    ep += 1
                                    for qq in range(4):
                                        b1, c1 = regs[(0, qq)]
                                        b2, c2 = regs[(1, qq)]
                                        ybi = qq % 2
                                        p.add("dve", lambda e, b1=b1, c1=c1: e.reciprocal(out=r1[:], in_=psb[b1][:, c1 + 128:c1 + 129]),
                                              reads=[PK(b1)], writes=["r1"])
                                        p.add("dve", lambda e, b2=b2, c2=c2: e.reciprocal(out=r2[:], in_=psb[b2][:, c2 + 128:c2 + 129]),
                                              reads=[PK(b2)], writes=["r2"])
                                        p.add("dve", lambda e: e.tensor_tensor(out=r2[:], in0=r2[:], in1=nlam[:], op=ALU.mult),
                                              reads=["r2", "nlam"], writes=["r2"])
                                        p.add("dve", lambda e, b1=b1, c1=c1: e.tensor_scalar(out=of[:], in0=psb[b1][:, c1:c1 + 128], scalar1=r1[:],
                                                                                             scalar2=None, op0=ALU.mult),
                                              reads=[PK(b1), "r1"], writes=["of"])
                                        p.add("dve", lambda e, b2=b2, c2=c2: e.scalar_tensor_tensor(out=of[:], in0=psb[b2][:, c2:c2 + 128], scalar=r2[:],
                                                                                                    in1=of[:], op0=ALU.mult, op1=ALU.add),
                                              reads=[PK(b2), "r2", "of"], writes=["of"])
                                        p.add("dve", lambda e: e.memset(oss[:], 0.0), writes=["oss"])
                                        p.add("act", lambda e: e.activation(out=osq[:], in_=of[:], func=AF.Square, accum_out=oss[:]),
                                              reads=["of", "oss"], writes=["osq", "oss"])
                                        rstd_ops(oss[:], "oss", 1.0 / 128)
                                        p.add("dve", lambda e, ybi=ybi: e.scalar_tensor_tensor(out=yb[ybi][:], in0=of[:], scalar=oss[:], in1=dsub[:],
                                                                                               op0=ALU.mult, op1=ALU.mult),
                                              reads=["of", "oss", "dsub"], writes=[("yb", ybi)])
                                        p.add("pe", lambda e, qq=qq, ybi=ybi: e.transpose(out=psb16[7][:, qq * 128:(qq + 1) * 128], in_=yb[ybi][:],
                                                                                          identity=ident[:]), reads=[("yb", ybi)], writes=[PK(7)])
                                    copy_op("act", yds[ydi][:], psb16[7][:, 0:512], [PK(7)], [("yds", ydi)])
                                    r0 = 1536 + h * 128
                                    dma("pool", yT_d[r0:r0 + 128, qb * 512:(qb + 1) * 512], yds[ydi][:], reads=[("yds", ydi)],
                                        writes=[("yd", h, qb)])
                            pipe.push(s0, s1, s2)
            pipe.flush()

        if "D" in phases:
            _phD()

        def _phE(l=l, xsrc_d=xsrc_d, lambda_init=lambda_init):
            phase_reset()
            G = sb.alloc("G", [128, NT, 16], F32)
            dma("sp", G[:], ag_d.ap().rearrange("(t p) j -> p t j", p=128), writes=["G"])
            mk = [sb.alloc("mk", [128, 128], F32) for _ in range(2)]
            onesf = sb.alloc("onesf", [128, 128], F32)
            dma("sp", mk[0][:], maskf_d.ap(), writes=["mk"])
            dma("sp", mk[1][:], maskb_d.ap(), writes=["mk"])
            dma("sp", onesf[:], ones_d.ap(), writes=["mk"])
            E1 = sb.alloc("E1", [128, NT, 2, 4], F32)
            BN = sb.alloc("BN", [128, NT, 16], F32)
            T1 = sb.alloc("T1", [128, NT, 2, 4], F32)
            A1 = sb.alloc("A1", [128, NT, 2, 4], F32)
            A2 = sb.alloc("A2", [128, NT, 2, 4], F32)
            WD = sb.alloc("WD", [128, NT, 2, 4], F32)
            FL = sb.alloc("FL", [128, NT, 2, 4], F32)
            ang = sb.alloc("ang", [128, 512], F32)
            dma("sp", ang[:], ang_d[l], writes=["ang"])
            Gv = G[:].rearrange("p t (y h) -> p t y h", y=4)
            fsel = bc_ap(Gv[:, :, 1, :], [[16, NT], [8, 2], [1, 4]])
            isel = bc_ap(Gv[:, :, 0, :], [[16, NT], [8, 2], [1, 4]])
            p.add("act", lambda e: e.activation(out=E1[:], in_=fsel, func=AF.Exp, scale=-1.0), reads=["G"], writes=["E1"])
            p.add("act", lambda e: e.activation(out=E1[:], in_=E1[:], func=AF.Ln, bias=1.0), reads=["E1"], writes=["E1"])
            for t in range(NT):
                p.add("pe", lambda e, t=t: e.matmul(psb[0][:, t * 16:t * 16 + 4], lhsT=mk[0][:], rhs=E1[:, t, 0, :], start=True, stop=True),
                      reads=["E1", "mk"], writes=[PK(0)])
                p.add("pe", lambda e, t=t: e.matmul(psb[0][:, t * 16 + 4:t * 16 + 8], lhsT=mk[1][:], rhs=E1[:, t, 1, :], start=True, stop=True),
                      reads=["E1", "mk"], writes=[PK(0)])
                p.add("pe", lambda e, t=t: e.matmul(psb[0][:, t * 16 + 8:t * 16 + 16], lhsT=onesf[:],
                                                    rhs=E1[:, t, :, :].rearrange("p a h -> p (a h)"), start=True, stop=True),
                      reads=["E1", "mk"], writes=[PK(0)])
            p.add("dve", lambda e: e.tensor_copy(out=BN[:].rearrange("p t j -> p (t j)"), in_=psb[0][:, 0:NT * 16]), reads=[PK(0)], writes=["BN"])
            bneg = BN[:, :, 0:8].rearrange("p t (a h) -> p t a h", a=2)
            tot = BN[:, :, 8:16].rearrange("p t (a h) -> p t a h", a=2)
            p.add("dve", lambda e: e.tensor_tensor(out=T1[:], in0=isel, in1=bneg, op=ALU.add), reads=["G", "BN"], writes=["T1"])
            p.add("act", lambda e: e.activation(out=A1[:], in_=T1[:], func=AF.Exp), reads=["T1"], writes=["A1"])
            p.add("dve", lambda e: e.tensor_tensor(out=T1[:], in0=T1[:], in1=tot, op=ALU.subtract), reads=["T1", "BN"], writes=["T1"])
            p.add("act", lambda e: e.activation(out=A2[:], in_=T1[:], func=AF.Exp), reads=["T1"], writes=["A2"])
            ksc = 128.0 ** -0.5
            p.add("dve", lambda e: e.tensor_scalar(out=A1[:], in0=A1[:], scalar1=ksc, scalar2=None, op0=ALU.mult), reads=["A1"], writes=["A1"])
            p.add("dve", lambda e: e.tensor_scalar(out=A2[:], in0=A2[:], scalar1=ksc, scalar2=None, op0=ALU.mult), reads=["A2"], writes=["A2"])
            p.add("act", lambda e: e.activation(out=WD[:], in_=tot, func=AF.Exp, scale=-1.0), reads=["BN"], writes=["WD"])
            p.add("act", lambda e: e.activation(out=FL[:], in_=bneg, func=AF.Exp), reads=["BN"], writes=["FL"])
            qTh = sb.alloc("qTh", [128, S], BF16)
            kTh = sb.alloc("kTh", [128, S], BF16)
            ktok = sb.alloc("ktok", [128, NT, 128], BF16)
            v1h = sb.alloc("v1h", [128, NT, 129], BF16)
            sgo = sb.alloc("sgo", [128, NT, 128], BF16)
            hacc = sb.alloc("hacc", [128, NT, 128], F32)
            xc = sb.alloc("xc", [128, NT, 128], F32)
            sq2 = sb.alloc("sq2", [128, NT, 128], F32)
            yab = sb.alloc("yab", [128, NT, 128], BF16)
            yas = sb.alloc("yas", [128, S], BF16)
            mean = sb.alloc("mean", [128, NT], F32)
            var = sb.alloc("var", [128, NT], F32)
            Cst = [sb.alloc("Cst", [128, 129], F32) for _ in range(2)]
            Cb = [sb.alloc("Cb", [128, 129], BF16) for _ in range(2)]
            wT = [sb.alloc("wT", [128, 128], BF16) for _ in range(4)]
            kS = [sb.alloc("kS", [128, 128], BF16) for _ in range(4)]
            den = [sb.alloc("den", [128, 1], F32) for _ in range(2)]
            for h in range(4):
                p.barrier()
                dma("sp", qTh[:], aqk_d[h * 128:(h + 1) * 128, :], writes=["qTh"])
                dma("sp", kTh[:], aqk_d[512 + h * 128:512 + (h + 1) * 128, :], writes=["kTh"])
                dma("sp", v1h[:], av1_d.ap()[:, h, :].rearrange("(t p) e -> p t e", p=128), writes=["v1h"])
                dma("sp", sgo[:], ao_d.ap()[:, h * 128:(h + 1) * 128].rearrange("(t p) d -> p t d", p=128), writes=["sgo"])
                p.add("pool", lambda e: e.memset(hacc[:], 0.0), writes=["hacc"])
                for dr in range(2):
                    p.add("pool", lambda e, dr=dr: e.memset(Cst[dr][:], 0.0), writes=[("Cst", dr)])
                    p.add("pool", lambda e, dr=dr: e.memset(Cb[dr][:], 0.0), writes=[("Cb", dr)])
                for t8 in range(NT // 8):
                    pb = 6 + t8 % 2
                    for c8 in range(8):
                        c = t8 * 8 + c8
                        p.add("pe", lambda e, c=c, c8=c8, pb=pb: e.transpose(out=psb16[pb][:, c8 * 128:(c8 + 1) * 128], in_=kTh[:, c * 128:(c + 1) * 128],
                                                                             identity=ident[:]), reads=["kTh"], writes=[PK(pb)])
                    copy_op("act", ktok[:, t8 * 8:(t8 + 1) * 8, :], psb16[pb][:, 0:1024].rearrange("p (c d) -> p c d", c=8), [PK(pb)], ["ktok"])
                wi = 0
                for step in range(NT):
                    for dr in range(2):
                        c = step if dr == 0 else NT - 1 - step
                        w = wi % 4
                        wi += 1
                        ps_s, ps_o, ps_c = dr * 3, dr * 3 + 1, dr * 3 + 2
                        cs = slice(c * 128, (c + 1) * 128)
                        p.add("pe", lambda e, cs=cs, ps_s=ps_s: e.matmul(psb[ps_s][:, 0:128], lhsT=kTh[:, cs], rhs=qTh[:, cs], start=True, stop=True),
                              reads=["qTh", "kTh"], writes=[PK(ps_s)])
                        p.add("dve", lambda e, c=c, dr=dr, h=h, w=w, ps_s=ps_s: e.scalar_tensor_tensor(
                            out=wT[w][:], in0=psb[ps_s][:, 0:128], scalar=A1[:, c, dr, h:h + 1], in1=mk[dr][:], op0=ALU.mult, op1=ALU.mult),
                            reads=[PK(ps_s), "A1"], writes=[("wT", w)])
                        p.add("pool", lambda e, c=c, dr=dr, h=h, w=w: e.tensor_scalar(out=kS[w][:], in0=ktok[:, c, :], scalar1=A2[:, c, dr, h:h + 1],
                                                                                     scalar2=None, op0=ALU.mult),
                              reads=["ktok", "A2"], writes=[("kS", w)])
                        p.add("pe", lambda e, c=c, w=w, ps_o=ps_o: e.matmul(psb[ps_o][:, 0:129], lhsT=wT[w][:], rhs=v1h[:, c, :], start=True, stop=False),
                              reads=[("wT", w), "v1h"], writes=[PK(ps_o)])
                        p.add("pe", lambda e, cs=cs, dr=dr, ps_o=ps_o: e.matmul(psb[ps_o][:, 0:129], lhsT=qTh[:, cs], rhs=Cb[dr][:], start=False, stop=True),
                              reads=[("Cb", dr)], writes=[PK(ps_o)])
                        p.add("pe", lambda e, c=c, w=w, ps_c=ps_c: e.matmul(psb[ps_c][:, 0:129], lhsT=kS[w][:], rhs=v1h[:, c, :], start=True, stop=True),
                              reads=[("kS", w)], writes=[PK(ps_c)])
                        p.add("dve", lambda e, c=c, dr=dr, h=h, ps_o=ps_o: e.scalar_tensor_tensor(
                            out=den[dr][:], in0=psb[ps_o][:, 128:129], scalar=-1.0, in1=FL[:, c, dr, h:h + 1], op0=ALU.mult, op1=ALU.max),
                            reads=[PK(ps_o), "FL"], writes=[("den", dr)])
                        p.add("dve", lambda e, dr=dr, ps_o=ps_o: e.tensor_tensor(
                            out=den[dr][:], in0=den[dr][:], in1=psb[ps_o][:, 128:129], op=ALU.max),
                            reads=[("den", dr), PK(ps_o)], writes=[("den", dr)])
                        p.add("dve", lambda e, dr=dr: e.reciprocal(out=den[dr][:], in_=den[dr][:]), reads=[("den", dr)], writes=[("den", dr)])
                        p.add("dve", lambda e, c=c, dr=dr, ps_o=ps_o: e.scalar_tensor_tensor(
                            out=hacc[:, c, :], in0=psb[ps_o][:, 0:128], scalar=den[dr][:], in1=hacc[:, c, :], op0=ALU.mult, op1=ALU.add),
                            reads=[PK(ps_o), ("den", dr), ("hacc", c)], writes=[("hacc", c)])
                        p.add("dve", lambda e, c=c, dr=dr, h=h, ps_c=ps_c: e.scalar_tensor_tensor(
                            out=Cst[dr][:], in0=Cst[dr][:], scalar=WD[:, c, dr, h:h + 1], in1=psb[ps_c][:, 0:129], op0=ALU.mult, op1=ALU.add),
                            reads=[PK(ps_c), "WD", ("Cst", dr)], writes=[("Cst", dr)])
                        p.add("act", lambda e, dr=dr: e.copy(out=Cb[dr][:], in_=Cst[dr][:]), reads=[("Cst", dr)], writes=[("Cb", dr)])
                p.barrier()
                p.add("dve", lambda e: e.tensor_reduce(out=mean[:], in_=hacc[:], axis=AX.X, op=ALU.add), writes=["mean"])
                p.add("dve", lambda e: e.tensor_scalar(out=mean[:], in0=mean[:], scalar1=1.0 / 128, scalar2=None, op0=ALU.mult), reads=["mean"], writes=["mean"])
                p.add("dve", lambda e: e.tensor_tensor(out=xc[:], in0=hacc[:], in1=bc_ap(mean[:], [[1, NT], [0, 128]]), op=ALU.subtract),
                      reads=["mean"], writes=["xc"])
                p.add("pool", lambda e: e.tensor_tensor(out=sq2[:], in0=xc[:], in1=xc[:], op=ALU.mult), reads=["xc"], writes=["sq2"])
                p.add("dve", lambda e: e.tensor_reduce(out=var[:], in_=sq2[:], axis=AX.X, op=ALU.add), reads=["sq2"], writes=["var"])
                rstd_ops(var[:], "var", 1.0 / 128)
                p.add("dve", lambda e: e.tensor_tensor(out=xc[:], in0=xc[:], in1=bc_ap(var[:], [[1, NT], [0, 128]]), op=ALU.mult),
                      reads=["xc", "var"], writes=["xc"])
                p.add("pool", lambda e, h=h: e.tensor_tensor(out=xc[:], in0=xc[:], in1=bc_ap(ang[:, h * 128:(h + 1) * 128], [[0, NT], [1, 128]]), op=ALU.mult),
                      reads=["xc", "ang"], writes=["xc"])
                p.add("dve", lambda e: e.tensor_tensor(out=yab[:], in0=xc[:], in1=sgo[:], op=ALU.mult), reads=["xc", "sgo"], writes=["yab"])
                for t8 in range(NT // 8):
                    pb = 6 + t8 % 2
                    for c8 in range(8):
                        c = t8 * 8 + c8
                        p.add("pe", lambda e, c=c, c8=c8, pb=pb: e.transpose(out=psb16[pb][:, c8 * 128:(c8 + 1) * 128], in_=yab[:, c, :],
                                                                             identity=ident[:]), reads=["yab"], writes=[PK(pb)])
                    copy_op("act", yas[:, t8 * 1024:(t8 + 1) * 1024], psb16[pb][:, 0:1024], [PK(pb)], ["yas"])
                dma("pool", yT_d[h * 128:(h + 1) * 128, :], yas[:], reads=["yas"], writes=[("ya", h)])

        if "E" in phases:
            _phE()

        def _phF(l=l, xsrc_d=xsrc_d, lambda_init=lambda_init):
            phase_reset()
            wg = sb.alloc("wg", [128, 8, 4096], BF16)
            wu = sb.alloc("wu", [128, 16, 1024], BF16)
            wo = sb.alloc("wo", [128, 8, 1024], BF16)
            mF = sb.mark()
            stg = [sb.alloc("stg", [128, 8, 512], F32) for _ in range(2)]
            load_cast(lambda c0, c1: wg[:, :, c0:c1],
                      lambda c0, c1: w_in_d[l, :, 5904 + c0:5904 + c1].rearrange("(c p) n -> p c n", p=128), 8, 4096, stg, "wg")
            for bi in range(4):
                load_cast(lambda c0, c1, bi=bi: wu[:, bi * 4:(bi + 1) * 4, c0:c1],
                          lambda c0, c1, bi=bi: w_up_d[bi][l, :, c0:c1].rearrange("(c p) n -> p c n", p=128), 4, 1024, stg, "wu%d" % bi)
            load_cast(lambda c0, c1: wo[:, :, c0:c1],
                      lambda c0, c1: w_out_d[l, :, c0:c1].rearrange("(c p) n -> p c n", p=128), 8, 1024, stg, "wo")
            p.barrier()
            sb.reset(mF)
            hTb = [sb.alloc("hTb", [128, 8, 512], BF16) for _ in range(2)]
            yTb = [sb.alloc("yTb", [128, 16, 512], BF16) for _ in range(2)]
            mT = [sb.alloc("mT", [128, 8, 512], BF16) for _ in range(2)]
            sg = [sb.alloc("sg", [128, 512], F32) for _ in range(2)]
            acc = sb.alloc("acc", [128, 512], F32)
            tmpm = sb.alloc("tmpm", [128, 512], F32)
            xtl = [sb.alloc("xtl", [128, D], F32) for _ in range(2)]
            bankc = 0
            xi = 0
            for b in range(NB):
                bi = b % 2
                bs = slice(b * 512, (b + 1) * 512)
                dma("sp", hTb[bi][:], hT_d.ap()[:, bs].rearrange("(c p) s -> p c s", p=128), writes=[("hTb", bi)])
                dma("sp", yTb[bi][:], yT_d.ap()[:, bs].rearrange("(c p) s -> p c s", p=128), writes=[("yTb", bi)])
                for dc in range(8):
                    for g in range(4):
                        pg = bankc % 6
                        pu = (bankc + 1) % 6
                        bankc += 2
                        for k in range(8):
                            p.add("pe", lambda e, k=k, g=g, dc=dc, pg=pg, bi=bi: e.matmul(
                                psb[pg][:], lhsT=wg[:, k, g * 1024 + dc * 128:g * 1024 + (dc + 1) * 128], rhs=hTb[bi][:, k, :],
                                start=(k == 0), stop=(k == 7)), reads=[("hTb", bi)], writes=[PK(pg)])
                        for k in range(4):
                            p.add("pe", lambda e, k=k, g=g, dc=dc, pu=pu, bi=bi: e.matmul(
                                psb[pu][:], lhsT=wu[:, g * 4 + k, dc * 128:(dc + 1) * 128], rhs=yTb[bi][:, g * 4 + k, :],
                                start=(k == 0), stop=(k == 3)), reads=[("yTb", bi)], writes=[PK(pu)])
                        sgi = g % 2
                        p.add("act", lambda e, pg=pg, sgi=sgi: e.activation(out=sg[sgi][:], in_=psb[pg][:], func=AF.Sigmoid),
                              reads=[PK(pg)], writes=[("sg", sgi)])
                        if g == 0:
                            p.add("dve", lambda e, pu=pu, sgi=sgi: e.tensor_tensor(out=acc[:], in0=sg[sgi][:], in1=psb[pu][:], op=ALU.mult),
                                  reads=[PK(pu), ("sg", sgi)], writes=["acc"])
                        else:
                            p.add("dve", lambda e, pu=pu, sgi=sgi: e.tensor_tensor(out=tmpm[:], in0=sg[sgi][:], in1=psb[pu][:], op=ALU.mult),
                                  reads=[PK(pu), ("sg", sgi)], writes=["tmpm"])
                            if g < 3:
                                p.add("pool", lambda e: e.tensor_tensor(out=acc[:], in0=acc[:], in1=tmpm[:], op=ALU.add),
                                      reads=["tmpm", "acc"], writes=["acc"])
                            else:
                                p.add("pool", lambda e, dc=dc, bi=bi: e.tensor_tensor(out=mT[bi][:, dc, :], in0=acc[:], in1=tmpm[:], op=ALU.add),
                                      reads=["tmpm", "acc"], writes=[("mT", bi, dc)])
                for tt in range(4):
                    xt = xtl[xi % 2]
                    xk = ("xtl", xi % 2)
                    xi += 1
                    tok = slice(b * 512 + tt * 128, b * 512 + (tt + 1) * 128)
                    dma("sp", xt[:], xsrc_d[tok, :], writes=[xk])
                    for half in range(2):
                        po = 6 + half
                        for k in range(8):
                            p.add("pe", lambda e, k=k, tt=tt, half=half, po=po, bi=bi: e.matmul(
                                psb[po][:], lhsT=mT[bi][:, k, tt * 128:(tt + 1) * 128], rhs=wo[:, k, half * 512:(half + 1) * 512],
                                start=(k == 0), stop=(k == 7)), reads=[("mT", bi, k)], writes=[PK(po)])
                        p.add("dve", lambda e, xt=xt, half=half, po=po: e.tensor_tensor(out=xt[:, half * 512:(half + 1) * 512],
                                                                                        in0=xt[:, half * 512:(half + 1) * 512], in1=psb[po][:], op=ALU.add),
                              reads=[PK(po), xk], writes=[xk])
                    dma("pool", xres_d[tok, :], xt[:], reads=[xk], writes=[("xres", b, tt)])

        if "F" in phases:
            _phF()

        def _phG(l=l, xsrc_d=xsrc_d, lambda_init=lambda_init):
            phase_reset()
            wgt = sb.alloc("wgt", [128, 8, D_FF], BF16)
            wup = sb.alloc("wup", [128, 8, D_FF], BF16)
            wdn = sb.alloc("wdn", [128, 22, D], BF16)
            g2 = sb.alloc("g2", [128, D], F32)
            dma("sp", g2[:], n2g_d[l], writes=["g2"])
            mG = sb.mark()
            stg = [sb.alloc("stg", [128, 8, 512], F32) for _ in range(2)]
            load_cast(lambda c0, c1: wgt[:, :, c0:c1], lambda c0, c1: w_fg_d[l, :, c0:c1].rearrange("(c p) n -> p c n", p=128), 8, D_FF, stg, "wgt")
            load_cast(lambda c0, c1: wup[:, :, c0:c1], lambda c0, c1: w_fu_d[l, :, c0:c1].rearrange("(c p) n -> p c n", p=128), 8, D_FF, stg, "wup")
            for r in range(0, 22, 8):
                n = min(8, 22 - r)
                load_cast(lambda c0, c1, r=r, n=n: wdn[:, r:r + n, c0:c1],
                          lambda c0, c1, r=r, n=n: w_fd_d[l, r * 128:(r + n) * 128, c0:c1].rearrange("(c p) n -> p c n", p=128), n, D, stg, "wdn%d" % r)
            p.barrier()
            sb.reset(mG)
            xtl = [sb.alloc("xtl", [128, D], F32) for _ in range(4)]
            sqj = sb.alloc("sqj", [128, D], F32)
            ssb = [sb.alloc("ss", [128, 1], F32) for _ in range(2)]
            hbb = [sb.alloc("hb", [128, D], BF16) for _ in range(2)]
            h2T = sb.alloc("h2T", [128, 8, 512], BF16)
            aT = sb.alloc("aT", [128, 22, 512], BF16)
            sg = [sb.alloc("sg", [128, 512], F32) for _ in range(2)]
            bankc = 0
            for b in range(NB):
                for tt in range(4):
                    tok = slice(b * 512 + tt * 128, b * 512 + (tt + 1) * 128)
                    dma("sp", xtl[tt][:], xres_d[tok, :], reads=[("xres", b, tt)], writes=[("xtl", tt)])
                    norm_tile(xtl[tt][:], ("xtl", tt), g2[:], "g2", sqj[:], ssb[tt % 2][:], ("ss", tt % 2), hbb[tt % 2][:], ("hb", tt % 2),
                              6 + tt % 2, h2T[:, :, tt * 128:(tt + 1) * 128], ("h2T", tt))
                for fc in range(22):
                    pg = bankc % 6
                    pu = (bankc + 1) % 6
                    bankc += 2
                    for k in range(8):
                        p.add("pe", lambda e, k=k, fc=fc, pg=pg: e.matmul(psb[pg][:], lhsT=wgt[:, k, fc * 128:(fc + 1) * 128], rhs=h2T[:, k, :],
                                                                          start=(k == 0), stop=(k == 7)),
                              reads=[("h2T", 0), ("h2T", 1), ("h2T", 2), ("h2T", 3)], writes=[PK(pg)])
                    for k in range(8):
                        p.add("pe", lambda e, k=k, fc=fc, pu=pu: e.matmul(psb[pu][:], lhsT=wup[:, k, fc * 128:(fc + 1) * 128], rhs=h2T[:, k, :],
                                                                          start=(k == 0), stop=(k == 7)),
                              reads=[("h2T", 0), ("h2T", 1), ("h2T", 2), ("h2T", 3)], writes=[PK(pu)])
                    sgi = fc % 2
                    p.add("act", lambda e, pg=pg, sgi=sgi: e.activation(out=sg[sgi][:], in_=psb[pg][:], func=AF.Silu), reads=[PK(pg)], writes=[("sg", sgi)])
                    p.add("dve", lambda e, pu=pu, sgi=sgi, fc=fc: e.tensor_tensor(out=aT[:, fc, :], in0=sg[sgi][:], in1=psb[pu][:], op=ALU.mult),
                          reads=[PK(pu), ("sg", sgi)], writes=[("aT", fc)])
                for tt in range(4):
                    tok = slice(b * 512 + tt * 128, b * 512 + (tt + 1) * 128)
                    xt = xtl[tt]
                    for half in range(2):
                        po = 6 + half
                        for fc in range(22):
                            p.add("pe", lambda e, fc=fc, tt=tt, half=half, po=po: e.matmul(
                                psb[po][:], lhsT=aT[:, fc, tt * 128:(tt + 1) * 128], rhs=wdn[:, fc, half * 512:(half + 1) * 512],
                                start=(fc == 0), stop=(fc == 21)), reads=[("aT", fc)], writes=[PK(po)])
                        p.add("dve", lambda e, xt=xt, half=half, po=po: e.tensor_tensor(out=xt[:, half * 512:(half + 1) * 512],
                                                                                        in0=xt[:, half * 512:(half + 1) * 512], in1=psb[po][:], op=ALU.add),
                              reads=[PK(po), ("xtl", tt)], writes=[("xtl", tt)])
                    dma("pool", xres_d[tok, :], xt[:], reads=[("xtl", tt)], writes=[("xres", b, tt)])

        if "G" in phases:
            _phG()

    if "Z" in phases:
        phase_reset()
        gf = sb.alloc("gf", [128, D], F32)
        dma("sp", gf[:], fg_d.ap(), writes=["gf"])
        xb = [sb.alloc("xb", [128, D], F32) for _ in range(2)]
        ob_ = [sb.alloc("ob", [128, D], F32) for _ in range(2)]
        sqj = sb.alloc("sqj", [128, D], F32)
        ssb = [sb.alloc("ss", [128, 1], F32) for _ in range(2)]
        for t in range(NT):
            i = t % 2
            tok = slice(t * 128, (t + 1) * 128)
            dma("sp", xb[i][:], xres_d[tok, :], writes=[("xb", i)])
            p.add("dve", lambda e, i=i: e.memset(ssb[i][:], 0.0), writes=[("ss", i)])
            p.add("act", lambda e, i=i: e.activation(out=sqj[:], in_=xb[i][:], func=AF.Square, accum_out=ssb[i][:]),
                  reads=[("xb", i)], writes=[("ss", i), "sqj"])
            rstd_ops(ssb[i][:], ("ss", i), 1.0 / D)
            p.add("dve", lambda e, i=i: e.scalar_tensor_tensor(out=ob_[i][:], in0=xb[i][:], scalar=ssb[i][:], in1=gf[:], op0=ALU.mult, op1=ALU.mult),
                  reads=[("xb", i), ("ss", i), "gf"], writes=[("ob", i)])
            dma("pool", out_d[tok, :], ob_[i][:], reads=[("ob", i)], writes=[("out", t)])
    p.barrier()
    p.wait_all("pool", [])
    p.emit()
    return nc


def natten_tables(rpb, S):
    rows = S // GRID_W
    NT = S // 128
    wr, wc = 8, 16
    out = np.full((5, 128, 5, 8, 128), NEG, np.float32)
    reps = [0, 1, 2, NT - 2, NT - 1]
    for pi, i in enumerate(reps):
        kb0 = min(max(i - 2, 0), NT - 5)
        q = np.arange(i * 128, (i + 1) * 128)
        r = q // GRID_W
        c = q % GRID_W
        rs = np.clip(r - wr // 2, 0, rows - wr)
        cs = np.clip(c - wc // 2, 0, GRID_W - wc)
        keys = np.arange(kb0 * 128, (kb0 + 5) * 128)
        kr = keys // GRID_W
        kc = keys % GRID_W
        inwin = ((kr[None, :] >= rs[:, None]) & (kr[None, :] < rs[:, None] + wr) &
                 (kc[None, :] >= cs[:, None]) & (kc[None, :] < cs[:, None] + wc))
        offr = np.clip(kr[None, :] - r[:, None] + (wr - 1), 0, 2 * wr - 2)
        offc = np.clip(kc[None, :] - c[:, None] + (wc - 1), 0, 2 * wc - 2)
        g = rpb[:, offr, offc]
        g = np.where(inwin[None], g, np.float32(NEG)).astype(np.float32)
        g = g.reshape(8, 128, 5, 128).transpose(3, 2, 0, 1)
        out[pi] = g
    return out


def host_consts(S):
    t = np.arange(S)
    row = (t // GRID_W).astype(np.float32)
    col = (t % GRID_W).astype(np.float32)
    nf = 16
    inv = (10000.0 ** (-np.arange(nf, dtype=np.float32) / nf)).astype(np.float32)
    ar = row[:, None] * inv
    ac = col[:, None] * inv
    cos64 = np.concatenate([np.cos(ar), np.cos(ar), np.cos(ac), np.cos(ac)], axis=1).astype(np.float32)
    sin64 = np.concatenate([-np.sin(ar), np.sin(ar), -np.sin(ac), np.sin(ac)], axis=1).astype(np.float32)
    W = 2 * S - 128
    C0 = (S // 128 - 1) * 128
    pp = np.arange(128)[:, None]
    cc = np.arange(W)[None, :]
    tz = (-np.abs(cc - pp - C0)).astype(np.float32)
    s_ = np.arange(128)[:, None]
    t_ = np.arange(128)[None, :]
    maskf = (s_ <= t_).astype(np.float32)
    maskb = (s_ >= t_).astype(np.float32)
    ones = np.ones((128, 128), np.float32)
    sel = np.zeros((65, 64), np.float32)
    sel[64, :] = 1.0
    ident = np.eye(128, dtype=np.float32).astype(ml_dtypes.bfloat16)
    return dict(cos64=cos64, sin64=sin64, tz=tz, maskf=maskf, maskb=maskb, ones=ones, sel=sel, ident=ident)


def rep128(a):
    a = np.asarray(a, np.float32)
    return np.ascontiguousarray(np.broadcast_to(a[:, None, :], (a.shape[0], 128, a.shape[1])))


def prep_shared(inp, S):
    L = inp["w_in"].shape[0]
    f = lambda k: np.ascontiguousarray(np.asarray(inp[k], np.float32))
    sh = dict(
        w_in=f("w_in"), w_up_a=f("w_up_a"), w_up_b=f("w_up_b"), w_up_c=f("w_up_c"), w_up_d=f("w_up_d"),
        w_out=f("w_out"), w_ffn_gate=f("w_ffn_gate"), w_ffn_up=f("w_ffn_up"), w_ffn_down=f("w_ffn_down"),
        norm1_g_r=rep128(inp["norm1_g"]), norm2_g_r=rep128(inp["norm2_g"]),
        final_g_r=np.ascontiguousarray(np.broadcast_to(np.asarray(inp["final_g"], np.float32)[None, :], (128, D))),
        conv_w_r=np.ascontiguousarray(np.asarray(inp["a_conv_w"], np.float32).reshape(L, 3, 8, 128).transpose(0, 3, 2, 1)),
        gbias_r=rep128(inp["a_gate_bias"]), anorm_g_r=rep128(inp["a_norm_g"]),
        bqg_r=rep128(inp["b_qnorm_g"]), bkg_r=rep128(inp["b_knorm_g"]),
        dl_r=np.ascontiguousarray(np.broadcast_to(
            np.stack([np.asarray(inp[k], np.float32) for k in ("d_lambda_q1", "d_lambda_k1", "d_lambda_q2", "d_lambda_k2")], axis=1)[:, None],
            (L, 128, 4, 64))),
        dsub_r=rep128(inp["d_subln_g"]),
        natb=np.stack([natten_tables(np.asarray(inp["c_rpb"], np.float32)[l], S) for l in range(L)]),
    )
    sh.update(host_consts(S))
    return sh


_NC_CACHE = {}


def kernel(**inputs):
    x = np.asarray(inputs["x"], np.float32)
    B, S, _ = x.shape
    key = (S,)
    if key not in _NC_CACHE:
        _NC_CACHE[key] = build_program(S=S)
    nc = _NC_CACHE[key]
    sh = prep_shared(inputs, S)
    in_maps = []
    for b in range(B):
        m = dict(sh)
        m["x"] = np.ascontiguousarray(x[b])
        in_maps.append(m)
    res = run_bass_kernel_spmd(nc, in_maps, core_ids=list(range(B)))
    return np.stack([np.asarray(r["out"], np.float32) for r in res.results], axis=0)
```

```python
import math
import numpy as np
import ml_dtypes
import concourse.bass as bass
import concourse.mybir as mybir
from concourse.bass_utils import run_bass_kernel_spmd

F32 = mybir.dt.float32
BF16 = mybir.dt.bfloat16
AF = mybir.ActivationFunctionType
ALU = mybir.AluOpType
AX = mybir.AxisListType

D = 1024
SEQ = 4096
DEPTH = 2
GRID_W = 64
EPS = 1e-6
D_IN = 10000
D_FF = 2816
NEG = -30000.0

ENGS = ("pe", "act", "dve", "pool", "sp")
DMA_RING = 8
NO_SELF_SYNC = ("pe",)


class Op:
    __slots__ = ("eng", "fn", "idx", "dma", "waits", "val", "need_inc", "clock", "ring", "semval")

    def __init__(self, eng, fn, idx, dma):
        self.eng = eng
        self.fn = fn
        self.idx = idx
        self.dma = dma
        self.waits = []
        self.need_inc = False
        self.ring = None
        self.semval = None


class Prog:
    def __init__(self, nc, same_engine_sync=True):
        self.nc = nc
        self.ops = {e: [] for e in ENGS}
        self.writer = {}
        self.readers = {}
        self.know = {e: {} for e in ENGS}
        self.dma_count = {e: 0 for e in ENGS}
        self.dma_ops = {e: [] for e in ENGS}
        self.same_engine_sync = same_engine_sync
        self.barrier_deps = {e: [] for e in ENGS}

    def add(self, eng, fn, reads=(), writes=(), dma=False):
        lst = self.ops[eng]
        op = Op(eng, fn, len(lst), dma)
        deps = []
        for k in reads:
            w = self.writer.get(k)
            if w is not None:
                deps.append(w)
        for k in writes:
            w = self.writer.get(k)
            if w is not None:
                deps.append(w)
            deps.extend(self.readers.get(k, ()))
        if self.barrier_deps[eng]:
            deps.extend(self.barrier_deps[eng])
            self.barrier_deps[eng] = []
        if dma:
            n = self.dma_count[eng]
            op.ring = n % DMA_RING
            if n >= DMA_RING:
                deps.append(self.dma_ops[eng][n - DMA_RING])
            self.dma_count[eng] = n + 1
            self.dma_ops[eng].append(op)
        know = self.know[eng]
        for d in deps:
            if d is op:
                continue
            if d.dma:
                key = ("dma", d.eng, d.ring)
                val = d.val
            else:
                if d.eng == eng and (eng in NO_SELF_SYNC or not self.same_engine_sync):
                    continue
                key = d.eng
                val = d.idx
            if know.get(key, -1) >= val:
                continue
            op.waits.append(d)
            d.need_inc = True
            for k2, v2 in d.clock.items():
                if know.get(k2, -1) < v2:
                    know[k2] = v2
        if dma:
            op.val = (self.dma_count[eng] - 1) // DMA_RING
            ck = ("dma", eng, op.ring)
        else:
            op.val = op.idx
            ck = eng
        op.clock = dict(know)
        op.clock[ck] = op.val
        lst.append(op)
        for k in writes:
            self.writer[k] = op
            self.readers[k] = []
        for k in reads:
            if k not in writes:
                self.readers.setdefault(k, []).append(op)
        return op

    def wait_all(self, eng, keys):
        return self.add(eng, lambda e: None, reads=list(keys))

    def barrier(self):
        lasts = []
        for e in ENGS:
            if self.ops[e]:
                lasts.append(self.ops[e][-1])
            lasts.extend(self.dma_ops[e][-DMA_RING:])
        for e in ENGS:
            self.barrier_deps[e] = list(lasts)

    def emit(self):
        nc = self.nc
        from contextlib import ExitStack
        with ExitStack() as es:
            csem = {e: es.enter_context(nc.semaphore("c_" + e)) for e in ENGS}
            dsem = {(e, r): es.enter_context(nc.semaphore("d_%s_%d" % (e, r)))
                    for e in ENGS for r in range(DMA_RING) if self.dma_count[e] > 0}
            for e in ENGS:
                cnt = 0
                for op in self.ops[e]:
                    if not op.dma and op.need_inc:
                        cnt += 1
                        op.semval = cnt
            block = es.enter_context(nc.Block())

            def run(e, eng):
                for op in self.ops[e]:
                    for d in op.waits:
                        if d.dma:
                            eng.wait_ge(dsem[(d.eng, d.ring)], 16 * (d.val + 1))
                        else:
                            eng.wait_ge(csem[d.eng], d.semval)
                    ins = op.fn(eng)
                    if ins is None:
                        continue
                    if op.dma:
                        ins.then_inc(dsem[(e, op.ring)], 16)
                    elif op.need_inc:
                        ins.then_inc(csem[e], 1)

            @block.sync
            def _(eng):
                run("sp", eng)

            @block.scalar
            def _(eng):
                run("act", eng)

            @block.vector
            def _(eng):
                run("dve", eng)

            @block.gpsimd
            def _(eng):
                run("pool", eng)

            @block.tensor
            def _(eng):
                run("pe", eng)


class Pipe:
    def __init__(self, offs):
        self.offs = offs
        self.items = []

    def push(self, *fns):
        self.items.append(fns)

    def flush(self):
        n = len(self.items)
        m = max(self.offs)
        for t in range(n + m):
            for j, o in enumerate(self.offs):
                i = t - o
                if 0 <= i < n and self.items[i][j] is not None:
                    self.items[i][j]()
        self.items = []


SB_BASE = 16640
SB_LIMIT = 229376


class Arena:
    def __init__(self, nc):
        self.nc = nc
        self.off = SB_BASE
        self.n = 0

    def alloc(self, name, shape, dt):
        esz = 4 if dt == F32 else 2
        nbytes = int(np.prod(shape[1:])) * esz
        nbytes = (nbytes + 63) // 64 * 64
        assert self.off + nbytes <= SB_LIMIT, "SBUF overflow at %s: %d" % (name, self.off + nbytes)
        self.n += 1
        t = self.nc.alloc_sbuf_tensor_at("%s_%d" % (name, self.n), list(shape), dt, offset=self.off)
        self.off += nbytes
        return t

    def mark(self):
        return self.off

    def reset(self, to=SB_BASE):
        self.off = to


def bc_ap(ap, dims):
    return bass.AP(tensor=ap.tensor, offset=ap.offset, ap=[list(ap.ap[0])] + [list(d) for d in dims])


def build_program(S=SEQ, depth=DEPTH, debug=False, phases="ABCDEFGZ"):
    NT = S // 128
    NB = S // 512
    rows = S // GRID_W
    nc = bass.Bass("TRN2", target_bir_lowering=False)
    p = Prog(nc)
    sb = Arena(nc)
    L = depth

    def din(name, shape, dt=F32):
        return nc.dram_tensor(name, list(shape), dt, kind="ExternalInput")

    def dscr(name, shape, dt=BF16):
        return nc.dram_tensor(name, list(shape), dt, kind=("ExternalOutput" if debug else "Internal"))

    x_d = din("x", [S, D])
    w_in_d = din("w_in", [L, D, D_IN])
    w_up_d = [din("w_up_%s" % c, [L, 512, D]) for c in "abcd"]
    w_out_d = din("w_out", [L, D, D])
    w_fg_d = din("w_ffn_gate", [L, D, D_FF])
    w_fu_d = din("w_ffn_up", [L, D, D_FF])
    w_fd_d = din("w_ffn_down", [L, D_FF, D])
    n1g_d = din("norm1_g_r", [L, 128, D])
    n2g_d = din("norm2_g_r", [L, 128, D])
    fg_d = din("final_g_r", [128, D])
    convw_d = din("conv_w_r", [L, 128, 8, 3])
    gbias_d = din("gbias_r", [L, 128, 16])
    ang_d = din("anorm_g_r", [L, 128, 512])
    bqg_d = din("bqg_r", [L, 128, 64])
    bkg_d = din("bkg_r", [L, 128, 64])
    dl_d = din("dl_r", [L, 128, 4, 64])
    dsub_d = din("dsub_r", [L, 128, 128])
    natb_d = din("natb", [L, 5, 128, 5, 8, 128])
    ident_d = din("ident", [128, 128], BF16)
    cos_d = din("cos64", [S, 64])
    sin_d = din("sin64", [S, 64])
    tz_d = din("tz", [128, 2 * S - 128])
    maskf_d = din("maskf", [128, 128])
    maskb_d = din("maskb", [128, 128])
    ones_d = din("ones", [128, 128])
    sel_d = din("sel", [65, 64])
    out_d = nc.dram_tensor("out", [S, D], F32, kind="ExternalOutput")

    xres_d = dscr("xres", [S, D], F32)
    hT_d = dscr("hT_s", [D, S])
    aqk_d = dscr("aqk_s", [1024, S])
    av1_d = dscr("av1_s", [S, 4, 129])
    ao_d = dscr("ao_s", [S, 512])
    ag_d = dscr("ag_s", [S, 16], F32)
    bqT_d = dscr("bqT_s", [512, S])
    bkT_d = dscr("bkT_s", [128, S])
    bv1_d = dscr("bv1_s", [S, 2, 65])
    cqT_d = dscr("cqT_s", [512, S])
    ckT_d = dscr("ckT_s", [512, S])
    cv1_d = dscr("cv1_s", [S, 8, 65])
    dqT_d = dscr("dqT_s", [512, S])
    dkT_d = dscr("dkT_s", [512, S])
    dv1_d = dscr("dv1_s", [S, 4, 129])
    yT_d = dscr("yT_s", [2048, S])

    psb = [nc.alloc_psum_tensor("psb%d" % i, [128, 512], F32) for i in range(8)]
    psb16 = [b.bitcast(BF16) for b in psb]

    def PK(i):
        return ("ps", i)

    def dma(eng, out, in_, reads=(), writes=(), **kw):
        return p.add(eng, lambda e: e.dma_start(out=out, in_=in_, **kw), reads=reads, writes=writes, dma=True)

    def new_phase():
        p.barrier()
        sb.reset()

    cnt = [0]

    def uid():
        cnt[0] += 1
        return cnt[0]

    ident = sb.alloc("ident", [128, 128], BF16)
    dma("sp", ident[:], ident_d.ap(), writes=["ident"])
    base_mark = sb.mark()

    def phase_reset():
        p.barrier()
        sb.reset(base_mark)

    evac_rr = [0]

    def evac_engine():
        evac_rr[0] += 1
        return "act" if evac_rr[0] % 2 else "dve"

    def copy_op(eng, out, in_, reads, writes, scale=None):
        if eng == "act":
            if scale is None:
                return p.add("act", lambda e: e.copy(out=out, in_=in_), reads=reads, writes=writes)
            return p.add("act", lambda e: e.mul(out=out, in_=in_, mul=scale), reads=reads, writes=writes)
        else:
            if scale is None:
                return p.add(eng, lambda e: e.tensor_copy(out=out, in_=in_), reads=reads, writes=writes)
            return p.add(eng, lambda e: e.tensor_scalar(out=out, in0=in_, scalar1=scale, scalar2=None, op0=ALU.mult),
                         reads=reads, writes=writes)

    def rstd_ops(v, key, n_scale, eps=EPS):
        p.add("dve", lambda e: e.tensor_scalar(out=v, in0=v, scalar1=n_scale, scalar2=eps, op0=ALU.mult, op1=ALU.add),
              reads=[key], writes=[key])
        p.add("act", lambda e: e.activation(out=v, in_=v, func=AF.Ln), reads=[key], writes=[key])
        p.add("act", lambda e: e.activation(out=v, in_=v, func=AF.Exp, scale=-0.5), reads=[key], writes=[key])

    stage_rr = [0]

    def load_cast(dst_ap_fn, src_ap_fn, nchunk, ncols, stage_f, tag, cols_per=512):
        for c0 in range(0, ncols, cols_per):
            c1 = min(ncols, c0 + cols_per)
            i = stage_rr[0]
            stage_rr[0] += 1
            st = stage_f[i % len(stage_f)]
            sk = ("stage", i % len(stage_f))
            dma("sp", st[:, 0:nchunk, 0:c1 - c0], src_ap_fn(c0, c1), writes=[sk])
            if i % 2 == 0:
                p.add("act", lambda e, st=st, c0=c0, c1=c1: e.copy(out=dst_ap_fn(c0, c1), in_=st[:, 0:nchunk, 0:c1 - c0]),
                      reads=[sk], writes=[("w", tag, c0)])
            else:
                p.add("dve", lambda e, st=st, c0=c0, c1=c1: e.tensor_copy(out=dst_ap_fn(c0, c1), in_=st[:, 0:nchunk, 0:c1 - c0]),
                      reads=[sk], writes=[("w", tag, c0)])

    def norm_tile(xt, xk, gt, gk, sqj, ss, ssk, hb, hbk, ptr_i, dst_ap, dst_key):
        p.add("dve", lambda e: e.memset(ss, 0.0), writes=[ssk])
        p.add("act", lambda e: e.activation(out=sqj, in_=xt, func=AF.Square, accum_out=ss), reads=[xk], writes=[ssk, "sqj"])
        rstd_ops(ss, ssk, 1.0 / D)
        p.add("dve", lambda e: e.scalar_tensor_tensor(out=hb, in0=xt, scalar=ss, in1=gt, op0=ALU.mult, op1=ALU.mult),
              reads=[xk, ssk, gk], writes=[hbk])
        for c in range(8):
            p.add("pe", lambda e, c=c: e.transpose(out=psb16[ptr_i][:, c * 128:(c + 1) * 128], in_=hb[:, c * 128:(c + 1) * 128],
                                                   identity=ident[:]), reads=[hbk], writes=[PK(ptr_i)])
        copy_op("act", dst_ap, psb16[ptr_i][:, 0:1024].rearrange("p (c t) -> p c t", c=8), [PK(ptr_i)], [dst_key])

    for l in range(L):
        xsrc_d = x_d if l == 0 else xres_d
        lambda_init = 0.8 - 0.6 * math.exp(-0.3 * l)

        def _phA(l=l, xsrc_d=xsrc_d, lambda_init=lambda_init):
            phase_reset()
            hT = sb.alloc("hT", [128, 8, S], BF16)
            g1 = sb.alloc("g1", [128, D], F32)
            dma("sp", g1[:], n1g_d[l], writes=["g1"])
            xb = [sb.alloc("xb", [128, D], F32) for _ in range(2)]
            sqj = sb.alloc("sqj", [128, D], F32)
            ssb = [sb.alloc("ss", [128, 1], F32) for _ in range(2)]
            hbb = [sb.alloc("hb", [128, D], BF16) for _ in range(2)]
            mA = sb.mark()
            for t in range(NT):
                xt = xb[t % 2]
                dma("sp", xt[:], xsrc_d[t * 128:(t + 1) * 128, :], writes=[("xb", t % 2)])
                norm_tile(xt[:], ("xb", t % 2), g1[:], "g1", sqj[:], ssb[t % 2][:], ("ss", t % 2), hbb[t % 2][:], ("hb", t % 2),
                          6 + t % 2, hT[:, :, t * 128:(t + 1) * 128], ("hT", t))
            p.barrier()
            dma("pool", hT_d.ap().rearrange("(c p) s -> p c s", p=128), hT[:], writes=["hT_d"])

            wf = [sb.alloc("wf", [128, 8, 512], F32) for _ in range(2)]
            wb = [sb.alloc("wb", [128, 8, 512], BF16) for _ in range(2)]
            stgf = [sb.alloc("stgf", [128, S + 2], F32) for _ in range(2)]
            stgb = [sb.alloc("stgb", [128, S], BF16) for _ in range(2)]
            cw = sb.alloc("cw", [128, 8, 3], F32)
            ctmp = sb.alloc("ctmp", [128, S], F32)
            dma("sp", cw[:], convw_d[l], writes=["cw"])
            for i in range(2):
                p.add("dve", lambda e, i=i: e.memset(stgf[i][:], 0.0), writes=[("stgf", i, b) for b in range(NB)])
            groups = [("aq", 0, aqk_d, 0, None), ("ak", 512, aqk_d, 512, None),
                      ("cq", 2832, cqT_d, 0, 0.125), ("ck", 3344, ckT_d, 0, None),
                      ("dq", 4368, dqT_d, 0, 0.125), ("dk", 4880, dkT_d, 0, None)]
            cidx = 0
            bank = 0
            for gi, (gname, col0, dst_d, drow0, scale) in enumerate(groups):
                wfi, wbi = wf[gi % 2], wb[gi % 2]
                dma("sp", wfi[:], w_in_d[l, :, col0:col0 + 512].rearrange("(c p) n -> p c n", p=128), writes=[("wf", gi % 2)])
                p.add("act", lambda e, wfi=wfi, wbi=wbi: e.copy(out=wbi[:], in_=wfi[:]),
                      reads=[("wf", gi % 2)], writes=[("wb", gi % 2)])
                is_a = gname in ("aq", "ak")
                for cc in range(4):
                    si = cidx % 2
                    cidx += 1
                    for b in range(NB):
                        bk = bank % 6
                        bank += 1
                        for k in range(8):
                            p.add("pe", lambda e, k=k, cc=cc, b=b, bk=bk, wbi=wbi: e.matmul(
                                psb[bk][:], lhsT=wbi[:, k, cc * 128:(cc + 1) * 128], rhs=hT[:, k, b * 512:(b + 1) * 512],
                                start=(k == 0), stop=(k == 7)), reads=[("wb", gi % 2)], writes=[PK(bk)])
                        if is_a:
                            copy_op(evac_engine(), stgf[si][:, 1 + b * 512:1 + (b + 1) * 512], psb[bk][:], [PK(bk)], [("stgf", si, b)])
                        else:
                            copy_op(evac_engine(), stgb[si][:, b * 512:(b + 1) * 512], psb[bk][:], [PK(bk)], [("stgb", si, b)], scale=scale)
                    if is_a:
                        ch = (col0 // 128) + cc
                        sf = stgf[si]
                        p.add("dve", lambda e, sf=sf, ch=ch: e.tensor_scalar(out=ctmp[:], in0=sf[:, 1:S + 1], scalar1=cw[:, ch, 1:2],
                                                                             scalar2=None, op0=ALU.mult),
                              reads=[("stgf", si, b_) for b_ in range(NB)] + ["cw"], writes=["ctmp"])
                        p.add("dve", lambda e, sf=sf, ch=ch: e.scalar_tensor_tensor(out=ctmp[:], in0=sf[:, 0:S], scalar=cw[:, ch, 0:1],
                                                                                    in1=ctmp[:], op0=ALU.mult, op1=ALU.add),
                              reads=[("stgf", si, b_) for b_ in range(NB)] + ["cw"], writes=["ctmp"])
                        p.add("dve", lambda e, sf=sf, ch=ch: e.scalar_tensor_tensor(out=ctmp[:], in0=sf[:, 2:S + 2], scalar=cw[:, ch, 2:3],
                                                                                    in1=ctmp[:], op0=ALU.mult, op1=ALU.add),
                              reads=[("stgf", si, b_) for b_ in range(NB)] + ["cw"], writes=["ctmp"])
                        p.add("act", lambda e, si=si: e.activation(out=stgb[si][:], in_=ctmp[:], func=AF.Silu),
                              reads=["ctmp"], writes=[("stgb", si, b_) for b_ in range(NB)])
                    r0 = drow0 + cc * 128
                    dma("pool", dst_d[r0:r0 + 128, :], stgb[si][:], reads=[("stgb", si, b_) for b_ in range(NB)], writes=[(gname, cc)])

            p.barrier()
            sb.reset(mA)
            wf = [sb.alloc("wf", [128, 8, 512], F32) for _ in range(2)]
            wb = [sb.alloc("wb", [128, 8, 512], BF16) for _ in range(2)]
            cos_s = sb.alloc("cos", [128, NT, 64], F32)
            sin_s = sb.alloc("sin", [128, NT, 64], F32)
            dma("sp", cos_s[:], cos_d.ap().rearrange("(t p) f -> p t f", p=128), writes=["cos"])
            dma("sp", sin_s[:], sin_d.ap().rearrange("(t p) f -> p t f", p=128), writes=["sin"])
            gb = sb.alloc("gb", [128, 16], F32)
            dma("sp", gb[:], gbias_d[l], writes=["gb"])
            bqg = sb.alloc("bqg", [128, 64], F32)
            bkg = sb.alloc("bkg", [128, 64], F32)
            dma("sp", bqg[:], bqg_d[l], writes=["bqg"])
            dma("sp", bkg[:], bkg_d[l], writes=["bkg"])
            p.add("dve", lambda e: e.tensor_scalar(out=bqg[:], in0=bqg[:], scalar1=0.125, scalar2=None, op0=ALU.mult),
                  reads=["bqg"], writes=["bqg"])
            st129 = [sb.alloc("st129", [128, 4, 129], BF16) for _ in range(2)]
            st65 = [sb.alloc("st65", [128, 8, 65], BF16) for _ in range(2)]
            stb = [sb.alloc("stb", [128, 512], BF16) for _ in range(2)]
            stg16 = [sb.alloc("stg16", [128, 16], F32) for _ in range(2)]
            for i in range(2):
                p.add("dve", lambda e, i=i: e.memset(st129[i][:], 1.0), writes=[("st129", i)])
                p.add("dve", lambda e, i=i: e.memset(st65[i][:], 1.0), writes=[("st65", i)])
            qf = sb.alloc("qf", [128, 512], F32)
            qn = sb.alloc("qn", [128, 512], F32)
            t1 = sb.alloc("t1", [128, 512], F32)
            t2 = sb.alloc("t2", [128, 512], F32)
            ssh = sb.alloc("ssh", [128, 8], F32)
            qrb = [sb.alloc("qrb", [128, 512], BF16) for _ in range(2)]
            bqT_s = sb.alloc("bqT_s", [128, 4, S], BF16)
            bkT_s = sb.alloc("bkT_s", [128, S], BF16)

            def rope_norm(ps_ap, psk, nh, gtile, t, outb, outk):
                W = nh * 64
                qf_, qn_, t1_, t2_ = qf[:, 0:W], qn[:, 0:W], t1[:, 0:W], t2[:, 0:W]
                p.add("act", lambda e: e.copy(out=qf_, in_=ps_ap), reads=[psk], writes=["qf"])
                p.add("dve", lambda e: e.tensor_tensor(out=t1_, in0=qf_, in1=qf_, op=ALU.mult), reads=["qf"], writes=["t1"])
                p.add("dve", lambda e: e.tensor_reduce(out=ssh[:, 0:nh], in_=t1_.rearrange("p (h d) -> p h d", h=nh), axis=AX.X, op=ALU.add),
                      reads=["t1"], writes=["ssh"])
                rstd_ops(ssh[:, 0:nh], "ssh", 1.0 / 64)
                p.add("dve", lambda e: e.tensor_tensor(out=qn_.rearrange("p (h d) -> p h d", h=nh), in0=qf_.rearrange("p (h d) -> p h d", h=nh),
                                                       in1=bc_ap(ssh[:, 0:nh], [[1, nh], [0, 64]]), op=ALU.mult),
                      reads=["qf", "ssh"], writes=["qn"])
                p.add("dve", lambda e: e.tensor_tensor(out=qn_.rearrange("p (h d) -> p h d", h=nh), in0=qn_.rearrange("p (h d) -> p h d", h=nh),
                                                       in1=bc_ap(gtile[:], [[0, nh], [1, 64]]), op=ALU.mult),
                      reads=["qn", "bqg", "bkg"], writes=["qn"])
                cos_b = bc_ap(cos_s[:, t, :], [[0, nh], [1, 64]])
                p.add("dve", lambda e: e.tensor_tensor(out=t1_.rearrange("p (h d) -> p h d", h=nh), in0=qn_.rearrange("p (h d) -> p h d", h=nh),
                                                       in1=cos_b, op=ALU.mult), reads=["qn", "cos"], writes=["t1"])
                sin_lo = bc_ap(sin_s[:, t, 0:16], [[0, nh], [32, 2], [1, 16]])
                sin_hi = bc_ap(sin_s[:, t, 16:32], [[0, nh], [32, 2], [1, 16]])
                x4 = qn_.rearrange("p (h r f) -> p h r f", h=nh, r=2)
                o4 = t2_.rearrange("p (h r f) -> p h r f", h=nh, r=2)
                p.add("dve", lambda e: e.tensor_tensor(out=o4[:, :, :, 0:16], in0=x4[:, :, :, 16:32], in1=sin_lo, op=ALU.mult),
                      reads=["qn", "sin"], writes=["t2"])
                p.add("dve", lambda e: e.tensor_tensor(out=o4[:, :, :, 16:32], in0=x4[:, :, :, 0:16], in1=sin_hi, op=ALU.mult),
                      reads=["qn", "sin"], writes=["t2"])
                p.add("dve", lambda e: e.tensor_tensor(out=outb, in0=t1_, in1=t2_, op=ALU.add), reads=["t1", "t2"], writes=[outk])

            tm_groups = [("av", 1024, 512), ("ao", 1536, 512), ("ag", 2048, 16), ("bq", 2064, 512),
                         ("bkv", 2576, 256), ("cv", 3856, 512), ("dv", 5392, 512)]
            for gi, (gname, col0, ncol) in enumerate(tm_groups):
                wfi, wbi = wf[gi % 2], wb[gi % 2]
                dma("sp", wfi[:, :, 0:ncol], w_in_d[l, :, col0:col0 + ncol].rearrange("(c p) n -> p c n", p=128),
                    writes=[("wf", gi % 2)])
                p.add("act", lambda e, wfi=wfi, wbi=wbi, ncol=ncol: e.copy(out=wbi[:, :, 0:ncol], in_=wfi[:, :, 0:ncol]),
                      reads=[("wf", gi % 2)], writes=[("wb", gi % 2)])
                for t in range(NT):
                    bk = t % 4
                    si = t % 2
                    for k in range(8):
                        p.add("pe", lambda e, k=k, t=t, bk=bk, wbi=wbi, ncol=ncol: e.matmul(
                            psb[bk][:, 0:ncol], lhsT=hT[:, k, t * 128:(t + 1) * 128], rhs=wbi[:, k, 0:ncol],
                            start=(k == 0), stop=(k == 7)), reads=[("wb", gi % 2)], writes=[PK(bk)])
                    tok = slice(t * 128, (t + 1) * 128)
                    if gname == "av" or gname == "dv":
                        dst = av1_d if gname == "av" else dv1_d
                        copy_op(evac_engine(), st129[si][:, :, 0:128], psb[bk][:, 0:512].rearrange("p (h d) -> p h d", h=4),
                                [PK(bk)], [("st129", si)])
                        dma("pool", dst[tok], st129[si][:], reads=[("st129", si)], writes=[(gname, t)])
                    elif gname == "cv":
                        copy_op(evac_engine(), st65[si][:, :, 0:64], psb[bk][:, 0:512].rearrange("p (h d) -> p h d", h=8),
                                [PK(bk)], [("st65", si)])
                        dma("pool", cv1_d[tok], st65[si][:], reads=[("st65", si)], writes=[(gname, t)])
                    elif gname == "ao":
                        p.add("act", lambda e, bk=bk, si=si: e.activation(out=stb[si][:], in_=psb[bk][:], func=AF.Sigmoid),
                              reads=[PK(bk)], writes=[("stb", si)])
                        dma("pool", ao_d[tok, :], stb[si][:], reads=[("stb", si)], writes=[(gname, t)])
                    elif gname == "ag":
                        p.add("dve", lambda e, bk=bk, si=si: e.tensor_tensor(out=stg16[si][:], in0=psb[bk][:, 0:16], in1=gb[:], op=ALU.add),
                              reads=[PK(bk), "gb"], writes=[("stg16", si)])
                        dma("pool", ag_d[tok, :], stg16[si][:], reads=[("stg16", si)], writes=[(gname, t)])
                    elif gname == "bq":
                        rope_norm(psb[bk][:, 0:512], PK(bk), 8, bqg, t, qrb[si][:], ("qrb", si))
                        for c in range(4):
                            p.add("pe", lambda e, c=c, si=si: e.transpose(out=psb16[6 + si][:, c * 128:(c + 1) * 128],
                                                                          in_=qrb[si][:, c * 128:(c + 1) * 128], identity=ident[:]),
                                  reads=[("qrb", si)], writes=[PK(6 + si)])
                        copy_op("act", bqT_s[:, :, t * 128:(t + 1) * 128], psb16[6 + si][:, 0:512].rearrange("p (c t) -> p c t", c=4),
                                [PK(6 + si)], [("bqT_s", t)])
                    elif gname == "bkv":
                        rope_norm(psb[bk][:, 0:128], PK(bk), 2, bkg, t, qrb[si][:, 0:128], ("qrb", si))
                        p.add("pe", lambda e, si=si: e.transpose(out=psb16[6 + si][:, 0:128], in_=qrb[si][:, 0:128], identity=ident[:]),
                              reads=[("qrb", si)], writes=[PK(6 + si)])
                        copy_op("act", bkT_s[:, t * 128:(t + 1) * 128], psb16[6 + si][:, 0:128], [PK(6 + si)], [("bkT_s", t)])
                        copy_op("dve", st65[si][:, 0:2, 0:64], psb[bk][:, 128:256].rearrange("p (h d) -> p h d", h=2),
                                [PK(bk)], [("st65", si)])
                        dma("pool", bv1_d[tok], st65[si][:, 0:2, :], reads=[("st65", si)], writes=[("bv", t)])
                if gname == "bq":
                    p.barrier()
                    dma("pool", bqT_d.ap().rearrange("(c p) s -> p c s", p=128), bqT_s[:], writes=["bqT_d"])
                if gname == "bkv":
                    p.barrier()
                    dma("pool", bkT_d.ap(), bkT_s[:], writes=["bkT_d"])

        if "A" in phases:
            _phA()

        def _phB(l=l, xsrc_d=xsrc_d, lambda_init=lambda_init):
            phase_reset()
            qT = sb.alloc("qT", [128, 4, S], BF16)
            kT2 = sb.alloc("kT2", [128, 2, S], BF16)
            v1 = sb.alloc("v1", [128, NT, 2, 65], BF16)
            sel = sb.alloc("sel", [65, 64], F32)
            dma("sp", qT[:], bqT_d.ap().rearrange("(c p) s -> p c s", p=128), writes=["qT"])
            for g in range(2):
                dma("sp", kT2[0:64, g, :], bkT_d[g * 64:(g + 1) * 64, :], writes=[("kT2", g, 0)])
                dma("sp", kT2[64:128, g, :], bkT_d[g * 64:(g + 1) * 64, :], writes=[("kT2", g, 1)])
            dma("sp", v1[:], bv1_d.ap().rearrange("(t p) g e -> p t g e", p=128), writes=["v1"])
            dma("sp", sel[:], sel_d.ap(), writes=["sel"])
            pT = [sb.alloc("pT", [128, 512], BF16) for _ in range(3)]
            osb = [sb.alloc("osb", [65, 512], F32) for _ in range(2)]
            rb = [sb.alloc("rb", [64, 512], F32) for _ in range(2)]
            ybs = [sb.alloc("ybs", [64, 512], BF16) for _ in range(2)]
            p.barrier()
            pipe = Pipe([0, 1, 2])
            it = 0
            for h in range(8):
                g = h // 4
                j = h // 2
                pr = slice((h % 2) * 64, (h % 2) * 64 + 64)
                for qb in range(NB):
                    ob = 3 + (it // NT) % 2
                    for kt in range(NT):
                        sbk = it % 3
                        it += 1

                        def s0(sbk=sbk, g=g, j=j, pr=pr, kt=kt, qb=qb):
                            p.add("pe", lambda e: e.matmul(psb[sbk][:], lhsT=kT2[pr, g, kt * 128:(kt + 1) * 128],
                                                           rhs=qT[pr, j, qb * 512:(qb + 1) * 512], start=True, stop=True),
                                  writes=[PK(sbk)])

                        def s1(sbk=sbk):
                            p.add("act", lambda e: e.activation(out=pT[sbk][:], in_=psb[sbk][:], func=AF.Exp),
                                  reads=[PK(sbk)], writes=[("pT", sbk)])

                        def s2(sbk=sbk, ob=ob, kt=kt, g=g, h=h, qb=qb):
                            p.add("pe", lambda e: e.matmul(psb[ob][0:65, :], lhsT=v1[:, kt, g, :], rhs=pT[sbk][:],
                                                           start=(kt == 0), stop=(kt == NT - 1)),
                                  reads=[("pT", sbk)], writes=[PK(ob)])
                            if kt == NT - 1:
                                oi = ob - 3
                                p.add("act", lambda e: e.copy(out=osb[oi][:], in_=psb[ob][0:65, :]), reads=[PK(ob)], writes=[("osb", oi)])
                                p.add("pe", lambda e: e.matmul(psb[5][0:64, :], lhsT=sel[:], rhs=osb[oi][:], start=True, stop=True),
                                      reads=[("osb", oi), "sel"], writes=[PK(5)])
                                p.add("dve", lambda e: e.reciprocal(out=rb[oi][:], in_=psb[5][0:64, :]), reads=[PK(5)], writes=[("rb", oi)])
                                p.add("dve", lambda e: e.tensor_tensor(out=ybs[oi][:], in0=osb[oi][0:64, :], in1=rb[oi][:], op=ALU.mult),
                                      reads=[("osb", oi), ("rb", oi)], writes=[("ybs", oi)])
                                r0 = 512 + h * 64
                                dma("pool", yT_d[r0:r0 + 64, qb * 512:(qb + 1) * 512], ybs[oi][:], reads=[("ybs", oi)],
                                    writes=[("yb", h, qb)])
                        pipe.push(s0, s1, s2)
            pipe.flush()

        if "B" in phases:
            _phB()

        def _phC(l=l, xsrc_d=xsrc_d, lambda_init=lambda_init):
            phase_reset()
            qT = sb.alloc("qT", [128, 4, S], BF16)
            kT = sb.alloc("kT", [128, 4, S], BF16)
            v1 = sb.alloc("v1", [128, NT, 8, 65], BF16)
            sel = sb.alloc("sel", [65, 64], F32)
            nb_int = sb.alloc("nb_int", [128, 5, 8, 128], F32)
            nb_edge = sb.alloc("nb_edge", [128, 5, 8, 128], F32)
            dma("sp", qT[:], cqT_d.ap().rearrange("(c p) s -> p c s", p=128), writes=["qT"])
            dma("sp", kT[:], ckT_d.ap().rearrange("(c p) s -> p c s", p=128), writes=["kT"])
            dma("sp", v1[:], cv1_d.ap().rearrange("(t p) g e -> p t g e", p=128), writes=["v1"])
            dma("sp", sel[:], sel_d.ap(), writes=["sel"])
            dma("sp", nb_int[:], natb_d[l, 2], writes=["nb_int"])
            pT = [sb.alloc("pT", [128, 512], BF16) for _ in range(3)]
            tmpf = [sb.alloc("tmpf", [128, 512], F32) for _ in range(3)]
            osb = [sb.alloc("osb", [65, 512], F32) for _ in range(2)]
            rb = [sb.alloc("rb", [64, 512], F32) for _ in range(2)]
            ybs = [sb.alloc("ybs", [64, 512], BF16) for _ in range(2)]
            p.barrier()
            pipe = Pipe([0, 1, 2])
            it = 0
            oit = 0
            for i in range(NT):
                pat = 0 if i == 0 else 1 if i == 1 else 3 if i == NT - 2 else 4 if i == NT - 1 else 2
                kb0 = min(max(i - 2, 0), NT - 5)
                if pat != 2:
                    dma("sp", nb_edge[:], natb_d[l, pat], writes=["nb_edge"])
                nbt = nb_int if pat == 2 else nb_edge
                nbk = "nb_int" if pat == 2 else "nb_edge"
                for hg in range(2):
                    ob = 3 + oit % 2
                    oit += 1
                    for kk in range(5):
                        kt = kb0 + kk
                        sbk = it % 3
                        sbanks = [(0, 1), (2, 6)][it % 2]
                        it += 1

                        def s0(sbanks=sbanks, hg=hg, kt=kt, i=i):
                            for hh in range(4):
                                h = hg * 4 + hh
                                pr = slice((h % 2) * 64, (h % 2) * 64 + 64)
                                bkx = sbanks[h % 2]
                                p.add("pe", lambda e, hh=hh, h=h, pr=pr, bkx=bkx: e.matmul(
                                    psb[bkx][:, (hh // 2) * 128:(hh // 2 + 1) * 128], lhsT=kT[pr, h // 2, kt * 128:(kt + 1) * 128],
                                    rhs=qT[pr, h // 2, i * 128:(i + 1) * 128], start=True, stop=True), writes=[PK(bkx)])

                        def s1(sbk=sbk, sbanks=sbanks, kk=kk, hg=hg, nbt=nbt, nbk=nbk):
                            for par in range(2):
                                bkx = sbanks[par]
                                p.add("dve", lambda e, par=par, bkx=bkx: e.tensor_tensor(
                                    out=tmpf[sbk][:, par * 256:(par + 1) * 256].rearrange("p (h q) -> p h q", h=2),
                                    in0=psb[bkx][:, 0:256].rearrange("p (h q) -> p h q", h=2),
                                    in1=bc_ap(nbt[:, kk, hg * 4 + par, :], [[256, 2], [1, 128]]), op=ALU.add),
                                    reads=[PK(bkx), nbk], writes=[("tmpf", sbk)])
                            p.add("act", lambda e: e.activation(out=pT[sbk][:], in_=tmpf[sbk][:], func=AF.Exp),
                                  reads=[("tmpf", sbk)], writes=[("pT", sbk)])

                        def s2(sbk=sbk, ob=ob, kk=kk, kt=kt, hg=hg, i=i):
                            for hh in range(4):
                                h = hg * 4 + hh
                                p.add("pe", lambda e, hh=hh, h=h: e.matmul(
                                    psb[ob][0:65, hh * 128:(hh + 1) * 128], lhsT=v1[:, kt, h, :],
                                    rhs=pT[sbk][:, (hh % 2) * 256 + (hh // 2) * 128:(hh % 2) * 256 + (hh // 2 + 1) * 128],
                                    start=(kk == 0 and hh == 0), stop=(kk == 4), skip_group_check=True), reads=[("pT", sbk)], writes=[PK(ob)])
                            if kk == 4:
                                oi = ob - 3
                                p.add("act", lambda e: e.copy(out=osb[oi][:], in_=psb[ob][0:65, :]), reads=[PK(ob)], writes=[("osb", oi)])
                                p.add("pe", lambda e: e.matmul(psb[5][0:64, :], lhsT=sel[:], rhs=osb[oi][:], start=True, stop=True),
                                      reads=[("osb", oi), "sel"], writes=[PK(5)])
                                p.add("dve", lambda e: e.reciprocal(out=rb[oi][:], in_=psb[5][0:64, :]), reads=[PK(5)], writes=[("rb", oi)])
                                p.add("dve", lambda e: e.tensor_tensor(out=ybs[oi][:], in0=osb[oi][0:64, :], in1=rb[oi][:], op=ALU.mult),
                                      reads=[("osb", oi), ("rb", oi)], writes=[("ybs", oi)])
                                for hh in range(4):
                                    r0 = 1024 + (hg * 4 + hh) * 64
                                    dma("pool", yT_d[r0:r0 + 64, i * 128:(i + 1) * 128], ybs[oi][:, hh * 128:(hh + 1) * 128],
                                        reads=[("ybs", oi)], writes=[("yc", hg * 4 + hh, i)])
                        pipe.push(s0, s1, s2)
                if pat != 2:
                    pipe.flush()
            pipe.flush()

        if "C" in phases:
            _phC()

        def _phD(l=l, xsrc_d=xsrc_d, lambda_init=lambda_init):
            phase_reset()
            qT = sb.alloc("qT", [128, 4, S], BF16)
            kT = sb.alloc("kT", [128, 4, S], BF16)
            v1 = sb.alloc("v1", [128, NT, 4, 129], BF16)
            tz = sb.alloc("tz", [128, 2 * S - 128], F32)
            dlr = sb.alloc("dlr", [128, 4, 64], F32)
            dsub = sb.alloc("dsub", [128, 128], F32)
            dma("sp", qT[:], dqT_d.ap().rearrange("(c p) s -> p c s", p=128), writes=["qT"])
            dma("sp", kT[:], dkT_d.ap().rearrange("(c p) s -> p c s", p=128), writes=["kT"])
            dma("sp", v1[:], dv1_d.ap().rearrange("(t p) g e -> p t g e", p=128), writes=["v1"])
            dma("sp", tz[:], tz_d.ap(), writes=["tz"])
            dma("sp", dlr[:], dl_d[l], writes=["dlr"])
            dma("sp", dsub[:], dsub_d[l], writes=["dsub"])
            lt = sb.alloc("lt", [128, 2, 64], F32)
            ls = sb.alloc("ls", [128, 2], F32)
            nlam = sb.alloc("nlam", [128, 1], F32)
            p.add("dve", lambda e: e.tensor_tensor(out=lt[:, 0, :], in0=dlr[:, 0, :], in1=dlr[:, 1, :], op=ALU.mult), reads=["dlr"], writes=["lt"])
            p.add("dve", lambda e: e.tensor_tensor(out=lt[:, 1, :], in0=dlr[:, 2, :], in1=dlr[:, 3, :], op=ALU.mult), reads=["dlr"], writes=["lt"])
            p.add("dve", lambda e: e.tensor_reduce(out=ls[:], in_=lt[:], axis=AX.X, op=ALU.add), reads=["lt"], writes=["ls"])
            p.add("act", lambda e: e.activation(out=ls[:], in_=ls[:], func=AF.Exp), reads=["ls"], writes=["ls"])
            p.add("dve", lambda e: e.tensor_tensor(out=nlam[:], in0=ls[:, 1:2], in1=ls[:, 0:1], op=ALU.subtract), reads=["ls"], writes=["nlam"])
            p.add("dve", lambda e: e.tensor_scalar(out=nlam[:], in0=nlam[:], scalar1=-lambda_init, scalar2=None, op0=ALU.add),
                  reads=["nlam"], writes=["nlam"])
            p.add("dve", lambda e: e.tensor_scalar(out=dsub[:], in0=dsub[:], scalar1=1.0 - lambda_init, scalar2=None, op0=ALU.mult),
                  reads=["dsub"], writes=["dsub"])
            pT = [sb.alloc("pT", [128, 512], BF16) for _ in range(3)]
            tmpf = [sb.alloc("tmpf", [128, 512], F32) for _ in range(3)]
            r1 = sb.alloc("r1", [128, 1], F32)
            r2 = sb.alloc("r2", [128, 1], F32)
            of = sb.alloc("of", [128, 128], F32)
            osq = sb.alloc("osq", [128, 128], F32)
            oss = sb.alloc("oss", [128, 1], F32)
            yb = [sb.alloc("yb", [128, 128], BF16) for _ in range(2)]
            yds = [sb.alloc("yds", [128, 512], BF16) for _ in range(2)]
            p.barrier()
            regs = {}
            ri = 0
            for c in range(2):
                for qq in range(4):
                    regs[(c, qq)] = (3 + ri // 3, (ri % 3) * 160)
                    ri += 1
            pipe = Pipe([0, 1, 2])
            it = 0
            ep = 0
            for h in range(4):
                slope = 2.0 ** (-8.0 * (h + 1) / 4)
                for qb in range(NB):
                    for kt in range(NT):
                        for c in range(2):
                            f0 = c * 256 + h * 64
                            j = f0 // 128
                            pr = slice(f0 % 128, f0 % 128 + 64)
                            sbk = it % 3
                            it += 1
                            off = qb * 512 - kt * 128 + (NT - 1) * 128

                            def s0(sbk=sbk, j=j, pr=pr, kt=kt, qb=qb):
                                p.add("pe", lambda e: e.matmul(psb[sbk][:], lhsT=kT[pr, j, kt * 128:(kt + 1) * 128],
                                                               rhs=qT[pr, j, qb * 512:(qb + 1) * 512], start=True, stop=True),
                                      writes=[PK(sbk)])

                            def s1(sbk=sbk, off=off, slope=slope):
                                p.add("dve", lambda e: e.scalar_tensor_tensor(out=tmpf[sbk][:], in0=tz[:, off:off + 512], scalar=slope,
                                                                              in1=psb[sbk][:], op0=ALU.mult, op1=ALU.add),
                                      reads=[PK(sbk)], writes=[("tmpf", sbk)])
                                p.add("act", lambda e: e.activation(out=pT[sbk][:], in_=tmpf[sbk][:], func=AF.Exp),
                                      reads=[("tmpf", sbk)], writes=[("pT", sbk)])

                            def s2(sbk=sbk, c=c, kt=kt, h=h, qb=qb):
                                nonlocal ep
                                for qq in range(4):
                                    bkk, co = regs[(c, qq)]
                                    p.add("pe", lambda e, qq=qq, bkk=bkk, co=co: e.matmul(
                                        psb[bkk][:, co:co + 129], lhsT=pT[sbk][:, qq * 128:(qq + 1) * 128], rhs=v1[:, kt, h, :],
                                        start=(kt == 0 and co == 0), stop=(kt == NT - 1), skip_group_check=True),
                                        reads=[("pT", sbk)], writes=[PK(bkk)])
                                if kt == NT - 1 and c == 1:
                                    ydi = ep % 2
                                    ep += 1
                                    for qq in range(4):
                                        b1, c1 = regs[(0, qq)]
                                        b2, c2 = regs[(1, qq)]
                                        ybi = qq % 2
                                        p.add("dve", lambda e, b1=b1, c1=c1: e.reciprocal(out=r1[:], in_=psb[b1][:, c1 + 128:c1 + 129]),
                                              reads=[PK(b1)], writes=["r1"])
                                        p.add("dve", lambda e, b2=b2, c2=c2: e.reciprocal(out=r2[:], in_=psb[b2][:, c2 + 128:c2 + 129]),
                                              reads=[PK(b2)], writes=["r2"])
                                        p.add("dve", lambda e: e.tensor_tensor(out=r2[:], in0=r2[:], in1=nlam[:], op=ALU.mult),
                                              reads=["r2", "nlam"], writes=["r2"])
                                        p.add("dve", lambda e, b1=b1, c1=c1: e.tensor_scalar(out=of[:], in0=psb[b1][:, c1:c1 + 128], scalar1=r1[:],
                                                                                             scalar2=None, op0=ALU.mult),
                                              reads=[PK(b1), "r1"], writes=["of"])
                                        p.add("dve", lambda e, b2=b2, c2=c2: e.scalar_tensor_tensor(out=of[:], in0=psb[b2][:, c2:c2 + 128], scalar=r2[:],
                                                                                                    in1=of[:], op0=ALU.mult, op1=ALU.add),
                                              reads=[PK(b2), "r2", "of"], writes=["of"])
                                        p.add("dve", lambda e: e.memset(oss[:], 0.0), writes=["oss"])
                                        p.add("act", lambda e: e.activation(out=osq[:], in_=of[:], func=AF.Square, accum_out=oss[:]),
                                              reads=["of", "oss"], writes=["osq", "oss"])
                                        rstd_ops(oss[:], "oss", 1.0 / 128)
                                        p.add("dve", lambda e, ybi=ybi: e.scalar_tensor_tensor(out=yb[ybi][:], in0=of[:], scalar=oss[:], in1=dsub[:],
                                                                                               op0=ALU.mult, op1=ALU.mult),
                                              reads=["of", "oss", "dsub"], writes=[("yb", ybi)])
                                        p.add("pe", lambda e, qq=qq, ybi=ybi: e.transpose(out=psb16[7][:, qq * 128:(qq + 1) * 128], in_=yb[ybi][:],
                                                                                          identity=ident[:]), reads=[("yb", ybi)], writes=[PK(7)])
                                    copy_op("act", yds[ydi][:], psb16[7][:, 0:512], [PK(7)], [("yds", ydi)])
                                    r0 = 1536 + h * 128
                                    dma("pool", yT_d[r0:r0 + 128, qb * 512:(qb + 1) * 512], yds[ydi][:], reads=[("yds", ydi)],
                                        writes=[("yd", h, qb)])
                            pipe.push(s0, s1, s2)
            pipe.flush()

        if "D" in phases:
            _phD()

        def _phE(l=l, xsrc_d=xsrc_d, lambda_init=lambda_init):
            phase_reset()
            G = sb.alloc("G", [128, NT, 16], F32)
            dma("sp", G[:], ag_d.ap().rearrange("(t p) j -> p t j", p=128), writes=["G"])
            mk = [sb.alloc("mk", [128, 128], F32) for _ in range(2)]
            onesf = sb.alloc("onesf", [128, 128], F32)
            dma("sp", mk[0][:], maskf_d.ap(), writes=["mk"])
            dma("sp", mk[1][:], maskb_d.ap(), writes=["mk"])
            dma("sp", onesf[:], ones_d.ap(), writes=["mk"])
            E1 = sb.alloc("E1", [128, NT, 2, 4], F32)
            BN = sb.alloc("BN", [128, NT, 16], F32)
            T1 = sb.alloc("T1", [128, NT, 2, 4], F32)
            A1 = sb.alloc("A1", [128, NT, 2, 4], F32)
            A2 = sb.alloc("A2", [128, NT, 2, 4], F32)
            WD = sb.alloc("WD", [128, NT, 2, 4], F32)
            FL = sb.alloc("FL", [128, NT, 2, 4], F32)
            ang = sb.alloc("ang", [128, 512], F32)
            dma("sp", ang[:], ang_d[l], writes=["ang"])
            Gv = G[:].rearrange("p t (y h) -> p t y h", y=4)
            fsel = bc_ap(Gv[:, :, 1, :], [[16, NT], [8, 2], [1, 4]])
            isel = bc_ap(Gv[:, :, 0, :], [[16, NT], [8, 2], [1, 4]])
            p.add("act", lambda e: e.activation(out=E1[:], in_=fsel, func=AF.Exp, scale=-1.0), reads=["G"], writes=["E1"])
            p.add("act", lambda e: e.activation(out=E1[:], in_=E1[:], func=AF.Ln, bias=1.0), reads=["E1"], writes=["E1"])
            for t in range(NT):
                p.add("pe", lambda e, t=t: e.matmul(psb[0][:, t * 16:t * 16 + 4], lhsT=mk[0][:], rhs=E1[:, t, 0, :], start=True, stop=True),
                      reads=["E1", "mk"], writes=[PK(0)])
                p.add("pe", lambda e, t=t: e.matmul(psb[0][:, t * 16 + 4:t * 16 + 8], lhsT=mk[1][:], rhs=E1[:, t, 1, :], start=True, stop=True),
                      reads=["E1", "mk"], writes=[PK(0)])
                p.add("pe", lambda e, t=t: e.matmul(psb[0][:, t * 16 + 8:t * 16 + 16], lhsT=onesf[:],
                                                    rhs=E1[:, t, :, :].rearrange("p a h -> p (a h)"), start=True, stop=True),
                      reads=["E1", "mk"], writes=[PK(0)])
            p.add("dve", lambda e: e.tensor_copy(out=BN[:].rearrange("p t j -> p (t j)"), in_=psb[0][:, 0:NT * 16]), reads=[PK(0)], writes=["BN"])
            bneg = BN[:, :, 0:8].rearrange("p t (a h) -> p t a h", a=2)
            tot = BN[:, :, 8:16].rearrange("p t (a h) -> p t a h", a=2)
            p.add("dve", lambda e: e.tensor_tensor(out=T1[:], in0=isel, in1=bneg, op=ALU.add), reads=["G", "BN"], writes=["T1"])
            p.add("act", lambda e: e.activation(out=A1[:], in_=T1[:], func=AF.Exp), reads=["T1"], writes=["A1"])
            p.add("dve", lambda e: e.tensor_tensor(out=T1[:], in0=T1[:], in1=tot, op=ALU.subtract), reads=["T1", "BN"], writes=["T1"])
            p.add("act", lambda e: e.activation(out=A2[:], in_=T1[:], func=AF.Exp), reads=["T1"], writes=["A2"])
            ksc = 128.0 ** -0.5
            p.add("dve", lambda e: e.tensor_scalar(out=A1[:], in0=A1[:], scalar1=ksc, scalar2=None, op0=ALU.mult), reads=["A1"], writes=["A1"])
            p.add("dve", lambda e: e.tensor_scalar(out=A2[:], in0=A2[:], scalar1=ksc, scalar2=None, op0=ALU.mult), reads=["A2"], writes=["A2"])
            p.add("act", lambda e: e.activation(out=WD[:], in_=tot, func=AF.Exp, scale=-1.0), reads=["BN"], writes=["WD"])
            p.add("act", lambda e: e.activation(out=FL[:], in_=bneg, func=AF.Exp), reads=["BN"], writes=["FL"])
            qTh = sb.alloc("qTh", [128, S], BF16)
            kTh = sb.alloc("kTh", [128, S], BF16)
            ktok = sb.alloc("ktok", [128, NT, 128], BF16)
            v1h = sb.alloc("v1h", [128, NT, 129], BF16)
            sgo = sb.alloc("sgo", [128, NT, 128], BF16)
            hacc = sb.alloc("hacc", [128, NT, 128], F32)
            xc = sb.alloc("xc", [128, NT, 128], F32)
            sq2 = sb.alloc("sq2", [128, NT, 128], F32)
            yab = sb.alloc("yab", [128, NT, 128], BF16)
            yas = sb.alloc("yas", [128, S], BF16)
            mean = sb.alloc("mean", [128, NT], F32)
            var = sb.alloc("var", [128, NT], F32)
            Cst = [sb.alloc("Cst", [128, 129], F32) for _ in range(2)]
            Cb = [sb.alloc("Cb", [128, 129], BF16) for _ in range(2)]
            wT = [sb.alloc("wT", [128, 128], BF16) for _ in range(4)]
            kS = [sb.alloc("kS", [128, 128], BF16) for _ in range(4)]
            den = [sb.alloc("den", [128, 1], F32) for _ in range(2)]
            for h in range(4):
                p.barrier()
                dma("sp", qTh[:], aqk_d[h * 128:(h + 1) * 128, :], writes=["qTh"])
                dma("sp", kTh[:], aqk_d[512 + h * 128:512 + (h + 1) * 128, :], writes=["kTh"])
                dma("sp", v1h[:], av1_d.ap()[:, h, :].rearrange("(t p) e -> p t e", p=128), writes=["v1h"])
                dma("sp", sgo[:], ao_d.ap()[:, h * 128:(h + 1) * 128].rearrange("(t p) d -> p t d", p=128), writes=["sgo"])
                p.add("dve", lambda e: e.memset(hacc[:], 0.0), writes=["hacc"])
                for dr in range(2):
                    p.add("dve", lambda e, dr=dr: e.memset(Cst[dr][:], 0.0), writes=[("Cst", dr)])
                    p.add("dve", lambda e, dr=dr: e.memset(Cb[dr][:], 0.0), writes=[("Cb", dr)])
                for t8 in range(NT // 8):
                    pb = 6 + t8 % 2
                    for c8 in range(8):
                        c = t8 * 8 + c8
                        p.add("pe", lambda e, c=c, c8=c8, pb=pb: e.transpose(out=psb16[pb][:, c8 * 128:(c8 + 1) * 128], in_=kTh[:, c * 128:(c + 1) * 128],
                                                                             identity=ident[:]), reads=["kTh"], writes=[PK(pb)])
                    copy_op("act", ktok[:, t8 * 8:(t8 + 1) * 8, :], psb16[pb][:, 0:1024].rearrange("p (c d) -> p c d", c=8), [PK(pb)], ["ktok"])
                wi = 0
                for step in range(NT):
                    for dr in range(2):
                        c = step if dr == 0 else NT - 1 - step
                        w = wi % 4
                        wi += 1
                        ps_s, ps_o, ps_c = dr * 3, dr * 3 + 1, dr * 3 + 2
                        cs = slice(c * 128, (c + 1) * 128)
                        p.add("pe", lambda e, cs=cs, ps_s=ps_s: e.matmul(psb[ps_s][:, 0:128], lhsT=kTh[:, cs], rhs=qTh[:, cs], start=True, stop=True),
                              reads=["qTh", "kTh"], writes=[PK(ps_s)])
                        p.add("dve", lambda e, c=c, dr=dr, h=h, w=w, ps_s=ps_s: e.scalar_tensor_tensor(
                            out=wT[w][:], in0=psb[ps_s][:, 0:128], scalar=A1[:, c, dr, h:h + 1], in1=mk[dr][:], op0=ALU.mult, op1=ALU.mult),
                            reads=[PK(ps_s), "A1"], writes=[("wT", w)])
                        p.add("act", lambda e, c=c, dr=dr, h=h, w=w: e.activation(out=kS[w][:], in_=ktok[:, c, :], func=AF.Copy,
                                                                                  scale=A2[:, c, dr, h:h + 1]),
                              reads=["ktok", "A2"], writes=[("kS", w)])
                        p.add("pe", lambda e, c=c, w=w, ps_o=ps_o: e.matmul(psb[ps_o][:, 0:129], lhsT=wT[w][:], rhs=v1h[:, c, :], start=True, stop=False),
                              reads=[("wT", w), "v1h"], writes=[PK(ps_o)])
                        p.add("pe", lambda e, cs=cs, dr=dr, ps_o=ps_o: e.matmul(psb[ps_o][:, 0:129], lhsT=qTh[:, cs], rhs=Cb[dr][:], start=False, stop=True),
                              reads=[("Cb", dr)], writes=[PK(ps_o)])
                        p.add("pe", lambda e, c=c, w=w, ps_c=ps_c: e.matmul(psb[ps_c][:, 0:129], lhsT=kS[w][:], rhs=v1h[:, c, :], start=True, stop=True),
                              reads=[("kS", w)], writes=[PK(ps_c)])
                        p.add("dve", lambda e, c=c, dr=dr, h=h, ps_o=ps_o: e.scalar_tensor_tensor(
                            out=den[dr][:], in0=psb[ps_o][:, 128:129], scalar=-1.0, in1=FL[:, c, dr, h:h + 1], op0=ALU.mult, op1=ALU.max),
                            reads=[PK(ps_o), "FL"], writes=[("den", dr)])
                        p.add("dve", lambda e, dr=dr, ps_o=ps_o: e.tensor_tensor(
                            out=den[dr][:], in0=den[dr][:], in1=psb[ps_o][:, 128:129], op=ALU.max),
                            reads=[("den", dr), PK(ps_o)], writes=[("den", dr)])
                        p.add("dve", lambda e, dr=dr: e.reciprocal(out=den[dr][:], in_=den[dr][:]), reads=[("den", dr)], writes=[("den", dr)])
                        p.add("dve", lambda e, c=c, dr=dr, ps_o=ps_o: e.scalar_tensor_tensor(
                            out=hacc[:, c, :], in0=psb[ps_o][:, 0:128], scalar=den[dr][:], in1=hacc[:, c, :], op0=ALU.mult, op1=ALU.add),
                            reads=[PK(ps_o), ("den", dr), ("hacc", c)], writes=[("hacc", c)])
                        p.add("dve", lambda e, c=c, dr=dr, h=h, ps_c=ps_c: e.scalar_tensor_tensor(
                            out=Cst[dr][:], in0=Cst[dr][:], scalar=WD[:, c, dr, h:h + 1], in1=psb[ps_c][:, 0:129], op0=ALU.mult, op1=ALU.add),
                            reads=[PK(ps_c), "WD", ("Cst", dr)], writes=[("Cst", dr)])
                        p.add("act", lambda e, dr=dr: e.copy(out=Cb[dr][:], in_=Cst[dr][:]), reads=[("Cst", dr)], writes=[("Cb", dr)])
                p.barrier()
                p.add("dve", lambda e: e.tensor_reduce(out=mean[:], in_=hacc[:], axis=AX.X, op=ALU.add), writes=["mean"])
                p.add("dve", lambda e: e.tensor_scalar(out=mean[:], in0=mean[:], scalar1=1.0 / 128, scalar2=None, op0=ALU.mult), reads=["mean"], writes=["mean"])
                p.add("dve", lambda e: e.tensor_tensor(out=xc[:], in0=hacc[:], in1=bc_ap(mean[:], [[1, NT], [0, 128]]), op=ALU.subtract),
                      reads=["mean"], writes=["xc"])
                p.add("act", lambda e: e.activation(out=sq2[:], in_=xc[:], func=AF.Square), reads=["xc"], writes=["sq2"])
                p.add("dve", lambda e: e.tensor_reduce(out=var[:], in_=sq2[:], axis=AX.X, op=ALU.add), reads=["sq2"], writes=["var"])
                rstd_ops(var[:], "var", 1.0 / 128)
                p.add("dve", lambda e: e.tensor_tensor(out=xc[:], in0=xc[:], in1=bc_ap(var[:], [[1, NT], [0, 128]]), op=ALU.mult),
                      reads=["xc", "var"], writes=["xc"])
                p.add("dve", lambda e, h=h: e.tensor_tensor(out=xc[:], in0=xc[:], in1=bc_ap(ang[:, h * 128:(h + 1) * 128], [[0, NT], [1, 128]]), op=ALU.mult),
                      reads=["xc", "ang"], writes=["xc"])
                p.add("dve", lambda e: e.tensor_tensor(out=yab[:], in0=xc[:], in1=sgo[:], op=ALU.mult), reads=["xc", "sgo"], writes=["yab"])
                for t8 in range(NT // 8):
                    pb = 6 + t8 % 2
                    for c8 in range(8):
                        c = t8 * 8 + c8
                        p.add("pe", lambda e, c=c, c8=c8, pb=pb: e.transpose(out=psb16[pb][:, c8 * 128:(c8 + 1) * 128], in_=yab[:, c, :],
                                                                             identity=ident[:]), reads=["yab"], writes=[PK(pb)])
                    copy_op("act", yas[:, t8 * 1024:(t8 + 1) * 1024], psb16[pb][:, 0:1024], [PK(pb)], ["yas"])
                dma("pool", yT_d[h * 128:(h + 1) * 128, :], yas[:], reads=["yas"], writes=[("ya", h)])

        if "E" in phases:
            _phE()

        def _phF(l=l, xsrc_d=xsrc_d, lambda_init=lambda_init):
            phase_reset()
            wg = sb.alloc("wg", [128, 8, 4096], BF16)
            wu = sb.alloc("wu", [128, 16, 1024], BF16)
            wo = sb.alloc("wo", [128, 8, 1024], BF16)
            mF = sb.mark()
            stg = [sb.alloc("stg", [128, 8, 512], F32) for _ in range(2)]
            load_cast(lambda c0, c1: wg[:, :, c0:c1],
                      lambda c0, c1: w_in_d[l, :, 5904 + c0:5904 + c1].rearrange("(c p) n -> p c n", p=128), 8, 4096, stg, "wg")
            for bi in range(4):
                load_cast(lambda c0, c1, bi=bi: wu[:, bi * 4:(bi + 1) * 4, c0:c1],
                          lambda c0, c1, bi=bi: w_up_d[bi][l, :, c0:c1].rearrange("(c p) n -> p c n", p=128), 4, 1024, stg, "wu%d" % bi)
            load_cast(lambda c0, c1: wo[:, :, c0:c1],
                      lambda c0, c1: w_out_d[l, :, c0:c1].rearrange("(c p) n -> p c n", p=128), 8, 1024, stg, "wo")
            p.barrier()
            sb.reset(mF)
            hTb = [sb.alloc("hTb", [128, 8, 512], BF16) for _ in range(2)]
            yTb = [sb.alloc("yTb", [128, 16, 512], BF16) for _ in range(2)]
            mT = [sb.alloc("mT", [128, 8, 512], BF16) for _ in range(2)]
            sg = [sb.alloc("sg", [128, 512], F32) for _ in range(2)]
            acc = sb.alloc("acc", [128, 512], F32)
            tmpm = sb.alloc("tmpm", [128, 512], F32)
            xtl = [sb.alloc("xtl", [128, D], F32) for _ in range(2)]
            bankc = 0
            xi = 0
            for b in range(NB):
                bi = b % 2
                bs = slice(b * 512, (b + 1) * 512)
                dma("sp", hTb[bi][:], hT_d.ap()[:, bs].rearrange("(c p) s -> p c s", p=128), writes=[("hTb", bi)])
                dma("sp", yTb[bi][:], yT_d.ap()[:, bs].rearrange("(c p) s -> p c s", p=128), writes=[("yTb", bi)])
                for dc in range(8):
                    for g in range(4):
                        pg = bankc % 6
                        pu = (bankc + 1) % 6
                        bankc += 2
                        for k in range(8):
                            p.add("pe", lambda e, k=k, g=g, dc=dc, pg=pg, bi=bi: e.matmul(
                                psb[pg][:], lhsT=wg[:, k, g * 1024 + dc * 128:g * 1024 + (dc + 1) * 128], rhs=hTb[bi][:, k, :],
                                start=(k == 0), stop=(k == 7)), reads=[("hTb", bi)], writes=[PK(pg)])
                        for k in range(4):
                            p.add("pe", lambda e, k=k, g=g, dc=dc, pu=pu, bi=bi: e.matmul(
                                psb[pu][:], lhsT=wu[:, g * 4 + k, dc * 128:(dc + 1) * 128], rhs=yTb[bi][:, g * 4 + k, :],
                                start=(k == 0), stop=(k == 3)), reads=[("yTb", bi)], writes=[PK(pu)])
                        sgi = g % 2
                        p.add("act", lambda e, pg=pg, sgi=sgi: e.activation(out=sg[sgi][:], in_=psb[pg][:], func=AF.Sigmoid),
                              reads=[PK(pg)], writes=[("sg", sgi)])
                        if g == 0:
                            p.add("dve", lambda e, pu=pu, sgi=sgi: e.tensor_tensor(out=acc[:], in0=sg[sgi][:], in1=psb[pu][:], op=ALU.mult),
                                  reads=[PK(pu), ("sg", sgi)], writes=["acc"])
                        else:
                            p.add("dve", lambda e, pu=pu, sgi=sgi: e.tensor_tensor(out=tmpm[:], in0=sg[sgi][:], in1=psb[pu][:], op=ALU.mult),
                                  reads=[PK(pu), ("sg", sgi)], writes=["tmpm"])
                            if g < 3:
                                p.add("dve", lambda e: e.tensor_tensor(out=acc[:], in0=acc[:], in1=tmpm[:], op=ALU.add),
                                      reads=["tmpm", "acc"], writes=["acc"])
                            else:
                                p.add("dve", lambda e, dc=dc, bi=bi: e.tensor_tensor(out=mT[bi][:, dc, :], in0=acc[:], in1=tmpm[:], op=ALU.add),
                                      reads=["tmpm", "acc"], writes=[("mT", bi, dc)])
                for tt in range(4):
                    xt = xtl[xi % 2]
                    xk = ("xtl", xi % 2)
                    xi += 1
                    tok = slice(b * 512 + tt * 128, b * 512 + (tt + 1) * 128)
                    dma("sp", xt[:], xsrc_d[tok, :], writes=[xk])
                    for half in range(2):
                        po = 6 + half
                        for k in range(8):
                            p.add("pe", lambda e, k=k, tt=tt, half=half, po=po, bi=bi: e.matmul(
                                psb[po][:], lhsT=mT[bi][:, k, tt * 128:(tt + 1) * 128], rhs=wo[:, k, half * 512:(half + 1) * 512],
                                start=(k == 0), stop=(k == 7)), reads=[("mT", bi, k)], writes=[PK(po)])
                        p.add("dve", lambda e, xt=xt, half=half, po=po: e.tensor_tensor(out=xt[:, half * 512:(half + 1) * 512],
                                                                                        in0=xt[:, half * 512:(half + 1) * 512], in1=psb[po][:], op=ALU.add),
                              reads=[PK(po), xk], writes=[xk])
                    dma("pool", xres_d[tok, :], xt[:], reads=[xk], writes=[("xres", b, tt)])

        if "F" in phases:
            _phF()

        def _phG(l=l, xsrc_d=xsrc_d, lambda_init=lambda_init):
            phase_reset()
            wgt = sb.alloc("wgt", [128, 8, D_FF], BF16)
            wup = sb.alloc("wup", [128, 8, D_FF], BF16)
            wdn = sb.alloc("wdn", [128, 22, D], BF16)
            g2 = sb.alloc("g2", [128, D], F32)
            dma("sp", g2[:], n2g_d[l], writes=["g2"])
            mG = sb.mark()
            stg = [sb.alloc("stg", [128, 8, 512], F32) for _ in range(2)]
            load_cast(lambda c0, c1: wgt[:, :, c0:c1], lambda c0, c1: w_fg_d[l, :, c0:c1].rearrange("(c p) n -> p c n", p=128), 8, D_FF, stg, "wgt")
            load_cast(lambda c0, c1: wup[:, :, c0:c1], lambda c0, c1: w_fu_d[l, :, c0:c1].rearrange("(c p) n -> p c n", p=128), 8, D_FF, stg, "wup")
            for r in range(0, 22, 8):
                n = min(8, 22 - r)
                load_cast(lambda c0, c1, r=r, n=n: wdn[:, r:r + n, c0:c1],
                          lambda c0, c1, r=r, n=n: w_fd_d[l, r * 128:(r + n) * 128, c0:c1].rearrange("(c p) n -> p c n", p=128), n, D, stg, "wdn%d" % r)
            p.barrier()
            sb.reset(mG)
            xtl = [sb.alloc("xtl", [128, D], F32) for _ in range(4)]
            sqj = sb.alloc("sqj", [128, D], F32)
            ssb = [sb.alloc("ss", [128, 1], F32) for _ in range(2)]
            hbb = [sb.alloc("hb", [128, D], BF16) for _ in range(2)]
            h2T = sb.alloc("h2T", [128, 8, 512], BF16)
            aT = sb.alloc("aT", [128, 22, 512], BF16)
            sg = [sb.alloc("sg", [128, 512], F32) for _ in range(2)]
            bankc = 0
            for b in range(NB):
                for tt in range(4):
                    tok = slice(b * 512 + tt * 128, b * 512 + (tt + 1) * 128)
                    dma("sp", xtl[tt][:], xres_d[tok, :], reads=[("xres", b, tt)], writes=[("xtl", tt)])
                    norm_tile(xtl[tt][:], ("xtl", tt), g2[:], "g2", sqj[:], ssb[tt % 2][:], ("ss", tt % 2), hbb[tt % 2][:], ("hb", tt % 2),
                              6 + tt % 2, h2T[:, :, tt * 128:(tt + 1) * 128], ("h2T", tt))
                for fc in range(22):
                    pg = bankc % 6
                    pu = (bankc + 1) % 6
                    bankc += 2
                    for k in range(8):
                        p.add("pe", lambda e, k=k, fc=fc, pg=pg: e.matmul(psb[pg][:], lhsT=wgt[:, k, fc * 128:(fc + 1) * 128], rhs=h2T[:, k, :],
                                                                          start=(k == 0), stop=(k == 7)),
                              reads=[("h2T", 0), ("h2T", 1), ("h2T", 2), ("h2T", 3)], writes=[PK(pg)])
                    for k in range(8):
                        p.add("pe", lambda e, k=k, fc=fc, pu=pu: e.matmul(psb[pu][:], lhsT=wup[:, k, fc * 128:(fc + 1) * 128], rhs=h2T[:, k, :],
                                                                          start=(k == 0), stop=(k == 7)),
                              reads=[("h2T", 0), ("h2T", 1), ("h2T", 2), ("h2T", 3)], writes=[PK(pu)])
                    sgi = fc % 2
                    p.add("act", lambda e, pg=pg, sgi=sgi: e.activation(out=sg[sgi][:], in_=psb[pg][:], func=AF.Silu), reads=[PK(pg)], writes=[("sg", sgi)])
                    p.add("dve", lambda e, pu=pu, sgi=sgi, fc=fc: e.tensor_tensor(out=aT[:, fc, :], in0=sg[sgi][:], in1=psb[pu][:], op=ALU.mult),
                          reads=[PK(pu), ("sg", sgi)], writes=[("aT", fc)])
                for tt in range(4):
                    tok = slice(b * 512 + tt * 128, b * 512 + (tt + 1) * 128)
                    xt = xtl[tt]
                    for half in range(2):
                        po = 6 + half
                        for fc in range(22):
                            p.add("pe", lambda e, fc=fc, tt=tt, half=half, po=po: e.matmul(
                                psb[po][:], lhsT=aT[:, fc, tt * 128:(tt + 1) * 128], rhs=wdn[:, fc, half * 512:(half + 1) * 512],
                                start=(fc == 0), stop=(fc == 21)), reads=[("aT", fc)], writes=[PK(po)])
                        p.add("dve", lambda e, xt=xt, half=half, po=po: e.tensor_tensor(out=xt[:, half * 512:(half + 1) * 512],
                                                                                        in0=xt[:, half * 512:(half + 1) * 512], in1=psb[po][:], op=ALU.add),
                              reads=[PK(po), ("xtl", tt)], writes=[("xtl", tt)])
                    dma("pool", xres_d[tok, :], xt[:], reads=[("xtl", tt)], writes=[("xres", b, tt)])

        if "G" in phases:
            _phG()

    if "Z" in phases:
        phase_reset()
        gf = sb.alloc("gf", [128, D], F32)
        dma("sp", gf[:], fg_d.ap(), writes=["gf"])
        xb = [sb.alloc("xb", [128, D], F32) for _ in range(2)]
        ob_ = [sb.alloc("ob", [128, D], F32) for _ in range(2)]
        sqj = sb.alloc("sqj", [128, D], F32)
        ssb = [sb.alloc("ss", [128, 1], F32) for _ in range(2)]
        for t in range(NT):
            i = t % 2
            tok = slice(t * 128, (t + 1) * 128)
            dma("sp", xb[i][:], xres_d[tok, :], writes=[("xb", i)])
            p.add("dve", lambda e, i=i: e.memset(ssb[i][:], 0.0), writes=[("ss", i)])
            p.add("act", lambda e, i=i: e.activation(out=sqj[:], in_=xb[i][:], func=AF.Square, accum_out=ssb[i][:]),
                  reads=[("xb", i)], writes=[("ss", i), "sqj"])
            rstd_ops(ssb[i][:], ("ss", i), 1.0 / D)
            p.add("dve", lambda e, i=i: e.scalar_tensor_tensor(out=ob_[i][:], in0=xb[i][:], scalar=ssb[i][:], in1=gf[:], op0=ALU.mult, op1=ALU.mult),
                  reads=[("xb", i), ("ss", i), "gf"], writes=[("ob", i)])
            dma("pool", out_d[tok, :], ob_[i][:], reads=[("ob", i)], writes=[("out", t)])
    p.barrier()
    p.wait_all("pool", [])
    p.emit()
    return nc


def natten_tables(rpb, S):
    rows = S // GRID_W
    NT = S // 128
    wr, wc = 8, 16
    out = np.full((5, 128, 5, 8, 128), NEG, np.float32)
    reps = [0, 1, 2, NT - 2, NT - 1]
    for pi, i in enumerate(reps):
        kb0 = min(max(i - 2, 0), NT - 5)
        q = np.arange(i * 128, (i + 1) * 128)
        r = q // GRID_W
        c = q % GRID_W
        rs = np.clip(r - wr // 2, 0, rows - wr)
        cs = np.clip(c - wc // 2, 0, GRID_W - wc)
        keys = np.arange(kb0 * 128, (kb0 + 5) * 128)
        kr = keys // GRID_W
        kc = keys % GRID_W
        inwin = ((kr[None, :] >= rs[:, None]) & (kr[None, :] < rs[:, None] + wr) &
                 (kc[None, :] >= cs[:, None]) & (kc[None, :] < cs[:, None] + wc))
        offr = np.clip(kr[None, :] - r[:, None] + (wr - 1), 0, 2 * wr - 2)
        offc = np.clip(kc[None, :] - c[:, None] + (wc - 1), 0, 2 * wc - 2)
        g = rpb[:, offr, offc]
        g = np.where(inwin[None], g, np.float32(NEG)).astype(np.float32)
        g = g.reshape(8, 128, 5, 128).transpose(3, 2, 0, 1)
        out[pi] = g
    return out


def host_consts(S):
    t = np.arange(S)
    row = (t // GRID_W).astype(np.float32)
    col = (t % GRID_W).astype(np.float32)
    nf = 16
    inv = (10000.0 ** (-np.arange(nf, dtype=np.float32) / nf)).astype(np.float32)
    ar = row[:, None] * inv
    ac = col[:, None] * inv
    cos64 = np.concatenate([np.cos(ar), np.cos(ar), np.cos(ac), np.cos(ac)], axis=1).astype(np.float32)
    sin64 = np.concatenate([-np.sin(ar), np.sin(ar), -np.sin(ac), np.sin(ac)], axis=1).astype(np.float32)
    W = 2 * S - 128
    C0 = (S // 128 - 1) * 128
    pp = np.arange(128)[:, None]
    cc = np.arange(W)[None, :]
    tz = (-np.abs(cc - pp - C0)).astype(np.float32)
    s_ = np.arange(128)[:, None]
    t_ = np.arange(128)[None, :]
    maskf = (s_ <= t_).astype(np.float32)
    maskb = (s_ >= t_).astype(np.float32)
    ones = np.ones((128, 128), np.float32)
    sel = np.zeros((65, 64), np.float32)
    sel[64, :] = 1.0
    ident = np.eye(128, dtype=np.float32).astype(ml_dtypes.bfloat16)
    return dict(cos64=cos64, sin64=sin64, tz=tz, maskf=maskf, maskb=maskb, ones=ones, sel=sel, ident=ident)


def rep128(a):
    a = np.asarray(a, np.float32)
    return np.ascontiguousarray(np.broadcast_to(a[:, None, :], (a.shape[0], 128, a.shape[1])))


def prep_shared(inp, S):
    L = inp["w_in"].shape[0]
    f = lambda k: np.ascontiguousarray(np.asarray(inp[k], np.float32))
    sh = dict(
        w_in=f("w_in"), w_up_a=f("w_up_a"), w_up_b=f("w_up_b"), w_up_c=f("w_up_c"), w_up_d=f("w_up_d"),
        w_out=f("w_out"), w_ffn_gate=f("w_ffn_gate"), w_ffn_up=f("w_ffn_up"), w_ffn_down=f("w_ffn_down"),
        norm1_g_r=rep128(inp["norm1_g"]), norm2_g_r=rep128(inp["norm2_g"]),
        final_g_r=np.ascontiguousarray(np.broadcast_to(np.asarray(inp["final_g"], np.float32)[None, :], (128, D))),
        conv_w_r=np.ascontiguousarray(np.asarray(inp["a_conv_w"], np.float32).reshape(L, 3, 8, 128).transpose(0, 3, 2, 1)),
        gbias_r=rep128(inp["a_gate_bias"]), anorm_g_r=rep128(inp["a_norm_g"]),
        bqg_r=rep128(inp["b_qnorm_g"]), bkg_r=rep128(inp["b_knorm_g"]),
        dl_r=np.ascontiguousarray(np.broadcast_to(
            np.stack([np.asarray(inp[k], np.float32) for k in ("d_lambda_q1", "d_lambda_k1", "d_lambda_q2", "d_lambda_k2")], axis=1)[:, None],
            (L, 128, 4, 64))),
        dsub_r=rep128(inp["d_subln_g"]),
        natb=np.stack([natten_tables(np.asarray(inp["c_rpb"], np.float32)[l], S) for l in range(L)]),
    )
    sh.update(host_consts(S))
    return sh


_NC_CACHE = {}


def kernel(**inputs):
    x = np.asarray(inputs["x"], np.float32)
    B, S, _ = x.shape
    key = (S,)
    if key not in _NC_CACHE:
        _NC_CACHE[key] = build_program(S=S)
    nc = _NC_CACHE[key]
    sh = prep_shared(inputs, S)
    in_maps = []
    for b in range(B):
        m = dict(sh)
        m["x"] = np.ascontiguousarray(x[b])
        in_maps.append(m)
    res = run_bass_kernel_spmd(nc, in_maps, core_ids=list(range(B)))
    return np.stack([np.asarray(r["out"], np.float32) for r in res.results], axis=0)
```

```python
import math
import numpy as np
import ml_dtypes
import concourse.bass as bass
import concourse.mybir as mybir
from concourse.bass_utils import run_bass_kernel_spmd

F32 = mybir.dt.float32
BF16 = mybir.dt.bfloat16
AF = mybir.ActivationFunctionType
ALU = mybir.AluOpType
AX = mybir.AxisListType

D = 1024
SEQ = 4096
DEPTH = 2
GRID_W = 64
EPS = 1e-6
D_IN = 10000
D_FF = 2816
NEG = -30000.0

ENGS = ("pe", "act", "dve", "pool", "sp")
DMA_RING = 8
NO_SELF_SYNC = ("pe",)


class Op:
    __slots__ = ("eng", "fn", "idx", "dma", "waits", "val", "need_inc", "clock", "ring", "semval")

    def __init__(self, eng, fn, idx, dma):
        self.eng = eng
        self.fn = fn
        self.idx = idx
        self.dma = dma
        self.waits = []
        self.need_inc = False
        self.ring = None
        self.semval = None


class Prog:
    def __init__(self, nc, same_engine_sync=True):
        self.nc = nc
        self.ops = {e: [] for e in ENGS}
        self.writer = {}
        self.readers = {}
        self.know = {e: {} for e in ENGS}
        self.dma_count = {e: 0 for e in ENGS}
        self.dma_ops = {e: [] for e in ENGS}
        self.same_engine_sync = same_engine_sync
        self.barrier_deps = {e: [] for e in ENGS}

    def add(self, eng, fn, reads=(), writes=(), dma=False):
        lst = self.ops[eng]
        op = Op(eng, fn, len(lst), dma)
        deps = []
        for k in reads:
            w = self.writer.get(k)
            if w is not None:
                deps.append(w)
        for k in writes:
            w = self.writer.get(k)
            if w is not None:
                deps.append(w)
            deps.extend(self.readers.get(k, ()))
        if self.barrier_deps[eng]:
            deps.extend(self.barrier_deps[eng])
            self.barrier_deps[eng] = []
        if dma:
            n = self.dma_count[eng]
            op.ring = n % DMA_RING
            if n >= DMA_RING:
                deps.append(self.dma_ops[eng][n - DMA_RING])
            self.dma_count[eng] = n + 1
            self.dma_ops[eng].append(op)
        know = self.know[eng]
        for d in deps:
            if d is op:
                continue
            if d.dma:
                key = ("dma", d.eng, d.ring)
                val = d.val
            else:
                if d.eng == eng and (eng in NO_SELF_SYNC or not self.same_engine_sync):
                    continue
                key = d.eng
                val = d.idx
            if know.get(key, -1) >= val:
                continue
            op.waits.append(d)
            d.need_inc = True
            for k2, v2 in d.clock.items():
                if know.get(k2, -1) < v2:
                    know[k2] = v2
        if dma:
            op.val = (self.dma_count[eng] - 1) // DMA_RING
            ck = ("dma", eng, op.ring)
        else:
            op.val = op.idx
            ck = eng
        op.clock = dict(know)
        op.clock[ck] = op.val
        lst.append(op)
        for k in writes:
            self.writer[k] = op
            self.readers[k] = []
        for k in reads:
            if k not in writes:
                self.readers.setdefault(k, []).append(op)
        return op

    def wait_all(self, eng, keys):
        return self.add(eng, lambda e: None, reads=list(keys))

    def barrier(self):
        lasts = []
        for e in ENGS:
            if self.ops[e]:
                lasts.append(self.ops[e][-1])
            lasts.extend(self.dma_ops[e][-DMA_RING:])
        for e in ENGS:
            self.barrier_deps[e] = list(lasts)

    def emit(self):
        nc = self.nc
        from contextlib import ExitStack
        with ExitStack() as es:
            csem = {e: es.enter_context(nc.semaphore("c_" + e)) for e in ENGS}
            dsem = {(e, r): es.enter_context(nc.semaphore("d_%s_%d" % (e, r)))
                    for e in ENGS for r in range(DMA_RING) if self.dma_count[e] > 0}
            for e in ENGS:
                cnt = 0
                for op in self.ops[e]:
                    if not op.dma and op.need_inc:
                        cnt += 1
                        op.semval = cnt
            block = es.enter_context(nc.Block())

            def run(e, eng):
                for op in self.ops[e]:
                    for d in op.waits:
                        if d.dma:
                            eng.wait_ge(dsem[(d.eng, d.ring)], 16 * (d.val + 1))
                        else:
                            eng.wait_ge(csem[d.eng], d.semval)
                    ins = op.fn(eng)
                    if ins is None:
                        continue
                    if op.dma:
                        ins.then_inc(dsem[(e, op.ring)], 16)
                    elif op.need_inc:
                        ins.then_inc(csem[e], 1)

            @block.sync
            def _(eng):
                run("sp", eng)

            @block.scalar
            def _(eng):
                run("act", eng)

            @block.vector
            def _(eng):
                run("dve", eng)

            @block.gpsimd
            def _(eng):
                run("pool", eng)

            @block.tensor
            def _(eng):
                run("pe", eng)


class Pipe:
    def __init__(self, offs):
        self.offs = offs
        self.items = []

    def push(self, *fns):
        self.items.append(fns)

    def flush(self):
        n = len(self.items)
        m = max(self.offs)
        for t in range(n + m):
            for j, o in enumerate(self.offs):
                i = t - o
                if 0 <= i < n and self.items[i][j] is not None:
                    self.items[i][j]()
        self.items = []


SB_BASE = 16640
SB_LIMIT = 229376


class Arena:
    def __init__(self, nc):
        self.nc = nc
        self.off = SB_BASE
        self.n = 0

    def alloc(self, name, shape, dt):
        esz = 4 if dt == F32 else 2
        nbytes = int(np.prod(shape[1:])) * esz
        nbytes = (nbytes + 63) // 64 * 64
        assert self.off + nbytes <= SB_LIMIT, "SBUF overflow at %s: %d" % (name, self.off + nbytes)
        self.n += 1
        t = self.nc.alloc_sbuf_tensor_at("%s_%d" % (name, self.n), list(shape), dt, offset=self.off)
        self.off += nbytes
        return t

    def mark(self):
        return self.off

    def reset(self, to=SB_BASE):
        self.off = to


def bc_ap(ap, dims):
    return bass.AP(tensor=ap.tensor, offset=ap.offset, ap=[list(ap.ap[0])] + [list(d) for d in dims])


def build_program(S=SEQ, depth=DEPTH, debug=False, phases="ABCDEFGZ"):
    NT = S // 128
    NB = S // 512
    rows = S // GRID_W
    nc = bass.Bass("TRN2", target_bir_lowering=False)
    p = Prog(nc)
    sb = Arena(nc)
    L = depth

    def din(name, shape, dt=F32):
        return nc.dram_tensor(name, list(shape), dt, kind="ExternalInput")

    def dscr(name, shape, dt=BF16):
        return nc.dram_tensor(name, list(shape), dt, kind=("ExternalOutput" if debug else "Internal"))

    x_d = din("x", [S, D])
    w_in_d = din("w_in", [L, D, D_IN])
    w_up_d = [din("w_up_%s" % c, [L, 512, D]) for c in "abcd"]
    w_out_d = din("w_out", [L, D, D])
    w_fg_d = din("w_ffn_gate", [L, D, D_FF])
    w_fu_d = din("w_ffn_up", [L, D, D_FF])
    w_fd_d = din("w_ffn_down", [L, D_FF, D])
    n1g_d = din("norm1_g_r", [L, 128, D])
    n2g_d = din("norm2_g_r", [L, 128, D])
    fg_d = din("final_g_r", [128, D])
    convw_d = din("conv_w_r", [L, 128, 8, 3])
    gbias_d = din("gbias_r", [L, 128, 16])
    ang_d = din("anorm_g_r", [L, 128, 512])
    bqg_d = din("bqg_r", [L, 128, 64])
    bkg_d = din("bkg_r", [L, 128, 64])
    dl_d = din("dl_r", [L, 128, 4, 64])
    dsub_d = din("dsub_r", [L, 128, 128])
    natb_d = din("natb", [L, 5, 128, 5, 8, 128])
    ident_d = din("ident", [128, 128], BF16)
    cos_d = din("cos64", [S, 64])
    sin_d = din("sin64", [S, 64])
    tz_d = din("tz", [128, 2 * S - 128])
    maskf_d = din("maskf", [128, 128])
    maskb_d = din("maskb", [128, 128])
    ones_d = din("ones", [128, 128])
    sel_d = din("sel", [65, 64])
    out_d = nc.dram_tensor("out", [S, D], F32, kind="ExternalOutput")

    xres_d = dscr("xres", [S, D], F32)
    hT_d = dscr("hT_s", [D, S])
    aqk_d = dscr("aqk_s", [1024, S])
    av1_d = dscr("av1_s", [S, 4, 129])
    ao_d = dscr("ao_s", [S, 512])
    ag_d = dscr("ag_s", [S, 16], F32)
    bqT_d = dscr("bqT_s", [512, S])
    bkT_d = dscr("bkT_s", [128, S])
    bv1_d = dscr("bv1_s", [S, 2, 65])
    cqT_d = dscr("cqT_s", [512, S])
    ckT_d = dscr("ckT_s", [512, S])
    cv1_d = dscr("cv1_s", [S, 8, 65])
    dqT_d = dscr("dqT_s", [512, S])
    dkT_d = dscr("dkT_s", [512, S])
    dv1_d = dscr("dv1_s", [S, 4, 129])
    yT_d = dscr("yT_s", [2048, S])

    psb = [nc.alloc_psum_tensor("psb%d" % i, [128, 512], F32) for i in range(8)]
    psb16 = [b.bitcast(BF16) for b in psb]

    def PK(i):
        return ("ps", i)

    def dma(eng, out, in_, reads=(), writes=(), **kw):
        return p.add(eng, lambda e: e.dma_start(out=out, in_=in_, **kw), reads=reads, writes=writes, dma=True)

    def new_phase():
        p.barrier()
        sb.reset()

    cnt = [0]

    def uid():
        cnt[0] += 1
        return cnt[0]

    ident = sb.alloc("ident", [128, 128], BF16)
    dma("sp", ident[:], ident_d.ap(), writes=["ident"])
    base_mark = sb.mark()

    def phase_reset():
        p.barrier()
        sb.reset(base_mark)

    evac_rr = [0]

    def evac_engine():
        evac_rr[0] += 1
        return "act" if evac_rr[0] % 2 else "dve"

    def copy_op(eng, out, in_, reads, writes, scale=None):
        if eng == "act":
            if scale is None:
                return p.add("act", lambda e: e.copy(out=out, in_=in_), reads=reads, writes=writes)
            return p.add("act", lambda e: e.mul(out=out, in_=in_, mul=scale), reads=reads, writes=writes)
        else:
            if scale is None:
                return p.add(eng, lambda e: e.tensor_copy(out=out, in_=in_), reads=reads, writes=writes)
            return p.add(eng, lambda e: e.tensor_scalar(out=out, in0=in_, scalar1=scale, scalar2=None, op0=ALU.mult),
                         reads=reads, writes=writes)

    def rstd_ops(v, key, n_scale, eps=EPS):
        p.add("dve", lambda e: e.tensor_scalar(out=v, in0=v, scalar1=n_scale, scalar2=eps, op0=ALU.mult, op1=ALU.add),
              reads=[key], writes=[key])
        p.add("act", lambda e: e.activation(out=v, in_=v, func=AF.Ln), reads=[key], writes=[key])
        p.add("act", lambda e: e.activation(out=v, in_=v, func=AF.Exp, scale=-0.5), reads=[key], writes=[key])

    stage_rr = [0]

    def load_cast(dst_ap_fn, src_ap_fn, nchunk, ncols, stage_f, tag, cols_per=512):
        for c0 in range(0, ncols, cols_per):
            c1 = min(ncols, c0 + cols_per)
            i = stage_rr[0]
            stage_rr[0] += 1
            st = stage_f[i % len(stage_f)]
            sk = ("stage", i % len(stage_f))
            dma("sp", st[:, 0:nchunk, 0:c1 - c0], src_ap_fn(c0, c1), writes=[sk])
            if i % 2 == 0:
                p.add("act", lambda e, st=st, c0=c0, c1=c1: e.copy(out=dst_ap_fn(c0, c1), in_=st[:, 0:nchunk, 0:c1 - c0]),
                      reads=[sk], writes=[("w", tag, c0)])
            else:
                p.add("dve", lambda e, st=st, c0=c0, c1=c1: e.tensor_copy(out=dst_ap_fn(c0, c1), in_=st[:, 0:nchunk, 0:c1 - c0]),
                      reads=[sk], writes=[("w", tag, c0)])

    def load_qpad(qTp, src_d):
        p.add("dve", lambda e: e.memset(qTp[:], 0.0), writes=["qT"])
        for h in range(8):
            r = (h % 2) * 64
            dma("sp", qTp[r:r + 64, h, :], src_d[h * 64:(h + 1) * 64, :], reads=["qT"], writes=[("qTh", h)])

    def norm_tile(xt, xk, gt, gk, sqj, ss, ssk, hb, hbk, ptr_i, dst_ap, dst_key):
        p.add("dve", lambda e: e.memset(ss, 0.0), writes=[ssk])
        p.add("act", lambda e: e.activation(out=sqj, in_=xt, func=AF.Square, accum_out=ss), reads=[xk], writes=[ssk, "sqj"])
        rstd_ops(ss, ssk, 1.0 / D)
        p.add("dve", lambda e: e.scalar_tensor_tensor(out=hb, in0=xt, scalar=ss, in1=gt, op0=ALU.mult, op1=ALU.mult),
              reads=[xk, ssk, gk], writes=[hbk])
        for c in range(8):
            p.add("pe", lambda e, c=c: e.transpose(out=psb16[ptr_i][:, c * 128:(c + 1) * 128], in_=hb[:, c * 128:(c + 1) * 128],
                                                   identity=ident[:]), reads=[hbk], writes=[PK(ptr_i)])
        copy_op("act", dst_ap, psb16[ptr_i][:, 0:1024].rearrange("p (c t) -> p c t", c=8), [PK(ptr_i)], [dst_key])

    for l in range(L):
        xsrc_d = x_d if l == 0 else xres_d
        lambda_init = 0.8 - 0.6 * math.exp(-0.3 * l)

        def _phA(l=l, xsrc_d=xsrc_d, lambda_init=lambda_init):
            phase_reset()
            hT = sb.alloc("hT", [128, 8, S], BF16)
            g1 = sb.alloc("g1", [128, D], F32)
            dma("sp", g1[:], n1g_d[l], writes=["g1"])
            xb = [sb.alloc("xb", [128, D], F32) for _ in range(2)]
            sqj = sb.alloc("sqj", [128, D], F32)
            ssb = [sb.alloc("ss", [128, 1], F32) for _ in range(2)]
            hbb = [sb.alloc("hb", [128, D], BF16) for _ in range(2)]
            mA = sb.mark()
            for t in range(NT):
                xt = xb[t % 2]
                dma("sp", xt[:], xsrc_d[t * 128:(t + 1) * 128, :], writes=[("xb", t % 2)])
                norm_tile(xt[:], ("xb", t % 2), g1[:], "g1", sqj[:], ssb[t % 2][:], ("ss", t % 2), hbb[t % 2][:], ("hb", t % 2),
                          6 + t % 2, hT[:, :, t * 128:(t + 1) * 128], ("hT", t))
            p.barrier()
            dma("pool", hT_d.ap().rearrange("(c p) s -> p c s", p=128), hT[:], writes=["hT_d"])

            wf = [sb.alloc("wf", [128, 8, 512], F32) for _ in range(2)]
            wb = [sb.alloc("wb", [128, 8, 512], BF16) for _ in range(2)]
            stgf = [sb.alloc("stgf", [128, S + 2], F32) for _ in range(2)]
            stgb = [sb.alloc("stgb", [128, S], BF16) for _ in range(2)]
            cw = sb.alloc("cw", [128, 8, 3], F32)
            ctmp = sb.alloc("ctmp", [128, S], F32)
            dma("sp", cw[:], convw_d[l], writes=["cw"])
            for i in range(2):
                p.add("dve", lambda e, i=i: e.memset(stgf[i][:], 0.0), writes=[("stgf", i, b) for b in range(NB)])
            groups = [("aq", 0, aqk_d, 0, None), ("ak", 512, aqk_d, 512, None),
                      ("cq", 2832, cqT_d, 0, 0.125), ("ck", 3344, ckT_d, 0, None),
                      ("dq", 4368, dqT_d, 0, 0.125), ("dk", 4880, dkT_d, 0, None)]
            cidx = 0
            bank = 0
            for gi, (gname, col0, dst_d, drow0, scale) in enumerate(groups):
                wfi, wbi = wf[gi % 2], wb[gi % 2]
                dma("sp", wfi[:], w_in_d[l, :, col0:col0 + 512].rearrange("(c p) n -> p c n", p=128), writes=[("wf", gi % 2)])
                p.add("act", lambda e, wfi=wfi, wbi=wbi: e.copy(out=wbi[:], in_=wfi[:]),
                      reads=[("wf", gi % 2)], writes=[("wb", gi % 2)])
                is_a = gname in ("aq", "ak")
                for cc in range(4):
                    si = cidx % 2
                    cidx += 1
                    for b in range(NB):
                        bk = bank % 6
                        bank += 1
                        for k in range(8):
                            p.add("pe", lambda e, k=k, cc=cc, b=b, bk=bk, wbi=wbi: e.matmul(
                                psb[bk][:], lhsT=wbi[:, k, cc * 128:(cc + 1) * 128], rhs=hT[:, k, b * 512:(b + 1) * 512],
                                start=(k == 0), stop=(k == 7)), reads=[("wb", gi % 2)], writes=[PK(bk)])
                        if is_a:
                            copy_op(evac_engine(), stgf[si][:, 1 + b * 512:1 + (b + 1) * 512], psb[bk][:], [PK(bk)], [("stgf", si, b)])
                        else:
                            copy_op(evac_engine(), stgb[si][:, b * 512:(b + 1) * 512], psb[bk][:], [PK(bk)], [("stgb", si, b)], scale=scale)
                    if is_a:
                        ch = (col0 // 128) + cc
                        sf = stgf[si]
                        p.add("dve", lambda e, sf=sf, ch=ch: e.tensor_scalar(out=ctmp[:], in0=sf[:, 1:S + 1], scalar1=cw[:, ch, 1:2],
                                                                             scalar2=None, op0=ALU.mult),
                              reads=[("stgf", si, b_) for b_ in range(NB)] + ["cw"], writes=["ctmp"])
                        p.add("dve", lambda e, sf=sf, ch=ch: e.scalar_tensor_tensor(out=ctmp[:], in0=sf[:, 0:S], scalar=cw[:, ch, 0:1],
                                                                                    in1=ctmp[:], op0=ALU.mult, op1=ALU.add),
                              reads=[("stgf", si, b_) for b_ in range(NB)] + ["cw"], writes=["ctmp"])
                        p.add("dve", lambda e, sf=sf, ch=ch: e.scalar_tensor_tensor(out=ctmp[:], in0=sf[:, 2:S + 2], scalar=cw[:, ch, 2:3],
                                                                                    in1=ctmp[:], op0=ALU.mult, op1=ALU.add),
                              reads=[("stgf", si, b_) for b_ in range(NB)] + ["cw"], writes=["ctmp"])
                        p.add("act", lambda e, si=si: e.activation(out=stgb[si][:], in_=ctmp[:], func=AF.Silu),
                              reads=["ctmp"], writes=[("stgb", si, b_) for b_ in range(NB)])
                    r0 = drow0 + cc * 128
                    dma("pool", dst_d[r0:r0 + 128, :], stgb[si][:], reads=[("stgb", si, b_) for b_ in range(NB)], writes=[(gname, cc)])

            p.barrier()
            sb.reset(mA)
            wf = [sb.alloc("wf", [128, 8, 512], F32) for _ in range(2)]
            wb = [sb.alloc("wb", [128, 8, 512], BF16) for _ in range(2)]
            cos_s = sb.alloc("cos", [128, NT, 64], F32)
            sin_s = sb.alloc("sin", [128, NT, 64], F32)
            dma("sp", cos_s[:], cos_d.ap().rearrange("(t p) f -> p t f", p=128), writes=["cos"])
            dma("sp", sin_s[:], sin_d.ap().rearrange("(t p) f -> p t f", p=128), writes=["sin"])
            gb = sb.alloc("gb", [128, 16], F32)
            dma("sp", gb[:], gbias_d[l], writes=["gb"])
            bqg = sb.alloc("bqg", [128, 64], F32)
            bkg = sb.alloc("bkg", [128, 64], F32)
            dma("sp", bqg[:], bqg_d[l], writes=["bqg"])
            dma("sp", bkg[:], bkg_d[l], writes=["bkg"])
            p.add("dve", lambda e: e.tensor_scalar(out=bqg[:], in0=bqg[:], scalar1=0.125, scalar2=None, op0=ALU.mult),
                  reads=["bqg"], writes=["bqg"])
            st129 = [sb.alloc("st129", [128, 4, 129], BF16) for _ in range(2)]
            st65 = [sb.alloc("st65", [128, 8, 65], BF16) for _ in range(2)]
            stb = [sb.alloc("stb", [128, 512], BF16) for _ in range(2)]
            stg16 = [sb.alloc("stg16", [128, 16], F32) for _ in range(2)]
            for i in range(2):
                p.add("dve", lambda e, i=i: e.memset(st129[i][:], 1.0), writes=[("st129", i)])
                p.add("dve", lambda e, i=i: e.memset(st65[i][:], 1.0), writes=[("st65", i)])
            qf = sb.alloc("qf", [128, 512], F32)
            qn = sb.alloc("qn", [128, 512], F32)
            t1 = sb.alloc("t1", [128, 512], F32)
            t2 = sb.alloc("t2", [128, 512], F32)
            ssh = sb.alloc("ssh", [128, 8], F32)
            qrb = [sb.alloc("qrb", [128, 512], BF16) for _ in range(2)]
            bqT_s = sb.alloc("bqT_s", [128, 4, S], BF16)
            bkT_s = sb.alloc("bkT_s", [128, S], BF16)

            def rope_norm(ps_ap, psk, nh, gtile, t, outb, outk):
                W = nh * 64
                qf_, qn_, t1_, t2_ = qf[:, 0:W], qn[:, 0:W], t1[:, 0:W], t2[:, 0:W]
                p.add("act", lambda e: e.copy(out=qf_, in_=ps_ap), reads=[psk], writes=["qf"])
                p.add("dve", lambda e: e.tensor_tensor(out=t1_, in0=qf_, in1=qf_, op=ALU.mult), reads=["qf"], writes=["t1"])
                p.add("dve", lambda e: e.tensor_reduce(out=ssh[:, 0:nh], in_=t1_.rearrange("p (h d) -> p h d", h=nh), axis=AX.X, op=ALU.add),
                      reads=["t1"], writes=["ssh"])
                rstd_ops(ssh[:, 0:nh], "ssh", 1.0 / 64)
                p.add("dve", lambda e: e.tensor_tensor(out=qn_.rearrange("p (h d) -> p h d", h=nh), in0=qf_.rearrange("p (h d) -> p h d", h=nh),
                                                       in1=bc_ap(ssh[:, 0:nh], [[1, nh], [0, 64]]), op=ALU.mult),
                      reads=["qf", "ssh"], writes=["qn"])
                p.add("dve", lambda e: e.tensor_tensor(out=qn_.rearrange("p (h d) -> p h d", h=nh), in0=qn_.rearrange("p (h d) -> p h d", h=nh),
                                                       in1=bc_ap(gtile[:], [[0, nh], [1, 64]]), op=ALU.mult),
                      reads=["qn", "bqg", "bkg"], writes=["qn"])
                cos_b = bc_ap(cos_s[:, t, :], [[0, nh], [1, 64]])
                p.add("dve", lambda e: e.tensor_tensor(out=t1_.rearrange("p (h d) -> p h d", h=nh), in0=qn_.rearrange("p (h d) -> p h d", h=nh),
                                                       in1=cos_b, op=ALU.mult), reads=["qn", "cos"], writes=["t1"])
                sin_lo = bc_ap(sin_s[:, t, 0:16], [[0, nh], [32, 2], [1, 16]])
                sin_hi = bc_ap(sin_s[:, t, 16:32], [[0, nh], [32, 2], [1, 16]])
                x4 = qn_.rearrange("p (h r f) -> p h r f", h=nh, r=2)
                o4 = t2_.rearrange("p (h r f) -> p h r f", h=nh, r=2)
                p.add("dve", lambda e: e.tensor_tensor(out=o4[:, :, :, 0:16], in0=x4[:, :, :, 16:32], in1=sin_lo, op=ALU.mult),
                      reads=["qn", "sin"], writes=["t2"])
                p.add("dve", lambda e: e.tensor_tensor(out=o4[:, :, :, 16:32], in0=x4[:, :, :, 0:16], in1=sin_hi, op=ALU.mult),
                      reads=["qn", "sin"], writes=["t2"])
                p.add("dve", lambda e: e.tensor_tensor(out=outb, in0=t1_, in1=t2_, op=ALU.add), reads=["t1", "t2"], writes=[outk])

            tm_groups = [("av", 1024, 512), ("ao", 1536, 512), ("ag", 2048, 16), ("bq", 2064, 512),
                         ("bkv", 2576, 256), ("cv", 3856, 512), ("dv", 5392, 512)]
            for gi, (gname, col0, ncol) in enumerate(tm_groups):
                wfi, wbi = wf[gi % 2], wb[gi % 2]
                dma("sp", wfi[:, :, 0:ncol], w_in_d[l, :, col0:col0 + ncol].rearrange("(c p) n -> p c n", p=128),
                    writes=[("wf", gi % 2)])
                p.add("act", lambda e, wfi=wfi, wbi=wbi, ncol=ncol: e.copy(out=wbi[:, :, 0:ncol], in_=wfi[:, :, 0:ncol]),
                      reads=[("wf", gi % 2)], writes=[("wb", gi % 2)])
                for t in range(NT):
                    bk = t % 4
                    si = t % 2
                    for k in range(8):
                        p.add("pe", lambda e, k=k, t=t, bk=bk, wbi=wbi, ncol=ncol: e.matmul(
                            psb[bk][:, 0:ncol], lhsT=hT[:, k, t * 128:(t + 1) * 128], rhs=wbi[:, k, 0:ncol],
                            start=(k == 0), stop=(k == 7)), reads=[("wb", gi % 2)], writes=[PK(bk)])
                    tok = slice(t * 128, (t + 1) * 128)
                    if gname == "av" or gname == "dv":
                        dst = av1_d if gname == "av" else dv1_d
                        copy_op(evac_engine(), st129[si][:, :, 0:128], psb[bk][:, 0:512].rearrange("p (h d) -> p h d", h=4),
                                [PK(bk)], [("st129", si)])
                        dma("pool", dst[tok], st129[si][:], reads=[("st129", si)], writes=[(gname, t)])
                    elif gname == "cv":
                        copy_op(evac_engine(), st65[si][:, :, 0:64], psb[bk][:, 0:512].rearrange("p (h d) -> p h d", h=8),
                                [PK(bk)], [("st65", si)])
                        dma("pool", cv1_d[tok], st65[si][:], reads=[("st65", si)], writes=[(gname, t)])
                    elif gname == "ao":
                        p.add("act", lambda e, bk=bk, si=si: e.activation(out=stb[si][:], in_=psb[bk][:], func=AF.Sigmoid),
                              reads=[PK(bk)], writes=[("stb", si)])
                        dma("pool", ao_d[tok, :], stb[si][:], reads=[("stb", si)], writes=[(gname, t)])
                    elif gname == "ag":
                        p.add("dve", lambda e, bk=bk, si=si: e.tensor_tensor(out=stg16[si][:], in0=psb[bk][:, 0:16], in1=gb[:], op=ALU.add),
                              reads=[PK(bk), "gb"], writes=[("stg16", si)])
                        dma("pool", ag_d[tok, :], stg16[si][:], reads=[("stg16", si)], writes=[(gname, t)])
                    elif gname == "bq":
                        rope_norm(psb[bk][:, 0:512], PK(bk), 8, bqg, t, qrb[si][:], ("qrb", si))
                        for c in range(4):
                            p.add("pe", lambda e, c=c, si=si: e.transpose(out=psb16[6 + si][:, c * 128:(c + 1) * 128],
                                                                          in_=qrb[si][:, c * 128:(c + 1) * 128], identity=ident[:]),
                                  reads=[("qrb", si)], writes=[PK(6 + si)])
                        copy_op("act", bqT_s[:, :, t * 128:(t + 1) * 128], psb16[6 + si][:, 0:512].rearrange("p (c t) -> p c t", c=4),
                                [PK(6 + si)], [("bqT_s", t)])
                    elif gname == "bkv":
                        rope_norm(psb[bk][:, 0:128], PK(bk), 2, bkg, t, qrb[si][:, 0:128], ("qrb", si))
                        p.add("pe", lambda e, si=si: e.transpose(out=psb16[6 + si][:, 0:128], in_=qrb[si][:, 0:128], identity=ident[:]),
                              reads=[("qrb", si)], writes=[PK(6 + si)])
                        copy_op("act", bkT_s[:, t * 128:(t + 1) * 128], psb16[6 + si][:, 0:128], [PK(6 + si)], [("bkT_s", t)])
                        copy_op("dve", st65[si][:, 0:2, 0:64], psb[bk][:, 128:256].rearrange("p (h d) -> p h d", h=2),
                                [PK(bk)], [("st65", si)])
                        dma("pool", bv1_d[tok], st65[si][:, 0:2, :], reads=[("st65", si)], writes=[("bv", t)])
                if gname == "bq":
                    p.barrier()
                    dma("pool", bqT_d.ap().rearrange("(c p) s -> p c s", p=128), bqT_s[:], writes=["bqT_d"])
                if gname == "bkv":
                    p.barrier()
                    dma("pool", bkT_d.ap(), bkT_s[:], writes=["bkT_d"])

        if "A" in phases:
            _phA()

        def _phB(l=l, xsrc_d=xsrc_d, lambda_init=lambda_init):
            phase_reset()
            qT = sb.alloc("qTp", [128, 8, S], BF16)
            kT2 = sb.alloc("kT2", [128, 2, S], BF16)
            v1 = sb.alloc("v1", [128, NT, 2, 65], BF16)
            sel = sb.alloc("sel", [65, 64], F32)
            load_qpad(qT, bqT_d)
            for g in range(2):
                dma("sp", kT2[0:64, g, :], bkT_d[g * 64:(g + 1) * 64, :], writes=[("kT2", g, 0)])
                dma("sp", kT2[64:128, g, :], bkT_d[g * 64:(g + 1) * 64, :], writes=[("kT2", g, 1)])
            dma("sp", v1[:], bv1_d.ap().rearrange("(t p) g e -> p t g e", p=128), writes=["v1"])
            dma("sp", sel[:], sel_d.ap(), writes=["sel"])
            pT = [sb.alloc("pT", [128, 512], BF16) for _ in range(3)]
            osb = [sb.alloc("osb", [65, 512], F32) for _ in range(2)]
            rb = [sb.alloc("rb", [64, 512], F32) for _ in range(2)]
            ybs = [sb.alloc("ybs", [64, 512], BF16) for _ in range(2)]
            p.barrier()
            pipe = Pipe([0, 1, 2])
            it = 0
            for h in range(8):
                g = h // 4
                j = h // 2
                pr = slice((h % 2) * 64, (h % 2) * 64 + 64)
                for qb in range(NB):
                    ob = 3 + (it // NT) % 2
                    for kt in range(NT):
                        sbk = it % 3
                        it += 1

                        def s0(sbk=sbk, g=g, h=h, kt=kt, qb=qb):
                            p.add("pe", lambda e: e.matmul(psb[sbk][:], lhsT=kT2[:, g, kt * 128:(kt + 1) * 128],
                                                           rhs=qT[:, h, qb * 512:(qb + 1) * 512], start=True, stop=True),
                                  writes=[PK(sbk)])

                        def s1(sbk=sbk):
                            p.add("act", lambda e: e.activation(out=pT[sbk][:], in_=psb[sbk][:], func=AF.Exp),
                                  reads=[PK(sbk)], writes=[("pT", sbk)])

                        def s2(sbk=sbk, ob=ob, kt=kt, g=g, h=h, qb=qb):
                            p.add("pe", lambda e: e.matmul(psb[ob][0:65, :], lhsT=v1[:, kt, g, :], rhs=pT[sbk][:],
                                                           start=(kt == 0), stop=(kt == NT - 1)),
                                  reads=[("pT", sbk)], writes=[PK(ob)])
                            if kt == NT - 1:
                                oi = ob - 3
                                p.add("act", lambda e: e.copy(out=osb[oi][:], in_=psb[ob][0:65, :]), reads=[PK(ob)], writes=[("osb", oi)])
                                p.add("pe", lambda e: e.matmul(psb[5][0:64, :], lhsT=sel[:], rhs=osb[oi][:], start=True, stop=True),
                                      reads=[("osb", oi), "sel"], writes=[PK(5)])
                                p.add("dve", lambda e: e.reciprocal(out=rb[oi][:], in_=psb[5][0:64, :]), reads=[PK(5)], writes=[("rb", oi)])
                                p.add("dve", lambda e: e.tensor_tensor(out=ybs[oi][:], in0=osb[oi][0:64, :], in1=rb[oi][:], op=ALU.mult),
                                      reads=[("osb", oi), ("rb", oi)], writes=[("ybs", oi)])
                                r0 = 512 + h * 64
                                dma("pool", yT_d[r0:r0 + 64, qb * 512:(qb + 1) * 512], ybs[oi][:], reads=[("ybs", oi)],
                                    writes=[("yb", h, qb)])
                        pipe.push(s0, s1, s2)
            pipe.flush()

        if "B" in phases:
            _phB()

        def _phC(l=l, xsrc_d=xsrc_d, lambda_init=lambda_init):
            phase_reset()
            qT = sb.alloc("qTp", [128, 8, S], BF16)
            kT = sb.alloc("kT", [128, 4, S], BF16)
            v1 = sb.alloc("v1", [128, NT, 8, 65], BF16)
            sel = sb.alloc("sel", [65, 64], F32)
            nb_int = sb.alloc("nb_int", [128, 5, 8, 128], F32)
            nb_edge = sb.alloc("nb_edge", [128, 5, 8, 128], F32)
            load_qpad(qT, cqT_d)
            dma("sp", kT[:], ckT_d.ap().rearrange("(c p) s -> p c s", p=128), writes=["kT"])
            dma("sp", v1[:], cv1_d.ap().rearrange("(t p) g e -> p t g e", p=128), writes=["v1"])
            dma("sp", sel[:], sel_d.ap(), writes=["sel"])
            dma("sp", nb_int[:], natb_d[l, 2], writes=["nb_int"])
            pT = [sb.alloc("pT", [128, 512], BF16) for _ in range(3)]
            tmpf = [sb.alloc("tmpf", [128, 512], F32) for _ in range(3)]
            osb = [sb.alloc("osb", [65, 512], F32) for _ in range(2)]
            rb = [sb.alloc("rb", [64, 512], F32) for _ in range(2)]
            ybs = [sb.alloc("ybs", [64, 512], BF16) for _ in range(2)]
            p.barrier()
            pipe = Pipe([0, 1, 2])
            it = 0
            oit = 0
            for i in range(NT):
                pat = 0 if i == 0 else 1 if i == 1 else 3 if i == NT - 2 else 4 if i == NT - 1 else 2
                kb0 = min(max(i - 2, 0), NT - 5)
                if pat != 2:
                    dma("sp", nb_edge[:], natb_d[l, pat], writes=["nb_edge"])
                nbt = nb_int if pat == 2 else nb_edge
                nbk = "nb_int" if pat == 2 else "nb_edge"
                for hg in range(2):
                    ob = 3 + oit % 2
                    oit += 1
                    for kk in range(5):
                        kt = kb0 + kk
                        sbk = it % 3
                        sbanks = [(0, 1), (2, 6)][it % 2]
                        it += 1

                        def s0(sbanks=sbanks, hg=hg, kt=kt, i=i):
                            for hh in range(4):
                                h = hg * 4 + hh
                                pr = slice((h % 2) * 64, (h % 2) * 64 + 64)
                                bkx = sbanks[h % 2]
                                p.add("pe", lambda e, hh=hh, h=h, pr=pr, bkx=bkx: e.matmul(
                                    psb[bkx][:, (hh // 2) * 128:(hh // 2 + 1) * 128], lhsT=kT[:, h // 2, kt * 128:(kt + 1) * 128],
                                    rhs=qT[:, h, i * 128:(i + 1) * 128], start=True, stop=True), writes=[PK(bkx)])

                        def s1(sbk=sbk, sbanks=sbanks, kk=kk, hg=hg, nbt=nbt, nbk=nbk):
                            for par in range(2):
                                bkx = sbanks[par]
                                p.add("dve", lambda e, par=par, bkx=bkx: e.tensor_tensor(
                                    out=tmpf[sbk][:, par * 256:(par + 1) * 256].rearrange("p (h q) -> p h q", h=2),
                                    in0=psb[bkx][:, 0:256].rearrange("p (h q) -> p h q", h=2),
                                    in1=bc_ap(nbt[:, kk, hg * 4 + par, :], [[256, 2], [1, 128]]), op=ALU.add),
                                    reads=[PK(bkx), nbk], writes=[("tmpf", sbk)])
                            p.add("act", lambda e: e.activation(out=pT[sbk][:], in_=tmpf[sbk][:], func=AF.Exp),
                                  reads=[("tmpf", sbk)], writes=[("pT", sbk)])

                        def s2(sbk=sbk, ob=ob, kk=kk, kt=kt, hg=hg, i=i):
                            for hh in range(4):
                                h = hg * 4 + hh
                                p.add("pe", lambda e, hh=hh, h=h: e.matmul(
                                    psb[ob][0:65, hh * 128:(hh + 1) * 128], lhsT=v1[:, kt, h, :],
                                    rhs=pT[sbk][:, (hh % 2) * 256 + (hh // 2) * 128:(hh % 2) * 256 + (hh // 2 + 1) * 128],
                                    start=(kk == 0 and hh == 0), stop=(kk == 4), skip_group_check=True), reads=[("pT", sbk)], writes=[PK(ob)])
                            if kk == 4:
                                oi = ob - 3
                                p.add("act", lambda e: e.copy(out=osb[oi][:], in_=psb[ob][0:65, :]), reads=[PK(ob)], writes=[("osb", oi)])
                                p.add("pe", lambda e: e.matmul(psb[5][0:64, :], lhsT=sel[:], rhs=osb[oi][:], start=True, stop=True),
                                      reads=[("osb", oi), "sel"], writes=[PK(5)])
                                p.add("dve", lambda e: e.reciprocal(out=rb[oi][:], in_=psb[5][0:64, :]), reads=[PK(5)], writes=[("rb", oi)])
                                p.add("dve", lambda e: e.tensor_tensor(out=ybs[oi][:], in0=osb[oi][0:64, :], in1=rb[oi][:], op=ALU.mult),
                                      reads=[("osb", oi), ("rb", oi)], writes=[("ybs", oi)])
                                for hh in range(4):
                                    r0 = 1024 + (hg * 4 + hh) * 64
                                    dma("pool", yT_d[r0:r0 + 64, i * 128:(i + 1) * 128], ybs[oi][:, hh * 128:(hh + 1) * 128],
                                        reads=[("ybs", oi)], writes=[("yc", hg * 4 + hh, i)])
                        pipe.push(s0, s1, s2)
                if pat != 2:
                    pipe.flush()
            pipe.flush()

        if "C" in phases:
            _phC()

        def _phD(l=l, xsrc_d=xsrc_d, lambda_init=lambda_init):
            phase_reset()
            qT = sb.alloc("qTp", [128, 8, S], BF16)
            kT = sb.alloc("kT", [128, 4, S], BF16)
            v1 = sb.alloc("v1", [128, NT, 4, 129], BF16)
            tz = sb.alloc("tz", [128, 2 * S - 128], F32)
            dlr = sb.alloc("dlr", [128, 4, 64], F32)
            dsub = sb.alloc("dsub", [128, 128], F32)
            load_qpad(qT, dqT_d)
            dma("sp", kT[:], dkT_d.ap().rearrange("(c p) s -> p c s", p=128), writes=["kT"])
            dma("sp", v1[:], dv1_d.ap().rearrange("(t p) g e -> p t g e", p=128), writes=["v1"])
            dma("sp", tz[:], tz_d.ap(), writes=["tz"])
            dma("sp", dlr[:], dl_d[l], writes=["dlr"])
            dma("sp", dsub[:], dsub_d[l], writes=["dsub"])
            lt = sb.alloc("lt", [128, 2, 64], F32)
            ls = sb.alloc("ls", [128, 2], F32)
            nlam = sb.alloc("nlam", [128, 1], F32)
            p.add("dve", lambda e: e.tensor_tensor(out=lt[:, 0, :], in0=dlr[:, 0, :], in1=dlr[:, 1, :], op=ALU.mult), reads=["dlr"], writes=["lt"])
            p.add("dve", lambda e: e.tensor_tensor(out=lt[:, 1, :], in0=dlr[:, 2, :], in1=dlr[:, 3, :], op=ALU.mult), reads=["dlr"], writes=["lt"])
            p.add("dve", lambda e: e.tensor_reduce(out=ls[:], in_=lt[:], axis=AX.X, op=ALU.add), reads=["lt"], writes=["ls"])
            p.add("act", lambda e: e.activation(out=ls[:], in_=ls[:], func=AF.Exp), reads=["ls"], writes=["ls"])
            p.add("dve", lambda e: e.tensor_tensor(out=nlam[:], in0=ls[:, 1:2], in1=ls[:, 0:1], op=ALU.subtract), reads=["ls"], writes=["nlam"])
            p.add("dve", lambda e: e.tensor_scalar(out=nlam[:], in0=nlam[:], scalar1=-lambda_init, scalar2=None, op0=ALU.add),
                  reads=["nlam"], writes=["nlam"])
            p.add("dve", lambda e: e.tensor_scalar(out=dsub[:], in0=dsub[:], scalar1=1.0 - lambda_init, scalar2=None, op0=ALU.mult),
                  reads=["dsub"], writes=["dsub"])
            pT = [sb.alloc("pT", [128, 512], BF16) for _ in range(3)]
            tmpf = [sb.alloc("tmpf", [128, 512], F32) for _ in range(3)]
            r1 = sb.alloc("r1", [128, 1], F32)
            r2 = sb.alloc("r2", [128, 1], F32)
            of = sb.alloc("of", [128, 128], F32)
            osq = sb.alloc("osq", [128, 128], F32)
            oss = sb.alloc("oss", [128, 1], F32)
            yb = [sb.alloc("yb", [128, 128], BF16) for _ in range(2)]
            yds = [sb.alloc("yds", [128, 512], BF16) for _ in range(2)]
            p.barrier()
            regs = {}
            ri = 0
            for c in range(2):
                for qq in range(4):
                    regs[(c, qq)] = (3 + ri // 3, (ri % 3) * 160)
                    ri += 1
            pipe = Pipe([0, 1, 2])
            it = 0
            ep = 0
            for h in range(4):
                slope = 2.0 ** (-8.0 * (h + 1) / 4)
                for qb in range(NB):
                    for kt in range(NT):
                        for c in range(2):
                            f0 = c * 256 + h * 64
                            j = f0 // 128
                            pr = slice(f0 % 128, f0 % 128 + 64)
                            sbk = it % 3
                            it += 1
                            off = qb * 512 - kt * 128 + (NT - 1) * 128

                            def s0(sbk=sbk, j=j, hd=f0 // 64, kt=kt, qb=qb):
                                p.add("pe", lambda e: e.matmul(psb[sbk][:], lhsT=kT[:, j, kt * 128:(kt + 1) * 128],
                                                               rhs=qT[:, hd, qb * 512:(qb + 1) * 512], start=True, stop=True),
                                      writes=[PK(sbk)])

                            def s1(sbk=sbk, off=off, slope=slope):
                                p.add("dve", lambda e: e.scalar_tensor_tensor(out=tmpf[sbk][:], in0=tz[:, off:off + 512], scalar=slope,
                                                                              in1=psb[sbk][:], op0=ALU.mult, op1=ALU.add),
                                      reads=[PK(sbk)], writes=[("tmpf", sbk)])
                                p.add("act", lambda e: e.activation(out=pT[sbk][:], in_=tmpf[sbk][:], func=AF.Exp),
                                      reads=[("tmpf", sbk)], writes=[("pT", sbk)])

                            def s2(sbk=sbk, c=c, kt=kt, h=h, qb=qb):
                                nonlocal ep
                                for qq in range(4):
                                    bkk, co = regs[(c, qq)]
                                    p.add("pe", lambda e, qq=qq, bkk=bkk, co=co: e.matmul(
                                        psb[bkk][:, co:co + 129], lhsT=pT[sbk][:, qq * 128:(qq + 1) * 128], rhs=v1[:, kt, h, :],
                                        start=(kt == 0 and co == 0), stop=(kt == NT - 1), skip_group_check=True),
                                        reads=[("pT", sbk)], writes=[PK(bkk)])
                                if kt == NT - 1 and c == 1:
                                    ydi = ep % 2
                                    ep += 1
                                    for qq in range(4):
                                        b1, c1 = regs[(0, qq)]
                                        b2, c2 = regs[(1, qq)]
                                        ybi = qq % 2
                                        p.add("dve", lambda e, b1=b1, c1=c1: e.reciprocal(out=r1[:], in_=psb[b1][:, c1 + 128:c1 + 129]),
                                              reads=[PK(b1)], writes=["r1"])
                                        p.add("dve", lambda e, b2=b2, c2=c2: e.reciprocal(out=r2[:], in_=psb[b2][:, c2 + 128:c2 + 129]),
                                              reads=[PK(b2)], writes=["r2"])
                                        p.add("dve", lambda e: e.tensor_tensor(out=r2[:], in0=r2[:], in1=nlam[:], op=ALU.mult),
                                              reads=["r2", "nlam"], writes=["r2"])
                                        p.add("dve", lambda e, b1=b1, c1=c1: e.tensor_scalar(out=of[:], in0=psb[b1][:, c1:c1 + 128], scalar1=r1[:],
                                                                                             scalar2=None, op0=ALU.mult),
                                              reads=[PK(b1), "r1"], writes=["of"])
                                        p.add("dve", lambda e, b2=b2, c2=c2: e.scalar_tensor_tensor(out=of[:], in0=psb[b2][:, c2:c2 + 128], scalar=r2[:],
                                                                                                    in1=of[:], op0=ALU.mult, op1=ALU.add),
                                              reads=[PK(b2), "r2", "of"], writes=["of"])
                                        p.add("dve", lambda e: e.memset(oss[:], 0.0), writes=["oss"])
                                        p.add("act", lambda e: e.activation(out=osq[:], in_=of[:], func=AF.Square, accum_out=oss[:]),
                                              reads=["of", "oss"], writes=["osq", "oss"])
                                        rstd_ops(oss[:], "oss", 1.0 / 128)
                                        p.add("dve", lambda e, ybi=ybi: e.scalar_tensor_tensor(out=yb[ybi][:], in0=of[:], scalar=oss[:], in1=dsub[:],
                                                                                               op0=ALU.mult, op1=ALU.mult),
                                              reads=["of", "oss", "dsub"], writes=[("yb", ybi)])
                                        p.add("pe", lambda e, qq=qq, ybi=ybi: e.transpose(out=psb16[7][:, qq * 128:(qq + 1) * 128], in_=yb[ybi][:],
                                                                                          identity=ident[:]), reads=[("yb", ybi)], writes=[PK(7)])
                                    copy_op("act", yds[ydi][:], psb16[7][:, 0:512], [PK(7)], [("yds", ydi)])
                                    r0 = 1536 + h * 128
                                    dma("pool", yT_d[r0:r0 + 128, qb * 512:(qb + 1) * 512], yds[ydi][:], reads=[("yds", ydi)],
                                        writes=[("yd", h, qb)])
                            pipe.push(s0, s1, s2)
            pipe.flush()

        if "D" in phases:
            _phD()

        def _phE(l=l, xsrc_d=xsrc_d, lambda_init=lambda_init):
            phase_reset()
            G = sb.alloc("G", [128, NT, 16], F32)
            dma("sp", G[:], ag_d.ap().rearrange("(t p) j -> p t j", p=128), writes=["G"])
            mk = [sb.alloc("mk", [128, 128], F32) for _ in range(2)]
            onesf = sb.alloc("onesf", [128, 128], F32)
            dma("sp", mk[0][:], maskf_d.ap(), writes=["mk"])
            dma("sp", mk[1][:], maskb_d.ap(), writes=["mk"])
            dma("sp", onesf[:], ones_d.ap(), writes=["mk"])
            E1 = sb.alloc("E1", [128, NT, 2, 4], F32)
            BN = sb.alloc("BN", [128, NT, 16], F32)
            T1 = sb.alloc("T1", [128, NT, 2, 4], F32)
            A1 = sb.alloc("A1", [128, NT, 2, 4], F32)
            A2 = sb.alloc("A2", [128, NT, 2, 4], F32)
            WD = sb.alloc("WD", [128, NT, 2, 4], F32)
            FL = sb.alloc("FL", [128, NT, 2, 4], F32)
            ang = sb.alloc("ang", [128, 512], F32)
            dma("sp", ang[:], ang_d[l], writes=["ang"])
            Gv = G[:].rearrange("p t (y h) -> p t y h", y=4)
            fsel = bc_ap(Gv[:, :, 1, :], [[16, NT], [8, 2], [1, 4]])
            isel = bc_ap(Gv[:, :, 0, :], [[16, NT], [8, 2], [1, 4]])
            p.add("act", lambda e: e.activation(out=E1[:], in_=fsel, func=AF.Exp, scale=-1.0), reads=["G"], writes=["E1"])
            p.add("act", lambda e: e.activation(out=E1[:], in_=E1[:], func=AF.Ln, bias=1.0), reads=["E1"], writes=["E1"])
            for t in range(NT):
                p.add("pe", lambda e, t=t: e.matmul(psb[0][:, t * 16:t * 16 + 4], lhsT=mk[0][:], rhs=E1[:, t, 0, :], start=True, stop=True),
                      reads=["E1", "mk"], writes=[PK(0)])
                p.add("pe", lambda e, t=t: e.matmul(psb[0][:, t * 16 + 4:t * 16 + 8], lhsT=mk[1][:], rhs=E1[:, t, 1, :], start=True, stop=True),
                      reads=["E1", "mk"], writes=[PK(0)])
                p.add("pe", lambda e, t=t: e.matmul(psb[0][:, t * 16 + 8:t * 16 + 16], lhsT=onesf[:],
                                                    rhs=E1[:, t, :, :].rearrange("p a h -> p (a h)"), start=True, stop=True),
                      reads=["E1", "mk"], writes=[PK(0)])
            p.add("dve", lambda e: e.tensor_copy(out=BN[:].rearrange("p t j -> p (t j)"), in_=psb[0][:, 0:NT * 16]), reads=[PK(0)], writes=["BN"])
            bneg = BN[:, :, 0:8].rearrange("p t (a h) -> p t a h", a=2)
            tot = BN[:, :, 8:16].rearrange("p t (a h) -> p t a h", a=2)
            p.add("dve", lambda e: e.tensor_tensor(out=T1[:], in0=isel, in1=bneg, op=ALU.add), reads=["G", "BN"], writes=["T1"])
            p.add("act", lambda e: e.activation(out=A1[:], in_=T1[:], func=AF.Exp), reads=["T1"], writes=["A1"])
            p.add("dve", lambda e: e.tensor_tensor(out=T1[:], in0=T1[:], in1=tot, op=ALU.subtract), reads=["T1", "BN"], writes=["T1"])
            p.add("act", lambda e: e.activation(out=A2[:], in_=T1[:], func=AF.Exp), reads=["T1"], writes=["A2"])
            ksc = 128.0 ** -0.5
            p.add("dve", lambda e: e.tensor_scalar(out=A1[:], in0=A1[:], scalar1=ksc, scalar2=None, op0=ALU.mult), reads=["A1"], writes=["A1"])
            p.add("dve", lambda e: e.tensor_scalar(out=A2[:], in0=A2[:], scalar1=ksc, scalar2=None, op0=ALU.mult), reads=["A2"], writes=["A2"])
            p.add("act", lambda e: e.activation(out=WD[:], in_=tot, func=AF.Exp, scale=-1.0), reads=["BN"], writes=["WD"])
            p.add("act", lambda e: e.activation(out=FL[:], in_=bneg, func=AF.Exp), reads=["BN"], writes=["FL"])
            qTh = sb.alloc("qTh", [128, S], BF16)
            kTh = sb.alloc("kTh", [128, S], BF16)
            ktok = sb.alloc("ktok", [128, NT, 128], BF16)
            v1h = sb.alloc("v1h", [128, NT, 129], BF16)
            sgo = sb.alloc("sgo", [128, NT, 128], BF16)
            hacc = sb.alloc("hacc", [128, NT, 128], F32)
            xc = sb.alloc("xc", [128, NT, 128], F32)
            sq2 = sb.alloc("sq2", [128, NT, 128], F32)
            yab = sb.alloc("yab", [128, NT, 128], BF16)
            yas = sb.alloc("yas", [128, S], BF16)
            mean = sb.alloc("mean", [128, NT], F32)
            var = sb.alloc("var", [128, NT], F32)
            Cst = [sb.alloc("Cst", [128, 129], F32) for _ in range(2)]
            Cb = [sb.alloc("Cb", [128, 129], BF16) for _ in range(2)]
            wT = [sb.alloc("wT", [128, 128], BF16) for _ in range(4)]
            kS = [sb.alloc("kS", [128, 128], BF16) for _ in range(4)]
            den = [sb.alloc("den", [128, 1], F32) for _ in range(2)]
            for h in range(4):
                p.barrier()
                dma("sp", qTh[:], aqk_d[h * 128:(h + 1) * 128, :], writes=["qTh"])
                dma("sp", kTh[:], aqk_d[512 + h * 128:512 + (h + 1) * 128, :], writes=["kTh"])
                dma("sp", v1h[:], av1_d.ap()[:, h, :].rearrange("(t p) e -> p t e", p=128), writes=["v1h"])
                dma("sp", sgo[:], ao_d.ap()[:, h * 128:(h + 1) * 128].rearrange("(t p) d -> p t d", p=128), writes=["sgo"])
                p.add("dve", lambda e: e.memset(hacc[:], 0.0), writes=["hacc"])
                for dr in range(2):
                    p.add("dve", lambda e, dr=dr: e.memset(Cst[dr][:], 0.0), writes=[("Cst", dr)])
                    p.add("dve", lambda e, dr=dr: e.memset(Cb[dr][:], 0.0), writes=[("Cb", dr)])
                for t8 in range(NT // 8):
                    pb = 6 + t8 % 2
                    for c8 in range(8):
                        c = t8 * 8 + c8
                        p.add("pe", lambda e, c=c, c8=c8, pb=pb: e.transpose(out=psb16[pb][:, c8 * 128:(c8 + 1) * 128], in_=kTh[:, c * 128:(c + 1) * 128],
                                                                             identity=ident[:]), reads=["kTh"], writes=[PK(pb)])
                    copy_op("act", ktok[:, t8 * 8:(t8 + 1) * 8, :], psb16[pb][:, 0:1024].rearrange("p (c d) -> p c d", c=8), [PK(pb)], ["ktok"])
                wi = 0
                for step in range(NT):
                    for dr in range(2):
                        c = step if dr == 0 else NT - 1 - step
                        w = wi % 4
                        wi += 1
                        ps_s, ps_o, ps_c = dr * 3, dr * 3 + 1, dr * 3 + 2
                        cs = slice(c * 128, (c + 1) * 128)
                        p.add("pe", lambda e, cs=cs, ps_s=ps_s: e.matmul(psb[ps_s][:, 0:128], lhsT=kTh[:, cs], rhs=qTh[:, cs], start=True, stop=True),
                              reads=["qTh", "kTh"], writes=[PK(ps_s)])
                        p.add("dve", lambda e, c=c, dr=dr, h=h, w=w, ps_s=ps_s: e.scalar_tensor_tensor(
                            out=wT[w][:], in0=psb[ps_s][:, 0:128], scalar=A1[:, c, dr, h:h + 1], in1=mk[dr][:], op0=ALU.mult, op1=ALU.mult),
                            reads=[PK(ps_s), "A1"], writes=[("wT", w)])
                        p.add("act", lambda e, c=c, dr=dr, h=h, w=w: e.activation(out=kS[w][:], in_=ktok[:, c, :], func=AF.Copy,
                                                                                  scale=A2[:, c, dr, h:h + 1]),
                              reads=["ktok", "A2"], writes=[("kS", w)])
                        p.add("pe", lambda e, c=c, w=w, ps_o=ps_o: e.matmul(psb[ps_o][:, 0:129], lhsT=wT[w][:], rhs=v1h[:, c, :], start=True, stop=False),
                              reads=[("wT", w), "v1h"], writes=[PK(ps_o)])
                        p.add("pe", lambda e, cs=cs, dr=dr, ps_o=ps_o: e.matmul(psb[ps_o][:, 0:129], lhsT=qTh[:, cs], rhs=Cb[dr][:], start=False, stop=True),
                              reads=[("Cb", dr)], writes=[PK(ps_o)])
                        p.add("pe", lambda e, c=c, w=w, ps_c=ps_c: e.matmul(psb[ps_c][:, 0:129], lhsT=kS[w][:], rhs=v1h[:, c, :], start=True, stop=True),
                              reads=[("kS", w)], writes=[PK(ps_c)])
                        p.add("dve", lambda e, c=c, dr=dr, h=h, ps_o=ps_o: e.scalar_tensor_tensor(
                            out=den[dr][:], in0=psb[ps_o][:, 128:129], scalar=-1.0, in1=FL[:, c, dr, h:h + 1], op0=ALU.mult, op1=ALU.max),
                            reads=[PK(ps_o), "FL"], writes=[("den", dr)])
                        p.add("dve", lambda e, dr=dr, ps_o=ps_o: e.tensor_tensor(
                            out=den[dr][:], in0=den[dr][:], in1=psb[ps_o][:, 128:129], op=ALU.max),
                            reads=[("den", dr), PK(ps_o)], writes=[("den", dr)])
                        p.add("dve", lambda e, dr=dr: e.reciprocal(out=den[dr][:], in_=den[dr][:]), reads=[("den", dr)], writes=[("den", dr)])
                        p.add("dve", lambda e, c=c, dr=dr, ps_o=ps_o: e.scalar_tensor_tensor(
                            out=hacc[:, c, :], in0=psb[ps_o][:, 0:128], scalar=den[dr][:], in1=hacc[:, c, :], op0=ALU.mult, op1=ALU.add),
                            reads=[PK(ps_o), ("den", dr), ("hacc", c)], writes=[("hacc", c)])
                        p.add("dve", lambda e, c=c, dr=dr, h=h, ps_c=ps_c: e.scalar_tensor_tensor(
                            out=Cst[dr][:], in0=Cst[dr][:], scalar=WD[:, c, dr, h:h + 1], in1=psb[ps_c][:, 0:129], op0=ALU.mult, op1=ALU.add),
                            reads=[PK(ps_c), "WD", ("Cst", dr)], writes=[("Cst", dr)])
                        p.add("act", lambda e, dr=dr: e.copy(out=Cb[dr][:], in_=Cst[dr][:]), reads=[("Cst", dr)], writes=[("Cb", dr)])
                p.barrier()
                p.add("dve", lambda e: e.tensor_reduce(out=mean[:], in_=hacc[:], axis=AX.X, op=ALU.add), writes=["mean"])
                p.add("dve", lambda e: e.tensor_scalar(out=mean[:], in0=mean[:], scalar1=1.0 / 128, scalar2=None, op0=ALU.mult), reads=["mean"], writes=["mean"])
                p.add("dve", lambda e: e.tensor_tensor(out=xc[:], in0=hacc[:], in1=bc_ap(mean[:], [[1, NT], [0, 128]]), op=ALU.subtract),
                      reads=["mean"], writes=["xc"])
                p.add("act", lambda e: e.activation(out=sq2[:], in_=xc[:], func=AF.Square), reads=["xc"], writes=["sq2"])
                p.add("dve", lambda e: e.tensor_reduce(out=var[:], in_=sq2[:], axis=AX.X, op=ALU.add), reads=["sq2"], writes=["var"])
                rstd_ops(var[:], "var", 1.0 / 128)
                p.add("dve", lambda e: e.tensor_tensor(out=xc[:], in0=xc[:], in1=bc_ap(var[:], [[1, NT], [0, 128]]), op=ALU.mult),
                      reads=["xc", "var"], writes=["xc"])
                p.add("dve", lambda e, h=h: e.tensor_tensor(out=xc[:], in0=xc[:], in1=bc_ap(ang[:, h * 128:(h + 1) * 128], [[0, NT], [1, 128]]), op=ALU.mult),
                      reads=["xc", "ang"], writes=["xc"])
                p.add("dve", lambda e: e.tensor_tensor(out=yab[:], in0=xc[:], in1=sgo[:], op=ALU.mult), reads=["xc", "sgo"], writes=["yab"])
                for t8 in range(NT // 8):
                    pb = 6 + t8 % 2
                    for c8 in range(8):
                        c = t8 * 8 + c8
                        p.add("pe", lambda e, c=c, c8=c8, pb=pb: e.transpose(out=psb16[pb][:, c8 * 128:(c8 + 1) * 128], in_=yab[:, c, :],
                                                                             identity=ident[:]), reads=["yab"], writes=[PK(pb)])
                    copy_op("act", yas[:, t8 * 1024:(t8 + 1) * 1024], psb16[pb][:, 0:1024], [PK(pb)], ["yas"])
                dma("pool", yT_d[h * 128:(h + 1) * 128, :], yas[:], reads=["yas"], writes=[("ya", h)])

        if "E" in phases:
            _phE()

        def _phF(l=l, xsrc_d=xsrc_d, lambda_init=lambda_init):
            phase_reset()
            wg = sb.alloc("wg", [128, 8, 4096], BF16)
            wu = sb.alloc("wu", [128, 16, 1024], BF16)
            wo = sb.alloc("wo", [128, 8, 1024], BF16)
            mF = sb.mark()
            stg = [sb.alloc("stg", [128, 8, 512], F32) for _ in range(2)]
            load_cast(lambda c0, c1: wg[:, :, c0:c1],
                      lambda c0, c1: w_in_d[l, :, 5904 + c0:5904 + c1].rearrange("(c p) n -> p c n", p=128), 8, 4096, stg, "wg")
            for bi in range(4):
                load_cast(lambda c0, c1, bi=bi: wu[:, bi * 4:(bi + 1) * 4, c0:c1],
                          lambda c0, c1, bi=bi: w_up_d[bi][l, :, c0:c1].rearrange("(c p) n -> p c n", p=128), 4, 1024, stg, "wu%d" % bi)
            load_cast(lambda c0, c1: wo[:, :, c0:c1],
                      lambda c0, c1: w_out_d[l, :, c0:c1].rearrange("(c p) n -> p c n", p=128), 8, 1024, stg, "wo")
            p.barrier()
            sb.reset(mF)
            hTb = [sb.alloc("hTb", [128, 8, 512], BF16) for _ in range(2)]
            yTb = [sb.alloc("yTb", [128, 16, 512], BF16) for _ in range(2)]
            mT = [sb.alloc("mT", [128, 8, 512], BF16) for _ in range(2)]
            sg = [sb.alloc("sg", [128, 512], F32) for _ in range(2)]
            acc = sb.alloc("acc", [128, 512], F32)
            tmpm = sb.alloc("tmpm", [128, 512], F32)
            xtl = [sb.alloc("xtl", [128, D], F32) for _ in range(2)]
            bankc = 0
            xi = 0
            for b in range(NB):
                bi = b % 2
                bs = slice(b * 512, (b + 1) * 512)
                dma("sp", hTb[bi][:], hT_d.ap()[:, bs].rearrange("(c p) s -> p c s", p=128), writes=[("hTb", bi)])
                dma("sp", yTb[bi][:], yT_d.ap()[:, bs].rearrange("(c p) s -> p c s", p=128), writes=[("yTb", bi)])
                for dc in range(8):
                    for g in range(4):
                        pg = bankc % 6
                        pu = (bankc + 1) % 6
                        bankc += 2
                        for k in range(8):
                            p.add("pe", lambda e, k=k, g=g, dc=dc, pg=pg, bi=bi: e.matmul(
                                psb[pg][:], lhsT=wg[:, k, g * 1024 + dc * 128:g * 1024 + (dc + 1) * 128], rhs=hTb[bi][:, k, :],
                                start=(k == 0), stop=(k == 7)), reads=[("hTb", bi)], writes=[PK(pg)])
                        for k in range(4):
                            p.add("pe", lambda e, k=k, g=g, dc=dc, pu=pu, bi=bi: e.matmul(
                                psb[pu][:], lhsT=wu[:, g * 4 + k, dc * 128:(dc + 1) * 128], rhs=yTb[bi][:, g * 4 + k, :],
                                start=(k == 0), stop=(k == 3)), reads=[("yTb", bi)], writes=[PK(pu)])
                        sgi = g % 2
                        p.add("act", lambda e, pg=pg, sgi=sgi: e.activation(out=sg[sgi][:], in_=psb[pg][:], func=AF.Sigmoid),
                              reads=[PK(pg)], writes=[("sg", sgi)])
                        if g == 0:
                            p.add("dve", lambda e, pu=pu, sgi=sgi: e.tensor_tensor(out=acc[:], in0=sg[sgi][:], in1=psb[pu][:], op=ALU.mult),
                                  reads=[PK(pu), ("sg", sgi)], writes=["acc"])
                        else:
                            p.add("dve", lambda e, pu=pu, sgi=sgi: e.tensor_tensor(out=tmpm[:], in0=sg[sgi][:], in1=psb[pu][:], op=ALU.mult),
                                  reads=[PK(pu), ("sg", sgi)], writes=["tmpm"])
                            if g < 3:
                                p.add("dve", lambda e: e.tensor_tensor(out=acc[:], in0=acc[:], in1=tmpm[:], op=ALU.add),
                                      reads=["tmpm", "acc"], writes=["acc"])
                            else:
                                p.add("dve", lambda e, dc=dc, bi=bi: e.tensor_tensor(out=mT[bi][:, dc, :], in0=acc[:], in1=tmpm[:], op=ALU.add),
                                      reads=["tmpm", "acc"], writes=[("mT", bi, dc)])
                for tt in range(4):
                    xt = xtl[xi % 2]
                    xk = ("xtl", xi % 2)
                    xi += 1
                    tok = slice(b * 512 + tt * 128, b * 512 + (tt + 1) * 128)
                    dma("sp", xt[:], xsrc_d[tok, :], writes=[xk])
                    for half in range(2):
                        po = 6 + half
                        for k in range(8):
                            p.add("pe", lambda e, k=k, tt=tt, half=half, po=po, bi=bi: e.matmul(
                                psb[po][:], lhsT=mT[bi][:, k, tt * 128:(tt + 1) * 128], rhs=wo[:, k, half * 512:(half + 1) * 512],
                                start=(k == 0), stop=(k == 7)), reads=[("mT", bi, k)], writes=[PK(po)])
                        p.add("dve", lambda e, xt=xt, half=half, po=po: e.tensor_tensor(out=xt[:, half * 512:(half + 1) * 512],
                                                                                        in0=xt[:, half * 512:(half + 1) * 512], in1=psb[po][:], op=ALU.add),
                              reads=[PK(po), xk], writes=[xk])
                    dma("pool", xres_d[tok, :], xt[:], reads=[xk], writes=[("xres", b, tt)])

        if "F" in phases:
            _phF()

        def _phG(l=l, xsrc_d=xsrc_d, lambda_init=lambda_init):
            phase_reset()
            wgt = sb.alloc("wgt", [128, 8, D_FF], BF16)
            wup = sb.alloc("wup", [128, 8, D_FF], BF16)
            wdn = sb.alloc("wdn", [128, 22, D], BF16)
            g2 = sb.alloc("g2", [128, D], F32)
            dma("sp", g2[:], n2g_d[l], writes=["g2"])
            mG = sb.mark()
            stg = [sb.alloc("stg", [128, 8, 512], F32) for _ in range(2)]
            load_cast(lambda c0, c1: wgt[:, :, c0:c1], lambda c0, c1: w_fg_d[l, :, c0:c1].rearrange("(c p) n -> p c n", p=128), 8, D_FF, stg, "wgt")
            load_cast(lambda c0, c1: wup[:, :, c0:c1], lambda c0, c1: w_fu_d[l, :, c0:c1].rearrange("(c p) n -> p c n", p=128), 8, D_FF, stg, "wup")
            for r in range(0, 22, 8):
                n = min(8, 22 - r)
                load_cast(lambda c0, c1, r=r, n=n: wdn[:, r:r + n, c0:c1],
                          lambda c0, c1, r=r, n=n: w_fd_d[l, r * 128:(r + n) * 128, c0:c1].rearrange("(c p) n -> p c n", p=128), n, D, stg, "wdn%d" % r)
            p.barrier()
            sb.reset(mG)
            xtl = [sb.alloc("xtl", [128, D], F32) for _ in range(4)]
            sqj = sb.alloc("sqj", [128, D], F32)
            ssb = [sb.alloc("ss", [128, 1], F32) for _ in range(2)]
            hbb = [sb.alloc("hb", [128, D], BF16) for _ in range(2)]
            h2T = sb.alloc("h2T", [128, 8, 512], BF16)
            aT = sb.alloc("aT", [128, 22, 512], BF16)
            sg = [sb.alloc("sg", [128, 512], F32) for _ in range(2)]
            bankc = 0
            for b in range(NB):
                for tt in range(4):
                    tok = slice(b * 512 + tt * 128, b * 512 + (tt + 1) * 128)
                    dma("sp", xtl[tt][:], xres_d[tok, :], reads=[("xres", b, tt)], writes=[("xtl", tt)])
                    norm_tile(xtl[tt][:], ("xtl", tt), g2[:], "g2", sqj[:], ssb[tt % 2][:], ("ss", tt % 2), hbb[tt % 2][:], ("hb", tt % 2),
                              6 + tt % 2, h2T[:, :, tt * 128:(tt + 1) * 128], ("h2T", tt))
                for fc in range(22):
                    pg = bankc % 6
                    pu = (bankc + 1) % 6
                    bankc += 2
                    for k in range(8):
                        p.add("pe", lambda e, k=k, fc=fc, pg=pg: e.matmul(psb[pg][:], lhsT=wgt[:, k, fc * 128:(fc + 1) * 128], rhs=h2T[:, k, :],
                                                                          start=(k == 0), stop=(k == 7)),
                              reads=[("h2T", 0), ("h2T", 1), ("h2T", 2), ("h2T", 3)], writes=[PK(pg)])
                    for k in range(8):
                        p.add("pe", lambda e, k=k, fc=fc, pu=pu: e.matmul(psb[pu][:], lhsT=wup[:, k, fc * 128:(fc + 1) * 128], rhs=h2T[:, k, :],
                                                                          start=(k == 0), stop=(k == 7)),
                              reads=[("h2T", 0), ("h2T", 1), ("h2T", 2), ("h2T", 3)], writes=[PK(pu)])
                    sgi = fc % 2
                    p.add("act", lambda e, pg=pg, sgi=sgi: e.activation(out=sg[sgi][:], in_=psb[pg][:], func=AF.Silu), reads=[PK(pg)], writes=[("sg", sgi)])
                    p.add("dve", lambda e, pu=pu, sgi=sgi, fc=fc: e.tensor_tensor(out=aT[:, fc, :], in0=sg[sgi][:], in1=psb[pu][:], op=ALU.mult),
                          reads=[PK(pu), ("sg", sgi)], writes=[("aT", fc)])
                for tt in range(4):
                    tok = slice(b * 512 + tt * 128, b * 512 + (tt + 1) * 128)
                    xt = xtl[tt]
                    for half in range(2):
                        po = 6 + half
                        for fc in range(22):
                            p.add("pe", lambda e, fc=fc, tt=tt, half=half, po=po: e.matmul(
                                psb[po][:], lhsT=aT[:, fc, tt * 128:(tt + 1) * 128], rhs=wdn[:, fc, half * 512:(half + 1) * 512],
                                start=(fc == 0), stop=(fc == 21)), reads=[("aT", fc)], writes=[PK(po)])
                        p.add("dve", lambda e, xt=xt, half=half, po=po: e.tensor_tensor(out=xt[:, half * 512:(half + 1) * 512],
                                                                                        in0=xt[:, half * 512:(half + 1) * 512], in1=psb[po][:], op=ALU.add),
                              reads=[PK(po), ("xtl", tt)], writes=[("xtl", tt)])
                    dma("pool", xres_d[tok, :], xt[:], reads=[("xtl", tt)], writes=[("xres", b, tt)])

        if "G" in phases:
            _phG()

    if "Z" in phases:
        phase_reset()
        gf = sb.alloc("gf", [128, D], F32)
        dma("sp", gf[:], fg_d.ap(), writes=["gf"])
        xb = [sb.alloc("xb", [128, D], F32) for _ in range(2)]
        ob_ = [sb.alloc("ob", [128, D], F32) for _ in range(2)]
        sqj = sb.alloc("sqj", [128, D], F32)
        ssb = [sb.alloc("ss", [128, 1], F32) for _ in range(2)]
        for t in range(NT):
            i = t % 2
            tok = slice(t * 128, (t + 1) * 128)
            dma("sp", xb[i][:], xres_d[tok, :], writes=[("xb", i)])
            p.add("dve", lambda e, i=i: e.memset(ssb[i][:], 0.0), writes=[("ss", i)])
            p.add("act", lambda e, i=i: e.activation(out=sqj[:], in_=xb[i][:], func=AF.Square, accum_out=ssb[i][:]),
                  reads=[("xb", i)], writes=[("ss", i), "sqj"])
            rstd_ops(ssb[i][:], ("ss", i), 1.0 / D)
            p.add("dve", lambda e, i=i: e.scalar_tensor_tensor(out=ob_[i][:], in0=xb[i][:], scalar=ssb[i][:], in1=gf[:], op0=ALU.mult, op1=ALU.mult),
                  reads=[("xb", i), ("ss", i), "gf"], writes=[("ob", i)])
            dma("pool", out_d[tok, :], ob_[i][:], reads=[("ob", i)], writes=[("out", t)])
    p.barrier()
    p.wait_all("pool", [])
    p.emit()
    return nc


def natten_tables(rpb, S):
    rows = S // GRID_W
    NT = S // 128
    wr, wc = 8, 16
    out = np.full((5, 128, 5, 8, 128), NEG, np.float32)
    reps = [0, 1, 2, NT - 2, NT - 1]
    for pi, i in enumerate(reps):
        kb0 = min(max(i - 2, 0), NT - 5)
        q = np.arange(i * 128, (i + 1) * 128)
        r = q // GRID_W
        c = q % GRID_W
        rs = np.clip(r - wr // 2, 0, rows - wr)
        cs = np.clip(c - wc // 2, 0, GRID_W - wc)
        keys = np.arange(kb0 * 128, (kb0 + 5) * 128)
        kr = keys // GRID_W
        kc = keys % GRID_W
        inwin = ((kr[None, :] >= rs[:, None]) & (kr[None, :] < rs[:, None] + wr) &
                 (kc[None, :] >= cs[:, None]) & (kc[None, :] < cs[:, None] + wc))
        offr = np.clip(kr[None, :] - r[:, None] + (wr - 1), 0, 2 * wr - 2)
        offc = np.clip(kc[None, :] - c[:, None] + (wc - 1), 0, 2 * wc - 2)
        g = rpb[:, offr, offc]
        g = np.where(inwin[None], g, np.float32(NEG)).astype(np.float32)
        g = g.reshape(8, 128, 5, 128).transpose(3, 2, 0, 1)
        out[pi] = g
    return out


def host_consts(S):
    t = np.arange(S)
    row = (t // GRID_W).astype(np.float32)
    col = (t % GRID_W).astype(np.float32)
    nf = 16
    inv = (10000.0 ** (-np.arange(nf, dtype=np.float32) / nf)).astype(np.float32)
    ar = row[:, None] * inv
    ac = col[:, None] * inv
    cos64 = np.concatenate([np.cos(ar), np.cos(ar), np.cos(ac), np.cos(ac)], axis=1).astype(np.float32)
    sin64 = np.concatenate([-np.sin(ar), np.sin(ar), -np.sin(ac), np.sin(ac)], axis=1).astype(np.float32)
    W = 2 * S - 128
    C0 = (S // 128 - 1) * 128
    pp = np.arange(128)[:, None]
    cc = np.arange(W)[None, :]
    tz = (-np.abs(cc - pp - C0)).astype(np.float32)
    s_ = np.arange(128)[:, None]
    t_ = np.arange(128)[None, :]
    maskf = (s_ <= t_).astype(np.float32)
    maskb = (s_ >= t_).astype(np.float32)
    ones = np.ones((128, 128), np.float32)
    sel = np.zeros((65, 64), np.float32)
    sel[64, :] = 1.0
    ident = np.eye(128, dtype=np.float32).astype(ml_dtypes.bfloat16)
    return dict(cos64=cos64, sin64=sin64, tz=tz, maskf=maskf, maskb=maskb, ones=ones, sel=sel, ident=ident)


def rep128(a):
    a = np.asarray(a, np.float32)
    return np.ascontiguousarray(np.broadcast_to(a[:, None, :], (a.shape[0], 128, a.shape[1])))


def prep_shared(inp, S):
    L = inp["w_in"].shape[0]
    f = lambda k: np.ascontiguousarray(np.asarray(inp[k], np.float32))
    sh = dict(
        w_in=f("w_in"), w_up_a=f("w_up_a"), w_up_b=f("w_up_b"), w_up_c=f("w_up_c"), w_up_d=f("w_up_d"),
        w_out=f("w_out"), w_ffn_gate=f("w_ffn_gate"), w_ffn_up=f("w_ffn_up"), w_ffn_down=f("w_ffn_down"),
        norm1_g_r=rep128(inp["norm1_g"]), norm2_g_r=rep128(inp["norm2_g"]),
        final_g_r=np.ascontiguousarray(np.broadcast_to(np.asarray(inp["final_g"], np.float32)[None, :], (128, D))),
        conv_w_r=np.ascontiguousarray(np.asarray(inp["a_conv_w"], np.float32).reshape(L, 3, 8, 128).transpose(0, 3, 2, 1)),
        gbias_r=rep128(inp["a_gate_bias"]), anorm_g_r=rep128(inp["a_norm_g"]),
        bqg_r=rep128(inp["b_qnorm_g"]), bkg_r=rep128(inp["b_knorm_g"]),
        dl_r=np.ascontiguousarray(np.broadcast_to(
            np.stack([np.asarray(inp[k], np.float32) for k in ("d_lambda_q1", "d_lambda_k1", "d_lambda_q2", "d_lambda_k2")], axis=1)[:, None],
            (L, 128, 4, 64))),
        dsub_r=rep128(inp["d_subln_g"]),
        natb=np.stack([natten_tables(np.asarray(inp["c_rpb"], np.float32)[l], S) for l in range(L)]),
    )
    sh.update(host_consts(S))
    return sh


_NC_CACHE = {}


def kernel(**inputs):
    x = np.asarray(inputs["x"], np.float32)
    B, S, _ = x.shape
    key = (S,)
    if key not in _NC_CACHE:
        _NC_CACHE[key] = build_program(S=S)
    nc = _NC_CACHE[key]
    sh = prep_shared(inputs, S)
    in_maps = []
    for b in range(B):
        m = dict(sh)
        m["x"] = np.ascontiguousarray(x[b])
        in_maps.append(m)
    res = run_bass_kernel_spmd(nc, in_maps, core_ids=list(range(B)))
    return np.stack([np.asarray(r["out"], np.float32) for r in res.results], axis=0)
```

```python
import math
import numpy as np
import ml_dtypes
import concourse.bass as bass
import concourse.mybir as mybir
from concourse.bass_utils import run_bass_kernel_spmd

F32 = mybir.dt.float32
BF16 = mybir.dt.bfloat16
AF = mybir.ActivationFunctionType
ALU = mybir.AluOpType
AX = mybir.AxisListType

D = 1024
SEQ = 4096
DEPTH = 2
GRID_W = 64
EPS = 1e-6
D_IN = 10000
D_FF = 2816
NEG = -30000.0

ENGS = ("pe", "act", "dve", "pool", "sp")
DMA_RING = 8
NO_SELF_SYNC = ("pe",)


class Op:
    __slots__ = ("eng", "fn", "idx", "dma", "waits", "val", "need_inc", "clock", "ring", "semval")

    def __init__(self, eng, fn, idx, dma):
        self.eng = eng
        self.fn = fn
        self.idx = idx
        self.dma = dma
        self.waits = []
        self.need_inc = False
        self.ring = None
        self.semval = None


class Prog:
    def __init__(self, nc, same_engine_sync=True):
        self.nc = nc
        self.ops = {e: [] for e in ENGS}
        self.writer = {}
        self.readers = {}
        self.know = {e: {} for e in ENGS}
        self.dma_count = {e: 0 for e in ENGS}
        self.dma_ops = {e: [] for e in ENGS}
        self.same_engine_sync = same_engine_sync
        self.barrier_deps = {e: [] for e in ENGS}

    def add(self, eng, fn, reads=(), writes=(), dma=False):
        lst = self.ops[eng]
        op = Op(eng, fn, len(lst), dma)
        deps = []
        for k in reads:
            w = self.writer.get(k)
            if w is not None:
                deps.append(w)
        for k in writes:
            w = self.writer.get(k)
            if w is not None:
                deps.append(w)
            deps.extend(self.readers.get(k, ()))
        if self.barrier_deps[eng]:
            deps.extend(self.barrier_deps[eng])
            self.barrier_deps[eng] = []
        if dma:
            n = self.dma_count[eng]
            op.ring = n % DMA_RING
            if n >= DMA_RING:
                deps.append(self.dma_ops[eng][n - DMA_RING])
            self.dma_count[eng] = n + 1
            self.dma_ops[eng].append(op)
        know = self.know[eng]
        for d in deps:
            if d is op:
                continue
            if d.dma:
                key = ("dma", d.eng, d.ring)
                val = d.val
            else:
                if d.eng == eng and (eng in NO_SELF_SYNC or not self.same_engine_sync):
                    continue
                key = d.eng
                val = d.idx
            if know.get(key, -1) >= val:
                continue
            op.waits.append(d)
            d.need_inc = True
            for k2, v2 in d.clock.items():
                if know.get(k2, -1) < v2:
                    know[k2] = v2
        if dma:
            op.val = (self.dma_count[eng] - 1) // DMA_RING
            ck = ("dma", eng, op.ring)
        else:
            op.val = op.idx
            ck = eng
        op.clock = dict(know)
        op.clock[ck] = op.val
        lst.append(op)
        for k in writes:
            self.writer[k] = op
            self.readers[k] = []
        for k in reads:
            if k not in writes:
                self.readers.setdefault(k, []).append(op)
        return op

    def wait_all(self, eng, keys):
        return self.add(eng, lambda e: None, reads=list(keys))

    def barrier(self):
        lasts = []
        for e in ENGS:
            if self.ops[e]:
                lasts.append(self.ops[e][-1])
            lasts.extend(self.dma_ops[e][-DMA_RING:])
        for e in ENGS:
            self.barrier_deps[e] = list(lasts)

    def emit(self):
        nc = self.nc
        from contextlib import ExitStack
        with ExitStack() as es:
            csem = {e: es.enter_context(nc.semaphore("c_" + e)) for e in ENGS}
            dsem = {(e, r): es.enter_context(nc.semaphore("d_%s_%d" % (e, r)))
                    for e in ENGS for r in range(DMA_RING) if self.dma_count[e] > 0}
            for e in ENGS:
                cnt = 0
                for op in self.ops[e]:
                    if not op.dma and op.need_inc:
                        cnt += 1
                        op.semval = cnt
            block = es.enter_context(nc.Block())

            def run(e, eng):
                for op in self.ops[e]:
                    for d in op.waits:
                        if d.dma:
                            eng.wait_ge(dsem[(d.eng, d.ring)], 16 * (d.val + 1))
                        else:
                            eng.wait_ge(csem[d.eng], d.semval)
                    ins = op.fn(eng)
                    if ins is None:
                        continue
                    if op.dma:
                        ins.then_inc(dsem[(e, op.ring)], 16)
                    elif op.need_inc:
                        ins.then_inc(csem[e], 1)

            @block.sync
            def _(eng):
                run("sp", eng)

            @block.scalar
            def _(eng):
                run("act", eng)

            @block.vector
            def _(eng):
                run("dve", eng)

            @block.gpsimd
            def _(eng):
                run("pool", eng)

            @block.tensor
            def _(eng):
                run("pe", eng)


class Pipe:
    def __init__(self, offs):
        self.offs = offs
        self.items = []

    def push(self, *fns):
        self.items.append(fns)

    def flush(self):
        n = len(self.items)
        m = max(self.offs)
        for t in range(n + m):
            for j, o in enumerate(self.offs):
                i = t - o
                if 0 <= i < n and self.items[i][j] is not None:
                    self.items[i][j]()
        self.items = []


SB_BASE = 16640
SB_LIMIT = 229376


class Arena:
    def __init__(self, nc):
        self.nc = nc
        self.off = SB_BASE
        self.n = 0

    def alloc(self, name, shape, dt):
        esz = 4 if dt == F32 else 2
        nbytes = int(np.prod(shape[1:])) * esz
        nbytes = (nbytes + 63) // 64 * 64
        assert self.off + nbytes <= SB_LIMIT, "SBUF overflow at %s: %d" % (name, self.off + nbytes)
        self.n += 1
        t = self.nc.alloc_sbuf_tensor_at("%s_%d" % (name, self.n), list(shape), dt, offset=self.off)
        self.off += nbytes
        return t

    def mark(self):
        return self.off

    def reset(self, to=SB_BASE):
        self.off = to


def bc_ap(ap, dims):
    return bass.AP(tensor=ap.tensor, offset=ap.offset, ap=[list(ap.ap[0])] + [list(d) for d in dims])


def build_program(S=SEQ, depth=DEPTH, debug=False, phases="ABCDEFGZ"):
    NT = S // 128
    NB = S // 512
    rows = S // GRID_W
    nc = bass.Bass("TRN2", target_bir_lowering=False)
    p = Prog(nc)
    sb = Arena(nc)
    L = depth

    def din(name, shape, dt=F32):
        return nc.dram_tensor(name, list(shape), dt, kind="ExternalInput")

    def dscr(name, shape, dt=BF16):
        return nc.dram_tensor(name, list(shape), dt, kind=("ExternalOutput" if debug else "Internal"))

    x_d = din("x", [S, D])
    w_in_d = din("w_in", [L, D, D_IN])
    w_up_d = [din("w_up_%s" % c, [L, 512, D]) for c in "abcd"]
    w_out_d = din("w_out", [L, D, D])
    w_fg_d = din("w_ffn_gate", [L, D, D_FF])
    w_fu_d = din("w_ffn_up", [L, D, D_FF])
    w_fd_d = din("w_ffn_down", [L, D_FF, D])
    n1g_d = din("norm1_g_r", [L, 128, D])
    n2g_d = din("norm2_g_r", [L, 128, D])
    fg_d = din("final_g_r", [128, D])
    convw_d = din("conv_w_r", [L, 128, 8, 3])
    gbias_d = din("gbias_r", [L, 128, 16])
    ang_d = din("anorm_g_r", [L, 128, 512])
    bqg_d = din("bqg_r", [L, 128, 64])
    bkg_d = din("bkg_r", [L, 128, 64])
    dl_d = din("dl_r", [L, 128, 4, 64])
    dsub_d = din("dsub_r", [L, 128, 128])
    natb_d = din("natb", [L, 5, 128, 5, 8, 128])
    ident_d = din("ident", [128, 128], BF16)
    cos_d = din("cos64", [S, 64])
    sin_d = din("sin64", [S, 64])
    tz_d = din("tz", [128, 2 * S - 128])
    maskf_d = din("maskf", [128, 128])
    maskb_d = din("maskb", [128, 128])
    ones_d = din("ones", [128, 128])
    sel_d = din("sel", [65, 64])
    out_d = nc.dram_tensor("out", [S, D], F32, kind="ExternalOutput")

    xres_d = dscr("xres", [S, D], F32)
    hT_d = dscr("hT_s", [D, S])
    aqk_d = dscr("aqk_s", [1024, S])
    av1_d = dscr("av1_s", [S, 4, 129])
    ao_d = dscr("ao_s", [S, 512])
    ag_d = dscr("ag_s", [S, 16], F32)
    bqT_d = dscr("bqT_s", [512, S])
    bkT_d = dscr("bkT_s", [128, S])
    bv1_d = dscr("bv1_s", [S, 2, 65])
    cqT_d = dscr("cqT_s", [512, S])
    ckT_d = dscr("ckT_s", [512, S])
    cv1_d = dscr("cv1_s", [S, 8, 65])
    dqT_d = dscr("dqT_s", [512, S])
    dkT_d = dscr("dkT_s", [512, S])
    dv1_d = dscr("dv1_s", [S, 4, 129])
    yT_d = dscr("yT_s", [2048, S])

    psb = [nc.alloc_psum_tensor("psb%d" % i, [128, 512], F32) for i in range(8)]
    psb16 = [b.bitcast(BF16) for b in psb]

    def PK(i):
        return ("ps", i)

    def dma(eng, out, in_, reads=(), writes=(), **kw):
        return p.add(eng, lambda e: e.dma_start(out=out, in_=in_, **kw), reads=reads, writes=writes, dma=True)

    def new_phase():
        p.barrier()
        sb.reset()

    cnt = [0]

    def uid():
        cnt[0] += 1
        return cnt[0]

    ident = sb.alloc("ident", [128, 128], BF16)
    dma("sp", ident[:], ident_d.ap(), writes=["ident"])
    base_mark = sb.mark()

    def phase_reset():
        p.barrier()
        sb.reset(base_mark)

    evac_rr = [0]

    def evac_engine():
        evac_rr[0] += 1
        return "act" if evac_rr[0] % 2 else "dve"

    def copy_op(eng, out, in_, reads, writes, scale=None):
        if eng == "act":
            if scale is None:
                return p.add("act", lambda e: e.copy(out=out, in_=in_), reads=reads, writes=writes)
            return p.add("act", lambda e: e.mul(out=out, in_=in_, mul=scale), reads=reads, writes=writes)
        else:
            if scale is None:
                return p.add(eng, lambda e: e.tensor_copy(out=out, in_=in_), reads=reads, writes=writes)
            return p.add(eng, lambda e: e.tensor_scalar(out=out, in0=in_, scalar1=scale, scalar2=None, op0=ALU.mult),
                         reads=reads, writes=writes)

    def rstd_ops(v, key, n_scale, eps=EPS):
        p.add("dve", lambda e: e.tensor_scalar(out=v, in0=v, scalar1=n_scale, scalar2=eps, op0=ALU.mult, op1=ALU.add),
              reads=[key], writes=[key])
        p.add("act", lambda e: e.activation(out=v, in_=v, func=AF.Ln), reads=[key], writes=[key])
        p.add("act", lambda e: e.activation(out=v, in_=v, func=AF.Exp, scale=-0.5), reads=[key], writes=[key])

    stage_rr = [0]

    def load_cast(dst_ap_fn, src_ap_fn, nchunk, ncols, stage_f, tag, cols_per=512):
        for c0 in range(0, ncols, cols_per):
            c1 = min(ncols, c0 + cols_per)
            i = stage_rr[0]
            stage_rr[0] += 1
            st = stage_f[i % len(stage_f)]
            sk = ("stage", i % len(stage_f))
            dma("sp", st[:, 0:nchunk, 0:c1 - c0], src_ap_fn(c0, c1), writes=[sk])
            if i % 2 == 0:
                p.add("act", lambda e, st=st, c0=c0, c1=c1: e.copy(out=dst_ap_fn(c0, c1), in_=st[:, 0:nchunk, 0:c1 - c0]),
                      reads=[sk], writes=[("w", tag, c0)])
            else:
                p.add("dve", lambda e, st=st, c0=c0, c1=c1: e.tensor_copy(out=dst_ap_fn(c0, c1), in_=st[:, 0:nchunk, 0:c1 - c0]),
                      reads=[sk], writes=[("w", tag, c0)])

    def load_qpad(qTp, src_d):
        p.add("dve", lambda e: e.memset(qTp[:], 0.0), writes=["qT"])
        for h in range(8):
            r = (h % 2) * 64
            dma("sp", qTp[r:r + 64, h, :], src_d[h * 64:(h + 1) * 64, :], reads=["qT"], writes=[("qTh", h)])

    def norm_tile(xt, xk, gt, gk, sqj, ss, ssk, hb, hbk, ptr_i, dst_ap, dst_key):
        p.add("dve", lambda e: e.memset(ss, 0.0), writes=[ssk])
        p.add("act", lambda e: e.activation(out=sqj, in_=xt, func=AF.Square, accum_out=ss), reads=[xk], writes=[ssk, "sqj"])
        rstd_ops(ss, ssk, 1.0 / D)
        p.add("dve", lambda e: e.scalar_tensor_tensor(out=hb, in0=xt, scalar=ss, in1=gt, op0=ALU.mult, op1=ALU.mult),
              reads=[xk, ssk, gk], writes=[hbk])
        for c in range(8):
            p.add("pe", lambda e, c=c: e.transpose(out=psb16[ptr_i][:, c * 128:(c + 1) * 128], in_=hb[:, c * 128:(c + 1) * 128],
                                                   identity=ident[:]), reads=[hbk], writes=[PK(ptr_i)])
        copy_op("act", dst_ap, psb16[ptr_i][:, 0:1024].rearrange("p (c t) -> p c t", c=8), [PK(ptr_i)], [dst_key])

    for l in range(L):
        xsrc_d = x_d if l == 0 else xres_d
        lambda_init = 0.8 - 0.6 * math.exp(-0.3 * l)

        def _phA(l=l, xsrc_d=xsrc_d, lambda_init=lambda_init):
            phase_reset()
            hT = sb.alloc("hT", [128, 8, S], BF16)
            g1 = sb.alloc("g1", [128, D], F32)
            dma("sp", g1[:], n1g_d[l], writes=["g1"])
            xb = [sb.alloc("xb", [128, D], F32) for _ in range(2)]
            sqj = sb.alloc("sqj", [128, D], F32)
            ssb = [sb.alloc("ss", [128, 1], F32) for _ in range(2)]
            hbb = [sb.alloc("hb", [128, D], BF16) for _ in range(2)]
            mA = sb.mark()
            for t in range(NT):
                xt = xb[t % 2]
                dma("sp", xt[:], xsrc_d[t * 128:(t + 1) * 128, :], writes=[("xb", t % 2)])
                norm_tile(xt[:], ("xb", t % 2), g1[:], "g1", sqj[:], ssb[t % 2][:], ("ss", t % 2), hbb[t % 2][:], ("hb", t % 2),
                          6 + t % 2, hT[:, :, t * 128:(t + 1) * 128], ("hT", t))
            p.barrier()
            dma("pool", hT_d.ap().rearrange("(c p) s -> p c s", p=128), hT[:], writes=["hT_d"])

            wf = [sb.alloc("wf", [128, 8, 512], F32) for _ in range(2)]
            wb = [sb.alloc("wb", [128, 8, 512], BF16) for _ in range(2)]
            stgf = [sb.alloc("stgf", [128, S + 2], F32) for _ in range(2)]
            stgb = [sb.alloc("stgb", [128, S], BF16) for _ in range(2)]
            cw = sb.alloc("cw", [128, 8, 3], F32)
            ctmp = sb.alloc("ctmp", [128, S], F32)
            dma("sp", cw[:], convw_d[l], writes=["cw"])
            for i in range(2):
                p.add("dve", lambda e, i=i: e.memset(stgf[i][:], 0.0), writes=[("stgf", i, b) for b in range(NB)])
            groups = [("aq", 0, aqk_d, 0, None), ("ak", 512, aqk_d, 512, None),
                      ("cq", 2832, cqT_d, 0, 0.125), ("ck", 3344, ckT_d, 0, None),
                      ("dq", 4368, dqT_d, 0, 0.125), ("dk", 4880, dkT_d, 0, None)]
            cidx = 0
            bank = 0
            for gi, (gname, col0, dst_d, drow0, scale) in enumerate(groups):
                wfi, wbi = wf[gi % 2], wb[gi % 2]
                dma("sp", wfi[:], w_in_d[l, :, col0:col0 + 512].rearrange("(c p) n -> p c n", p=128), writes=[("wf", gi % 2)])
                p.add("act", lambda e, wfi=wfi, wbi=wbi: e.copy(out=wbi[:], in_=wfi[:]),
                      reads=[("wf", gi % 2)], writes=[("wb", gi % 2)])
                is_a = gname in ("aq", "ak")
                for cc in range(4):
                    si = cidx % 2
                    cidx += 1
                    for b in range(NB):
                        bk = bank % 6
                        bank += 1
                        for k in range(8):
                            p.add("pe", lambda e, k=k, cc=cc, b=b, bk=bk, wbi=wbi: e.matmul(
                                psb[bk][:], lhsT=wbi[:, k, cc * 128:(cc + 1) * 128], rhs=hT[:, k, b * 512:(b + 1) * 512],
                                start=(k == 0), stop=(k == 7)), reads=[("wb", gi % 2)], writes=[PK(bk)])
                        if is_a:
                            copy_op(evac_engine(), stgf[si][:, 1 + b * 512:1 + (b + 1) * 512], psb[bk][:], [PK(bk)], [("stgf", si, b)])
                        else:
                            copy_op(evac_engine(), stgb[si][:, b * 512:(b + 1) * 512], psb[bk][:], [PK(bk)], [("stgb", si, b)], scale=scale)
                    if is_a:
                        ch = (col0 // 128) + cc
                        sf = stgf[si]
                        p.add("dve", lambda e, sf=sf, ch=ch: e.tensor_scalar(out=ctmp[:], in0=sf[:, 1:S + 1], scalar1=cw[:, ch, 1:2],
                                                                             scalar2=None, op0=ALU.mult),
                              reads=[("stgf", si, b_) for b_ in range(NB)] + ["cw"], writes=["ctmp"])
                        p.add("dve", lambda e, sf=sf, ch=ch: e.scalar_tensor_tensor(out=ctmp[:], in0=sf[:, 0:S], scalar=cw[:, ch, 0:1],
                                                                                    in1=ctmp[:], op0=ALU.mult, op1=ALU.add),
                              reads=[("stgf", si, b_) for b_ in range(NB)] + ["cw"], writes=["ctmp"])
                        p.add("dve", lambda e, sf=sf, ch=ch: e.scalar_tensor_tensor(out=ctmp[:], in0=sf[:, 2:S + 2], scalar=cw[:, ch, 2:3],
                                                                                    in1=ctmp[:], op0=ALU.mult, op1=ALU.add),
                              reads=[("stgf", si, b_) for b_ in range(NB)] + ["cw"], writes=["ctmp"])
                        p.add("act", lambda e, si=si: e.activation(out=stgb[si][:], in_=ctmp[:], func=AF.Silu),
                              reads=["ctmp"], writes=[("stgb", si, b_) for b_ in range(NB)])
                    r0 = drow0 + cc * 128
                    dma("pool", dst_d[r0:r0 + 128, :], stgb[si][:], reads=[("stgb", si, b_) for b_ in range(NB)], writes=[(gname, cc)])

            p.barrier()
            sb.reset(mA)
            wf = [sb.alloc("wf", [128, 8, 512], F32) for _ in range(2)]
            wb = [sb.alloc("wb", [128, 8, 512], BF16) for _ in range(2)]
            cos_s = sb.alloc("cos", [128, NT, 64], F32)
            sin_s = sb.alloc("sin", [128, NT, 64], F32)
            dma("sp", cos_s[:], cos_d.ap().rearrange("(t p) f -> p t f", p=128), writes=["cos"])
            dma("sp", sin_s[:], sin_d.ap().rearrange("(t p) f -> p t f", p=128), writes=["sin"])
            gb = sb.alloc("gb", [128, 16], F32)
            dma("sp", gb[:], gbias_d[l], writes=["gb"])
            bqg = sb.alloc("bqg", [128, 64], F32)
            bkg = sb.alloc("bkg", [128, 64], F32)
            dma("sp", bqg[:], bqg_d[l], writes=["bqg"])
            dma("sp", bkg[:], bkg_d[l], writes=["bkg"])
            p.add("dve", lambda e: e.tensor_scalar(out=bqg[:], in0=bqg[:], scalar1=0.125, scalar2=None, op0=ALU.mult),
                  reads=["bqg"], writes=["bqg"])
            st129 = [sb.alloc("st129", [128, 4, 129], BF16) for _ in range(2)]
            st65 = [sb.alloc("st65", [128, 8, 65], BF16) for _ in range(2)]
            stb = [sb.alloc("stb", [128, 512], BF16) for _ in range(2)]
            stg16 = [sb.alloc("stg16", [128, 16], F32) for _ in range(2)]
            for i in range(2):
                p.add("dve", lambda e, i=i: e.memset(st129[i][:], 1.0), writes=[("st129", i)])
                p.add("dve", lambda e, i=i: e.memset(st65[i][:], 1.0), writes=[("st65", i)])
            qf = sb.alloc("qf", [128, 512], F32)
            qn = sb.alloc("qn", [128, 512], F32)
            t1 = sb.alloc("t1", [128, 512], F32)
            t2 = sb.alloc("t2", [128, 512], F32)
            ssh = sb.alloc("ssh", [128, 8], F32)
            qrb = [sb.alloc("qrb", [128, 512], BF16) for _ in range(2)]
            bqT_s = sb.alloc("bqT_s", [128, 4, S], BF16)
            bkT_s = sb.alloc("bkT_s", [128, S], BF16)

            def rope_norm(ps_ap, psk, nh, gtile, t, outb, outk):
                W = nh * 64
                qf_, qn_, t1_, t2_ = qf[:, 0:W], qn[:, 0:W], t1[:, 0:W], t2[:, 0:W]
                p.add("act", lambda e: e.copy(out=qf_, in_=ps_ap), reads=[psk], writes=["qf"])
                p.add("dve", lambda e: e.tensor_tensor(out=t1_, in0=qf_, in1=qf_, op=ALU.mult), reads=["qf"], writes=["t1"])
                p.add("dve", lambda e: e.tensor_reduce(out=ssh[:, 0:nh], in_=t1_.rearrange("p (h d) -> p h d", h=nh), axis=AX.X, op=ALU.add),
                      reads=["t1"], writes=["ssh"])
                rstd_ops(ssh[:, 0:nh], "ssh", 1.0 / 64)
                p.add("dve", lambda e: e.tensor_tensor(out=qn_.rearrange("p (h d) -> p h d", h=nh), in0=qf_.rearrange("p (h d) -> p h d", h=nh),
                                                       in1=bc_ap(ssh[:, 0:nh], [[1, nh], [0, 64]]), op=ALU.mult),
                      reads=["qf", "ssh"], writes=["qn"])
                p.add("dve", lambda e: e.tensor_tensor(out=qn_.rearrange("p (h d) -> p h d", h=nh), in0=qn_.rearrange("p (h d) -> p h d", h=nh),
                                                       in1=bc_ap(gtile[:], [[0, nh], [1, 64]]), op=ALU.mult),
                      reads=["qn", "bqg", "bkg"], writes=["qn"])
                cos_b = bc_ap(cos_s[:, t, :], [[0, nh], [1, 64]])
                p.add("dve", lambda e: e.tensor_tensor(out=t1_.rearrange("p (h d) -> p h d", h=nh), in0=qn_.rearrange("p (h d) -> p h d", h=nh),
                                                       in1=cos_b, op=ALU.mult), reads=["qn", "cos"], writes=["t1"])
                sin_lo = bc_ap(sin_s[:, t, 0:16], [[0, nh], [32, 2], [1, 16]])
                sin_hi = bc_ap(sin_s[:, t, 16:32], [[0, nh], [32, 2], [1, 16]])
                x4 = qn_.rearrange("p (h r f) -> p h r f", h=nh, r=2)
                o4 = t2_.rearrange("p (h r f) -> p h r f", h=nh, r=2)
                p.add("dve", lambda e: e.tensor_tensor(out=o4[:, :, :, 0:16], in0=x4[:, :, :, 16:32], in1=sin_lo, op=ALU.mult),
                      reads=["qn", "sin"], writes=["t2"])
                p.add("dve", lambda e: e.tensor_tensor(out=o4[:, :, :, 16:32], in0=x4[:, :, :, 0:16], in1=sin_hi, op=ALU.mult),
                      reads=["qn", "sin"], writes=["t2"])
                p.add("dve", lambda e: e.tensor_tensor(out=outb, in0=t1_, in1=t2_, op=ALU.add), reads=["t1", "t2"], writes=[outk])

            tm_groups = [("av", 1024, 512), ("ao", 1536, 512), ("ag", 2048, 16), ("bq", 2064, 512),
                         ("bkv", 2576, 256), ("cv", 3856, 512), ("dv", 5392, 512)]
            for gi, (gname, col0, ncol) in enumerate(tm_groups):
                wfi, wbi = wf[gi % 2], wb[gi % 2]
                dma("sp", wfi[:, :, 0:ncol], w_in_d[l, :, col0:col0 + ncol].rearrange("(c p) n -> p c n", p=128),
                    writes=[("wf", gi % 2)])
                p.add("act", lambda e, wfi=wfi, wbi=wbi, ncol=ncol: e.copy(out=wbi[:, :, 0:ncol], in_=wfi[:, :, 0:ncol]),
                      reads=[("wf", gi % 2)], writes=[("wb", gi % 2)])
                for t in range(NT):
                    bk = t % 4
                    si = t % 2
                    for k in range(8):
                        p.add("pe", lambda e, k=k, t=t, bk=bk, wbi=wbi, ncol=ncol: e.matmul(
                            psb[bk][:, 0:ncol], lhsT=hT[:, k, t * 128:(t + 1) * 128], rhs=wbi[:, k, 0:ncol],
                            start=(k == 0), stop=(k == 7)), reads=[("wb", gi % 2)], writes=[PK(bk)])
                    tok = slice(t * 128, (t + 1) * 128)
                    if gname == "av" or gname == "dv":
                        dst = av1_d if gname == "av" else dv1_d
                        copy_op(evac_engine(), st129[si][:, :, 0:128], psb[bk][:, 0:512].rearrange("p (h d) -> p h d", h=4),
                                [PK(bk)], [("st129", si)])
                        dma("pool", dst[tok], st129[si][:], reads=[("st129", si)], writes=[(gname, t)])
                    elif gname == "cv":
                        copy_op(evac_engine(), st65[si][:, :, 0:64], psb[bk][:, 0:512].rearrange("p (h d) -> p h d", h=8),
                                [PK(bk)], [("st65", si)])
                        dma("pool", cv1_d[tok], st65[si][:], reads=[("st65", si)], writes=[(gname, t)])
                    elif gname == "ao":
                        p.add("act", lambda e, bk=bk, si=si: e.activation(out=stb[si][:], in_=psb[bk][:], func=AF.Sigmoid),
                              reads=[PK(bk)], writes=[("stb", si)])
                        dma("pool", ao_d[tok, :], stb[si][:], reads=[("stb", si)], writes=[(gname, t)])
                    elif gname == "ag":
                        p.add("dve", lambda e, bk=bk, si=si: e.tensor_tensor(out=stg16[si][:], in0=psb[bk][:, 0:16], in1=gb[:], op=ALU.add),
                              reads=[PK(bk), "gb"], writes=[("stg16", si)])
                        dma("pool", ag_d[tok, :], stg16[si][:], reads=[("stg16", si)], writes=[(gname, t)])
                    elif gname == "bq":
                        rope_norm(psb[bk][:, 0:512], PK(bk), 8, bqg, t, qrb[si][:], ("qrb", si))
                        for c in range(4):
                            p.add("pe", lambda e, c=c, si=si: e.transpose(out=psb16[6 + si][:, c * 128:(c + 1) * 128],
                                                                          in_=qrb[si][:, c * 128:(c + 1) * 128], identity=ident[:]),
                                  reads=[("qrb", si)], writes=[PK(6 + si)])
                        copy_op("act", bqT_s[:, :, t * 128:(t + 1) * 128], psb16[6 + si][:, 0:512].rearrange("p (c t) -> p c t", c=4),
                                [PK(6 + si)], [("bqT_s", t)])
                    elif gname == "bkv":
                        rope_norm(psb[bk][:, 0:128], PK(bk), 2, bkg, t, qrb[si][:, 0:128], ("qrb", si))
                        p.add("pe", lambda e, si=si: e.transpose(out=psb16[6 + si][:, 0:128], in_=qrb[si][:, 0:128], identity=ident[:]),
                              reads=[("qrb", si)], writes=[PK(6 + si)])
                        copy_op("act", bkT_s[:, t * 128:(t + 1) * 128], psb16[6 + si][:, 0:128], [PK(6 + si)], [("bkT_s", t)])
                        copy_op("dve", st65[si][:, 0:2, 0:64], psb[bk][:, 128:256].rearrange("p (h d) -> p h d", h=2),
                                [PK(bk)], [("st65", si)])
                        dma("pool", bv1_d[tok], st65[si][:, 0:2, :], reads=[("st65", si)], writes=[("bv", t)])
                if gname == "bq":
                    p.barrier()
                    dma("pool", bqT_d.ap().rearrange("(c p) s -> p c s", p=128), bqT_s[:], writes=["bqT_d"])
                if gname == "bkv":
                    p.barrier()
                    dma("pool", bkT_d.ap(), bkT_s[:], writes=["bkT_d"])

        if "A" in phases:
            _phA()

        def _phB(l=l, xsrc_d=xsrc_d, lambda_init=lambda_init):
            phase_reset()
            qT = sb.alloc("qTp", [128, 8, S], BF16)
            kT2 = sb.alloc("kT2", [128, 2, S], BF16)
            v1 = sb.alloc("v1", [128, NT, 2, 65], BF16)
            sel = sb.alloc("sel", [65, 64], F32)
            load_qpad(qT, bqT_d)
            for g in range(2):
                dma("sp", kT2[0:64, g, :], bkT_d[g * 64:(g + 1) * 64, :], writes=[("kT2", g, 0)])
                dma("sp", kT2[64:128, g, :], bkT_d[g * 64:(g + 1) * 64, :], writes=[("kT2", g, 1)])
            dma("sp", v1[:], bv1_d.ap().rearrange("(t p) g e -> p t g e", p=128), writes=["v1"])
            dma("sp", sel[:], sel_d.ap(), writes=["sel"])
            pT = [sb.alloc("pT", [128, 512], BF16) for _ in range(3)]
            osb = [sb.alloc("osb", [65, 512], F32) for _ in range(2)]
            rb = [sb.alloc("rb", [64, 512], F32) for _ in range(2)]
            ybs = [sb.alloc("ybs", [64, 512], BF16) for _ in range(2)]
            p.barrier()
            pipe = Pipe([0, 1, 2])
            it = 0
            for h in range(8):
                g = h // 4
                j = h // 2
                pr = slice((h % 2) * 64, (h % 2) * 64 + 64)
                for qb in range(NB):
                    ob = 3 + (it // NT) % 2
                    for kt in range(NT):
                        sbk = it % 3
                        it += 1

                        def s0(sbk=sbk, g=g, h=h, kt=kt, qb=qb):
                            p.add("pe", lambda e: e.matmul(psb[sbk][:], lhsT=kT2[:, g, kt * 128:(kt + 1) * 128],
                                                           rhs=qT[:, h, qb * 512:(qb + 1) * 512], start=True, stop=True),
                                  writes=[PK(sbk)])

                        def s1(sbk=sbk):
                            p.add("act", lambda e: e.activation(out=pT[sbk][:], in_=psb[sbk][:], func=AF.Exp),
                                  reads=[PK(sbk)], writes=[("pT", sbk)])

                        def s2(sbk=sbk, ob=ob, kt=kt, g=g, h=h, qb=qb):
                            p.add("pe", lambda e: e.matmul(psb[ob][0:65, :], lhsT=v1[:, kt, g, :], rhs=pT[sbk][:],
                                                           start=(kt == 0), stop=(kt == NT - 1)),
                                  reads=[("pT", sbk)], writes=[PK(ob)])
                            if kt == NT - 1:
                                oi = ob - 3
                                p.add("dve", lambda e: e.tensor_copy(out=osb[oi][:], in_=psb[ob][0:65, :]), reads=[PK(ob)], writes=[("osb", oi)])
                                p.add("pe", lambda e: e.matmul(psb[5][0:64, :], lhsT=sel[:], rhs=osb[oi][:], start=True, stop=True),
                                      reads=[("osb", oi), "sel"], writes=[PK(5)])
                                p.add("dve", lambda e: e.reciprocal(out=rb[oi][:], in_=psb[5][0:64, :]), reads=[PK(5)], writes=[("rb", oi)])
                                p.add("dve", lambda e: e.tensor_tensor(out=ybs[oi][:], in0=osb[oi][0:64, :], in1=rb[oi][:], op=ALU.mult),
                                      reads=[("osb", oi), ("rb", oi)], writes=[("ybs", oi)])
                                r0 = 512 + h * 64
                                dma("pool", yT_d[r0:r0 + 64, qb * 512:(qb + 1) * 512], ybs[oi][:], reads=[("ybs", oi)],
                                    writes=[("yb", h, qb)])
                        pipe.push(s0, s1, s2)
            pipe.flush()

        if "B" in phases:
            _phB()

        def _phC(l=l, xsrc_d=xsrc_d, lambda_init=lambda_init):
            phase_reset()
            qT = sb.alloc("qTp", [128, 8, S], BF16)
            kT = sb.alloc("kT", [128, 4, S], BF16)
            v1 = sb.alloc("v1", [128, NT, 8, 65], BF16)
            sel = sb.alloc("sel", [65, 64], F32)
            nbf = sb.alloc("nbf", [128, 5, 8, 128], F32)
            en_int = sb.alloc("en_int", [128, 5, 8, 128], BF16)
            en_edge = sb.alloc("en_edge", [128, 5, 8, 128], BF16)
            load_qpad(qT, cqT_d)
            dma("sp", kT[:], ckT_d.ap().rearrange("(c p) s -> p c s", p=128), writes=["kT"])
            dma("sp", v1[:], cv1_d.ap().rearrange("(t p) g e -> p t g e", p=128), writes=["v1"])
            dma("sp", sel[:], sel_d.ap(), writes=["sel"])
            dma("sp", nbf[:], natb_d[l, 2], writes=["nbf"])
            p.add("act", lambda e: e.activation(out=en_int[:], in_=nbf[:], func=AF.Exp), reads=["nbf"], writes=["en_int"])
            pTa = [sb.alloc("pTa", [128, 512], BF16) for _ in range(4)]
            pT = [sb.alloc("pT", [128, 512], BF16) for _ in range(4)]
            osb = [sb.alloc("osb", [65, 512], F32) for _ in range(2)]
            rb = [sb.alloc("rb", [64, 512], F32) for _ in range(2)]
            ybs = [sb.alloc("ybs", [64, 512], BF16) for _ in range(2)]
            p.barrier()
            pipe = Pipe([0, 1, 2, 3])
            it = 0
            oit = 0
            for i in range(NT):
                pat = 0 if i == 0 else 1 if i == 1 else 3 if i == NT - 2 else 4 if i == NT - 1 else 2
                kb0 = min(max(i - 2, 0), NT - 5)
                if pat != 2:
                    dma("sp", nbf[:], natb_d[l, pat], reads=["en_int", "en_edge"], writes=["nbf"])
                    p.add("act", lambda e: e.activation(out=en_edge[:], in_=nbf[:], func=AF.Exp), reads=["nbf"], writes=["en_edge"])
                nbt = en_int if pat == 2 else en_edge
                nbk = "en_int" if pat == 2 else "en_edge"
                for hg in range(2):
                    ob = 4 + oit % 2
                    oit += 1
                    for kk in range(5):
                        kt = kb0 + kk
                        sbk = it % 4
                        it += 1

                        def s0(sbk=sbk, hg=hg, kt=kt, i=i):
                            for hh in range(4):
                                h = hg * 4 + hh
                                p.add("pe", lambda e, hh=hh, h=h: e.matmul(
                                    psb[sbk][:, hh * 128:(hh + 1) * 128], lhsT=kT[:, h // 2, kt * 128:(kt + 1) * 128],
                                    rhs=qT[:, h, i * 128:(i + 1) * 128], start=True, stop=True), writes=[PK(sbk)])

                        def s1(sbk=sbk):
                            p.add("act", lambda e: e.activation(out=pTa[sbk][:], in_=psb[sbk][:], func=AF.Exp),
                                  reads=[PK(sbk)], writes=[("pTa", sbk)])

                        def s2(sbk=sbk, kk=kk, hg=hg, nbt=nbt, nbk=nbk):
                            p.add("dve", lambda e: e.tensor_tensor(
                                out=pT[sbk][:].rearrange("p (h q) -> p h q", h=4), in0=pTa[sbk][:].rearrange("p (h q) -> p h q", h=4),
                                in1=nbt[:, kk, hg * 4:(hg + 1) * 4, :], op=ALU.mult), reads=[("pTa", sbk), nbk], writes=[("pT", sbk)])

                        def s3(sbk=sbk, ob=ob, kk=kk, kt=kt, hg=hg, i=i):
                            for hh in range(4):
                                h = hg * 4 + hh
                                p.add("pe", lambda e, hh=hh, h=h: e.matmul(
                                    psb[ob][0:65, hh * 128:(hh + 1) * 128], lhsT=v1[:, kt, h, :], rhs=pT[sbk][:, hh * 128:(hh + 1) * 128],
                                    start=(kk == 0 and hh == 0), stop=(kk == 4), skip_group_check=True), reads=[("pT", sbk)], writes=[PK(ob)])
                            if kk == 4:
                                oi = ob - 4
                                p.add("act", lambda e: e.copy(out=osb[oi][:], in_=psb[ob][0:65, :]), reads=[PK(ob)], writes=[("osb", oi)])
                                p.add("pe", lambda e: e.matmul(psb[6][0:64, :], lhsT=sel[:], rhs=osb[oi][:], start=True, stop=True),
                                      reads=[("osb", oi), "sel"], writes=[PK(6)])
                                p.add("act", lambda e: e.activation(out=rb[oi][:], in_=psb[6][0:64, :], func=AF.Ln), reads=[PK(6)], writes=[("rb", oi)])
                                p.add("act", lambda e: e.activation(out=rb[oi][:], in_=rb[oi][:], func=AF.Exp, scale=-1.0),
                                      reads=[("rb", oi)], writes=[("rb", oi)])
                                p.add("pool", lambda e: e.tensor_tensor(out=ybs[oi][:], in0=osb[oi][0:64, :], in1=rb[oi][:], op=ALU.mult),
                                      reads=[("osb", oi), ("rb", oi)], writes=[("ybs", oi)])
                                for hh in range(4):
                                    r0 = 1024 + (hg * 4 + hh) * 64
                                    dma("pool", yT_d[r0:r0 + 64, i * 128:(i + 1) * 128], ybs[oi][:, hh * 128:(hh + 1) * 128],
                                        reads=[("ybs", oi)], writes=[("yc", hg * 4 + hh, i)])
                        pipe.push(s0, s1, s2, s3)
                if pat != 2:
                    pipe.flush()
            pipe.flush()

        if "C" in phases:
            _phC()

        def _phD(l=l, xsrc_d=xsrc_d, lambda_init=lambda_init):
            phase_reset()
            qT = sb.alloc("qTp", [128, 8, S], BF16)
            kT = sb.alloc("kT", [128, 4, S], BF16)
            v1 = sb.alloc("v1", [128, NT, 4, 129], BF16)
            tz = sb.alloc("tz", [128, 2 * S - 128], F32)
            dlr = sb.alloc("dlr", [128, 4, 64], F32)
            dsub = sb.alloc("dsub", [128, 128], F32)
            load_qpad(qT, dqT_d)
            dma("sp", kT[:], dkT_d.ap().rearrange("(c p) s -> p c s", p=128), writes=["kT"])
            dma("sp", v1[:], dv1_d.ap().rearrange("(t p) g e -> p t g e", p=128), writes=["v1"])
            dma("sp", tz[:], tz_d.ap(), writes=["tz"])
            dma("sp", dlr[:], dl_d[l], writes=["dlr"])
            dma("sp", dsub[:], dsub_d[l], writes=["dsub"])
            lt = sb.alloc("lt", [128, 2, 64], F32)
            ls = sb.alloc("ls", [128, 2], F32)
            nlam = sb.alloc("nlam", [128, 1], F32)
            p.add("dve", lambda e: e.tensor_tensor(out=lt[:, 0, :], in0=dlr[:, 0, :], in1=dlr[:, 1, :], op=ALU.mult), reads=["dlr"], writes=["lt"])
            p.add("dve", lambda e: e.tensor_tensor(out=lt[:, 1, :], in0=dlr[:, 2, :], in1=dlr[:, 3, :], op=ALU.mult), reads=["dlr"], writes=["lt"])
            p.add("dve", lambda e: e.tensor_reduce(out=ls[:], in_=lt[:], axis=AX.X, op=ALU.add), reads=["lt"], writes=["ls"])
            p.add("act", lambda e: e.activation(out=ls[:], in_=ls[:], func=AF.Exp), reads=["ls"], writes=["ls"])
            p.add("dve", lambda e: e.tensor_tensor(out=nlam[:], in0=ls[:, 1:2], in1=ls[:, 0:1], op=ALU.subtract), reads=["ls"], writes=["nlam"])
            p.add("dve", lambda e: e.tensor_scalar(out=nlam[:], in0=nlam[:], scalar1=-lambda_init, scalar2=None, op0=ALU.add),
                  reads=["nlam"], writes=["nlam"])
            p.add("dve", lambda e: e.tensor_scalar(out=dsub[:], in0=dsub[:], scalar1=1.0 - lambda_init, scalar2=None, op0=ALU.mult),
                  reads=["dsub"], writes=["dsub"])
            pT = [sb.alloc("pT", [128, 512], BF16) for _ in range(4)]
            pTa = [sb.alloc("pTa", [128, 512], BF16) for _ in range(4)]
            etab = sb.alloc("etab", [128, 2 * S - 128], BF16)
            r1 = sb.alloc("r1", [128, 1], F32)
            r2 = sb.alloc("r2", [128, 1], F32)
            of = sb.alloc("of", [128, 128], F32)
            osq = sb.alloc("osq", [128, 128], F32)
            oss = sb.alloc("oss", [128, 1], F32)
            yb = [sb.alloc("yb", [128, 128], BF16) for _ in range(2)]
            yds = [sb.alloc("yds", [128, 512], BF16) for _ in range(2)]
            p.barrier()
            regs = {}
            ri = 0
            for c in range(2):
                for qq in range(4):
                    regs[(c, qq)] = (4 + ri // 3, (ri % 3) * 160)
                    ri += 1
            pipe = Pipe([0, 1, 2, 3])
            it = 0
            ep = 0
            for h in range(4):
                slope = 2.0 ** (-8.0 * (h + 1) / 4)
                pipe.flush()
                p.add("act", lambda e, slope=slope: e.activation(out=etab[:], in_=tz[:], func=AF.Exp, scale=slope),
                      reads=["tz", ("pT", 0), ("pT", 1), ("pT", 2), ("pT", 3)], writes=["etab"])
                for qb in range(NB):
                    for kt in range(NT):
                        for c in range(2):
                            f0 = c * 256 + h * 64
                            j = f0 // 128
                            pr = slice(f0 % 128, f0 % 128 + 64)
                            sbk = it % 4
                            it += 1
                            off = qb * 512 - kt * 128 + (NT - 1) * 128

                            def s0(sbk=sbk, j=j, hd=f0 // 64, kt=kt, qb=qb):
                                p.add("pe", lambda e: e.matmul(psb[sbk][:], lhsT=kT[:, j, kt * 128:(kt + 1) * 128],
                                                               rhs=qT[:, hd, qb * 512:(qb + 1) * 512], start=True, stop=True),
                                      writes=[PK(sbk)])

                            def s1(sbk=sbk):
                                p.add("act", lambda e: e.activation(out=pTa[sbk][:], in_=psb[sbk][:], func=AF.Exp),
                                      reads=[PK(sbk)], writes=[("pTa", sbk)])

                            def s2m(sbk=sbk, off=off):
                                p.add("dve", lambda e: e.tensor_tensor(out=pT[sbk][:], in0=pTa[sbk][:], in1=etab[:, off:off + 512], op=ALU.mult),
                                      reads=[("pTa", sbk), "etab"], writes=[("pT", sbk)])

                            def s2(sbk=sbk, c=c, kt=kt, h=h, qb=qb):
                                nonlocal ep
                                for qq in range(4):
                                    bkk, co = regs[(c, qq)]
                                    p.add("pe", lambda e, qq=qq, bkk=bkk, co=co: e.matmul(
                                        psb[bkk][:, co:co + 129], lhsT=pT[sbk][:, qq * 128:(qq + 1) * 128], rhs=v1[:, kt, h, :],
                                        start=(kt == 0 and co == 0), stop=(kt == NT - 1), skip_group_check=True),
                                        reads=[("pT", sbk)], writes=[PK(bkk)])
                                if kt == NT - 1 and c == 1:
                                    ydi = ep % 2
                                    ep += 1
                                    for qq in range(4):
                                        b1, c1 = regs[(0, qq)]
                                        b2, c2 = regs[(1, qq)]
                                        ybi = qq % 2
                                        p.add("dve", lambda e, b1=b1, c1=c1: e.reciprocal(out=r1[:], in_=psb[b1][:, c1 + 128:c1 + 129]),
                                              reads=[PK(b1)], writes=["r1"])
                                        p.add("dve", lambda e, b2=b2, c2=c2: e.reciprocal(out=r2[:], in_=psb[b2][:, c2 + 128:c2 + 129]),
                                              reads=[PK(b2)], writes=["r2"])
                                        p.add("dve", lambda e: e.tensor_tensor(out=r2[:], in0=r2[:], in1=nlam[:], op=ALU.mult),
                                              reads=["r2", "nlam"], writes=["r2"])
                                        p.add("dve", lambda e, b1=b1, c1=c1: e.tensor_scalar(out=of[:], in0=psb[b1][:, c1:c1 + 128], scalar1=r1[:],
                                                                                             scalar2=None, op0=ALU.mult),
                                              reads=[PK(b1), "r1"], writes=["of"])
                                        p.add("dve", lambda e, b2=b2, c2=c2: e.scalar_tensor_tensor(out=of[:], in0=psb[b2][:, c2:c2 + 128], scalar=r2[:],
                                                                                                    in1=of[:], op0=ALU.mult, op1=ALU.add),
                                              reads=[PK(b2), "r2", "of"], writes=["of"])
                                        p.add("dve", lambda e: e.memset(oss[:], 0.0), writes=["oss"])
                                        p.add("act", lambda e: e.activation(out=osq[:], in_=of[:], func=AF.Square, accum_out=oss[:]),
                                              reads=["of", "oss"], writes=["osq", "oss"])
                                        rstd_ops(oss[:], "oss", 1.0 / 128)
                                        p.add("dve", lambda e, ybi=ybi: e.scalar_tensor_tensor(out=yb[ybi][:], in0=of[:], scalar=oss[:], in1=dsub[:],
                                                                                               op0=ALU.mult, op1=ALU.mult),
                                              reads=["of", "oss", "dsub"], writes=[("yb", ybi)])
                                        p.add("pe", lambda e, qq=qq, ybi=ybi: e.transpose(out=psb16[7][:, qq * 128:(qq + 1) * 128], in_=yb[ybi][:],
                                                                                          identity=ident[:]), reads=[("yb", ybi)], writes=[PK(7)])
                                    copy_op("act", yds[ydi][:], psb16[7][:, 0:512], [PK(7)], [("yds", ydi)])
                                    r0 = 1536 + h * 128
                                    dma("pool", yT_d[r0:r0 + 128, qb * 512:(qb + 1) * 512], yds[ydi][:], reads=[("yds", ydi)],
                                        writes=[("yd", h, qb)])
                            pipe.push(s0, s1, s2m, s2)
            pipe.flush()

        if "D" in phases:
            _phD()

        def _phE(l=l, xsrc_d=xsrc_d, lambda_init=lambda_init):
            phase_reset()
            G = sb.alloc("G", [128, NT, 16], F32)
            dma("sp", G[:], ag_d.ap().rearrange("(t p) j -> p t j", p=128), writes=["G"])
            mk = [sb.alloc("mk", [128, 128], F32) for _ in range(2)]
            onesf = sb.alloc("onesf", [128, 128], F32)
            dma("sp", mk[0][:], maskf_d.ap(), writes=["mk"])
            dma("sp", mk[1][:], maskb_d.ap(), writes=["mk"])
            dma("sp", onesf[:], ones_d.ap(), writes=["mk"])
            E1 = sb.alloc("E1", [128, NT, 2, 4], F32)
            BN = sb.alloc("BN", [128, NT, 16], F32)
            T1 = sb.alloc("T1", [128, NT, 2, 4], F32)
            A1 = sb.alloc("A1", [128, NT, 2, 4], F32)
            A2 = sb.alloc("A2", [128, NT, 2, 4], F32)
            WD = sb.alloc("WD", [128, NT, 2, 4], F32)
            FL = sb.alloc("FL", [128, NT, 2, 4], F32)
            ang = sb.alloc("ang", [128, 512], F32)
            dma("sp", ang[:], ang_d[l], writes=["ang"])
            Gv = G[:].rearrange("p t (y h) -> p t y h", y=4)
            fsel = bc_ap(Gv[:, :, 1, :], [[16, NT], [8, 2], [1, 4]])
            isel = bc_ap(Gv[:, :, 0, :], [[16, NT], [8, 2], [1, 4]])
            p.add("act", lambda e: e.activation(out=E1[:], in_=fsel, func=AF.Exp, scale=-1.0), reads=["G"], writes=["E1"])
            p.add("act", lambda e: e.activation(out=E1[:], in_=E1[:], func=AF.Ln, bias=1.0), reads=["E1"], writes=["E1"])
            for t in range(NT):
                p.add("pe", lambda e, t=t: e.matmul(psb[0][:, t * 16:t * 16 + 4], lhsT=mk[0][:], rhs=E1[:, t, 0, :], start=True, stop=True),
                      reads=["E1", "mk"], writes=[PK(0)])
                p.add("pe", lambda e, t=t: e.matmul(psb[0][:, t * 16 + 4:t * 16 + 8], lhsT=mk[1][:], rhs=E1[:, t, 1, :], start=True, stop=True),
                      reads=["E1", "mk"], writes=[PK(0)])
                p.add("pe", lambda e, t=t: e.matmul(psb[0][:, t * 16 + 8:t * 16 + 16], lhsT=onesf[:],
                                                    rhs=E1[:, t, :, :].rearrange("p a h -> p (a h)"), start=True, stop=True),
                      reads=["E1", "mk"], writes=[PK(0)])
            p.add("dve", lambda e: e.tensor_copy(out=BN[:].rearrange("p t j -> p (t j)"), in_=psb[0][:, 0:NT * 16]), reads=[PK(0)], writes=["BN"])
            bneg = BN[:, :, 0:8].rearrange("p t (a h) -> p t a h", a=2)
            tot = BN[:, :, 8:16].rearrange("p t (a h) -> p t a h", a=2)
            p.add("dve", lambda e: e.tensor_tensor(out=T1[:], in0=isel, in1=bneg, op=ALU.add), reads=["G", "BN"], writes=["T1"])
            p.add("act", lambda e: e.activation(out=A1[:], in_=T1[:], func=AF.Exp), reads=["T1"], writes=["A1"])
            p.add("dve", lambda e: e.tensor_tensor(out=T1[:], in0=T1[:], in1=tot, op=ALU.subtract), reads=["T1", "BN"], writes=["T1"])
            p.add("act", lambda e: e.activation(out=A2[:], in_=T1[:], func=AF.Exp), reads=["T1"], writes=["A2"])
            ksc = 128.0 ** -0.5
            p.add("dve", lambda e: e.tensor_scalar(out=A1[:], in0=A1[:], scalar1=ksc, scalar2=None, op0=ALU.mult), reads=["A1"], writes=["A1"])
            p.add("dve", lambda e: e.tensor_scalar(out=A2[:], in0=A2[:], scalar1=ksc, scalar2=None, op0=ALU.mult), reads=["A2"], writes=["A2"])
            p.add("act", lambda e: e.activation(out=WD[:], in_=tot, func=AF.Exp, scale=-1.0), reads=["BN"], writes=["WD"])
            p.add("act", lambda e: e.activation(out=FL[:], in_=bneg, func=AF.Exp), reads=["BN"], writes=["FL"])
            qTh = sb.alloc("qTh", [128, S], BF16)
            kTh = sb.alloc("kTh", [128, S], BF16)
            ktok = sb.alloc("ktok", [128, NT, 128], BF16)
            v1h = sb.alloc("v1h", [128, NT, 129], BF16)
            sgo = sb.alloc("sgo", [128, NT, 128], BF16)
            hacc = sb.alloc("hacc", [128, NT, 128], F32)
            xc = sb.alloc("xc", [128, NT, 128], F32)
            sq2 = sb.alloc("sq2", [128, NT, 128], F32)
            yab = sb.alloc("yab", [128, NT, 128], BF16)
            yas = sb.alloc("yas", [128, S], BF16)
            mean = sb.alloc("mean", [128, NT], F32)
            var = sb.alloc("var", [128, NT], F32)
            Cst = [sb.alloc("Cst", [128, 129], F32) for _ in range(2)]
            Cb = [sb.alloc("Cb", [128, 129], BF16) for _ in range(2)]
            wT = [sb.alloc("wT", [128, 128], BF16) for _ in range(4)]
            kS = [sb.alloc("kS", [128, 128], BF16) for _ in range(4)]
            den = [sb.alloc("den", [128, 1], F32) for _ in range(2)]
            for h in range(4):
                p.barrier()
                dma("sp", qTh[:], aqk_d[h * 128:(h + 1) * 128, :], writes=["qTh"])
                dma("sp", kTh[:], aqk_d[512 + h * 128:512 + (h + 1) * 128, :], writes=["kTh"])
                dma("sp", v1h[:], av1_d.ap()[:, h, :].rearrange("(t p) e -> p t e", p=128), writes=["v1h"])
                dma("sp", sgo[:], ao_d.ap()[:, h * 128:(h + 1) * 128].rearrange("(t p) d -> p t d", p=128), writes=["sgo"])
                p.add("dve", lambda e: e.memset(hacc[:], 0.0), writes=["hacc"])
                for dr in range(2):
                    p.add("dve", lambda e, dr=dr: e.memset(Cst[dr][:], 0.0), writes=[("Cst", dr)])
                    p.add("dve", lambda e, dr=dr: e.memset(Cb[dr][:], 0.0), writes=[("Cb", dr)])
                for t8 in range(NT // 8):
                    pb = 6 + t8 % 2
                    for c8 in range(8):
                        c = t8 * 8 + c8
                        p.add("pe", lambda e, c=c, c8=c8, pb=pb: e.transpose(out=psb16[pb][:, c8 * 128:(c8 + 1) * 128], in_=kTh[:, c * 128:(c + 1) * 128],
                                                                             identity=ident[:]), reads=["kTh"], writes=[PK(pb)])
                    copy_op("act", ktok[:, t8 * 8:(t8 + 1) * 8, :], psb16[pb][:, 0:1024].rearrange("p (c d) -> p c d", c=8), [PK(pb)], ["ktok"])
                wi = 0
                epipe = Pipe([0, 1, 2, 3])
                for step in range(NT):
                    for dr in range(2):
                        c = step if dr == 0 else NT - 1 - step
                        w = wi % 4
                        wi += 1
                        ps_s, ps_o, ps_c = dr * 3, dr * 3 + 1, dr * 3 + 2
                        cs = slice(c * 128, (c + 1) * 128)

                        def e0(cs=cs, ps_s=ps_s):
                            p.add("pe", lambda e: e.matmul(psb[ps_s][:, 0:128], lhsT=kTh[:, cs], rhs=qTh[:, cs], start=True, stop=True),
                                  reads=["qTh", "kTh"], writes=[PK(ps_s)])

                        def e1(c=c, dr=dr, h=h, w=w, ps_s=ps_s):
                            p.add("dve", lambda e: e.scalar_tensor_tensor(
                                out=wT[w][:], in0=psb[ps_s][:, 0:128], scalar=A1[:, c, dr, h:h + 1], in1=mk[dr][:], op0=ALU.mult, op1=ALU.mult),
                                reads=[PK(ps_s), "A1"], writes=[("wT", w)])
                            p.add("act", lambda e: e.activation(out=kS[w][:], in_=ktok[:, c, :], func=AF.Copy, scale=A2[:, c, dr, h:h + 1]),
                                  reads=["ktok", "A2"], writes=[("kS", w)])

                        def e2(c=c, cs=cs, dr=dr, w=w, ps_o=ps_o, ps_c=ps_c):
                            p.add("pe", lambda e: e.matmul(psb[ps_o][:, 0:129], lhsT=wT[w][:], rhs=v1h[:, c, :], start=True, stop=False),
                                  reads=[("wT", w), "v1h"], writes=[PK(ps_o)])
                            p.add("pe", lambda e: e.matmul(psb[ps_o][:, 0:129], lhsT=qTh[:, cs], rhs=Cb[dr][:], start=False, stop=True),
                                  reads=[("Cb", dr)], writes=[PK(ps_o)])
                            p.add("pe", lambda e: e.matmul(psb[ps_c][:, 0:129], lhsT=kS[w][:], rhs=v1h[:, c, :], start=True, stop=True),
                                  reads=[("kS", w)], writes=[PK(ps_c)])

                        def e3(c=c, dr=dr, h=h, ps_o=ps_o, ps_c=ps_c):
                            p.add("dve", lambda e: e.scalar_tensor_tensor(
                                out=Cst[dr][:], in0=Cst[dr][:], scalar=WD[:, c, dr, h:h + 1], in1=psb[ps_c][:, 0:129], op0=ALU.mult, op1=ALU.add),
                                reads=[PK(ps_c), "WD", ("Cst", dr)], writes=[("Cst", dr)])
                            p.add("act", lambda e: e.copy(out=Cb[dr][:], in_=Cst[dr][:]), reads=[("Cst", dr)], writes=[("Cb", dr)])
                            p.add("dve", lambda e: e.scalar_tensor_tensor(
                                out=den[dr][:], in0=psb[ps_o][:, 128:129], scalar=-1.0, in1=FL[:, c, dr, h:h + 1], op0=ALU.mult, op1=ALU.max),
                                reads=[PK(ps_o), "FL"], writes=[("den", dr)])
                            p.add("dve", lambda e: e.tensor_tensor(
                                out=den[dr][:], in0=den[dr][:], in1=psb[ps_o][:, 128:129], op=ALU.max),
                                reads=[("den", dr), PK(ps_o)], writes=[("den", dr)])
                            p.add("dve", lambda e: e.reciprocal(out=den[dr][:], in_=den[dr][:]), reads=[("den", dr)], writes=[("den", dr)])
                            p.add("dve", lambda e: e.scalar_tensor_tensor(
                                out=hacc[:, c, :], in0=psb[ps_o][:, 0:128], scalar=den[dr][:], in1=hacc[:, c, :], op0=ALU.mult, op1=ALU.add),
                                reads=[PK(ps_o), ("den", dr), ("hacc", c)], writes=[("hacc", c)])
                        epipe.push(e0, e1, e2, e3)
                epipe.flush()
                p.barrier()
                p.add("dve", lambda e: e.tensor_reduce(out=mean[:], in_=hacc[:], axis=AX.X, op=ALU.add), writes=["mean"])
                p.add("dve", lambda e: e.tensor_scalar(out=mean[:], in0=mean[:], scalar1=1.0 / 128, scalar2=None, op0=ALU.mult), reads=["mean"], writes=["mean"])
                p.add("dve", lambda e: e.tensor_tensor(out=xc[:], in0=hacc[:], in1=bc_ap(mean[:], [[1, NT], [0, 128]]), op=ALU.subtract),
                      reads=["mean"], writes=["xc"])
                p.add("act", lambda e: e.activation(out=sq2[:], in_=xc[:], func=AF.Square), reads=["xc"], writes=["sq2"])
                p.add("dve", lambda e: e.tensor_reduce(out=var[:], in_=sq2[:], axis=AX.X, op=ALU.add), reads=["sq2"], writes=["var"])
                rstd_ops(var[:], "var", 1.0 / 128)
                p.add("dve", lambda e: e.tensor_tensor(out=xc[:], in0=xc[:], in1=bc_ap(var[:], [[1, NT], [0, 128]]), op=ALU.mult),
                      reads=["xc", "var"], writes=["xc"])
                p.add("dve", lambda e, h=h: e.tensor_tensor(out=xc[:], in0=xc[:], in1=bc_ap(ang[:, h * 128:(h + 1) * 128], [[0, NT], [1, 128]]), op=ALU.mult),
                      reads=["xc", "ang"], writes=["xc"])
                p.add("dve", lambda e: e.tensor_tensor(out=yab[:], in0=xc[:], in1=sgo[:], op=ALU.mult), reads=["xc", "sgo"], writes=["yab"])
                for t8 in range(NT // 8):
                    pb = 6 + t8 % 2
                    for c8 in range(8):
                        c = t8 * 8 + c8
                        p.add("pe", lambda e, c=c, c8=c8, pb=pb: e.transpose(out=psb16[pb][:, c8 * 128:(c8 + 1) * 128], in_=yab[:, c, :],
                                                                             identity=ident[:]), reads=["yab"], writes=[PK(pb)])
                    copy_op("act", yas[:, t8 * 1024:(t8 + 1) * 1024], psb16[pb][:, 0:1024], [PK(pb)], ["yas"])
                dma("pool", yT_d[h * 128:(h + 1) * 128, :], yas[:], reads=["yas"], writes=[("ya", h)])

        if "E" in phases:
            _phE()

        def _phF(l=l, xsrc_d=xsrc_d, lambda_init=lambda_init):
            phase_reset()
            wg = sb.alloc("wg", [128, 8, 4096], BF16)
            wu = sb.alloc("wu", [128, 16, 1024], BF16)
            wo = sb.alloc("wo", [128, 8, 1024], BF16)
            mF = sb.mark()
            stg = [sb.alloc("stg", [128, 8, 512], F32) for _ in range(2)]
            load_cast(lambda c0, c1: wg[:, :, c0:c1],
                      lambda c0, c1: w_in_d[l, :, 5904 + c0:5904 + c1].rearrange("(c p) n -> p c n", p=128), 8, 4096, stg, "wg")
            for bi in range(4):
                load_cast(lambda c0, c1, bi=bi: wu[:, bi * 4:(bi + 1) * 4, c0:c1],
                          lambda c0, c1, bi=bi: w_up_d[bi][l, :, c0:c1].rearrange("(c p) n -> p c n", p=128), 4, 1024, stg, "wu%d" % bi)
            load_cast(lambda c0, c1: wo[:, :, c0:c1],
                      lambda c0, c1: w_out_d[l, :, c0:c1].rearrange("(c p) n -> p c n", p=128), 8, 1024, stg, "wo")
            p.barrier()
            sb.reset(mF)
            hTb = [sb.alloc("hTb", [128, 8, 512], BF16) for _ in range(2)]
            yTb = [sb.alloc("yTb", [128, 16, 512], BF16) for _ in range(2)]
            mT = [sb.alloc("mT", [128, 8, 512], BF16) for _ in range(2)]
            sg = [sb.alloc("sg", [128, 512], F32) for _ in range(2)]
            acc = sb.alloc("acc", [128, 512], F32)
            tmpm = sb.alloc("tmpm", [128, 512], F32)
            xtl = [sb.alloc("xtl", [128, D], F32) for _ in range(2)]
            bankc = 0
            xi = 0
            for b in range(NB):
                bi = b % 2
                bs = slice(b * 512, (b + 1) * 512)
                dma("sp", hTb[bi][:], hT_d.ap()[:, bs].rearrange("(c p) s -> p c s", p=128), writes=[("hTb", bi)])
                dma("sp", yTb[bi][:], yT_d.ap()[:, bs].rearrange("(c p) s -> p c s", p=128), writes=[("yTb", bi)])
                for dc in range(8):
                    for g in range(4):
                        pg = bankc % 6
                        pu = (bankc + 1) % 6
                        bankc += 2
                        for k in range(8):
                            p.add("pe", lambda e, k=k, g=g, dc=dc, pg=pg, bi=bi: e.matmul(
                                psb[pg][:], lhsT=wg[:, k, g * 1024 + dc * 128:g * 1024 + (dc + 1) * 128], rhs=hTb[bi][:, k, :],
                                start=(k == 0), stop=(k == 7)), reads=[("hTb", bi)], writes=[PK(pg)])
                        for k in range(4):
                            p.add("pe", lambda e, k=k, g=g, dc=dc, pu=pu, bi=bi: e.matmul(
                                psb[pu][:], lhsT=wu[:, g * 4 + k, dc * 128:(dc + 1) * 128], rhs=yTb[bi][:, g * 4 + k, :],
                                start=(k == 0), stop=(k == 3)), reads=[("yTb", bi)], writes=[PK(pu)])
                        sgi = g % 2
                        p.add("act", lambda e, pg=pg, sgi=sgi: e.activation(out=sg[sgi][:], in_=psb[pg][:], func=AF.Sigmoid),
                              reads=[PK(pg)], writes=[("sg", sgi)])
                        if g == 0:
                            p.add("dve", lambda e, pu=pu, sgi=sgi: e.tensor_tensor(out=acc[:], in0=sg[sgi][:], in1=psb[pu][:], op=ALU.mult),
                                  reads=[PK(pu), ("sg", sgi)], writes=["acc"])
                        else:
                            p.add("dve", lambda e, pu=pu, sgi=sgi: e.tensor_tensor(out=tmpm[:], in0=sg[sgi][:], in1=psb[pu][:], op=ALU.mult),
                                  reads=[PK(pu), ("sg", sgi)], writes=["tmpm"])
                            if g < 3:
                                p.add("dve", lambda e: e.tensor_tensor(out=acc[:], in0=acc[:], in1=tmpm[:], op=ALU.add),
                                      reads=["tmpm", "acc"], writes=["acc"])
                            else:
                                p.add("dve", lambda e, dc=dc, bi=bi: e.tensor_tensor(out=mT[bi][:, dc, :], in0=acc[:], in1=tmpm[:], op=ALU.add),
                                      reads=["tmpm", "acc"], writes=[("mT", bi, dc)])
                for tt in range(4):
                    xt = xtl[xi % 2]
                    xk = ("xtl", xi % 2)
                    xi += 1
                    tok = slice(b * 512 + tt * 128, b * 512 + (tt + 1) * 128)
                    dma("sp", xt[:], xsrc_d[tok, :], writes=[xk])
                    for half in range(2):
                        po = 6 + half
                        for k in range(8):
                            p.add("pe", lambda e, k=k, tt=tt, half=half, po=po, bi=bi: e.matmul(
                                psb[po][:], lhsT=mT[bi][:, k, tt * 128:(tt + 1) * 128], rhs=wo[:, k, half * 512:(half + 1) * 512],
                                start=(k == 0), stop=(k == 7)), reads=[("mT", bi, k)], writes=[PK(po)])
                        p.add("dve", lambda e, xt=xt, half=half, po=po: e.tensor_tensor(out=xt[:, half * 512:(half + 1) * 512],
                                                                                        in0=xt[:, half * 512:(half + 1) * 512], in1=psb[po][:], op=ALU.add),
                              reads=[PK(po), xk], writes=[xk])
                    dma("pool", xres_d[tok, :], xt[:], reads=[xk], writes=[("xres", b, tt)])

        if "F" in phases:
            _phF()

        def _phG(l=l, xsrc_d=xsrc_d, lambda_init=lambda_init):
            phase_reset()
            wgt = sb.alloc("wgt", [128, 8, D_FF], BF16)
            wup = sb.alloc("wup", [128, 8, D_FF], BF16)
            wdn = sb.alloc("wdn", [128, 22, D], BF16)
            g2 = sb.alloc("g2", [128, D], F32)
            dma("sp", g2[:], n2g_d[l], writes=["g2"])
            mG = sb.mark()
            stg = [sb.alloc("stg", [128, 8, 512], F32) for _ in range(2)]
            load_cast(lambda c0, c1: wgt[:, :, c0:c1], lambda c0, c1: w_fg_d[l, :, c0:c1].rearrange("(c p) n -> p c n", p=128), 8, D_FF, stg, "wgt")
            load_cast(lambda c0, c1: wup[:, :, c0:c1], lambda c0, c1: w_fu_d[l, :, c0:c1].rearrange("(c p) n -> p c n", p=128), 8, D_FF, stg, "wup")
            for r in range(0, 22, 8):
                n = min(8, 22 - r)
                load_cast(lambda c0, c1, r=r, n=n: wdn[:, r:r + n, c0:c1],
                          lambda c0, c1, r=r, n=n: w_fd_d[l, r * 128:(r + n) * 128, c0:c1].rearrange("(c p) n -> p c n", p=128), n, D, stg, "wdn%d" % r)
            p.barrier()
            sb.reset(mG)
            xtl = [sb.alloc("xtl", [128, D], F32) for _ in range(4)]
            sqj = sb.alloc("sqj", [128, D], F32)
            ssb = [sb.alloc("ss", [128, 1], F32) for _ in range(2)]
            hbb = [sb.alloc("hb", [128, D], BF16) for _ in range(2)]
            h2T = sb.alloc("h2T", [128, 8, 512], BF16)
            aT = sb.alloc("aT", [128, 22, 512], BF16)
            sg = [sb.alloc("sg", [128, 512], F32) for _ in range(2)]
            bankc = 0
            for b in range(NB):
                for tt in range(4):
                    tok = slice(b * 512 + tt * 128, b * 512 + (tt + 1) * 128)
                    dma("sp", xtl[tt][:], xres_d[tok, :], reads=[("xres", b, tt)], writes=[("xtl", tt)])
                    norm_tile(xtl[tt][:], ("xtl", tt), g2[:], "g2", sqj[:], ssb[tt % 2][:], ("ss", tt % 2), hbb[tt % 2][:], ("hb", tt % 2),
                              6 + tt % 2, h2T[:, :, tt * 128:(tt + 1) * 128], ("h2T", tt))
                for fc in range(22):
                    pg = bankc % 6
                    pu = (bankc + 1) % 6
                    bankc += 2
                    for k in range(8):
                        p.add("pe", lambda e, k=k, fc=fc, pg=pg: e.matmul(psb[pg][:], lhsT=wgt[:, k, fc * 128:(fc + 1) * 128], rhs=h2T[:, k, :],
                                                                          start=(k == 0), stop=(k == 7)),
                              reads=[("h2T", 0), ("h2T", 1), ("h2T", 2), ("h2T", 3)], writes=[PK(pg)])
                    for k in range(8):
                        p.add("pe", lambda e, k=k, fc=fc, pu=pu: e.matmul(psb[pu][:], lhsT=wup[:, k, fc * 128:(fc + 1) * 128], rhs=h2T[:, k, :],
                                                                          start=(k == 0), stop=(k == 7)),
                              reads=[("h2T", 0), ("h2T", 1), ("h2T", 2), ("h2T", 3)], writes=[PK(pu)])
                    sgi = fc % 2
                    p.add("act", lambda e, pg=pg, sgi=sgi: e.activation(out=sg[sgi][:], in_=psb[pg][:], func=AF.Silu), reads=[PK(pg)], writes=[("sg", sgi)])
                    p.add("dve", lambda e, pu=pu, sgi=sgi, fc=fc: e.tensor_tensor(out=aT[:, fc, :], in0=sg[sgi][:], in1=psb[pu][:], op=ALU.mult),
                          reads=[PK(pu), ("sg", sgi)], writes=[("aT", fc)])
                for tt in range(4):
                    tok = slice(b * 512 + tt * 128, b * 512 + (tt + 1) * 128)
                    xt = xtl[tt]
                    for half in range(2):
                        po = 6 + half
                        for fc in range(22):
                            p.add("pe", lambda e, fc=fc, tt=tt, half=half, po=po: e.matmul(
                                psb[po][:], lhsT=aT[:, fc, tt * 128:(tt + 1) * 128], rhs=wdn[:, fc, half * 512:(half + 1) * 512],
                                start=(fc == 0), stop=(fc == 21)), reads=[("aT", fc)], writes=[PK(po)])
                        p.add("dve", lambda e, xt=xt, half=half, po=po: e.tensor_tensor(out=xt[:, half * 512:(half + 1) * 512],
                                                                                        in0=xt[:, half * 512:(half + 1) * 512], in1=psb[po][:], op=ALU.add),
                              reads=[PK(po), ("xtl", tt)], writes=[("xtl", tt)])
                    dma("pool", xres_d[tok, :], xt[:], reads=[("xtl", tt)], writes=[("xres", b, tt)])

        if "G" in phases:
            _phG()

    if "Z" in phases:
        phase_reset()
        gf = sb.alloc("gf", [128, D], F32)
        dma("sp", gf[:], fg_d.ap(), writes=["gf"])
        xb = [sb.alloc("xb", [128, D], F32) for _ in range(2)]
        ob_ = [sb.alloc("ob", [128, D], F32) for _ in range(2)]
        sqj = sb.alloc("sqj", [128, D], F32)
        ssb = [sb.alloc("ss", [128, 1], F32) for _ in range(2)]
        for t in range(NT):
            i = t % 2
            tok = slice(t * 128, (t + 1) * 128)
            dma("sp", xb[i][:], xres_d[tok, :], writes=[("xb", i)])
            p.add("dve", lambda e, i=i: e.memset(ssb[i][:], 0.0), writes=[("ss", i)])
            p.add("act", lambda e, i=i: e.activation(out=sqj[:], in_=xb[i][:], func=AF.Square, accum_out=ssb[i][:]),
                  reads=[("xb", i)], writes=[("ss", i), "sqj"])
            rstd_ops(ssb[i][:], ("ss", i), 1.0 / D)
            p.add("dve", lambda e, i=i: e.scalar_tensor_tensor(out=ob_[i][:], in0=xb[i][:], scalar=ssb[i][:], in1=gf[:], op0=ALU.mult, op1=ALU.mult),
                  reads=[("xb", i), ("ss", i), "gf"], writes=[("ob", i)])
            dma("pool", out_d[tok, :], ob_[i][:], reads=[("ob", i)], writes=[("out", t)])
    p.barrier()
    p.wait_all("pool", [])
    p.emit()
    return nc


def natten_tables(rpb, S):
    rows = S // GRID_W
    NT = S // 128
    wr, wc = 8, 16
    out = np.full((5, 128, 5, 8, 128), NEG, np.float32)
    reps = [0, 1, 2, NT - 2, NT - 1]
    for pi, i in enumerate(reps):
        kb0 = min(max(i - 2, 0), NT - 5)
        q = np.arange(i * 128, (i + 1) * 128)
        r = q // GRID_W
        c = q % GRID_W
        rs = np.clip(r - wr // 2, 0, rows - wr)
        cs = np.clip(c - wc // 2, 0, GRID_W - wc)
        keys = np.arange(kb0 * 128, (kb0 + 5) * 128)
        kr = keys // GRID_W
        kc = keys % GRID_W
        inwin = ((kr[None, :] >= rs[:, None]) & (kr[None, :] < rs[:, None] + wr) &
                 (kc[None, :] >= cs[:, None]) & (kc[None, :] < cs[:, None] + wc))
        offr = np.clip(kr[None, :] - r[:, None] + (wr - 1), 0, 2 * wr - 2)
        offc = np.clip(kc[None, :] - c[:, None] + (wc - 1), 0, 2 * wc - 2)
        g = rpb[:, offr, offc]
        g = np.where(inwin[None], g, np.float32(NEG)).astype(np.float32)
        g = g.reshape(8, 128, 5, 128).transpose(3, 2, 0, 1)
        out[pi] = g
    return out


def host_consts(S):
    t = np.arange(S)
    row = (t // GRID_W).astype(np.float32)
    col = (t % GRID_W).astype(np.float32)
    nf = 16
    inv = (10000.0 ** (-np.arange(nf, dtype=np.float32) / nf)).astype(np.float32)
    ar = row[:, None] * inv
    ac = col[:, None] * inv
    cos64 = np.concatenate([np.cos(ar), np.cos(ar), np.cos(ac), np.cos(ac)], axis=1).astype(np.float32)
    sin64 = np.concatenate([-np.sin(ar), np.sin(ar), -np.sin(ac), np.sin(ac)], axis=1).astype(np.float32)
    W = 2 * S - 128
    C0 = (S // 128 - 1) * 128
    pp = np.arange(128)[:, None]
    cc = np.arange(W)[None, :]
    tz = (-np.abs(cc - pp - C0)).astype(np.float32)
    s_ = np.arange(128)[:, None]
    t_ = np.arange(128)[None, :]
    maskf = (s_ <= t_).astype(np.float32)
    maskb = (s_ >= t_).astype(np.float32)
    ones = np.ones((128, 128), np.float32)
    sel = np.zeros((65, 64), np.float32)
    sel[64, :] = 1.0
    ident = np.eye(128, dtype=np.float32).astype(ml_dtypes.bfloat16)
    return dict(cos64=cos64, sin64=sin64, tz=tz, maskf=maskf, maskb=maskb, ones=ones, sel=sel, ident=ident)


def rep128(a):
    a = np.asarray(a, np.float32)
    return np.ascontiguousarray(np.broadcast_to(a[:, None, :], (a.shape[0], 128, a.shape[1])))


def prep_shared(inp, S):
    L = inp["w_in"].shape[0]
    f = lambda k: np.ascontiguousarray(np.asarray(inp[k], np.float32))
    sh = dict(
        w_in=f("w_in"), w_up_a=f("w_up_a"), w_up_b=f("w_up_b"), w_up_c=f("w_up_c"), w_up_d=f("w_up_d"),
        w_out=f("w_out"), w_ffn_gate=f("w_ffn_gate"), w_ffn_up=f("w_ffn_up"), w_ffn_down=f("w_ffn_down"),
        norm1_g_r=rep128(inp["norm1_g"]), norm2_g_r=rep128(inp["norm2_g"]),
        final_g_r=np.ascontiguousarray(np.broadcast_to(np.asarray(inp["final_g"], np.float32)[None, :], (128, D))),
        conv_w_r=np.ascontiguousarray(np.asarray(inp["a_conv_w"], np.float32).reshape(L, 3, 8, 128).transpose(0, 3, 2, 1)),
        gbias_r=rep128(inp["a_gate_bias"]), anorm_g_r=rep128(inp["a_norm_g"]),
        bqg_r=rep128(inp["b_qnorm_g"]), bkg_r=rep128(inp["b_knorm_g"]),
        dl_r=np.ascontiguousarray(np.broadcast_to(
            np.stack([np.asarray(inp[k], np.float32) for k in ("d_lambda_q1", "d_lambda_k1", "d_lambda_q2", "d_lambda_k2")], axis=1)[:, None],
            (L, 128, 4, 64))),
        dsub_r=rep128(inp["d_subln_g"]),
        natb=np.stack([natten_tables(np.asarray(inp["c_rpb"], np.float32)[l], S) for l in range(L)]),
    )
    sh.update(host_consts(S))
    return sh


_NC_CACHE = {}


def kernel(**inputs):
    x = np.asarray(inputs["x"], np.float32)
    B, S, _ = x.shape
    key = (S,)
    if key not in _NC_CACHE:
        _NC_CACHE[key] = build_program(S=S)
    nc = _NC_CACHE[key]
    sh = prep_shared(inputs, S)
    in_maps = []
    for b in range(B):
        m = dict(sh)
        m["x"] = np.ascontiguousarray(x[b])
        in_maps.append(m)
    res = run_bass_kernel_spmd(nc, in_maps, core_ids=list(range(B)))
    return np.stack([np.asarray(r["out"], np.float32) for r in res.results], axis=0)
```

```python
import math
import numpy as np
import ml_dtypes
import concourse.bass as bass
import concourse.mybir as mybir
from concourse.bass_utils import run_bass_kernel_spmd

F32 = mybir.dt.float32
BF16 = mybir.dt.bfloat16
AF = mybir.ActivationFunctionType
ALU = mybir.AluOpType
AX = mybir.AxisListType

D = 1024
SEQ = 4096
DEPTH = 2
GRID_W = 64
EPS = 1e-6
D_IN = 10000
D_FF = 2816
NEG = -30000.0

ENGS = ("pe", "act", "dve", "pool", "sp")
DMA_RING = 8
NO_SELF_SYNC = ("pe",)


class Op:
    __slots__ = ("eng", "fn", "idx", "dma", "waits", "val", "need_inc", "clock", "ring", "semval")

    def __init__(self, eng, fn, idx, dma):
        self.eng = eng
        self.fn = fn
        self.idx = idx
        self.dma = dma
        self.waits = []
        self.need_inc = False
        self.ring = None
        self.semval = None


class Prog:
    def __init__(self, nc, same_engine_sync=True):
        self.nc = nc
        self.ops = {e: [] for e in ENGS}
        self.writer = {}
        self.readers = {}
        self.know = {e: {} for e in ENGS}
        self.dma_count = {e: 0 for e in ENGS}
        self.dma_ops = {e: [] for e in ENGS}
        self.same_engine_sync = same_engine_sync
        self.barrier_deps = {e: [] for e in ENGS}

    def add(self, eng, fn, reads=(), writes=(), dma=False):
        lst = self.ops[eng]
        op = Op(eng, fn, len(lst), dma)
        deps = []
        for k in reads:
            w = self.writer.get(k)
            if w is not None:
                deps.append(w)
        for k in writes:
            w = self.writer.get(k)
            if w is not None:
                deps.append(w)
            deps.extend(self.readers.get(k, ()))
        if self.barrier_deps[eng]:
            deps.extend(self.barrier_deps[eng])
            self.barrier_deps[eng] = []
        if dma:
            n = self.dma_count[eng]
            op.ring = n % DMA_RING
            if n >= DMA_RING:
                deps.append(self.dma_ops[eng][n - DMA_RING])
            self.dma_count[eng] = n + 1
            self.dma_ops[eng].append(op)
        know = self.know[eng]
        for d in deps:
            if d is op:
                continue
            if d.dma:
                key = ("dma", d.eng, d.ring)
                val = d.val
            else:
                if d.eng == eng and (eng in NO_SELF_SYNC or not self.same_engine_sync):
                    continue
                key = d.eng
                val = d.idx
            if know.get(key, -1) >= val:
                continue
            op.waits.append(d)
            d.need_inc = True
            for k2, v2 in d.clock.items():
                if know.get(k2, -1) < v2:
                    know[k2] = v2
        if dma:
            op.val = (self.dma_count[eng] - 1) // DMA_RING
            ck = ("dma", eng, op.ring)
        else:
            op.val = op.idx
            ck = eng
        op.clock = dict(know)
        op.clock[ck] = op.val
        lst.append(op)
        for k in writes:
            self.writer[k] = op
            self.readers[k] = []
        for k in reads:
            if k not in writes:
                self.readers.setdefault(k, []).append(op)
        return op

    def wait_all(self, eng, keys):
        return self.add(eng, lambda e: None, reads=list(keys))

    def barrier(self):
        lasts = []
        for e in ENGS:
            if self.ops[e]:
                lasts.append(self.ops[e][-1])
            lasts.extend(self.dma_ops[e][-DMA_RING:])
        for e in ENGS:
            self.barrier_deps[e] = list(lasts)

    def emit(self):
        nc = self.nc
        from contextlib import ExitStack
        with ExitStack() as es:
            csem = {e: es.enter_context(nc.semaphore("c_" + e)) for e in ENGS}
            dsem = {(e, r): es.enter_context(nc.semaphore("d_%s_%d" % (e, r)))
                    for e in ENGS for r in range(DMA_RING) if self.dma_count[e] > 0}
            for e in ENGS:
                cnt = 0
                for op in self.ops[e]:
                    if not op.dma and op.need_inc:
                        cnt += 1
                        op.semval = cnt
            block = es.enter_context(nc.Block())

            def run(e, eng):
                for op in self.ops[e]:
                    for d in op.waits:
                        if d.dma:
                            eng.wait_ge(dsem[(d.eng, d.ring)], 16 * (d.val + 1))
                        else:
                            eng.wait_ge(csem[d.eng], d.semval)
                    ins = op.fn(eng)
                    if ins is None:
                        continue
                    if op.dma:
                        ins.then_inc(dsem[(e, op.ring)], 16)
                    elif op.need_inc:
                        ins.then_inc(csem[e], 1)

            @block.sync
            def _(eng):
                run("sp", eng)

            @block.scalar
            def _(eng):
                run("act", eng)

            @block.vector
            def _(eng):
                run("dve", eng)

            @block.gpsimd
            def _(eng):
                run("pool", eng)

            @block.tensor
            def _(eng):
                run("pe", eng)


class Pipe:
    def __init__(self, offs):
        self.offs = offs
        self.items = []

    def push(self, *fns):
        self.items.append(fns)

    def flush(self):
        n = len(self.items)
        m = max(self.offs)
        for t in range(n + m):
            for j, o in enumerate(self.offs):
                i = t - o
                if 0 <= i < n and self.items[i][j] is not None:
                    self.items[i][j]()
        self.items = []


SB_BASE = 16640
SB_LIMIT = 229376


class Arena:
    def __init__(self, nc):
        self.nc = nc
        self.off = SB_BASE
        self.n = 0

    def alloc(self, name, shape, dt):
        esz = 4 if dt == F32 else 2
        nbytes = int(np.prod(shape[1:])) * esz
        nbytes = (nbytes + 63) // 64 * 64
        assert self.off + nbytes <= SB_LIMIT, "SBUF overflow at %s: %d" % (name, self.off + nbytes)
        self.n += 1
        t = self.nc.alloc_sbuf_tensor_at("%s_%d" % (name, self.n), list(shape), dt, offset=self.off)
        self.off += nbytes
        return t

    def mark(self):
        return self.off

    def reset(self, to=SB_BASE):
        self.off = to


def bc_ap(ap, dims):
    return bass.AP(tensor=ap.tensor, offset=ap.offset, ap=[list(ap.ap[0])] + [list(d) for d in dims])


def build_program(S=SEQ, depth=DEPTH, debug=False, phases="ABCDEFGZ"):
    NT = S // 128
    NB = S // 512
    rows = S // GRID_W
    nc = bass.Bass("TRN2", target_bir_lowering=False)
    p = Prog(nc)
    sb = Arena(nc)
    L = depth

    def din(name, shape, dt=F32):
        return nc.dram_tensor(name, list(shape), dt, kind="ExternalInput")

    def dscr(name, shape, dt=BF16):
        return nc.dram_tensor(name, list(shape), dt, kind=("ExternalOutput" if debug else "Internal"))

    x_d = din("x", [S, D])
    w_in_d = din("w_in", [L, D, D_IN])
    w_up_d = [din("w_up_%s" % c, [L, 512, D]) for c in "abcd"]
    w_out_d = din("w_out", [L, D, D])
    w_fg_d = din("w_ffn_gate", [L, D, D_FF])
    w_fu_d = din("w_ffn_up", [L, D, D_FF])
    w_fd_d = din("w_ffn_down", [L, D_FF, D])
    n1g_d = din("norm1_g_r", [L, 128, D])
    n2g_d = din("norm2_g_r", [L, 128, D])
    fg_d = din("final_g_r", [128, D])
    convw_d = din("conv_w_r", [L, 128, 8, 3])
    gbias_d = din("gbias_r", [L, 128, 16])
    ang_d = din("anorm_g_r", [L, 128, 512])
    bqg_d = din("bqg_r", [L, 128, 64])
    bkg_d = din("bkg_r", [L, 128, 64])
    dl_d = din("dl_r", [L, 128, 4, 64])
    dsub_d = din("dsub_r", [L, 128, 128])
    natb_d = din("natb", [L, 5, 128, 5, 8, 128])
    ident_d = din("ident", [128, 128], BF16)
    cos_d = din("cos64", [S, 64])
    sin_d = din("sin64", [S, 64])
    tz_d = din("tz", [128, 2 * S - 128])
    maskf_d = din("maskf", [128, 128])
    maskb_d = din("maskb", [128, 128])
    ones_d = din("ones", [128, 128])
    sel_d = din("sel", [65, 64])
    out_d = nc.dram_tensor("out", [S, D], F32, kind="ExternalOutput")

    xres_d = dscr("xres", [S, D], F32)
    hT_d = dscr("hT_s", [D, S])
    aqk_d = dscr("aqk_s", [1024, S])
    av1_d = dscr("av1_s", [S, 4, 129])
    ao_d = dscr("ao_s", [S, 512])
    ag_d = dscr("ag_s", [S, 16], F32)
    bqT_d = dscr("bqT_s", [512, S])
    bkT_d = dscr("bkT_s", [128, S])
    bv1_d = dscr("bv1_s", [S, 2, 65])
    cqT_d = dscr("cqT_s", [512, S])
    ckT_d = dscr("ckT_s", [512, S])
    cv1_d = dscr("cv1_s", [S, 8, 65])
    dqT_d = dscr("dqT_s", [512, S])
    dkT_d = dscr("dkT_s", [512, S])
    dv1_d = dscr("dv1_s", [S, 4, 129])
    yT_d = dscr("yT_s", [2048, S])

    psall = nc.alloc_psum_tensor("psall", [128, 4096], F32)
    ps16all = psall.bitcast(BF16)
    psb = [psall[:, i * 512:(i + 1) * 512] for i in range(8)]
    psb16 = [ps16all[:, i * 1024:(i + 1) * 1024] for i in range(8)]

    def PK(i):
        return ("ps", i)

    def dma(eng, out, in_, reads=(), writes=(), **kw):
        return p.add(eng, lambda e: e.dma_start(out=out, in_=in_, **kw), reads=reads, writes=writes, dma=True)

    def new_phase():
        p.barrier()
        sb.reset()

    cnt = [0]

    def uid():
        cnt[0] += 1
        return cnt[0]

    ident = sb.alloc("ident", [128, 128], BF16)
    dma("sp", ident[:], ident_d.ap(), writes=["ident"])
    base_mark = sb.mark()

    def phase_reset():
        p.barrier()
        sb.reset(base_mark)

    evac_rr = [0]

    def evac_engine():
        evac_rr[0] += 1
        return "act" if evac_rr[0] % 2 else "dve"

    def copy_op(eng, out, in_, reads, writes, scale=None):
        if eng == "act":
            if scale is None:
                return p.add("act", lambda e: e.copy(out=out, in_=in_), reads=reads, writes=writes)
            return p.add("act", lambda e: e.mul(out=out, in_=in_, mul=scale), reads=reads, writes=writes)
        else:
            if scale is None:
                return p.add(eng, lambda e: e.tensor_copy(out=out, in_=in_), reads=reads, writes=writes)
            return p.add(eng, lambda e: e.tensor_scalar(out=out, in0=in_, scalar1=scale, scalar2=None, op0=ALU.mult),
                         reads=reads, writes=writes)

    def rstd_ops(v, key, n_scale, eps=EPS):
        p.add("dve", lambda e: e.tensor_scalar(out=v, in0=v, scalar1=n_scale, scalar2=eps, op0=ALU.mult, op1=ALU.add),
              reads=[key], writes=[key])
        p.add("act", lambda e: e.activation(out=v, in_=v, func=AF.Ln), reads=[key], writes=[key])
        p.add("act", lambda e: e.activation(out=v, in_=v, func=AF.Exp, scale=-0.5), reads=[key], writes=[key])

    stage_rr = [0]

    def load_cast(dst_ap_fn, src_ap_fn, nchunk, ncols, stage_f, tag, cols_per=512):
        for c0 in range(0, ncols, cols_per):
            c1 = min(ncols, c0 + cols_per)
            i = stage_rr[0]
            stage_rr[0] += 1
            st = stage_f[i % len(stage_f)]
            sk = ("stage", i % len(stage_f))
            dma("sp", st[:, 0:nchunk, 0:c1 - c0], src_ap_fn(c0, c1), writes=[sk])
            if i % 2 == 0:
                p.add("act", lambda e, st=st, c0=c0, c1=c1: e.copy(out=dst_ap_fn(c0, c1), in_=st[:, 0:nchunk, 0:c1 - c0]),
                      reads=[sk], writes=[("w", tag, c0)])
            else:
                p.add("dve", lambda e, st=st, c0=c0, c1=c1: e.tensor_copy(out=dst_ap_fn(c0, c1), in_=st[:, 0:nchunk, 0:c1 - c0]),
                      reads=[sk], writes=[("w", tag, c0)])

    def load_qpad(qTp, src_d):
        p.add("dve", lambda e: e.memset(qTp[:], 0.0), writes=["qT"])
        for h in range(8):
            r = (h % 2) * 64
            dma("sp", qTp[r:r + 64, h, :], src_d[h * 64:(h + 1) * 64, :], reads=["qT"], writes=[("qTh", h)])

    def norm_tile(xt, xk, gt, gk, sqj, ss, ssk, hb, hbk, ptr_i, dst_ap, dst_key):
        p.add("dve", lambda e: e.memset(ss, 0.0), writes=[ssk])
        p.add("act", lambda e: e.activation(out=sqj, in_=xt, func=AF.Square, accum_out=ss), reads=[xk], writes=[ssk, "sqj"])
        rstd_ops(ss, ssk, 1.0 / D)
        p.add("dve", lambda e: e.scalar_tensor_tensor(out=hb, in0=xt, scalar=ss, in1=gt, op0=ALU.mult, op1=ALU.mult),
              reads=[xk, ssk, gk], writes=[hbk])
        for c in range(8):
            p.add("pe", lambda e, c=c: e.transpose(out=psb16[ptr_i][:, c * 128:(c + 1) * 128], in_=hb[:, c * 128:(c + 1) * 128],
                                                   identity=ident[:]), reads=[hbk], writes=[PK(ptr_i)])
        copy_op("act", dst_ap, psb16[ptr_i][:, 0:1024].rearrange("p (c t) -> p c t", c=8), [PK(ptr_i)], [dst_key])

    for l in range(L):
        xsrc_d = x_d if l == 0 else xres_d
        lambda_init = 0.8 - 0.6 * math.exp(-0.3 * l)

        def _phA(l=l, xsrc_d=xsrc_d, lambda_init=lambda_init):
            phase_reset()
            hT = sb.alloc("hT", [128, 8, S], BF16)
            g1 = sb.alloc("g1", [128, D], F32)
            dma("sp", g1[:], n1g_d[l], writes=["g1"])
            xb = [sb.alloc("xb", [128, D], F32) for _ in range(2)]
            sqj = sb.alloc("sqj", [128, D], F32)
            ssb = [sb.alloc("ss", [128, 1], F32) for _ in range(2)]
            hbb = [sb.alloc("hb", [128, D], BF16) for _ in range(2)]
            mA = sb.mark()
            for t in range(NT):
                xt = xb[t % 2]
                dma("sp", xt[:], xsrc_d[t * 128:(t + 1) * 128, :], writes=[("xb", t % 2)])
                norm_tile(xt[:], ("xb", t % 2), g1[:], "g1", sqj[:], ssb[t % 2][:], ("ss", t % 2), hbb[t % 2][:], ("hb", t % 2),
                          6 + t % 2, hT[:, :, t * 128:(t + 1) * 128], ("hT", t))
            p.barrier()
            dma("pool", hT_d.ap().rearrange("(c p) s -> p c s", p=128), hT[:], writes=["hT_d"])

            wf = [sb.alloc("wf", [128, 8, 512], F32) for _ in range(2)]
            wb = [sb.alloc("wb", [128, 8, 512], BF16) for _ in range(2)]
            stgf = [sb.alloc("stgf", [128, S + 2], F32) for _ in range(2)]
            stgb = [sb.alloc("stgb", [128, S], BF16) for _ in range(2)]
            cw = sb.alloc("cw", [128, 8, 3], F32)
            ctmp = sb.alloc("ctmp", [128, S], F32)
            dma("sp", cw[:], convw_d[l], writes=["cw"])
            for i in range(2):
                p.add("dve", lambda e, i=i: e.memset(stgf[i][:], 0.0), writes=[("stgf", i, b) for b in range(NB)])
            groups = [("aq", 0, aqk_d, 0, None), ("ak", 512, aqk_d, 512, None),
                      ("cq", 2832, cqT_d, 0, 0.125), ("ck", 3344, ckT_d, 0, None),
                      ("dq", 4368, dqT_d, 0, 0.125), ("dk", 4880, dkT_d, 0, None)]
            cidx = 0
            bank = 0
            for gi, (gname, col0, dst_d, drow0, scale) in enumerate(groups):
                wfi, wbi = wf[gi % 2], wb[gi % 2]
                dma("sp", wfi[:], w_in_d[l, :, col0:col0 + 512].rearrange("(c p) n -> p c n", p=128), writes=[("wf", gi % 2)])
                p.add("act", lambda e, wfi=wfi, wbi=wbi: e.copy(out=wbi[:], in_=wfi[:]),
                      reads=[("wf", gi % 2)], writes=[("wb", gi % 2)])
                is_a = gname in ("aq", "ak")
                for cc in range(4):
                    si = cidx % 2
                    cidx += 1
                    for b in range(NB):
                        bk = bank % 6
                        bank += 1
                        for k in range(8):
                            p.add("pe", lambda e, k=k, cc=cc, b=b, bk=bk, wbi=wbi: e.matmul(
                                psb[bk][:], lhsT=wbi[:, k, cc * 128:(cc + 1) * 128], rhs=hT[:, k, b * 512:(b + 1) * 512],
                                start=(k == 0), stop=(k == 7)), reads=[("wb", gi % 2)], writes=[PK(bk)])
                        if is_a:
                            copy_op(evac_engine(), stgf[si][:, 1 + b * 512:1 + (b + 1) * 512], psb[bk][:], [PK(bk)], [("stgf", si, b)])
                        else:
                            copy_op(evac_engine(), stgb[si][:, b * 512:(b + 1) * 512], psb[bk][:], [PK(bk)], [("stgb", si, b)], scale=scale)
                    if is_a:
                        ch = (col0 // 128) + cc
                        sf = stgf[si]
                        p.add("dve", lambda e, sf=sf, ch=ch: e.tensor_scalar(out=ctmp[:], in0=sf[:, 1:S + 1], scalar1=cw[:, ch, 1:2],
                                                                             scalar2=None, op0=ALU.mult),
                              reads=[("stgf", si, b_) for b_ in range(NB)] + ["cw"], writes=["ctmp"])
                        p.add("dve", lambda e, sf=sf, ch=ch: e.scalar_tensor_tensor(out=ctmp[:], in0=sf[:, 0:S], scalar=cw[:, ch, 0:1],
                                                                                    in1=ctmp[:], op0=ALU.mult, op1=ALU.add),
                              reads=[("stgf", si, b_) for b_ in range(NB)] + ["cw"], writes=["ctmp"])
                        p.add("dve", lambda e, sf=sf, ch=ch: e.scalar_tensor_tensor(out=ctmp[:], in0=sf[:, 2:S + 2], scalar=cw[:, ch, 2:3],
                                                                                    in1=ctmp[:], op0=ALU.mult, op1=ALU.add),
                              reads=[("stgf", si, b_) for b_ in range(NB)] + ["cw"], writes=["ctmp"])
                        p.add("act", lambda e, si=si: e.activation(out=stgb[si][:], in_=ctmp[:], func=AF.Silu),
                              reads=["ctmp"], writes=[("stgb", si, b_) for b_ in range(NB)])
                    r0 = drow0 + cc * 128
                    dma("pool", dst_d[r0:r0 + 128, :], stgb[si][:], reads=[("stgb", si, b_) for b_ in range(NB)], writes=[(gname, cc)])

            p.barrier()
            sb.reset(mA)
            wf = [sb.alloc("wf", [128, 8, 512], F32) for _ in range(2)]
            wb = [sb.alloc("wb", [128, 8, 512], BF16) for _ in range(2)]
            cos_s = sb.alloc("cos", [128, NT, 64], F32)
            sin_s = sb.alloc("sin", [128, NT, 64], F32)
            dma("sp", cos_s[:], cos_d.ap().rearrange("(t p) f -> p t f", p=128), writes=["cos"])
            dma("sp", sin_s[:], sin_d.ap().rearrange("(t p) f -> p t f", p=128), writes=["sin"])
            gb = sb.alloc("gb", [128, 16], F32)
            dma("sp", gb[:], gbias_d[l], writes=["gb"])
            bqg = sb.alloc("bqg", [128, 64], F32)
            bkg = sb.alloc("bkg", [128, 64], F32)
            dma("sp", bqg[:], bqg_d[l], writes=["bqg"])
            dma("sp", bkg[:], bkg_d[l], writes=["bkg"])
            p.add("dve", lambda e: e.tensor_scalar(out=bqg[:], in0=bqg[:], scalar1=0.125, scalar2=None, op0=ALU.mult),
                  reads=["bqg"], writes=["bqg"])
            st129 = [sb.alloc("st129", [128, 4, 129], BF16) for _ in range(2)]
            st65 = [sb.alloc("st65", [128, 8, 65], BF16) for _ in range(2)]
            stb = [sb.alloc("stb", [128, 512], BF16) for _ in range(2)]
            stg16 = [sb.alloc("stg16", [128, 16], F32) for _ in range(2)]
            for i in range(2):
                p.add("dve", lambda e, i=i: e.memset(st129[i][:], 1.0), writes=[("st129", i)])
                p.add("dve", lambda e, i=i: e.memset(st65[i][:], 1.0), writes=[("st65", i)])
            qf = sb.alloc("qf", [128, 512], F32)
            qn = sb.alloc("qn", [128, 512], F32)
            t1 = sb.alloc("t1", [128, 512], F32)
            t2 = sb.alloc("t2", [128, 512], F32)
            ssh = sb.alloc("ssh", [128, 8], F32)
            qrb = [sb.alloc("qrb", [128, 512], BF16) for _ in range(2)]
            bqT_s = sb.alloc("bqT_s", [128, 4, S], BF16)
            bkT_s = sb.alloc("bkT_s", [128, S], BF16)

            def rope_norm(ps_ap, psk, nh, gtile, t, outb, outk):
                W = nh * 64
                qf_, qn_, t1_, t2_ = qf[:, 0:W], qn[:, 0:W], t1[:, 0:W], t2[:, 0:W]
                p.add("act", lambda e: e.copy(out=qf_, in_=ps_ap), reads=[psk], writes=["qf"])
                p.add("dve", lambda e: e.tensor_tensor(out=t1_, in0=qf_, in1=qf_, op=ALU.mult), reads=["qf"], writes=["t1"])
                p.add("dve", lambda e: e.tensor_reduce(out=ssh[:, 0:nh], in_=t1_.rearrange("p (h d) -> p h d", h=nh), axis=AX.X, op=ALU.add),
                      reads=["t1"], writes=["ssh"])
                rstd_ops(ssh[:, 0:nh], "ssh", 1.0 / 64)
                p.add("dve", lambda e: e.tensor_tensor(out=qn_.rearrange("p (h d) -> p h d", h=nh), in0=qf_.rearrange("p (h d) -> p h d", h=nh),
                                                       in1=bc_ap(ssh[:, 0:nh], [[1, nh], [0, 64]]), op=ALU.mult),
                      reads=["qf", "ssh"], writes=["qn"])
                p.add("dve", lambda e: e.tensor_tensor(out=qn_.rearrange("p (h d) -> p h d", h=nh), in0=qn_.rearrange("p (h d) -> p h d", h=nh),
                                                       in1=bc_ap(gtile[:], [[0, nh], [1, 64]]), op=ALU.mult),
                      reads=["qn", "bqg", "bkg"], writes=["qn"])
                cos_b = bc_ap(cos_s[:, t, :], [[0, nh], [1, 64]])
                p.add("dve", lambda e: e.tensor_tensor(out=t1_.rearrange("p (h d) -> p h d", h=nh), in0=qn_.rearrange("p (h d) -> p h d", h=nh),
                                                       in1=cos_b, op=ALU.mult), reads=["qn", "cos"], writes=["t1"])
                sin_lo = bc_ap(sin_s[:, t, 0:16], [[0, nh], [32, 2], [1, 16]])
                sin_hi = bc_ap(sin_s[:, t, 16:32], [[0, nh], [32, 2], [1, 16]])
                x4 = qn_.rearrange("p (h r f) -> p h r f", h=nh, r=2)
                o4 = t2_.rearrange("p (h r f) -> p h r f", h=nh, r=2)
                p.add("dve", lambda e: e.tensor_tensor(out=o4[:, :, :, 0:16], in0=x4[:, :, :, 16:32], in1=sin_lo, op=ALU.mult),
                      reads=["qn", "sin"], writes=["t2"])
                p.add("dve", lambda e: e.tensor_tensor(out=o4[:, :, :, 16:32], in0=x4[:, :, :, 0:16], in1=sin_hi, op=ALU.mult),
                      reads=["qn", "sin"], writes=["t2"])
                p.add("dve", lambda e: e.tensor_tensor(out=outb, in0=t1_, in1=t2_, op=ALU.add), reads=["t1", "t2"], writes=[outk])

            tm_groups = [("av", 1024, 512), ("ao", 1536, 512), ("ag", 2048, 16), ("bq", 2064, 512),
                         ("bkv", 2576, 256), ("cv", 3856, 512), ("dv", 5392, 512)]
            for gi, (gname, col0, ncol) in enumerate(tm_groups):
                wfi, wbi = wf[gi % 2], wb[gi % 2]
                dma("sp", wfi[:, :, 0:ncol], w_in_d[l, :, col0:col0 + ncol].rearrange("(c p) n -> p c n", p=128),
                    writes=[("wf", gi % 2)])
                p.add("act", lambda e, wfi=wfi, wbi=wbi, ncol=ncol: e.copy(out=wbi[:, :, 0:ncol], in_=wfi[:, :, 0:ncol]),
                      reads=[("wf", gi % 2)], writes=[("wb", gi % 2)])
                for t in range(NT):
                    bk = t % 4
                    si = t % 2
                    for k in range(8):
                        p.add("pe", lambda e, k=k, t=t, bk=bk, wbi=wbi, ncol=ncol: e.matmul(
                            psb[bk][:, 0:ncol], lhsT=hT[:, k, t * 128:(t + 1) * 128], rhs=wbi[:, k, 0:ncol],
                            start=(k == 0), stop=(k == 7)), reads=[("wb", gi % 2)], writes=[PK(bk)])
                    tok = slice(t * 128, (t + 1) * 128)
                    if gname == "av" or gname == "dv":
                        dst = av1_d if gname == "av" else dv1_d
                        copy_op(evac_engine(), st129[si][:, :, 0:128], psb[bk][:, 0:512].rearrange("p (h d) -> p h d", h=4),
                                [PK(bk)], [("st129", si)])
                        dma("pool", dst[tok], st129[si][:], reads=[("st129", si)], writes=[(gname, t)])
                    elif gname == "cv":
                        copy_op(evac_engine(), st65[si][:, :, 0:64], psb[bk][:, 0:512].rearrange("p (h d) -> p h d", h=8),
                                [PK(bk)], [("st65", si)])
                        dma("pool", cv1_d[tok], st65[si][:], reads=[("st65", si)], writes=[(gname, t)])
                    elif gname == "ao":
                        p.add("act", lambda e, bk=bk, si=si: e.activation(out=stb[si][:], in_=psb[bk][:], func=AF.Sigmoid),
                              reads=[PK(bk)], writes=[("stb", si)])
                        dma("pool", ao_d[tok, :], stb[si][:], reads=[("stb", si)], writes=[(gname, t)])
                    elif gname == "ag":
                        p.add("dve", lambda e, bk=bk, si=si: e.tensor_tensor(out=stg16[si][:], in0=psb[bk][:, 0:16], in1=gb[:], op=ALU.add),
                              reads=[PK(bk), "gb"], writes=[("stg16", si)])
                        dma("pool", ag_d[tok, :], stg16[si][:], reads=[("stg16", si)], writes=[(gname, t)])
                    elif gname == "bq":
                        rope_norm(psb[bk][:, 0:512], PK(bk), 8, bqg, t, qrb[si][:], ("qrb", si))
                        for c in range(4):
                            p.add("pe", lambda e, c=c, si=si: e.transpose(out=psb16[6 + si][:, c * 128:(c + 1) * 128],
                                                                          in_=qrb[si][:, c * 128:(c + 1) * 128], identity=ident[:]),
                                  reads=[("qrb", si)], writes=[PK(6 + si)])
                        copy_op("act", bqT_s[:, :, t * 128:(t + 1) * 128], psb16[6 + si][:, 0:512].rearrange("p (c t) -> p c t", c=4),
                                [PK(6 + si)], [("bqT_s", t)])
                    elif gname == "bkv":
                        rope_norm(psb[bk][:, 0:128], PK(bk), 2, bkg, t, qrb[si][:, 0:128], ("qrb", si))
                        p.add("pe", lambda e, si=si: e.transpose(out=psb16[6 + si][:, 0:128], in_=qrb[si][:, 0:128], identity=ident[:]),
                              reads=[("qrb", si)], writes=[PK(6 + si)])
                        copy_op("act", bkT_s[:, t * 128:(t + 1) * 128], psb16[6 + si][:, 0:128], [PK(6 + si)], [("bkT_s", t)])
                        copy_op("dve", st65[si][:, 0:2, 0:64], psb[bk][:, 128:256].rearrange("p (h d) -> p h d", h=2),
                                [PK(bk)], [("st65", si)])
                        dma("pool", bv1_d[tok], st65[si][:, 0:2, :], reads=[("st65", si)], writes=[("bv", t)])
                if gname == "bq":
                    p.barrier()
                    dma("pool", bqT_d.ap().rearrange("(c p) s -> p c s", p=128), bqT_s[:], writes=["bqT_d"])
                if gname == "bkv":
                    p.barrier()
                    dma("pool", bkT_d.ap(), bkT_s[:], writes=["bkT_d"])

        if "A" in phases:
            _phA()

        def _phB(l=l, xsrc_d=xsrc_d, lambda_init=lambda_init):
            phase_reset()
            qT = sb.alloc("qTp", [128, 8, S], BF16)
            kT2 = sb.alloc("kT2", [128, 2, S], BF16)
            v1 = sb.alloc("v1", [128, NT, 2, 65], BF16)
            sel = sb.alloc("sel", [65, 64], F32)
            load_qpad(qT, bqT_d)
            for g in range(2):
                dma("sp", kT2[0:64, g, :], bkT_d[g * 64:(g + 1) * 64, :], writes=[("kT2", g, 0)])
                dma("sp", kT2[64:128, g, :], bkT_d[g * 64:(g + 1) * 64, :], writes=[("kT2", g, 1)])
            dma("sp", v1[:], bv1_d.ap().rearrange("(t p) g e -> p t g e", p=128), writes=["v1"])
            dma("sp", sel[:], sel_d.ap(), writes=["sel"])
            pT = [sb.alloc("pT", [128, 1024], BF16) for _ in range(3)]
            osb = [sb.alloc("osb", [65, 512], F32) for _ in range(2)]
            rb = [sb.alloc("rb", [64, 512], F32) for _ in range(2)]
            ybs = [sb.alloc("ybs", [64, 512], BF16) for _ in range(2)]
            p.barrier()
            pipe = Pipe([0, 1, 2])
            it = 0
            oit = 0
            for h in range(8):
                g = h // 4
                for qb in range(NB):
                    ob = 4 + oit % 2
                    oit += 1
                    for kt2 in range(NT // 2):
                        sd = it % 2
                        pi = it % 3
                        it += 1

                        def s0(sd=sd, g=g, h=h, kt2=kt2, qb=qb):
                            for u in range(2):
                                kt = 2 * kt2 + u
                                p.add("pe", lambda e, u=u, kt=kt: e.matmul(psb[2 * sd + u][:], lhsT=kT2[:, g, kt * 128:(kt + 1) * 128],
                                                                           rhs=qT[:, h, qb * 512:(qb + 1) * 512], start=True, stop=True),
                                      writes=[PK(2 * sd + u)])

                        def s1(sd=sd, pi=pi):
                            p.add("act", lambda e: e.activation(out=pT[pi][:], in_=psall[:, sd * 1024:(sd + 1) * 1024], func=AF.Exp),
                                  reads=[PK(2 * sd), PK(2 * sd + 1)], writes=[("pT", pi)])

                        def s2(pi=pi, ob=ob, kt2=kt2, g=g, h=h, qb=qb):
                            for u in range(2):
                                kt = 2 * kt2 + u
                                p.add("pe", lambda e, u=u, kt=kt: e.matmul(psb[ob][0:65, :], lhsT=v1[:, kt, g, :], rhs=pT[pi][:, u * 512:(u + 1) * 512],
                                                                           start=(kt == 0), stop=(kt == NT - 1)),
                                      reads=[("pT", pi)], writes=[PK(ob)])
                            if kt2 == NT // 2 - 1:
                                oi = ob - 4
                                p.add("dve", lambda e: e.tensor_copy(out=osb[oi][:], in_=psb[ob][0:65, :]), reads=[PK(ob)], writes=[("osb", oi)])
                                p.add("pe", lambda e: e.matmul(psb[6][0:64, :], lhsT=sel[:], rhs=osb[oi][:], start=True, stop=True),
                                      reads=[("osb", oi), "sel"], writes=[PK(6)])
                                p.add("dve", lambda e: e.reciprocal(out=rb[oi][:], in_=psb[6][0:64, :]), reads=[PK(6)], writes=[("rb", oi)])
                                p.add("dve", lambda e: e.tensor_tensor(out=ybs[oi][:], in0=osb[oi][0:64, :], in1=rb[oi][:], op=ALU.mult),
                                      reads=[("osb", oi), ("rb", oi)], writes=[("ybs", oi)])
                                r0 = 512 + h * 64
                                dma("pool", yT_d[r0:r0 + 64, qb * 512:(qb + 1) * 512], ybs[oi][:], reads=[("ybs", oi)],
                                    writes=[("yb", h, qb)])
                        pipe.push(s0, s1, s2)
            pipe.flush()

        if "B" in phases:
            _phB()

        def _phC(l=l, xsrc_d=xsrc_d, lambda_init=lambda_init):
            phase_reset()
            qT = sb.alloc("qTp", [128, 8, S], BF16)
            kT = sb.alloc("kT", [128, 4, S], BF16)
            v1 = sb.alloc("v1", [128, NT, 8, 65], BF16)
            sel = sb.alloc("sel", [65, 64], F32)
            nbf = sb.alloc("nbf", [128, 5, 8, 128], F32)
            en_int = sb.alloc("en_int", [128, 5, 8, 128], BF16)
            en_edge = sb.alloc("en_edge", [128, 5, 8, 128], BF16)
            load_qpad(qT, cqT_d)
            dma("sp", kT[:], ckT_d.ap().rearrange("(c p) s -> p c s", p=128), writes=["kT"])
            dma("sp", v1[:], cv1_d.ap().rearrange("(t p) g e -> p t g e", p=128), writes=["v1"])
            dma("sp", sel[:], sel_d.ap(), writes=["sel"])
            dma("sp", nbf[:], natb_d[l, 2], writes=["nbf"])
            p.add("act", lambda e: e.activation(out=en_int[:], in_=nbf[:], func=AF.Exp), reads=["nbf"], writes=["en_int"])
            pTa = [sb.alloc("pTa", [128, 512], BF16) for _ in range(4)]
            pT = [sb.alloc("pT", [128, 512], BF16) for _ in range(4)]
            osb = [sb.alloc("osb", [65, 512], F32) for _ in range(2)]
            rb = [sb.alloc("rb", [64, 512], F32) for _ in range(2)]
            ybs = [sb.alloc("ybs", [64, 512], BF16) for _ in range(2)]
            p.barrier()
            pipe = Pipe([0, 1, 2, 3])
            it = 0
            oit = 0
            for i in range(NT):
                pat = 0 if i == 0 else 1 if i == 1 else 3 if i == NT - 2 else 4 if i == NT - 1 else 2
                kb0 = min(max(i - 2, 0), NT - 5)
                if pat != 2:
                    dma("sp", nbf[:], natb_d[l, pat], reads=["en_int", "en_edge"], writes=["nbf"])
                    p.add("act", lambda e: e.activation(out=en_edge[:], in_=nbf[:], func=AF.Exp), reads=["nbf"], writes=["en_edge"])
                nbt = en_int if pat == 2 else en_edge
                nbk = "en_int" if pat == 2 else "en_edge"
                for hg in range(2):
                    ob = 4 + oit % 2
                    oit += 1
                    for kk in range(5):
                        kt = kb0 + kk
                        sbk = it % 4
                        it += 1

                        def s0(sbk=sbk, hg=hg, kt=kt, i=i):
                            for hh in range(4):
                                h = hg * 4 + hh
                                p.add("pe", lambda e, hh=hh, h=h: e.matmul(
                                    psb[sbk][:, hh * 128:(hh + 1) * 128], lhsT=kT[:, h // 2, kt * 128:(kt + 1) * 128],
                                    rhs=qT[:, h, i * 128:(i + 1) * 128], start=True, stop=True), writes=[PK(sbk)])

                        def s1(sbk=sbk):
                            p.add("act", lambda e: e.activation(out=pTa[sbk][:], in_=psb[sbk][:], func=AF.Exp),
                                  reads=[PK(sbk)], writes=[("pTa", sbk)])

                        def s2(sbk=sbk, kk=kk, hg=hg, nbt=nbt, nbk=nbk):
                            p.add("dve", lambda e: e.tensor_tensor(
                                out=pT[sbk][:].rearrange("p (h q) -> p h q", h=4), in0=pTa[sbk][:].rearrange("p (h q) -> p h q", h=4),
                                in1=nbt[:, kk, hg * 4:(hg + 1) * 4, :], op=ALU.mult), reads=[("pTa", sbk), nbk], writes=[("pT", sbk)])

                        def s3(sbk=sbk, ob=ob, kk=kk, kt=kt, hg=hg, i=i):
                            for hh in range(4):
                                h = hg * 4 + hh
                                p.add("pe", lambda e, hh=hh, h=h: e.matmul(
                                    psb[ob][0:65, hh * 128:(hh + 1) * 128], lhsT=v1[:, kt, h, :], rhs=pT[sbk][:, hh * 128:(hh + 1) * 128],
                                    start=(kk == 0 and hh == 0), stop=(kk == 4), skip_group_check=True), reads=[("pT", sbk)], writes=[PK(ob)])
                            if kk == 4:
                                oi = ob - 4
                                p.add("act", lambda e: e.copy(out=osb[oi][:], in_=psb[ob][0:65, :]), reads=[PK(ob)], writes=[("osb", oi)])
                                p.add("pe", lambda e: e.matmul(psb[6][0:64, :], lhsT=sel[:], rhs=osb[oi][:], start=True, stop=True),
                                      reads=[("osb", oi), "sel"], writes=[PK(6)])
                                p.add("act", lambda e: e.activation(out=rb[oi][:], in_=psb[6][0:64, :], func=AF.Ln), reads=[PK(6)], writes=[("rb", oi)])
                                p.add("act", lambda e: e.activation(out=rb[oi][:], in_=rb[oi][:], func=AF.Exp, scale=-1.0),
                                      reads=[("rb", oi)], writes=[("rb", oi)])
                                p.add("pool", lambda e: e.tensor_tensor(out=ybs[oi][:], in0=osb[oi][0:64, :], in1=rb[oi][:], op=ALU.mult),
                                      reads=[("osb", oi), ("rb", oi)], writes=[("ybs", oi)])
                                for hh in range(4):
                                    r0 = 1024 + (hg * 4 + hh) * 64
                                    dma("pool", yT_d[r0:r0 + 64, i * 128:(i + 1) * 128], ybs[oi][:, hh * 128:(hh + 1) * 128],
                                        reads=[("ybs", oi)], writes=[("yc", hg * 4 + hh, i)])
                        pipe.push(s0, s1, s2, s3)
                if pat != 2:
                    pipe.flush()
            pipe.flush()

        if "C" in phases:
            _phC()

        def _phD(l=l, xsrc_d=xsrc_d, lambda_init=lambda_init):
            phase_reset()
            qT = sb.alloc("qTp", [128, 8, S], BF16)
            kT = sb.alloc("kT", [128, 4, S], BF16)
            v1 = sb.alloc("v1", [128, NT, 4, 129], BF16)
            tz = sb.alloc("tz", [128, 2 * S - 128], F32)
            dlr = sb.alloc("dlr", [128, 4, 64], F32)
            dsub = sb.alloc("dsub", [128, 128], F32)
            load_qpad(qT, dqT_d)
            dma("sp", kT[:], dkT_d.ap().rearrange("(c p) s -> p c s", p=128), writes=["kT"])
            dma("sp", v1[:], dv1_d.ap().rearrange("(t p) g e -> p t g e", p=128), writes=["v1"])
            dma("sp", tz[:], tz_d.ap(), writes=["tz"])
            dma("sp", dlr[:], dl_d[l], writes=["dlr"])
            dma("sp", dsub[:], dsub_d[l], writes=["dsub"])
            lt = sb.alloc("lt", [128, 2, 64], F32)
            ls = sb.alloc("ls", [128, 2], F32)
            nlam = sb.alloc("nlam", [128, 1], F32)
            p.add("dve", lambda e: e.tensor_tensor(out=lt[:, 0, :], in0=dlr[:, 0, :], in1=dlr[:, 1, :], op=ALU.mult), reads=["dlr"], writes=["lt"])
            p.add("dve", lambda e: e.tensor_tensor(out=lt[:, 1, :], in0=dlr[:, 2, :], in1=dlr[:, 3, :], op=ALU.mult), reads=["dlr"], writes=["lt"])
            p.add("dve", lambda e: e.tensor_reduce(out=ls[:], in_=lt[:], axis=AX.X, op=ALU.add), reads=["lt"], writes=["ls"])
            p.add("act", lambda e: e.activation(out=ls[:], in_=ls[:], func=AF.Exp), reads=["ls"], writes=["ls"])
            p.add("dve", lambda e: e.tensor_tensor(out=nlam[:], in0=ls[:, 1:2], in1=ls[:, 0:1], op=ALU.subtract), reads=["ls"], writes=["nlam"])
            p.add("dve", lambda e: e.tensor_scalar(out=nlam[:], in0=nlam[:], scalar1=-lambda_init, scalar2=None, op0=ALU.add),
                  reads=["nlam"], writes=["nlam"])
            p.add("dve", lambda e: e.tensor_scalar(out=dsub[:], in0=dsub[:], scalar1=1.0 - lambda_init, scalar2=None, op0=ALU.mult),
                  reads=["dsub"], writes=["dsub"])
            pT = [sb.alloc("pT", [128, 512], BF16) for _ in range(4)]
            pTa = [sb.alloc("pTa", [128, 512], BF16) for _ in range(4)]
            etab = sb.alloc("etab", [128, 2 * S - 128], BF16)
            r1 = sb.alloc("r1", [128, 1], F32)
            r2 = sb.alloc("r2", [128, 1], F32)
            of = sb.alloc("of", [128, 128], F32)
            osq = sb.alloc("osq", [128, 128], F32)
            oss = sb.alloc("oss", [128, 1], F32)
            yb = [sb.alloc("yb", [128, 128], BF16) for _ in range(2)]
            yds = [sb.alloc("yds", [128, 512], BF16) for _ in range(2)]
            p.barrier()
            regs = {}
            ri = 0
            for c in range(2):
                for qq in range(4):
                    regs[(c, qq)] = (4 + ri // 3, (ri % 3) * 160)
                    ri += 1
            pipe = Pipe([0, 1, 2, 3])
            it = 0
            ep = 0
            for h in range(4):
                slope = 2.0 ** (-8.0 * (h + 1) / 4)
                pipe.flush()
                p.add("act", lambda e, slope=slope: e.activation(out=etab[:], in_=tz[:], func=AF.Exp, scale=slope),
                      reads=["tz", ("pT", 0), ("pT", 1), ("pT", 2), ("pT", 3)], writes=["etab"])
                for qb in range(NB):
                    for kt in range(NT):
                        for c in range(2):
                            f0 = c * 256 + h * 64
                            j = f0 // 128
                            pr = slice(f0 % 128, f0 % 128 + 64)
                            sbk = it % 4
                            it += 1
                            off = qb * 512 - kt * 128 + (NT - 1) * 128

                            def s0(sbk=sbk, j=j, hd=f0 // 64, kt=kt, qb=qb):
                                p.add("pe", lambda e: e.matmul(psb[sbk][:], lhsT=kT[:, j, kt * 128:(kt + 1) * 128],
                                                               rhs=qT[:, hd, qb * 512:(qb + 1) * 512], start=True, stop=True),
                                      writes=[PK(sbk)])

                            def s1(sbk=sbk):
                                p.add("act", lambda e: e.activation(out=pTa[sbk][:], in_=psb[sbk][:], func=AF.Exp),
                                      reads=[PK(sbk)], writes=[("pTa", sbk)])

                            def s2m(sbk=sbk, off=off):
                                p.add("dve", lambda e: e.tensor_tensor(out=pT[sbk][:], in0=pTa[sbk][:], in1=etab[:, off:off + 512], op=ALU.mult),
                                      reads=[("pTa", sbk), "etab"], writes=[("pT", sbk)])

                            def s2(sbk=sbk, c=c, kt=kt, h=h, qb=qb):
                                nonlocal ep
                                for qq in range(4):
                                    bkk, co = regs[(c, qq)]
                                    p.add("pe", lambda e, qq=qq, bkk=bkk, co=co: e.matmul(
                                        psb[bkk][:, co:co + 129], lhsT=pT[sbk][:, qq * 128:(qq + 1) * 128], rhs=v1[:, kt, h, :],
                                        start=(kt == 0 and co == 0), stop=(kt == NT - 1), skip_group_check=True),
                                        reads=[("pT", sbk)], writes=[PK(bkk)])
                                if kt == NT - 1 and c == 1:
                                    ydi = ep % 2
                                    ep += 1
                                    for qq in range(4):
                                        b1, c1 = regs[(0, qq)]
                                        b2, c2 = regs[(1, qq)]
                                        ybi = qq % 2
                                        p.add("dve", lambda e, b1=b1, c1=c1: e.reciprocal(out=r1[:], in_=psb[b1][:, c1 + 128:c1 + 129]),
                                              reads=[PK(b1)], writes=["r1"])
                                        p.add("dve", lambda e, b2=b2, c2=c2: e.reciprocal(out=r2[:], in_=psb[b2][:, c2 + 128:c2 + 129]),
                                              reads=[PK(b2)], writes=["r2"])
                                        p.add("dve", lambda e: e.tensor_tensor(out=r2[:], in0=r2[:], in1=nlam[:], op=ALU.mult),
                                              reads=["r2", "nlam"], writes=["r2"])
                                        p.add("dve", lambda e, b1=b1, c1=c1: e.tensor_scalar(out=of[:], in0=psb[b1][:, c1:c1 + 128], scalar1=r1[:],
                                                                                             scalar2=None, op0=ALU.mult),
                                              reads=[PK(b1), "r1"], writes=["of"])
                                        p.add("dve", lambda e, b2=b2, c2=c2: e.scalar_tensor_tensor(out=of[:], in0=psb[b2][:, c2:c2 + 128], scalar=r2[:],
                                                                                                    in1=of[:], op0=ALU.mult, op1=ALU.add),
                                              reads=[PK(b2), "r2", "of"], writes=["of"])
                                        p.add("dve", lambda e: e.memset(oss[:], 0.0), writes=["oss"])
                                        p.add("act", lambda e: e.activation(out=osq[:], in_=of[:], func=AF.Square, accum_out=oss[:]),
                                              reads=["of", "oss"], writes=["osq", "oss"])
                                        rstd_ops(oss[:], "oss", 1.0 / 128)
                                        p.add("dve", lambda e, ybi=ybi: e.scalar_tensor_tensor(out=yb[ybi][:], in0=of[:], scalar=oss[:], in1=dsub[:],
                                                                                               op0=ALU.mult, op1=ALU.mult),
                                              reads=["of", "oss", "dsub"], writes=[("yb", ybi)])
                                        p.add("pe", lambda e, qq=qq, ybi=ybi: e.transpose(out=psb16[7][:, qq * 128:(qq + 1) * 128], in_=yb[ybi][:],
                                                                                          identity=ident[:]), reads=[("yb", ybi)], writes=[PK(7)])
                                    copy_op("act", yds[ydi][:], psb16[7][:, 0:512], [PK(7)], [("yds", ydi)])
                                    r0 = 1536 + h * 128
                                    dma("pool", yT_d[r0:r0 + 128, qb * 512:(qb + 1) * 512], yds[ydi][:], reads=[("yds", ydi)],
                                        writes=[("yd", h, qb)])
                            pipe.push(s0, s1, s2m, s2)
            pipe.flush()

        if "D" in phases:
            _phD()

        def _phE(l=l, xsrc_d=xsrc_d, lambda_init=lambda_init):
            phase_reset()
            G = sb.alloc("G", [128, NT, 16], F32)
            dma("sp", G[:], ag_d.ap().rearrange("(t p) j -> p t j", p=128), writes=["G"])
            mk = [sb.alloc("mk", [128, 128], F32) for _ in range(2)]
            onesf = sb.alloc("onesf", [128, 128], F32)
            dma("sp", mk[0][:], maskf_d.ap(), writes=["mk"])
            dma("sp", mk[1][:], maskb_d.ap(), writes=["mk"])
            dma("sp", onesf[:], ones_d.ap(), writes=["mk"])
            E1 = sb.alloc("E1", [128, NT, 2, 4], F32)
            BN = sb.alloc("BN", [128, NT, 16], F32)
            T1 = sb.alloc("T1", [128, NT, 2, 4], F32)
            A1 = sb.alloc("A1", [128, NT, 2, 4], F32)
            A2 = sb.alloc("A2", [128, NT, 2, 4], F32)
            WD = sb.alloc("WD", [128, NT, 2, 4], F32)
            FL = sb.alloc("FL", [128, NT, 2, 4], F32)
            ang = sb.alloc("ang", [128, 512], F32)
            dma("sp", ang[:], ang_d[l], writes=["ang"])
            Gv = G[:].rearrange("p t (y h) -> p t y h", y=4)
            fsel = bc_ap(Gv[:, :, 1, :], [[16, NT], [8, 2], [1, 4]])
            isel = bc_ap(Gv[:, :, 0, :], [[16, NT], [8, 2], [1, 4]])
            p.add("act", lambda e: e.activation(out=E1[:], in_=fsel, func=AF.Exp, scale=-1.0), reads=["G"], writes=["E1"])
            p.add("act", lambda e: e.activation(out=E1[:], in_=E1[:], func=AF.Ln, bias=1.0), reads=["E1"], writes=["E1"])
            for t in range(NT):
                p.add("pe", lambda e, t=t: e.matmul(psb[0][:, t * 16:t * 16 + 4], lhsT=mk[0][:], rhs=E1[:, t, 0, :], start=True, stop=True),
                      reads=["E1", "mk"], writes=[PK(0)])
                p.add("pe", lambda e, t=t: e.matmul(psb[0][:, t * 16 + 4:t * 16 + 8], lhsT=mk[1][:], rhs=E1[:, t, 1, :], start=True, stop=True),
                      reads=["E1", "mk"], writes=[PK(0)])
                p.add("pe", lambda e, t=t: e.matmul(psb[0][:, t * 16 + 8:t * 16 + 16], lhsT=onesf[:],
                                                    rhs=E1[:, t, :, :].rearrange("p a h -> p (a h)"), start=True, stop=True),
                      reads=["E1", "mk"], writes=[PK(0)])
            p.add("dve", lambda e: e.tensor_copy(out=BN[:].rearrange("p t j -> p (t j)"), in_=psb[0][:, 0:NT * 16]), reads=[PK(0)], writes=["BN"])
            bneg = BN[:, :, 0:8].rearrange("p t (a h) -> p t a h", a=2)
            tot = BN[:, :, 8:16].rearrange("p t (a h) -> p t a h", a=2)
            p.add("dve", lambda e: e.tensor_tensor(out=T1[:], in0=isel, in1=bneg, op=ALU.add), reads=["G", "BN"], writes=["T1"])
            p.add("act", lambda e: e.activation(out=A1[:], in_=T1[:], func=AF.Exp), reads=["T1"], writes=["A1"])
            p.add("dve", lambda e: e.tensor_tensor(out=T1[:], in0=T1[:], in1=tot, op=ALU.subtract), reads=["T1", "BN"], writes=["T1"])
            p.add("act", lambda e: e.activation(out=A2[:], in_=T1[:], func=AF.Exp), reads=["T1"], writes=["A2"])
            ksc = 128.0 ** -0.5
            p.add("dve", lambda e: e.tensor_scalar(out=A1[:], in0=A1[:], scalar1=ksc, scalar2=None, op0=ALU.mult), reads=["A1"], writes=["A1"])
            p.add("dve", lambda e: e.tensor_scalar(out=A2[:], in0=A2[:], scalar1=ksc, scalar2=None, op0=ALU.mult), reads=["A2"], writes=["A2"])
            p.add("act", lambda e: e.activation(out=WD[:], in_=tot, func=AF.Exp, scale=-1.0), reads=["BN"], writes=["WD"])
            p.add("act", lambda e: e.activation(out=FL[:], in_=bneg, func=AF.Exp), reads=["BN"], writes=["FL"])
            qTh = sb.alloc("qTh", [128, S], BF16)
            kTh = sb.alloc("kTh", [128, S], BF16)
            ktok = sb.alloc("ktok", [128, NT, 128], BF16)
            v1h = sb.alloc("v1h", [128, NT, 129], BF16)
            sgo = sb.alloc("sgo", [128, NT, 128], BF16)
            hacc = sb.alloc("hacc", [128, NT, 128], F32)
            xc = sb.alloc("xc", [128, NT, 128], F32)
            sq2 = sb.alloc("sq2", [128, NT, 128], F32)
            yab = sb.alloc("yab", [128, NT, 128], BF16)
            yas = sb.alloc("yas", [128, S], BF16)
            mean = sb.alloc("mean", [128, NT], F32)
            var = sb.alloc("var", [128, NT], F32)
            Cst = [sb.alloc("Cst", [128, 129], F32) for _ in range(2)]
            Cb = [sb.alloc("Cb", [128, 129], BF16) for _ in range(2)]
            wT = [sb.alloc("wT", [128, 128], BF16) for _ in range(4)]
            kS = [sb.alloc("kS", [128, 128], BF16) for _ in range(4)]
            den = [sb.alloc("den", [128, 1], F32) for _ in range(2)]
            for h in range(4):
                p.barrier()
                dma("sp", qTh[:], aqk_d[h * 128:(h + 1) * 128, :], writes=["qTh"])
                dma("sp", kTh[:], aqk_d[512 + h * 128:512 + (h + 1) * 128, :], writes=["kTh"])
                dma("sp", v1h[:], av1_d.ap()[:, h, :].rearrange("(t p) e -> p t e", p=128), writes=["v1h"])
                dma("sp", sgo[:], ao_d.ap()[:, h * 128:(h + 1) * 128].rearrange("(t p) d -> p t d", p=128), writes=["sgo"])
                p.add("dve", lambda e: e.memset(hacc[:], 0.0), writes=["hacc"])
                for dr in range(2):
                    p.add("dve", lambda e, dr=dr: e.memset(Cst[dr][:], 0.0), writes=[("Cst", dr)])
                    p.add("dve", lambda e, dr=dr: e.memset(Cb[dr][:], 0.0), writes=[("Cb", dr)])
                for t8 in range(NT // 8):
                    pb = 6 + t8 % 2
                    for c8 in range(8):
                        c = t8 * 8 + c8
                        p.add("pe", lambda e, c=c, c8=c8, pb=pb: e.transpose(out=psb16[pb][:, c8 * 128:(c8 + 1) * 128], in_=kTh[:, c * 128:(c + 1) * 128],
                                                                             identity=ident[:]), reads=["kTh"], writes=[PK(pb)])
                    copy_op("act", ktok[:, t8 * 8:(t8 + 1) * 8, :], psb16[pb][:, 0:1024].rearrange("p (c d) -> p c d", c=8), [PK(pb)], ["ktok"])
                wi = 0
                epipe = Pipe([0, 1, 2, 3])
                for step in range(NT):
                    for dr in range(2):
                        c = step if dr == 0 else NT - 1 - step
                        w = wi % 4
                        wi += 1
                        ps_s, ps_o, ps_c = dr * 3, dr * 3 + 1, dr * 3 + 2
                        cs = slice(c * 128, (c + 1) * 128)

                        def e0(cs=cs, ps_s=ps_s):
                            p.add("pe", lambda e: e.matmul(psb[ps_s][:, 0:128], lhsT=kTh[:, cs], rhs=qTh[:, cs], start=True, stop=True),
                                  reads=["qTh", "kTh"], writes=[PK(ps_s)])

                        def e1(c=c, dr=dr, h=h, w=w, ps_s=ps_s):
                            p.add("dve", lambda e: e.scalar_tensor_tensor(
                                out=wT[w][:], in0=psb[ps_s][:, 0:128], scalar=A1[:, c, dr, h:h + 1], in1=mk[dr][:], op0=ALU.mult, op1=ALU.mult),
                                reads=[PK(ps_s), "A1"], writes=[("wT", w)])
                            p.add("act", lambda e: e.activation(out=kS[w][:], in_=ktok[:, c, :], func=AF.Copy, scale=A2[:, c, dr, h:h + 1]),
                                  reads=["ktok", "A2"], writes=[("kS", w)])

                        def e2(c=c, cs=cs, dr=dr, w=w, ps_o=ps_o, ps_c=ps_c):
                            p.add("pe", lambda e: e.matmul(psb[ps_o][:, 0:129], lhsT=wT[w][:], rhs=v1h[:, c, :], start=True, stop=False),
                                  reads=[("wT", w), "v1h"], writes=[PK(ps_o)])
                            p.add("pe", lambda e: e.matmul(psb[ps_o][:, 0:129], lhsT=qTh[:, cs], rhs=Cb[dr][:], start=False, stop=True),
                                  reads=[("Cb", dr)], writes=[PK(ps_o)])
                            p.add("pe", lambda e: e.matmul(psb[ps_c][:, 0:129], lhsT=kS[w][:], rhs=v1h[:, c, :], start=True, stop=True),
                                  reads=[("kS", w)], writes=[PK(ps_c)])

                        def e3(c=c, dr=dr, h=h, ps_o=ps_o, ps_c=ps_c):
                            p.add("dve", lambda e: e.scalar_tensor_tensor(
                                out=Cst[dr][:], in0=Cst[dr][:], scalar=WD[:, c, dr, h:h + 1], in1=psb[ps_c][:, 0:129], op0=ALU.mult, op1=ALU.add),
                                reads=[PK(ps_c), "WD", ("Cst", dr)], writes=[("Cst", dr)])
                            p.add("act", lambda e: e.copy(out=Cb[dr][:], in_=Cst[dr][:]), reads=[("Cst", dr)], writes=[("Cb", dr)])
                            p.add("dve", lambda e: e.scalar_tensor_tensor(
                                out=den[dr][:], in0=psb[ps_o][:, 128:129], scalar=-1.0, in1=FL[:, c, dr, h:h + 1], op0=ALU.mult, op1=ALU.max),
                                reads=[PK(ps_o), "FL"], writes=[("den", dr)])
                            p.add("dve", lambda e: e.tensor_tensor(
                                out=den[dr][:], in0=den[dr][:], in1=psb[ps_o][:, 128:129], op=ALU.max),
                                reads=[("den", dr), PK(ps_o)], writes=[("den", dr)])
                            p.add("dve", lambda e: e.reciprocal(out=den[dr][:], in_=den[dr][:]), reads=[("den", dr)], writes=[("den", dr)])
                            p.add("dve", lambda e: e.scalar_tensor_tensor(
                                out=hacc[:, c, :], in0=psb[ps_o][:, 0:128], scalar=den[dr][:], in1=hacc[:, c, :], op0=ALU.mult, op1=ALU.add),
                                reads=[PK(ps_o), ("den", dr), ("hacc", c)], writes=[("hacc", c)])
                        epipe.push(e0, e1, e2, e3)
                epipe.flush()
                p.barrier()
                p.add("dve", lambda e: e.tensor_reduce(out=mean[:], in_=hacc[:], axis=AX.X, op=ALU.add), writes=["mean"])
                p.add("dve", lambda e: e.tensor_scalar(out=mean[:], in0=mean[:], scalar1=1.0 / 128, scalar2=None, op0=ALU.mult), reads=["mean"], writes=["mean"])
                p.add("dve", lambda e: e.tensor_tensor(out=xc[:], in0=hacc[:], in1=bc_ap(mean[:], [[1, NT], [0, 128]]), op=ALU.subtract),
                      reads=["mean"], writes=["xc"])
                p.add("act", lambda e: e.activation(out=sq2[:], in_=xc[:], func=AF.Square), reads=["xc"], writes=["sq2"])
                p.add("dve", lambda e: e.tensor_reduce(out=var[:], in_=sq2[:], axis=AX.X, op=ALU.add), reads=["sq2"], writes=["var"])
                rstd_ops(var[:], "var", 1.0 / 128)
                p.add("dve", lambda e: e.tensor_tensor(out=xc[:], in0=xc[:], in1=bc_ap(var[:], [[1, NT], [0, 128]]), op=ALU.mult),
                      reads=["xc", "var"], writes=["xc"])
                p.add("dve", lambda e, h=h: e.tensor_tensor(out=xc[:], in0=xc[:], in1=bc_ap(ang[:, h * 128:(h + 1) * 128], [[0, NT], [1, 128]]), op=ALU.mult),
                      reads=["xc", "ang"], writes=["xc"])
                p.add("dve", lambda e: e.tensor_tensor(out=yab[:], in0=xc[:], in1=sgo[:], op=ALU.mult), reads=["xc", "sgo"], writes=["yab"])
                for t8 in range(NT // 8):
                    pb = 6 + t8 % 2
                    for c8 in range(8):
                        c = t8 * 8 + c8
                        p.add("pe", lambda e, c=c, c8=c8, pb=pb: e.transpose(out=psb16[pb][:, c8 * 128:(c8 + 1) * 128], in_=yab[:, c, :],
                                                                             identity=ident[:]), reads=["yab"], writes=[PK(pb)])
                    copy_op("act", yas[:, t8 * 1024:(t8 + 1) * 1024], psb16[pb][:, 0:1024], [PK(pb)], ["yas"])
                dma("pool", yT_d[h * 128:(h + 1) * 128, :], yas[:], reads=["yas"], writes=[("ya", h)])

        if "E" in phases:
            _phE()

        def _phF(l=l, xsrc_d=xsrc_d, lambda_init=lambda_init):
            phase_reset()
            wg = sb.alloc("wg", [128, 8, 4096], BF16)
            wu = sb.alloc("wu", [128, 16, 1024], BF16)
            wo = sb.alloc("wo", [128, 8, 1024], BF16)
            mF = sb.mark()
            stg = [sb.alloc("stg", [128, 8, 512], F32) for _ in range(2)]
            load_cast(lambda c0, c1: wg[:, :, c0:c1],
                      lambda c0, c1: w_in_d[l, :, 5904 + c0:5904 + c1].rearrange("(c p) n -> p c n", p=128), 8, 4096, stg, "wg")
            for bi in range(4):
                load_cast(lambda c0, c1, bi=bi: wu[:, bi * 4:(bi + 1) * 4, c0:c1],
                          lambda c0, c1, bi=bi: w_up_d[bi][l, :, c0:c1].rearrange("(c p) n -> p c n", p=128), 4, 1024, stg, "wu%d" % bi)
            load_cast(lambda c0, c1: wo[:, :, c0:c1],
                      lambda c0, c1: w_out_d[l, :, c0:c1].rearrange("(c p) n -> p c n", p=128), 8, 1024, stg, "wo")
            p.barrier()
            sb.reset(mF)
            hTb = [sb.alloc("hTb", [128, 8, 512], BF16) for _ in range(2)]
            yTb = [sb.alloc("yTb", [128, 16, 512], BF16) for _ in range(2)]
            mT = [sb.alloc("mT", [128, 8, 512], BF16) for _ in range(2)]
            sg = [sb.alloc("sg", [128, 512], F32) for _ in range(2)]
            acc = sb.alloc("acc", [128, 512], F32)
            tmpm = sb.alloc("tmpm", [128, 512], F32)
            xtl = [sb.alloc("xtl", [128, D], F32) for _ in range(2)]
            bankc = 0
            xi = 0
            for b in range(NB):
                bi = b % 2
                bs = slice(b * 512, (b + 1) * 512)
                dma("sp", hTb[bi][:], hT_d.ap()[:, bs].rearrange("(c p) s -> p c s", p=128), writes=[("hTb", bi)])
                dma("sp", yTb[bi][:], yT_d.ap()[:, bs].rearrange("(c p) s -> p c s", p=128), writes=[("yTb", bi)])
                for dc in range(8):
                    for g in range(4):
                        pg = bankc % 6
                        pu = (bankc + 1) % 6
                        bankc += 2
                        for k in range(8):
                            p.add("pe", lambda e, k=k, g=g, dc=dc, pg=pg, bi=bi: e.matmul(
                                psb[pg][:], lhsT=wg[:, k, g * 1024 + dc * 128:g * 1024 + (dc + 1) * 128], rhs=hTb[bi][:, k, :],
                                start=(k == 0), stop=(k == 7)), reads=[("hTb", bi)], writes=[PK(pg)])
                        for k in range(4):
                            p.add("pe", lambda e, k=k, g=g, dc=dc, pu=pu, bi=bi: e.matmul(
                                psb[pu][:], lhsT=wu[:, g * 4 + k, dc * 128:(dc + 1) * 128], rhs=yTb[bi][:, g * 4 + k, :],
                                start=(k == 0), stop=(k == 3)), reads=[("yTb", bi)], writes=[PK(pu)])
                        sgi = g % 2
                        p.add("act", lambda e, pg=pg, sgi=sgi: e.activation(out=sg[sgi][:], in_=psb[pg][:], func=AF.Sigmoid),
                              reads=[PK(pg)], writes=[("sg", sgi)])
                        if g == 0:
                            p.add("dve", lambda e, pu=pu, sgi=sgi: e.tensor_tensor(out=acc[:], in0=sg[sgi][:], in1=psb[pu][:], op=ALU.mult),
                                  reads=[PK(pu), ("sg", sgi)], writes=["acc"])
                        else:
                            p.add("dve", lambda e, pu=pu, sgi=sgi: e.tensor_tensor(out=tmpm[:], in0=sg[sgi][:], in1=psb[pu][:], op=ALU.mult),
                                  reads=[PK(pu), ("sg", sgi)], writes=["tmpm"])
                            if g < 3:
                                p.add("dve", lambda e: e.tensor_tensor(out=acc[:], in0=acc[:], in1=tmpm[:], op=ALU.add),
                                      reads=["tmpm", "acc"], writes=["acc"])
                            else:
                                p.add("dve", lambda e, dc=dc, bi=bi: e.tensor_tensor(out=mT[bi][:, dc, :], in0=acc[:], in1=tmpm[:], op=ALU.add),
                                      reads=["tmpm", "acc"], writes=[("mT", bi, dc)])
                for tt in range(4):
                    xt = xtl[xi % 2]
                    xk = ("xtl", xi % 2)
                    xi += 1
                    tok = slice(b * 512 + tt * 128, b * 512 + (tt + 1) * 128)
                    dma("sp", xt[:], xsrc_d[tok, :], writes=[xk])
                    for half in range(2):
                        po = 6 + half
                        for k in range(8):
                            p.add("pe", lambda e, k=k, tt=tt, half=half, po=po, bi=bi: e.matmul(
                                psb[po][:], lhsT=mT[bi][:, k, tt * 128:(tt + 1) * 128], rhs=wo[:, k, half * 512:(half + 1) * 512],
                                start=(k == 0), stop=(k == 7)), reads=[("mT", bi, k)], writes=[PK(po)])
                        p.add("dve", lambda e, xt=xt, half=half, po=po: e.tensor_tensor(out=xt[:, half * 512:(half + 1) * 512],
                                                                                        in0=xt[:, half * 512:(half + 1) * 512], in1=psb[po][:], op=ALU.add),
                              reads=[PK(po), xk], writes=[xk])
                    dma("pool", xres_d[tok, :], xt[:], reads=[xk], writes=[("xres", b, tt)])

        if "F" in phases:
            _phF()

        def _phG(l=l, xsrc_d=xsrc_d, lambda_init=lambda_init):
            phase_reset()
            wgt = sb.alloc("wgt", [128, 8, D_FF], BF16)
            wup = sb.alloc("wup", [128, 8, D_FF], BF16)
            wdn = sb.alloc("wdn", [128, 22, D], BF16)
            g2 = sb.alloc("g2", [128, D], F32)
            dma("sp", g2[:], n2g_d[l], writes=["g2"])
            mG = sb.mark()
            stg = [sb.alloc("stg", [128, 8, 512], F32) for _ in range(2)]
            load_cast(lambda c0, c1: wgt[:, :, c0:c1], lambda c0, c1: w_fg_d[l, :, c0:c1].rearrange("(c p) n -> p c n", p=128), 8, D_FF, stg, "wgt")
            load_cast(lambda c0, c1: wup[:, :, c0:c1], lambda c0, c1: w_fu_d[l, :, c0:c1].rearrange("(c p) n -> p c n", p=128), 8, D_FF, stg, "wup")
            for r in range(0, 22, 8):
                n = min(8, 22 - r)
                load_cast(lambda c0, c1, r=r, n=n: wdn[:, r:r + n, c0:c1],
                          lambda c0, c1, r=r, n=n: w_fd_d[l, r * 128:(r + n) * 128, c0:c1].rearrange("(c p) n -> p c n", p=128), n, D, stg, "wdn%d" % r)
            p.barrier()
            sb.reset(mG)
            xtl = [sb.alloc("xtl", [128, D], F32) for _ in range(4)]
            sqj = sb.alloc("sqj", [128, D], F32)
            ssb = [sb.alloc("ss", [128, 1], F32) for _ in range(2)]
            hbb = [sb.alloc("hb", [128, D], BF16) for _ in range(2)]
            h2T = sb.alloc("h2T", [128, 8, 512], BF16)
            aT = sb.alloc("aT", [128, 22, 512], BF16)
            sg = [sb.alloc("sg", [128, 512], F32) for _ in range(2)]
            bankc = 0
            for b in range(NB):
                for tt in range(4):
                    tok = slice(b * 512 + tt * 128, b * 512 + (tt + 1) * 128)
                    dma("sp", xtl[tt][:], xres_d[tok, :], reads=[("xres", b, tt)], writes=[("xtl", tt)])
                    norm_tile(xtl[tt][:], ("xtl", tt), g2[:], "g2", sqj[:], ssb[tt % 2][:], ("ss", tt % 2), hbb[tt % 2][:], ("hb", tt % 2),
                              6 + tt % 2, h2T[:, :, tt * 128:(tt + 1) * 128], ("h2T", tt))
                for fc in range(22):
                    pg = bankc % 6
                    pu = (bankc + 1) % 6
                    bankc += 2
                    for k in range(8):
                        p.add("pe", lambda e, k=k, fc=fc, pg=pg: e.matmul(psb[pg][:], lhsT=wgt[:, k, fc * 128:(fc + 1) * 128], rhs=h2T[:, k, :],
                                                                          start=(k == 0), stop=(k == 7)),
                              reads=[("h2T", 0), ("h2T", 1), ("h2T", 2), ("h2T", 3)], writes=[PK(pg)])
                    for k in range(8):
                        p.add("pe", lambda e, k=k, fc=fc, pu=pu: e.matmul(psb[pu][:], lhsT=wup[:, k, fc * 128:(fc + 1) * 128], rhs=h2T[:, k, :],
                                                                          start=(k == 0), stop=(k == 7)),
                              reads=[("h2T", 0), ("h2T", 1), ("h2T", 2), ("h2T", 3)], writes=[PK(pu)])
                    sgi = fc % 2
                    p.add("act", lambda e, pg=pg, sgi=sgi: e.activation(out=sg[sgi][:], in_=psb[pg][:], func=AF.Silu), reads=[PK(pg)], writes=[("sg", sgi)])
                    p.add("dve", lambda e, pu=pu, sgi=sgi, fc=fc: e.tensor_tensor(out=aT[:, fc, :], in0=sg[sgi][:], in1=psb[pu][:], op=ALU.mult),
                          reads=[PK(pu), ("sg", sgi)], writes=[("aT", fc)])
                for tt in range(4):
                    tok = slice(b * 512 + tt * 128, b * 512 + (tt + 1) * 128)
                    xt = xtl[tt]
                    for half in range(2):
                        po = 6 + half
                        for fc in range(22):
                            p.add("pe", lambda e, fc=fc, tt=tt, half=half, po=po: e.matmul(
                                psb[po][:], lhsT=aT[:, fc, tt * 128:(tt + 1) * 128], rhs=wdn[:, fc, half * 512:(half + 1) * 512],
                                start=(fc == 0), stop=(fc == 21)), reads=[("aT", fc)], writes=[PK(po)])
                        p.add("dve", lambda e, xt=xt, half=half, po=po: e.tensor_tensor(out=xt[:, half * 512:(half + 1) * 512],
                                                                                        in0=xt[:, half * 512:(half + 1) * 512], in1=psb[po][:], op=ALU.add),
                              reads=[PK(po), ("xtl", tt)], writes=[("xtl", tt)])
                    dma("pool", xres_d[tok, :], xt[:], reads=[("xtl", tt)], writes=[("xres", b, tt)])

        if "G" in phases:
            _phG()

    if "Z" in phases:
        phase_reset()
        gf = sb.alloc("gf", [128, D], F32)
        dma("sp", gf[:], fg_d.ap(), writes=["gf"])
        xb = [sb.alloc("xb", [128, D], F32) for _ in range(2)]
        ob_ = [sb.alloc("ob", [128, D], F32) for _ in range(2)]
        sqj = sb.alloc("sqj", [128, D], F32)
        ssb = [sb.alloc("ss", [128, 1], F32) for _ in range(2)]
        for t in range(NT):
            i = t % 2
            tok = slice(t * 128, (t + 1) * 128)
            dma("sp", xb[i][:], xres_d[tok, :], writes=[("xb", i)])
            p.add("dve", lambda e, i=i: e.memset(ssb[i][:], 0.0), writes=[("ss", i)])
            p.add("act", lambda e, i=i: e.activation(out=sqj[:], in_=xb[i][:], func=AF.Square, accum_out=ssb[i][:]),
                  reads=[("xb", i)], writes=[("ss", i), "sqj"])
            rstd_ops(ssb[i][:], ("ss", i), 1.0 / D)
            p.add("dve", lambda e, i=i: e.scalar_tensor_tensor(out=ob_[i][:], in0=xb[i][:], scalar=ssb[i][:], in1=gf[:], op0=ALU.mult, op1=ALU.mult),
                  reads=[("xb", i), ("ss", i), "gf"], writes=[("ob", i)])
            dma("pool", out_d[tok, :], ob_[i][:], reads=[("ob", i)], writes=[("out", t)])
    p.barrier()
    p.wait_all("pool", [])
    p.emit()
    return nc


def natten_tables(rpb, S):
    rows = S // GRID_W
    NT = S // 128
    wr, wc = 8, 16
    out = np.full((5, 128, 5, 8, 128), NEG, np.float32)
    reps = [0, 1, 2, NT - 2, NT - 1]
    for pi, i in enumerate(reps):
        kb0 = min(max(i - 2, 0), NT - 5)
        q = np.arange(i * 128, (i + 1) * 128)
        r = q // GRID_W
        c = q % GRID_W
        rs = np.clip(r - wr // 2, 0, rows - wr)
        cs = np.clip(c - wc // 2, 0, GRID_W - wc)
        keys = np.arange(kb0 * 128, (kb0 + 5) * 128)
        kr = keys // GRID_W
        kc = keys % GRID_W
        inwin = ((kr[None, :] >= rs[:, None]) & (kr[None, :] < rs[:, None] + wr) &
                 (kc[None, :] >= cs[:, None]) & (kc[None, :] < cs[:, None] + wc))
        offr = np.clip(kr[None, :] - r[:, None] + (wr - 1), 0, 2 * wr - 2)
        offc = np.clip(kc[None, :] - c[:, None] + (wc - 1), 0, 2 * wc - 2)
        g = rpb[:, offr, offc]
        g = np.where(inwin[None], g, np.float32(NEG)).astype(np.float32)
        g = g.reshape(8, 128, 5, 128).transpose(3, 2, 0, 1)
        out[pi] = g
    return out


def host_consts(S):
    t = np.arange(S)
    row = (t // GRID_W).astype(np.float32)
    col = (t % GRID_W).astype(np.float32)
    nf = 16
    inv = (10000.0 ** (-np.arange(nf, dtype=np.float32) / nf)).astype(np.float32)
    ar = row[:, None] * inv
    ac = col[:, None] * inv
    cos64 = np.concatenate([np.cos(ar), np.cos(ar), np.cos(ac), np.cos(ac)], axis=1).astype(np.float32)
    sin64 = np.concatenate([-np.sin(ar), np.sin(ar), -np.sin(ac), np.sin(ac)], axis=1).astype(np.float32)
    W = 2 * S - 128
    C0 = (S // 128 - 1) * 128
    pp = np.arange(128)[:, None]
    cc = np.arange(W)[None, :]
    tz = (-np.abs(cc - pp - C0)).astype(np.float32)
    s_ = np.arange(128)[:, None]
    t_ = np.arange(128)[None, :]
    maskf = (s_ <= t_).astype(np.float32)
    maskb = (s_ >= t_).astype(np.float32)
    ones = np.ones((128, 128), np.float32)
    sel = np.zeros((65, 64), np.float32)
    sel[64, :] = 1.0
    ident = np.eye(128, dtype=np.float32).astype(ml_dtypes.bfloat16)
    return dict(cos64=cos64, sin64=sin64, tz=tz, maskf=maskf, maskb=maskb, ones=ones, sel=sel, ident=ident)


def rep128(a):
    a = np.asarray(a, np.float32)
    return np.ascontiguousarray(np.broadcast_to(a[:, None, :], (a.shape[0], 128, a.shape[1])))


def prep_shared(inp, S):
    L = inp["w_in"].shape[0]
    f = lambda k: np.ascontiguousarray(np.asarray(inp[k], np.float32))
    sh = dict(
        w_in=f("w_in"), w_up_a=f("w_up_a"), w_up_b=f("w_up_b"), w_up_c=f("w_up_c"), w_up_d=f("w_up_d"),
        w_out=f("w_out"), w_ffn_gate=f("w_ffn_gate"), w_ffn_up=f("w_ffn_up"), w_ffn_down=f("w_ffn_down"),
        norm1_g_r=rep128(inp["norm1_g"]), norm2_g_r=rep128(inp["norm2_g"]),
        final_g_r=np.ascontiguousarray(np.broadcast_to(np.asarray(inp["final_g"], np.float32)[None, :], (128, D))),
        conv_w_r=np.ascontiguousarray(np.asarray(inp["a_conv_w"], np.float32).reshape(L, 3, 8, 128).transpose(0, 3, 2, 1)),
        gbias_r=rep128(inp["a_gate_bias"]), anorm_g_r=rep128(inp["a_norm_g"]),
        bqg_r=rep128(inp["b_qnorm_g"]), bkg_r=rep128(inp["b_knorm_g"]),
        dl_r=np.ascontiguousarray(np.broadcast_to(
            np.stack([np.asarray(inp[k], np.float32) for k in ("d_lambda_q1", "d_lambda_k1", "d_lambda_q2", "d_lambda_k2")], axis=1)[:, None],
            (L, 128, 4, 64))),
        dsub_r=rep128(inp["d_subln_g"]),
        natb=np.stack([natten_tables(np.asarray(inp["c_rpb"], np.float32)[l], S) for l in range(L)]),
    )
    sh.update(host_consts(S))
    return sh


_NC_CACHE = {}


def kernel(**inputs):
    x = np.asarray(inputs["x"], np.float32)
    B, S, _ = x.shape
    key = (S,)
    if key not in _NC_CACHE:
        _NC_CACHE[key] = build_program(S=S)
    nc = _NC_CACHE[key]
    sh = prep_shared(inputs, S)
    in_maps = []
    for b in range(B):
        m = dict(sh)
        m["x"] = np.ascontiguousarray(x[b])
        in_maps.append(m)
    res = run_bass_kernel_spmd(nc, in_maps, core_ids=list(range(B)))
    return np.stack([np.asarray(r["out"], np.float32) for r in res.results], axis=0)
```

```python
import math
import numpy as np
import ml_dtypes
import concourse.bass as bass
import concourse.mybir as mybir
from concourse.bass_utils import run_bass_kernel_spmd

F32 = mybir.dt.float32
BF16 = mybir.dt.bfloat16
AF = mybir.ActivationFunctionType
ALU = mybir.AluOpType
AX = mybir.AxisListType

D = 1024
SEQ = 4096
DEPTH = 2
GRID_W = 64
EPS = 1e-6
D_IN = 10000
D_FF = 2816
NEG = -30000.0

ENGS = ("pe", "act", "dve", "pool", "sp")
DMA_RING = 8
NO_SELF_SYNC = ("pe",)


class Op:
    __slots__ = ("eng", "fn", "idx", "dma", "waits", "val", "need_inc", "clock", "ring", "semval")

    def __init__(self, eng, fn, idx, dma):
        self.eng = eng
        self.fn = fn
        self.idx = idx
        self.dma = dma
        self.waits = []
        self.need_inc = False
        self.ring = None
        self.semval = None


class Prog:
    def __init__(self, nc, same_engine_sync=True):
        self.nc = nc
        self.ops = {e: [] for e in ENGS}
        self.writer = {}
        self.readers = {}
        self.know = {e: {} for e in ENGS}
        self.dma_count = {e: 0 for e in ENGS}
        self.dma_ops = {e: [] for e in ENGS}
        self.same_engine_sync = same_engine_sync
        self.barrier_deps = {e: [] for e in ENGS}

    def add(self, eng, fn, reads=(), writes=(), dma=False):
        lst = self.ops[eng]
        op = Op(eng, fn, len(lst), dma)
        deps = []
        for k in reads:
            w = self.writer.get(k)
            if w is not None:
                deps.append(w)
        for k in writes:
            w = self.writer.get(k)
            if w is not None:
                deps.append(w)
            deps.extend(self.readers.get(k, ()))
        if self.barrier_deps[eng]:
            deps.extend(self.barrier_deps[eng])
            self.barrier_deps[eng] = []
        if dma:
            n = self.dma_count[eng]
            op.ring = n % DMA_RING
            if n >= DMA_RING:
                deps.append(self.dma_ops[eng][n - DMA_RING])
            self.dma_count[eng] = n + 1
            self.dma_ops[eng].append(op)
        know = self.know[eng]
        for d in deps:
            if d is op:
                continue
            if d.dma:
                key = ("dma", d.eng, d.ring)
                val = d.val
            else:
                if d.eng == eng and (eng in NO_SELF_SYNC or not self.same_engine_sync):
                    continue
                key = d.eng
                val = d.idx
            if know.get(key, -1) >= val:
                continue
            op.waits.append(d)
            d.need_inc = True
            for k2, v2 in d.clock.items():
                if know.get(k2, -1) < v2:
                    know[k2] = v2
        if dma:
            op.val = (self.dma_count[eng] - 1) // DMA_RING
            ck = ("dma", eng, op.ring)
        else:
            op.val = op.idx
            ck = eng
        op.clock = dict(know)
        op.clock[ck] = op.val
        lst.append(op)
        for k in writes:
            self.writer[k] = op
            self.readers[k] = []
        for k in reads:
            if k not in writes:
                self.readers.setdefault(k, []).append(op)
        return op

    def wait_all(self, eng, keys):
        return self.add(eng, lambda e: None, reads=list(keys))

    def barrier(self):
        lasts = []
        for e in ENGS:
            if self.ops[e]:
                lasts.append(self.ops[e][-1])
            lasts.extend(self.dma_ops[e][-DMA_RING:])
        for e in ENGS:
            self.barrier_deps[e] = list(lasts)

    def emit(self):
        nc = self.nc
        from contextlib import ExitStack
        with ExitStack() as es:
            csem = {e: es.enter_context(nc.semaphore("c_" + e)) for e in ENGS}
            dsem = {(e, r): es.enter_context(nc.semaphore("d_%s_%d" % (e, r)))
                    for e in ENGS for r in range(DMA_RING) if self.dma_count[e] > 0}
            for e in ENGS:
                cnt = 0
                for op in self.ops[e]:
                    if not op.dma and op.need_inc:
                        cnt += 1
                        op.semval = cnt
            block = es.enter_context(nc.Block())

            def run(e, eng):
                for op in self.ops[e]:
                    for d in op.waits:
                        if d.dma:
                            eng.wait_ge(dsem[(d.eng, d.ring)], 16 * (d.val + 1))
                        else:
                            eng.wait_ge(csem[d.eng], d.semval)
                    ins = op.fn(eng)
                    if ins is None:
                        continue
                    if op.dma:
                        ins.then_inc(dsem[(e, op.ring)], 16)
                    elif op.need_inc:
                        ins.then_inc(csem[e], 1)

            @block.sync
            def _(eng):
                run("sp", eng)

            @block.scalar
            def _(eng):
                run("act", eng)

            @block.vector
            def _(eng):
                run("dve", eng)

            @block.gpsimd
            def _(eng):
                run("pool", eng)

            @block.tensor
            def _(eng):
                run("pe", eng)


class Pipe:
    def __init__(self, offs):
        self.offs = offs
        self.items = []

    def push(self, *fns):
        self.items.append(fns)

    def flush(self):
        n = len(self.items)
        m = max(self.offs)
        for t in range(n + m):
            for j, o in enumerate(self.offs):
                i = t - o
                if 0 <= i < n and self.items[i][j] is not None:
                    self.items[i][j]()
        self.items = []


SB_BASE = 16640
SB_LIMIT = 229376


class Arena:
    def __init__(self, nc):
        self.nc = nc
        self.off = SB_BASE
        self.n = 0

    def alloc(self, name, shape, dt):
        esz = 4 if dt == F32 else 2
        nbytes = int(np.prod(shape[1:])) * esz
        nbytes = (nbytes + 63) // 64 * 64
        assert self.off + nbytes <= SB_LIMIT, "SBUF overflow at %s: %d" % (name, self.off + nbytes)
        self.n += 1
        t = self.nc.alloc_sbuf_tensor_at("%s_%d" % (name, self.n), list(shape), dt, offset=self.off)
        self.off += nbytes
        return t

    def mark(self):
        return self.off

    def reset(self, to=SB_BASE):
        self.off = to


def bc_ap(ap, dims):
    return bass.AP(tensor=ap.tensor, offset=ap.offset, ap=[list(ap.ap[0])] + [list(d) for d in dims])


def build_program(S=SEQ, depth=DEPTH, debug=False, phases="ABCDEFGZ"):
    NT = S // 128
    NB = S // 512
    rows = S // GRID_W
    nc = bass.Bass("TRN2", target_bir_lowering=False)
    p = Prog(nc)
    sb = Arena(nc)
    L = depth

    def din(name, shape, dt=F32):
        return nc.dram_tensor(name, list(shape), dt, kind="ExternalInput")

    def dscr(name, shape, dt=BF16):
        return nc.dram_tensor(name, list(shape), dt, kind=("ExternalOutput" if debug else "Internal"))

    x_d = din("x", [S, D])
    w_in_d = din("w_in", [L, D, D_IN])
    w_up_d = [din("w_up_%s" % c, [L, 512, D]) for c in "abcd"]
    w_out_d = din("w_out", [L, D, D])
    w_fg_d = din("w_ffn_gate", [L, D, D_FF])
    w_fu_d = din("w_ffn_up", [L, D, D_FF])
    w_fd_d = din("w_ffn_down", [L, D_FF, D])
    n1g_d = din("norm1_g_r", [L, 128, D])
    n2g_d = din("norm2_g_r", [L, 128, D])
    fg_d = din("final_g_r", [128, D])
    convw_d = din("conv_w_r", [L, 128, 8, 3])
    gbias_d = din("gbias_r", [L, 128, 16])
    ang_d = din("anorm_g_r", [L, 128, 512])
    bqg_d = din("bqg_r", [L, 128, 64])
    bkg_d = din("bkg_r", [L, 128, 64])
    dl_d = din("dl_r", [L, 128, 4, 64])
    dsub_d = din("dsub_r", [L, 128, 128])
    natb_d = din("natb", [L, 5, 128, 5, 8, 128])
    ident_d = din("ident", [128, 128], BF16)
    cos_d = din("cos64", [S, 64])
    sin_d = din("sin64", [S, 64])
    tz_d = din("tz", [128, 2 * S - 128])
    maskf_d = din("maskf", [128, 128])
    maskb_d = din("maskb", [128, 128])
    ones_d = din("ones", [128, 128])
    sel_d = din("sel", [65, 64])
    out_d = nc.dram_tensor("out", [S, D], F32, kind="ExternalOutput")

    xres_d = dscr("xres", [S, D], F32)
    hT_d = dscr("hT_s", [D, S])
    aqk_d = dscr("aqk_s", [1024, S])
    av1_d = dscr("av1_s", [S, 4, 129])
    ao_d = dscr("ao_s", [S, 512])
    ag_d = dscr("ag_s", [S, 16], F32)
    bqT_d = dscr("bqT_s", [512, S])
    bkT_d = dscr("bkT_s", [128, S])
    bv1_d = dscr("bv1_s", [S, 2, 65])
    cqT_d = dscr("cqT_s", [512, S])
    ckT_d = dscr("ckT_s", [512, S])
    cv1_d = dscr("cv1_s", [S, 8, 65])
    dqT_d = dscr("dqT_s", [512, S])
    dkT_d = dscr("dkT_s", [512, S])
    dv1_d = dscr("dv1_s", [S, 4, 129])
    yT_d = dscr("yT_s", [2048, S])

    psall = nc.alloc_psum_tensor("psall", [128, 4096], F32)
    ps16all = psall.bitcast(BF16)
    psb = [psall[:, i * 512:(i + 1) * 512] for i in range(8)]
    psb16 = [ps16all[:, i * 1024:(i + 1) * 1024] for i in range(8)]

    def PK(i):
        return ("ps", i)

    def dma(eng, out, in_, reads=(), writes=(), **kw):
        return p.add(eng, lambda e: e.dma_start(out=out, in_=in_, **kw), reads=reads, writes=writes, dma=True)

    def new_phase():
        p.barrier()
        sb.reset()

    cnt = [0]

    def uid():
        cnt[0] += 1
        return cnt[0]

    ident = sb.alloc("ident", [128, 128], BF16)
    dma("sp", ident[:], ident_d.ap(), writes=["ident"])
    base_mark = sb.mark()

    def phase_reset():
        p.barrier()
        sb.reset(base_mark)

    evac_rr = [0]

    def evac_engine():
        evac_rr[0] += 1
        return "act" if evac_rr[0] % 2 else "dve"

    def copy_op(eng, out, in_, reads, writes, scale=None):
        if eng == "act":
            if scale is None:
                return p.add("act", lambda e: e.copy(out=out, in_=in_), reads=reads, writes=writes)
            return p.add("act", lambda e: e.mul(out=out, in_=in_, mul=scale), reads=reads, writes=writes)
        else:
            if scale is None:
                return p.add(eng, lambda e: e.tensor_copy(out=out, in_=in_), reads=reads, writes=writes)
            return p.add(eng, lambda e: e.tensor_scalar(out=out, in0=in_, scalar1=scale, scalar2=None, op0=ALU.mult),
                         reads=reads, writes=writes)

    def rstd_ops(v, key, n_scale, eps=EPS):
        p.add("dve", lambda e: e.tensor_scalar(out=v, in0=v, scalar1=n_scale, scalar2=eps, op0=ALU.mult, op1=ALU.add),
              reads=[key], writes=[key])
        p.add("act", lambda e: e.activation(out=v, in_=v, func=AF.Ln), reads=[key], writes=[key])
        p.add("act", lambda e: e.activation(out=v, in_=v, func=AF.Exp, scale=-0.5), reads=[key], writes=[key])

    stage_rr = [0]

    def load_cast(dst_ap_fn, src_ap_fn, nchunk, ncols, stage_f, tag, cols_per=512):
        for c0 in range(0, ncols, cols_per):
            c1 = min(ncols, c0 + cols_per)
            i = stage_rr[0]
            stage_rr[0] += 1
            st = stage_f[i % len(stage_f)]
            sk = ("stage", i % len(stage_f))
            dma("sp", st[:, 0:nchunk, 0:c1 - c0], src_ap_fn(c0, c1), writes=[sk])
            if i % 2 == 0:
                p.add("act", lambda e, st=st, c0=c0, c1=c1: e.copy(out=dst_ap_fn(c0, c1), in_=st[:, 0:nchunk, 0:c1 - c0]),
                      reads=[sk], writes=[("w", tag, c0)])
            else:
                p.add("dve", lambda e, st=st, c0=c0, c1=c1: e.tensor_copy(out=dst_ap_fn(c0, c1), in_=st[:, 0:nchunk, 0:c1 - c0]),
                      reads=[sk], writes=[("w", tag, c0)])

    def load_qpad(qTp, src_d):
        p.add("dve", lambda e: e.memset(qTp[:], 0.0), writes=["qT"])
        for h in range(8):
            r = (h % 2) * 64
            dma("sp", qTp[r:r + 64, h, :], src_d[h * 64:(h + 1) * 64, :], reads=["qT"], writes=[("qTh", h)])

    def norm_tile(xt, xk, gt, gk, sqj, ss, ssk, hb, hbk, ptr_i, dst_ap, dst_key):
        p.add("dve", lambda e: e.memset(ss, 0.0), writes=[ssk])
        p.add("act", lambda e: e.activation(out=sqj, in_=xt, func=AF.Square, accum_out=ss), reads=[xk], writes=[ssk, "sqj"])
        rstd_ops(ss, ssk, 1.0 / D)
        p.add("dve", lambda e: e.scalar_tensor_tensor(out=hb, in0=xt, scalar=ss, in1=gt, op0=ALU.mult, op1=ALU.mult),
              reads=[xk, ssk, gk], writes=[hbk])
        for c in range(8):
            p.add("pe", lambda e, c=c: e.transpose(out=psb16[ptr_i][:, c * 128:(c + 1) * 128], in_=hb[:, c * 128:(c + 1) * 128],
                                                   identity=ident[:]), reads=[hbk], writes=[PK(ptr_i)])
        copy_op("act", dst_ap, psb16[ptr_i][:, 0:1024].rearrange("p (c t) -> p c t", c=8), [PK(ptr_i)], [dst_key])

    for l in range(L):
        xsrc_d = x_d if l == 0 else xres_d
        lambda_init = 0.8 - 0.6 * math.exp(-0.3 * l)

        def _phA(l=l, xsrc_d=xsrc_d, lambda_init=lambda_init):
            phase_reset()
            hT = sb.alloc("hT", [128, 8, S], BF16)
            mA = sb.mark()
            g1 = sb.alloc("g1", [128, D], F32)
            dma("sp", g1[:], n1g_d[l], writes=["g1"])
            xb = [sb.alloc("xb", [128, D], F32) for _ in range(2)]
            sqj = sb.alloc("sqj", [128, D], F32)
            ssb = [sb.alloc("ss", [128, 1], F32) for _ in range(2)]
            hbb = [sb.alloc("hb", [128, D], BF16) for _ in range(2)]
            for t in range(NT):
                xt = xb[t % 2]
                dma("sp", xt[:], xsrc_d[t * 128:(t + 1) * 128, :], writes=[("xb", t % 2)])
                norm_tile(xt[:], ("xb", t % 2), g1[:], "g1", sqj[:], ssb[t % 2][:], ("ss", t % 2), hbb[t % 2][:], ("hb", t % 2),
                          6 + t % 2, hT[:, :, t * 128:(t + 1) * 128], ("hT", t))
            p.barrier()
            dma("pool", hT_d.ap().rearrange("(c p) s -> p c s", p=128), hT[:], writes=["hT_d"])

            wf = [sb.alloc("wf", [128, 8, 512], F32) for _ in range(2)]
            wb = [sb.alloc("wb", [128, 8, 512], BF16) for _ in range(2)]
            stgf = [sb.alloc("stgf", [128, S + 2], F32) for _ in range(2)]
            stgb = [sb.alloc("stgb", [128, S], BF16) for _ in range(2)]
            cw = sb.alloc("cw", [128, 8, 3], F32)
            ctmp = sb.alloc("ctmp", [128, S], F32)
            dma("sp", cw[:], convw_d[l], writes=["cw"])
            for i in range(2):
                p.add("dve", lambda e, i=i: e.memset(stgf[i][:], 0.0), writes=[("stgf", i, b) for b in range(NB)])
            groups = [("aq", 0, aqk_d, 0, None), ("ak", 512, aqk_d, 512, None),
                      ("cq", 2832, cqT_d, 0, 0.125), ("ck", 3344, ckT_d, 0, None),
                      ("dq", 4368, dqT_d, 0, 0.125), ("dk", 4880, dkT_d, 0, None)]
            cidx = 0
            bank = 0
            for gi, (gname, col0, dst_d, drow0, scale) in enumerate(groups):
                wfi, wbi = wf[gi % 2], wb[gi % 2]
                dma("sp", wfi[:], w_in_d[l, :, col0:col0 + 512].rearrange("(c p) n -> p c n", p=128), writes=[("wf", gi % 2)])
                p.add("act", lambda e, wfi=wfi, wbi=wbi: e.copy(out=wbi[:], in_=wfi[:]),
                      reads=[("wf", gi % 2)], writes=[("wb", gi % 2)])
                is_a = gname in ("aq", "ak")
                for cc in range(4):
                    si = cidx % 2
                    cidx += 1
                    for b in range(NB):
                        bk = bank % 6
                        bank += 1
                        for k in range(8):
                            p.add("pe", lambda e, k=k, cc=cc, b=b, bk=bk, wbi=wbi: e.matmul(
                                psb[bk][:], lhsT=wbi[:, k, cc * 128:(cc + 1) * 128], rhs=hT[:, k, b * 512:(b + 1) * 512],
                                start=(k == 0), stop=(k == 7)), reads=[("wb", gi % 2)], writes=[PK(bk)])
                        if is_a:
                            copy_op(evac_engine(), stgf[si][:, 1 + b * 512:1 + (b + 1) * 512], psb[bk][:], [PK(bk)], [("stgf", si, b)])
                        else:
                            copy_op(evac_engine(), stgb[si][:, b * 512:(b + 1) * 512], psb[bk][:], [PK(bk)], [("stgb", si, b)], scale=scale)
                    if is_a:
                        ch = (col0 // 128) + cc
                        sf = stgf[si]
                        p.add("dve", lambda e, sf=sf, ch=ch: e.tensor_scalar(out=ctmp[:], in0=sf[:, 1:S + 1], scalar1=cw[:, ch, 1:2],
                                                                             scalar2=None, op0=ALU.mult),
                              reads=[("stgf", si, b_) for b_ in range(NB)] + ["cw"], writes=["ctmp"])
                        p.add("dve", lambda e, sf=sf, ch=ch: e.scalar_tensor_tensor(out=ctmp[:], in0=sf[:, 0:S], scalar=cw[:, ch, 0:1],
                                                                                    in1=ctmp[:], op0=ALU.mult, op1=ALU.add),
                              reads=[("stgf", si, b_) for b_ in range(NB)] + ["cw"], writes=["ctmp"])
                        p.add("dve", lambda e, sf=sf, ch=ch: e.scalar_tensor_tensor(out=ctmp[:], in0=sf[:, 2:S + 2], scalar=cw[:, ch, 2:3],
                                                                                    in1=ctmp[:], op0=ALU.mult, op1=ALU.add),
                              reads=[("stgf", si, b_) for b_ in range(NB)] + ["cw"], writes=["ctmp"])
                        p.add("act", lambda e, si=si: e.activation(out=stgb[si][:], in_=ctmp[:], func=AF.Silu),
                              reads=["ctmp"], writes=[("stgb", si, b_) for b_ in range(NB)])
                    r0 = drow0 + cc * 128
                    dma("pool", dst_d[r0:r0 + 128, :], stgb[si][:], reads=[("stgb", si, b_) for b_ in range(NB)], writes=[(gname, cc)])

            p.barrier()
            sb.reset(mA)
            wf = [sb.alloc("wf", [128, 8, 512], F32) for _ in range(2)]
            wb = [sb.alloc("wb", [128, 8, 512], BF16) for _ in range(2)]
            cos_s = sb.alloc("cos", [128, NT, 64], F32)
            sin_s = sb.alloc("sin", [128, NT, 64], F32)
            dma("sp", cos_s[:], cos_d.ap().rearrange("(t p) f -> p t f", p=128), writes=["cos"])
            dma("sp", sin_s[:], sin_d.ap().rearrange("(t p) f -> p t f", p=128), writes=["sin"])
            gb = sb.alloc("gb", [128, 16], F32)
            dma("sp", gb[:], gbias_d[l], writes=["gb"])
            bqg = sb.alloc("bqg", [128, 64], F32)
            bkg = sb.alloc("bkg", [128, 64], F32)
            dma("sp", bqg[:], bqg_d[l], writes=["bqg"])
            dma("sp", bkg[:], bkg_d[l], writes=["bkg"])
            p.add("dve", lambda e: e.tensor_scalar(out=bqg[:], in0=bqg[:], scalar1=0.125, scalar2=None, op0=ALU.mult),
                  reads=["bqg"], writes=["bqg"])
            st129 = [sb.alloc("st129", [128, 4, 129], BF16) for _ in range(4)]
            st65 = [sb.alloc("st65", [128, 8, 65], BF16) for _ in range(4)]
            stb = [sb.alloc("stb", [128, 512], BF16) for _ in range(4)]
            stg16 = [sb.alloc("stg16", [128, 16], F32) for _ in range(4)]
            for i in range(4):
                p.add("dve", lambda e, i=i: e.memset(st129[i][:], 1.0), writes=[("st129", i)])
                p.add("dve", lambda e, i=i: e.memset(st65[i][:], 1.0), writes=[("st65", i)])
            qf = sb.alloc("qf", [128, 512], F32)
            qn = sb.alloc("qn", [128, 512], F32)
            t1 = sb.alloc("t1", [128, 512], F32)
            t2 = sb.alloc("t2", [128, 512], F32)
            ssh = sb.alloc("ssh", [128, 8], F32)
            qrb = [sb.alloc("qrb", [128, 512], BF16) for _ in range(2)]
            bqT_s = sb.alloc("bqT_s", [128, 4, S], BF16)
            bkT_s = sb.alloc("bkT_s", [128, S], BF16)

            def rope_norm(ps_ap, psk, nh, gtile, t, outb, outk):
                W = nh * 64
                qf_, qn_, t1_, t2_ = qf[:, 0:W], qn[:, 0:W], t1[:, 0:W], t2[:, 0:W]
                p.add("act", lambda e: e.copy(out=qf_, in_=ps_ap), reads=[psk], writes=["qf"])
                p.add("dve", lambda e: e.tensor_tensor(out=t1_, in0=qf_, in1=qf_, op=ALU.mult), reads=["qf"], writes=["t1"])
                p.add("dve", lambda e: e.tensor_reduce(out=ssh[:, 0:nh], in_=t1_.rearrange("p (h d) -> p h d", h=nh), axis=AX.X, op=ALU.add),
                      reads=["t1"], writes=["ssh"])
                rstd_ops(ssh[:, 0:nh], "ssh", 1.0 / 64)
                p.add("dve", lambda e: e.tensor_tensor(out=qn_.rearrange("p (h d) -> p h d", h=nh), in0=qf_.rearrange("p (h d) -> p h d", h=nh),
                                                       in1=bc_ap(ssh[:, 0:nh], [[1, nh], [0, 64]]), op=ALU.mult),
                      reads=["qf", "ssh"], writes=["qn"])
                p.add("dve", lambda e: e.tensor_tensor(out=qn_.rearrange("p (h d) -> p h d", h=nh), in0=qn_.rearrange("p (h d) -> p h d", h=nh),
                                                       in1=bc_ap(gtile[:], [[0, nh], [1, 64]]), op=ALU.mult),
                      reads=["qn", "bqg", "bkg"], writes=["qn"])
                cos_b = bc_ap(cos_s[:, t, :], [[0, nh], [1, 64]])
                p.add("dve", lambda e: e.tensor_tensor(out=t1_.rearrange("p (h d) -> p h d", h=nh), in0=qn_.rearrange("p (h d) -> p h d", h=nh),
                                                       in1=cos_b, op=ALU.mult), reads=["qn", "cos"], writes=["t1"])
                sin_lo = bc_ap(sin_s[:, t, 0:16], [[0, nh], [32, 2], [1, 16]])
                sin_hi = bc_ap(sin_s[:, t, 16:32], [[0, nh], [32, 2], [1, 16]])
                x4 = qn_.rearrange("p (h r f) -> p h r f", h=nh, r=2)
                o4 = t2_.rearrange("p (h r f) -> p h r f", h=nh, r=2)
                p.add("dve", lambda e: e.tensor_tensor(out=o4[:, :, :, 0:16], in0=x4[:, :, :, 16:32], in1=sin_lo, op=ALU.mult),
                      reads=["qn", "sin"], writes=["t2"])
                p.add("dve", lambda e: e.tensor_tensor(out=o4[:, :, :, 16:32], in0=x4[:, :, :, 0:16], in1=sin_hi, op=ALU.mult),
                      reads=["qn", "sin"], writes=["t2"])
                p.add("dve", lambda e: e.tensor_tensor(out=outb, in0=t1_, in1=t2_, op=ALU.add), reads=["t1", "t2"], writes=[outk])

            tm_groups = [("av", 1024, 512), ("ao", 1536, 512), ("ag", 2048, 16), ("bq", 2064, 512),
                         ("bkv", 2576, 256), ("cv", 3856, 512), ("dv", 5392, 512)]
            for gi, (gname, col0, ncol) in enumerate(tm_groups):
                wfi, wbi = wf[gi % 2], wb[gi % 2]
                dma("sp", wfi[:, :, 0:ncol], w_in_d[l, :, col0:col0 + ncol].rearrange("(c p) n -> p c n", p=128),
                    writes=[("wf", gi % 2)])
                p.add("act", lambda e, wfi=wfi, wbi=wbi, ncol=ncol: e.copy(out=wbi[:, :, 0:ncol], in_=wfi[:, :, 0:ncol]),
                      reads=[("wf", gi % 2)], writes=[("wb", gi % 2)])
                for t in range(NT):
                    bk = t % 4
                    si = t % 2
                    s4 = t % 4
                    for k in range(8):
                        p.add("pe", lambda e, k=k, t=t, bk=bk, wbi=wbi, ncol=ncol: e.matmul(
                            psb[bk][:, 0:ncol], lhsT=hT[:, k, t * 128:(t + 1) * 128], rhs=wbi[:, k, 0:ncol],
                            start=(k == 0), stop=(k == 7)), reads=[("wb", gi % 2)], writes=[PK(bk)])
                    tok = slice(t * 128, (t + 1) * 128)
                    if gname == "av" or gname == "dv":
                        dst = av1_d if gname == "av" else dv1_d
                        copy_op(evac_engine(), st129[s4][:, :, 0:128], psb[bk][:, 0:512].rearrange("p (h d) -> p h d", h=4),
                                [PK(bk)], [("st129", s4)])
                        dma("sp", dst[tok], st129[s4][:], reads=[("st129", s4)], writes=[(gname, t)])
                    elif gname == "cv":
                        copy_op(evac_engine(), st65[s4][:, :, 0:64], psb[bk][:, 0:512].rearrange("p (h d) -> p h d", h=8),
                                [PK(bk)], [("st65", s4)])
                        dma("sp", cv1_d[tok], st65[s4][:], reads=[("st65", s4)], writes=[(gname, t)])
                    elif gname == "ao":
                        p.add("act", lambda e, bk=bk, s4=s4: e.activation(out=stb[s4][:], in_=psb[bk][:], func=AF.Sigmoid),
                              reads=[PK(bk)], writes=[("stb", s4)])
                        dma("sp", ao_d[tok, :], stb[s4][:], reads=[("stb", s4)], writes=[(gname, t)])
                    elif gname == "ag":
                        p.add("dve", lambda e, bk=bk, s4=s4: e.tensor_tensor(out=stg16[s4][:], in0=psb[bk][:, 0:16], in1=gb[:], op=ALU.add),
                              reads=[PK(bk), "gb"], writes=[("stg16", s4)])
                        dma("sp", ag_d[tok, :], stg16[s4][:], reads=[("stg16", s4)], writes=[(gname, t)])
                    elif gname == "bq":
                        rope_norm(psb[bk][:, 0:512], PK(bk), 8, bqg, t, qrb[si][:], ("qrb", si))
                        for c in range(4):
                            p.add("pe", lambda e, c=c, si=si: e.transpose(out=psb16[6 + si][:, c * 128:(c + 1) * 128],
                                                                          in_=qrb[si][:, c * 128:(c + 1) * 128], identity=ident[:]),
                                  reads=[("qrb", si)], writes=[PK(6 + si)])
                        copy_op("act", bqT_s[:, :, t * 128:(t + 1) * 128], psb16[6 + si][:, 0:512].rearrange("p (c t) -> p c t", c=4),
                                [PK(6 + si)], [("bqT_s", t)])
                    elif gname == "bkv":
                        rope_norm(psb[bk][:, 0:128], PK(bk), 2, bkg, t, qrb[si][:, 0:128], ("qrb", si))
                        p.add("pe", lambda e, si=si: e.transpose(out=psb16[6 + si][:, 0:128], in_=qrb[si][:, 0:128], identity=ident[:]),
                              reads=[("qrb", si)], writes=[PK(6 + si)])
                        copy_op("act", bkT_s[:, t * 128:(t + 1) * 128], psb16[6 + si][:, 0:128], [PK(6 + si)], [("bkT_s", t)])
                        copy_op("dve", st65[s4][:, 0:2, 0:64], psb[bk][:, 128:256].rearrange("p (h d) -> p h d", h=2),
                                [PK(bk)], [("st65", s4)])
                        dma("sp", bv1_d[tok], st65[s4][:, 0:2, :], reads=[("st65", s4)], writes=[("bv", t)])
                if gname == "bq":
                    p.barrier()
                    dma("pool", bqT_d.ap().rearrange("(c p) s -> p c s", p=128), bqT_s[:], writes=["bqT_d"])
                if gname == "bkv":
                    p.barrier()
                    dma("pool", bkT_d.ap(), bkT_s[:], writes=["bkT_d"])

        if "A" in phases:
            _phA()

        def _phB(l=l, xsrc_d=xsrc_d, lambda_init=lambda_init):
            phase_reset()
            qT = sb.alloc("qTp", [128, 8, S], BF16)
            kT2 = sb.alloc("kT2", [128, 2, S], BF16)
            v1 = sb.alloc("v1", [128, NT, 2, 65], BF16)
            sel = sb.alloc("sel", [65, 64], F32)
            load_qpad(qT, bqT_d)
            for g in range(2):
                dma("sp", kT2[0:64, g, :], bkT_d[g * 64:(g + 1) * 64, :], writes=[("kT2", g, 0)])
                dma("sp", kT2[64:128, g, :], bkT_d[g * 64:(g + 1) * 64, :], writes=[("kT2", g, 1)])
            dma("sp", v1[:], bv1_d.ap().rearrange("(t p) g e -> p t g e", p=128), writes=["v1"])
            dma("sp", sel[:], sel_d.ap(), writes=["sel"])
            pT = [sb.alloc("pT", [128, 1024], BF16) for _ in range(3)]
            osb = [sb.alloc("osb", [65, 512], F32) for _ in range(2)]
            rb = [sb.alloc("rb", [64, 512], F32) for _ in range(2)]
            ybs = [sb.alloc("ybs", [64, 512], BF16) for _ in range(2)]
            p.barrier()
            pipe = Pipe([0, 1, 2])
            it = 0
            oit = 0
            for h in range(8):
                g = h // 4
                for qb in range(NB):
                    ob = 4 + oit % 2
                    oit += 1
                    for kt2 in range(NT // 2):
                        sd = it % 2
                        pi = it % 3
                        it += 1

                        def s0(sd=sd, g=g, h=h, kt2=kt2, qb=qb):
                            for u in range(2):
                                kt = 2 * kt2 + u
                                p.add("pe", lambda e, u=u, kt=kt: e.matmul(psb[2 * sd + u][:], lhsT=kT2[:, g, kt * 128:(kt + 1) * 128],
                                                                           rhs=qT[:, h, qb * 512:(qb + 1) * 512], start=True, stop=True),
                                      writes=[PK(2 * sd + u)])

                        def s1(sd=sd, pi=pi):
                            p.add("act", lambda e: e.activation(out=pT[pi][:], in_=psall[:, sd * 1024:(sd + 1) * 1024], func=AF.Exp),
                                  reads=[PK(2 * sd), PK(2 * sd + 1)], writes=[("pT", pi)])

                        def s2(pi=pi, ob=ob, kt2=kt2, g=g, h=h, qb=qb):
                            for u in range(2):
                                kt = 2 * kt2 + u
                                p.add("pe", lambda e, u=u, kt=kt: e.matmul(psb[ob][0:65, :], lhsT=v1[:, kt, g, :], rhs=pT[pi][:, u * 512:(u + 1) * 512],
                                                                           start=(kt == 0), stop=(kt == NT - 1)),
                                      reads=[("pT", pi)], writes=[PK(ob)])
                            if kt2 == NT // 2 - 1:
                                oi = ob - 4
                                p.add("dve", lambda e: e.tensor_copy(out=osb[oi][:], in_=psb[ob][0:65, :]), reads=[PK(ob)], writes=[("osb", oi)])
                                p.add("pe", lambda e: e.matmul(psb[6][0:64, :], lhsT=sel[:], rhs=osb[oi][:], start=True, stop=True),
                                      reads=[("osb", oi), "sel"], writes=[PK(6)])
                                p.add("dve", lambda e: e.reciprocal(out=rb[oi][:], in_=psb[6][0:64, :]), reads=[PK(6)], writes=[("rb", oi)])
                                p.add("dve", lambda e: e.tensor_tensor(out=ybs[oi][:], in0=osb[oi][0:64, :], in1=rb[oi][:], op=ALU.mult),
                                      reads=[("osb", oi), ("rb", oi)], writes=[("ybs", oi)])
                                r0 = 512 + h * 64
                                dma("pool", yT_d[r0:r0 + 64, qb * 512:(qb + 1) * 512], ybs[oi][:], reads=[("ybs", oi)],
                                    writes=[("yb", h, qb)])
                        pipe.push(s0, s1, s2)
            pipe.flush()

        if "B" in phases:
            _phB()

        def _phC(l=l, xsrc_d=xsrc_d, lambda_init=lambda_init):
            phase_reset()
            qT = sb.alloc("qTp", [128, 8, S], BF16)
            kT = sb.alloc("kT", [128, 4, S], BF16)
            v1 = sb.alloc("v1", [128, NT, 8, 65], BF16)
            sel = sb.alloc("sel", [65, 64], F32)
            nbf = sb.alloc("nbf", [128, 5, 8, 128], F32)
            en_int = sb.alloc("en_int", [128, 5, 8, 128], BF16)
            en_edge = sb.alloc("en_edge", [128, 5, 8, 128], BF16)
            load_qpad(qT, cqT_d)
            dma("sp", kT[:], ckT_d.ap().rearrange("(c p) s -> p c s", p=128), writes=["kT"])
            dma("sp", v1[:], cv1_d.ap().rearrange("(t p) g e -> p t g e", p=128), writes=["v1"])
            dma("sp", sel[:], sel_d.ap(), writes=["sel"])
            dma("sp", nbf[:], natb_d[l, 2], writes=["nbf"])
            p.add("act", lambda e: e.activation(out=en_int[:], in_=nbf[:], func=AF.Exp), reads=["nbf"], writes=["en_int"])
            pTa = [sb.alloc("pTa", [128, 512], BF16) for _ in range(4)]
            pT = [sb.alloc("pT", [128, 512], BF16) for _ in range(4)]
            osb = [sb.alloc("osb", [65, 512], F32) for _ in range(2)]
            rb = [sb.alloc("rb", [64, 512], F32) for _ in range(2)]
            ybs = [sb.alloc("ybs", [64, 512], BF16) for _ in range(2)]
            p.barrier()
            pipe = Pipe([0, 1, 2, 3])
            it = 0
            oit = 0
            for i in range(NT):
                pat = 0 if i == 0 else 1 if i == 1 else 3 if i == NT - 2 else 4 if i == NT - 1 else 2
                kb0 = min(max(i - 2, 0), NT - 5)
                if pat != 2:
                    dma("sp", nbf[:], natb_d[l, pat], reads=["en_int", "en_edge"], writes=["nbf"])
                    p.add("act", lambda e: e.activation(out=en_edge[:], in_=nbf[:], func=AF.Exp), reads=["nbf"], writes=["en_edge"])
                nbt = en_int if pat == 2 else en_edge
                nbk = "en_int" if pat == 2 else "en_edge"
                for hg in range(2):
                    ob = 4 + oit % 2
                    oit += 1
                    for kk in range(5):
                        kt = kb0 + kk
                        sbk = it % 4
                        it += 1

                        def s0(sbk=sbk, hg=hg, kt=kt, i=i):
                            for hh in range(4):
                                h = hg * 4 + hh
                                p.add("pe", lambda e, hh=hh, h=h: e.matmul(
                                    psb[sbk][:, hh * 128:(hh + 1) * 128], lhsT=kT[:, h // 2, kt * 128:(kt + 1) * 128],
                                    rhs=qT[:, h, i * 128:(i + 1) * 128], start=True, stop=True), writes=[PK(sbk)])

                        def s1(sbk=sbk):
                            p.add("act", lambda e: e.activation(out=pTa[sbk][:], in_=psb[sbk][:], func=AF.Exp),
                                  reads=[PK(sbk)], writes=[("pTa", sbk)])

                        def s2(sbk=sbk, kk=kk, hg=hg, nbt=nbt, nbk=nbk):
                            p.add("dve", lambda e: e.tensor_tensor(
                                out=pT[sbk][:].rearrange("p (h q) -> p h q", h=4), in0=pTa[sbk][:].rearrange("p (h q) -> p h q", h=4),
                                in1=nbt[:, kk, hg * 4:(hg + 1) * 4, :], op=ALU.mult), reads=[("pTa", sbk), nbk], writes=[("pT", sbk)])

                        def s3(sbk=sbk, ob=ob, kk=kk, kt=kt, hg=hg, i=i):
                            for hh in range(4):
                                h = hg * 4 + hh
                                p.add("pe", lambda e, hh=hh, h=h: e.matmul(
                                    psb[ob][0:65, hh * 128:(hh + 1) * 128], lhsT=v1[:, kt, h, :], rhs=pT[sbk][:, hh * 128:(hh + 1) * 128],
                                    start=(kk == 0 and hh == 0), stop=(kk == 4), skip_group_check=True), reads=[("pT", sbk)], writes=[PK(ob)])
                            if kk == 4:
                                oi = ob - 4
                                p.add("act", lambda e: e.copy(out=osb[oi][:], in_=psb[ob][0:65, :]), reads=[PK(ob)], writes=[("osb", oi)])
                                p.add("pe", lambda e: e.matmul(psb[6][0:64, :], lhsT=sel[:], rhs=osb[oi][:], start=True, stop=True),
                                      reads=[("osb", oi), "sel"], writes=[PK(6)])
                                p.add("act", lambda e: e.activation(out=rb[oi][:], in_=psb[6][0:64, :], func=AF.Ln), reads=[PK(6)], writes=[("rb", oi)])
                                p.add("act", lambda e: e.activation(out=rb[oi][:], in_=rb[oi][:], func=AF.Exp, scale=-1.0),
                                      reads=[("rb", oi)], writes=[("rb", oi)])
                                p.add("pool", lambda e: e.tensor_tensor(out=ybs[oi][:], in0=osb[oi][0:64, :], in1=rb[oi][:], op=ALU.mult),
                                      reads=[("osb", oi), ("rb", oi)], writes=[("ybs", oi)])
                                for hh in range(4):
                                    r0 = 1024 + (hg * 4 + hh) * 64
                                    dma("pool", yT_d[r0:r0 + 64, i * 128:(i + 1) * 128], ybs[oi][:, hh * 128:(hh + 1) * 128],
                                        reads=[("ybs", oi)], writes=[("yc", hg * 4 + hh, i)])
                        pipe.push(s0, s1, s2, s3)
                if pat != 2:
                    pipe.flush()
            pipe.flush()

        if "C" in phases:
            _phC()

        def _phD(l=l, xsrc_d=xsrc_d, lambda_init=lambda_init):
            phase_reset()
            qT = sb.alloc("qTp", [128, 8, S], BF16)
            kT = sb.alloc("kT", [128, 4, S], BF16)
            v1 = sb.alloc("v1", [128, NT, 4, 129], BF16)
            tz = sb.alloc("tz", [128, 2 * S - 128], F32)
            dlr = sb.alloc("dlr", [128, 4, 64], F32)
            dsub = sb.alloc("dsub", [128, 128], F32)
            load_qpad(qT, dqT_d)
            dma("sp", kT[:], dkT_d.ap().rearrange("(c p) s -> p c s", p=128), writes=["kT"])
            dma("sp", v1[:], dv1_d.ap().rearrange("(t p) g e -> p t g e", p=128), writes=["v1"])
            dma("sp", tz[:], tz_d.ap(), writes=["tz"])
            dma("sp", dlr[:], dl_d[l], writes=["dlr"])
            dma("sp", dsub[:], dsub_d[l], writes=["dsub"])
            lt = sb.alloc("lt", [128, 2, 64], F32)
            ls = sb.alloc("ls", [128, 2], F32)
            nlam = sb.alloc("nlam", [128, 1], F32)
            p.add("dve", lambda e: e.tensor_tensor(out=lt[:, 0, :], in0=dlr[:, 0, :], in1=dlr[:, 1, :], op=ALU.mult), reads=["dlr"], writes=["lt"])
            p.add("dve", lambda e: e.tensor_tensor(out=lt[:, 1, :], in0=dlr[:, 2, :], in1=dlr[:, 3, :], op=ALU.mult), reads=["dlr"], writes=["lt"])
            p.add("dve", lambda e: e.tensor_reduce(out=ls[:], in_=lt[:], axis=AX.X, op=ALU.add), reads=["lt"], writes=["ls"])
            p.add("act", lambda e: e.activation(out=ls[:], in_=ls[:], func=AF.Exp), reads=["ls"], writes=["ls"])
            p.add("dve", lambda e: e.tensor_tensor(out=nlam[:], in0=ls[:, 1:2], in1=ls[:, 0:1], op=ALU.subtract), reads=["ls"], writes=["nlam"])
            p.add("dve", lambda e: e.tensor_scalar(out=nlam[:], in0=nlam[:], scalar1=-lambda_init, scalar2=None, op0=ALU.add),
                  reads=["nlam"], writes=["nlam"])
            p.add("dve", lambda e: e.tensor_scalar(out=dsub[:], in0=dsub[:], scalar1=1.0 - lambda_init, scalar2=None, op0=ALU.mult),
                  reads=["dsub"], writes=["dsub"])
            pT = [sb.alloc("pT", [128, 512], BF16) for _ in range(4)]
            pTa = [sb.alloc("pTa", [128, 512], BF16) for _ in range(4)]
            etab = sb.alloc("etab", [128, 2 * S - 128], BF16)
            r1 = sb.alloc("r1", [128, 1], F32)
            r2 = sb.alloc("r2", [128, 1], F32)
            of = sb.alloc("of", [128, 128], F32)
            osq = sb.alloc("osq", [128, 128], F32)
            oss = sb.alloc("oss", [128, 1], F32)
            yb = [sb.alloc("yb", [128, 128], BF16) for _ in range(2)]
            yds = [sb.alloc("yds", [128, 512], BF16) for _ in range(2)]
            p.barrier()
            regs = {}
            ri = 0
            for c in range(2):
                for qq in range(4):
                    regs[(c, qq)] = (4 + ri // 3, (ri % 3) * 160)
                    ri += 1
            pipe = Pipe([0, 1, 2, 3])
            it = 0
            ep = 0
            for h in range(4):
                slope = 2.0 ** (-8.0 * (h + 1) / 4)
                pipe.flush()
                p.add("act", lambda e, slope=slope: e.activation(out=etab[:], in_=tz[:], func=AF.Exp, scale=slope),
                      reads=["tz", ("pT", 0), ("pT", 1), ("pT", 2), ("pT", 3)], writes=["etab"])
                for qb in range(NB):
                    for kt in range(NT):
                        for c in range(2):
                            f0 = c * 256 + h * 64
                            j = f0 // 128
                            pr = slice(f0 % 128, f0 % 128 + 64)
                            sbk = it % 4
                            it += 1
                            off = qb * 512 - kt * 128 + (NT - 1) * 128

                            def s0(sbk=sbk, j=j, hd=f0 // 64, kt=kt, qb=qb):
                                p.add("pe", lambda e: e.matmul(psb[sbk][:], lhsT=kT[:, j, kt * 128:(kt + 1) * 128],
                                                               rhs=qT[:, hd, qb * 512:(qb + 1) * 512], start=True, stop=True),
                                      writes=[PK(sbk)])

                            def s1(sbk=sbk):
                                p.add("act", lambda e: e.activation(out=pTa[sbk][:], in_=psb[sbk][:], func=AF.Exp),
                                      reads=[PK(sbk)], writes=[("pTa", sbk)])

                            def s2m(sbk=sbk, off=off):
                                p.add("dve", lambda e: e.tensor_tensor(out=pT[sbk][:], in0=pTa[sbk][:], in1=etab[:, off:off + 512], op=ALU.mult),
                                      reads=[("pTa", sbk), "etab"], writes=[("pT", sbk)])

                            def s2(sbk=sbk, c=c, kt=kt, h=h, qb=qb):
                                nonlocal ep
                                for qq in range(4):
                                    bkk, co = regs[(c, qq)]
                                    p.add("pe", lambda e, qq=qq, bkk=bkk, co=co: e.matmul(
                                        psb[bkk][:, co:co + 129], lhsT=pT[sbk][:, qq * 128:(qq + 1) * 128], rhs=v1[:, kt, h, :],
                                        start=(kt == 0 and co == 0), stop=(kt == NT - 1), skip_group_check=True),
                                        reads=[("pT", sbk)], writes=[PK(bkk)])
                                if kt == NT - 1 and c == 1:
                                    ydi = ep % 2
                                    ep += 1
                                    for qq in range(4):
                                        b1, c1 = regs[(0, qq)]
                                        b2, c2 = regs[(1, qq)]
                                        ybi = qq % 2
                                        p.add("dve", lambda e, b1=b1, c1=c1: e.reciprocal(out=r1[:], in_=psb[b1][:, c1 + 128:c1 + 129]),
                                              reads=[PK(b1)], writes=["r1"])
                                        p.add("dve", lambda e, b2=b2, c2=c2: e.reciprocal(out=r2[:], in_=psb[b2][:, c2 + 128:c2 + 129]),
                                              reads=[PK(b2)], writes=["r2"])
                                        p.add("dve", lambda e: e.tensor_tensor(out=r2[:], in0=r2[:], in1=nlam[:], op=ALU.mult),
                                              reads=["r2", "nlam"], writes=["r2"])
                                        p.add("dve", lambda e, b1=b1, c1=c1: e.tensor_scalar(out=of[:], in0=psb[b1][:, c1:c1 + 128], scalar1=r1[:],
                                                                                             scalar2=None, op0=ALU.mult),
                                              reads=[PK(b1), "r1"], writes=["of"])
                                        p.add("dve", lambda e, b2=b2, c2=c2: e.scalar_tensor_tensor(out=of[:], in0=psb[b2][:, c2:c2 + 128], scalar=r2[:],
                                                                                                    in1=of[:], op0=ALU.mult, op1=ALU.add),
                                              reads=[PK(b2), "r2", "of"], writes=["of"])
                                        p.add("dve", lambda e: e.memset(oss[:], 0.0), writes=["oss"])
                                        p.add("act", lambda e: e.activation(out=osq[:], in_=of[:], func=AF.Square, accum_out=oss[:]),
                                              reads=["of", "oss"], writes=["osq", "oss"])
                                        rstd_ops(oss[:], "oss", 1.0 / 128)
                                        p.add("dve", lambda e, ybi=ybi: e.scalar_tensor_tensor(out=yb[ybi][:], in0=of[:], scalar=oss[:], in1=dsub[:],
                                                                                               op0=ALU.mult, op1=ALU.mult),
                                              reads=["of", "oss", "dsub"], writes=[("yb", ybi)])
                                        p.add("pe", lambda e, qq=qq, ybi=ybi: e.transpose(out=psb16[7][:, qq * 128:(qq + 1) * 128], in_=yb[ybi][:],
                                                                                          identity=ident[:]), reads=[("yb", ybi)], writes=[PK(7)])
                                    copy_op("act", yds[ydi][:], psb16[7][:, 0:512], [PK(7)], [("yds", ydi)])
                                    r0 = 1536 + h * 128
                                    dma("pool", yT_d[r0:r0 + 128, qb * 512:(qb + 1) * 512], yds[ydi][:], reads=[("yds", ydi)],
                                        writes=[("yd", h, qb)])
                            pipe.push(s0, s1, s2m, s2)
            pipe.flush()

        if "D" in phases:
            _phD()

        def _phE(l=l, xsrc_d=xsrc_d, lambda_init=lambda_init):
            phase_reset()
            G = sb.alloc("G", [128, NT, 16], F32)
            dma("sp", G[:], ag_d.ap().rearrange("(t p) j -> p t j", p=128), writes=["G"])
            mk = [sb.alloc("mk", [128, 128], F32) for _ in range(2)]
            onesf = sb.alloc("onesf", [128, 128], F32)
            dma("sp", mk[0][:], maskf_d.ap(), writes=["mk"])
            dma("sp", mk[1][:], maskb_d.ap(), writes=["mk"])
            dma("sp", onesf[:], ones_d.ap(), writes=["mk"])
            E1 = sb.alloc("E1", [128, NT, 2, 4], F32)
            BN = sb.alloc("BN", [128, NT, 16], F32)
            T1 = sb.alloc("T1", [128, NT, 2, 4], F32)
            A1 = sb.alloc("A1", [128, NT, 2, 4], F32)
            A2 = sb.alloc("A2", [128, NT, 2, 4], F32)
            WD = sb.alloc("WD", [128, NT, 2, 4], F32)
            FL = sb.alloc("FL", [128, NT, 2, 4], F32)
            ang = sb.alloc("ang", [128, 512], F32)
            dma("sp", ang[:], ang_d[l], writes=["ang"])
            Gv = G[:].rearrange("p t (y h) -> p t y h", y=4)
            fsel = bc_ap(Gv[:, :, 1, :], [[16, NT], [8, 2], [1, 4]])
            isel = bc_ap(Gv[:, :, 0, :], [[16, NT], [8, 2], [1, 4]])
            p.add("act", lambda e: e.activation(out=E1[:], in_=fsel, func=AF.Exp, scale=-1.0), reads=["G"], writes=["E1"])
            p.add("act", lambda e: e.activation(out=E1[:], in_=E1[:], func=AF.Ln, bias=1.0), reads=["E1"], writes=["E1"])
            for t in range(NT):
                p.add("pe", lambda e, t=t: e.matmul(psb[0][:, t * 16:t * 16 + 4], lhsT=mk[0][:], rhs=E1[:, t, 0, :], start=True, stop=True),
                      reads=["E1", "mk"], writes=[PK(0)])
                p.add("pe", lambda e, t=t: e.matmul(psb[0][:, t * 16 + 4:t * 16 + 8], lhsT=mk[1][:], rhs=E1[:, t, 1, :], start=True, stop=True),
                      reads=["E1", "mk"], writes=[PK(0)])
                p.add("pe", lambda e, t=t: e.matmul(psb[0][:, t * 16 + 8:t * 16 + 16], lhsT=onesf[:],
                                                    rhs=E1[:, t, :, :].rearrange("p a h -> p (a h)"), start=True, stop=True),
                      reads=["E1", "mk"], writes=[PK(0)])
            p.add("dve", lambda e: e.tensor_copy(out=BN[:].rearrange("p t j -> p (t j)"), in_=psb[0][:, 0:NT * 16]), reads=[PK(0)], writes=["BN"])
            bneg = BN[:, :, 0:8].rearrange("p t (a h) -> p t a h", a=2)
            tot = BN[:, :, 8:16].rearrange("p t (a h) -> p t a h", a=2)
            p.add("dve", lambda e: e.tensor_tensor(out=T1[:], in0=isel, in1=bneg, op=ALU.add), reads=["G", "BN"], writes=["T1"])
            p.add("act", lambda e: e.activation(out=A1[:], in_=T1[:], func=AF.Exp), reads=["T1"], writes=["A1"])
            p.add("dve", lambda e: e.tensor_tensor(out=T1[:], in0=T1[:], in1=tot, op=ALU.subtract), reads=["T1", "BN"], writes=["T1"])
            p.add("act", lambda e: e.activation(out=A2[:], in_=T1[:], func=AF.Exp), reads=["T1"], writes=["A2"])
            ksc = 128.0 ** -0.5
            p.add("dve", lambda e: e.tensor_scalar(out=A1[:], in0=A1[:], scalar1=ksc, scalar2=None, op0=ALU.mult), reads=["A1"], writes=["A1"])
            p.add("dve", lambda e: e.tensor_scalar(out=A2[:], in0=A2[:], scalar1=ksc, scalar2=None, op0=ALU.mult), reads=["A2"], writes=["A2"])
            p.add("act", lambda e: e.activation(out=WD[:], in_=tot, func=AF.Exp, scale=-1.0), reads=["BN"], writes=["WD"])
            p.add("act", lambda e: e.activation(out=FL[:], in_=bneg, func=AF.Exp), reads=["BN"], writes=["FL"])
            qTh = sb.alloc("qTh", [128, S], BF16)
            kTh = sb.alloc("kTh", [128, S], BF16)
            ktok = sb.alloc("ktok", [128, NT, 128], BF16)
            v1h = sb.alloc("v1h", [128, NT, 129], BF16)
            sgo = sb.alloc("sgo", [128, NT, 128], BF16)
            hacc = sb.alloc("hacc", [128, NT, 128], F32)
            xc = sb.alloc("xc", [128, NT, 128], F32)
            sq2 = sb.alloc("sq2", [128, NT, 128], F32)
            yab = sb.alloc("yab", [128, NT, 128], BF16)
            yas = sb.alloc("yas", [128, S], BF16)
            mean = sb.alloc("mean", [128, NT], F32)
            var = sb.alloc("var", [128, NT], F32)
            Cst = [sb.alloc("Cst", [128, 129], F32) for _ in range(2)]
            Cb = [sb.alloc("Cb", [128, 129], BF16) for _ in range(2)]
            wT = [sb.alloc("wT", [128, 128], BF16) for _ in range(4)]
            kS = [sb.alloc("kS", [128, 128], BF16) for _ in range(4)]
            den = [sb.alloc("den", [128, 1], F32) for _ in range(2)]
            for h in range(4):
                p.barrier()
                dma("sp", qTh[:], aqk_d[h * 128:(h + 1) * 128, :], writes=["qTh"])
                dma("sp", kTh[:], aqk_d[512 + h * 128:512 + (h + 1) * 128, :], writes=["kTh"])
                dma("sp", v1h[:], av1_d.ap()[:, h, :].rearrange("(t p) e -> p t e", p=128), writes=["v1h"])
                dma("sp", sgo[:], ao_d.ap()[:, h * 128:(h + 1) * 128].rearrange("(t p) d -> p t d", p=128), writes=["sgo"])
                p.add("dve", lambda e: e.memset(hacc[:], 0.0), writes=["hacc"])
                for dr in range(2):
                    p.add("dve", lambda e, dr=dr: e.memset(Cst[dr][:], 0.0), writes=[("Cst", dr)])
                    p.add("dve", lambda e, dr=dr: e.memset(Cb[dr][:], 0.0), writes=[("Cb", dr)])
                for t8 in range(NT // 8):
                    pb = 6 + t8 % 2
                    for c8 in range(8):
                        c = t8 * 8 + c8
                        p.add("pe", lambda e, c=c, c8=c8, pb=pb: e.transpose(out=psb16[pb][:, c8 * 128:(c8 + 1) * 128], in_=kTh[:, c * 128:(c + 1) * 128],
                                                                             identity=ident[:]), reads=["kTh"], writes=[PK(pb)])
                    copy_op("act", ktok[:, t8 * 8:(t8 + 1) * 8, :], psb16[pb][:, 0:1024].rearrange("p (c d) -> p c d", c=8), [PK(pb)], ["ktok"])
                wi = 0
                epipe = Pipe([0, 1, 2, 3])
                for step in range(NT):
                    for dr in range(2):
                        c = step if dr == 0 else NT - 1 - step
                        w = wi % 4
                        wi += 1
                        ps_s, ps_o, ps_c = dr * 3, dr * 3 + 1, dr * 3 + 2
                        cs = slice(c * 128, (c + 1) * 128)

                        def e0(cs=cs, ps_s=ps_s):
                            p.add("pe", lambda e: e.matmul(psb[ps_s][:, 0:128], lhsT=kTh[:, cs], rhs=qTh[:, cs], start=True, stop=True),
                                  reads=["qTh", "kTh"], writes=[PK(ps_s)])

                        def e1(c=c, dr=dr, h=h, w=w, ps_s=ps_s):
                            p.add("dve", lambda e: e.scalar_tensor_tensor(
                                out=wT[w][:], in0=psb[ps_s][:, 0:128], scalar=A1[:, c, dr, h:h + 1], in1=mk[dr][:], op0=ALU.mult, op1=ALU.mult),
                                reads=[PK(ps_s), "A1"], writes=[("wT", w)])
                            p.add("act", lambda e: e.activation(out=kS[w][:], in_=ktok[:, c, :], func=AF.Copy, scale=A2[:, c, dr, h:h + 1]),
                                  reads=["ktok", "A2"], writes=[("kS", w)])

                        def e2(c=c, cs=cs, dr=dr, w=w, ps_o=ps_o, ps_c=ps_c):
                            p.add("pe", lambda e: e.matmul(psb[ps_o][:, 0:129], lhsT=wT[w][:], rhs=v1h[:, c, :], start=True, stop=False),
                                  reads=[("wT", w), "v1h"], writes=[PK(ps_o)])
                            p.add("pe", lambda e: e.matmul(psb[ps_o][:, 0:129], lhsT=qTh[:, cs], rhs=Cb[dr][:], start=False, stop=True),
                                  reads=[("Cb", dr)], writes=[PK(ps_o)])
                            p.add("pe", lambda e: e.matmul(psb[ps_c][:, 0:129], lhsT=kS[w][:], rhs=v1h[:, c, :], start=True, stop=True),
                                  reads=[("kS", w)], writes=[PK(ps_c)])

                        def e3(c=c, dr=dr, h=h, ps_o=ps_o, ps_c=ps_c):
                            p.add("dve", lambda e: e.scalar_tensor_tensor(
                                out=Cst[dr][:], in0=Cst[dr][:], scalar=WD[:, c, dr, h:h + 1], in1=psb[ps_c][:, 0:129], op0=ALU.mult, op1=ALU.add),
                                reads=[PK(ps_c), "WD", ("Cst", dr)], writes=[("Cst", dr)])
                            p.add("act", lambda e: e.copy(out=Cb[dr][:], in_=Cst[dr][:]), reads=[("Cst", dr)], writes=[("Cb", dr)])
                            p.add("dve", lambda e: e.scalar_tensor_tensor(
                                out=den[dr][:], in0=psb[ps_o][:, 128:129], scalar=-1.0, in1=FL[:, c, dr, h:h + 1], op0=ALU.mult, op1=ALU.max),
                                reads=[PK(ps_o), "FL"], writes=[("den", dr)])
                            p.add("dve", lambda e: e.tensor_tensor(
                                out=den[dr][:], in0=den[dr][:], in1=psb[ps_o][:, 128:129], op=ALU.max),
                                reads=[("den", dr), PK(ps_o)], writes=[("den", dr)])
                            p.add("dve", lambda e: e.reciprocal(out=den[dr][:], in_=den[dr][:]), reads=[("den", dr)], writes=[("den", dr)])
                            p.add("dve", lambda e: e.scalar_tensor_tensor(
                                out=hacc[:, c, :], in0=psb[ps_o][:, 0:128], scalar=den[dr][:], in1=hacc[:, c, :], op0=ALU.mult, op1=ALU.add),
                                reads=[PK(ps_o), ("den", dr), ("hacc", c)], writes=[("hacc", c)])
                        epipe.push(e0, e1, e2, e3)
                epipe.flush()
                p.barrier()
                p.add("dve", lambda e: e.tensor_reduce(out=mean[:], in_=hacc[:], axis=AX.X, op=ALU.add), writes=["mean"])
                p.add("dve", lambda e: e.tensor_scalar(out=mean[:], in0=mean[:], scalar1=1.0 / 128, scalar2=None, op0=ALU.mult), reads=["mean"], writes=["mean"])
                p.add("dve", lambda e: e.tensor_tensor(out=xc[:], in0=hacc[:], in1=bc_ap(mean[:], [[1, NT], [0, 128]]), op=ALU.subtract),
                      reads=["mean"], writes=["xc"])
                p.add("act", lambda e: e.activation(out=sq2[:], in_=xc[:], func=AF.Square), reads=["xc"], writes=["sq2"])
                p.add("dve", lambda e: e.tensor_reduce(out=var[:], in_=sq2[:], axis=AX.X, op=ALU.add), reads=["sq2"], writes=["var"])
                rstd_ops(var[:], "var", 1.0 / 128)
                p.add("dve", lambda e: e.tensor_tensor(out=xc[:], in0=xc[:], in1=bc_ap(var[:], [[1, NT], [0, 128]]), op=ALU.mult),
                      reads=["xc", "var"], writes=["xc"])
                p.add("dve", lambda e, h=h: e.tensor_tensor(out=xc[:], in0=xc[:], in1=bc_ap(ang[:, h * 128:(h + 1) * 128], [[0, NT], [1, 128]]), op=ALU.mult),
                      reads=["xc", "ang"], writes=["xc"])
                p.add("dve", lambda e: e.tensor_tensor(out=yab[:], in0=xc[:], in1=sgo[:], op=ALU.mult), reads=["xc", "sgo"], writes=["yab"])
                for t8 in range(NT // 8):
                    pb = 6 + t8 % 2
                    for c8 in range(8):
                        c = t8 * 8 + c8
                        p.add("pe", lambda e, c=c, c8=c8, pb=pb: e.transpose(out=psb16[pb][:, c8 * 128:(c8 + 1) * 128], in_=yab[:, c, :],
                                                                             identity=ident[:]), reads=["yab"], writes=[PK(pb)])
                    copy_op("act", yas[:, t8 * 1024:(t8 + 1) * 1024], psb16[pb][:, 0:1024], [PK(pb)], ["yas"])
                dma("pool", yT_d[h * 128:(h + 1) * 128, :], yas[:], reads=["yas"], writes=[("ya", h)])

        if "E" in phases:
            _phE()

        def _phF(l=l, xsrc_d=xsrc_d, lambda_init=lambda_init):
            phase_reset()
            wg = sb.alloc("wg", [128, 8, 4096], BF16)
            wu = sb.alloc("wu", [128, 16, 1024], BF16)
            wo = sb.alloc("wo", [128, 8, 1024], BF16)
            mF = sb.mark()
            stg = [sb.alloc("stg", [128, 8, 512], F32) for _ in range(2)]
            load_cast(lambda c0, c1: wg[:, :, c0:c1],
                      lambda c0, c1: w_in_d[l, :, 5904 + c0:5904 + c1].rearrange("(c p) n -> p c n", p=128), 8, 4096, stg, "wg")
            for bi in range(4):
                load_cast(lambda c0, c1, bi=bi: wu[:, bi * 4:(bi + 1) * 4, c0:c1],
                          lambda c0, c1, bi=bi: w_up_d[bi][l, :, c0:c1].rearrange("(c p) n -> p c n", p=128), 4, 1024, stg, "wu%d" % bi)
            load_cast(lambda c0, c1: wo[:, :, c0:c1],
                      lambda c0, c1: w_out_d[l, :, c0:c1].rearrange("(c p) n -> p c n", p=128), 8, 1024, stg, "wo")
            p.barrier()
            sb.reset(mF)
            hTb = [sb.alloc("hTb", [128, 8, 512], BF16) for _ in range(2)]
            yTb = [sb.alloc("yTb", [128, 16, 512], BF16) for _ in range(2)]
            mT = [sb.alloc("mT", [128, 8, 512], BF16) for _ in range(2)]
            sg = [sb.alloc("sg", [128, 512], F32) for _ in range(2)]
            acc = sb.alloc("acc", [128, 512], F32)
            tmpm = sb.alloc("tmpm", [128, 512], F32)
            xtl = [sb.alloc("xtl", [128, D], F32) for _ in range(2)]
            bankc = 0
            xi = 0
            for b in range(NB):
                bi = b % 2
                bs = slice(b * 512, (b + 1) * 512)
                dma("sp", hTb[bi][:], hT_d.ap()[:, bs].rearrange("(c p) s -> p c s", p=128), writes=[("hTb", bi)])
                dma("sp", yTb[bi][:], yT_d.ap()[:, bs].rearrange("(c p) s -> p c s", p=128), writes=[("yTb", bi)])
                for dc in range(8):
                    for g in range(4):
                        pg = bankc % 6
                        pu = (bankc + 1) % 6
                        bankc += 2
                        for k in range(8):
                            p.add("pe", lambda e, k=k, g=g, dc=dc, pg=pg, bi=bi: e.matmul(
                                psb[pg][:], lhsT=wg[:, k, g * 1024 + dc * 128:g * 1024 + (dc + 1) * 128], rhs=hTb[bi][:, k, :],
                                start=(k == 0), stop=(k == 7)), reads=[("hTb", bi)], writes=[PK(pg)])
                        for k in range(4):
                            p.add("pe", lambda e, k=k, g=g, dc=dc, pu=pu, bi=bi: e.matmul(
                                psb[pu][:], lhsT=wu[:, g * 4 + k, dc * 128:(dc + 1) * 128], rhs=yTb[bi][:, g * 4 + k, :],
                                start=(k == 0), stop=(k == 3)), reads=[("yTb", bi)], writes=[PK(pu)])
                        sgi = g % 2
                        p.add("act", lambda e, pg=pg, sgi=sgi: e.activation(out=sg[sgi][:], in_=psb[pg][:], func=AF.Sigmoid),
                              reads=[PK(pg)], writes=[("sg", sgi)])
                        if g == 0:
                            p.add("dve", lambda e, pu=pu, sgi=sgi: e.tensor_tensor(out=acc[:], in0=sg[sgi][:], in1=psb[pu][:], op=ALU.mult),
                                  reads=[PK(pu), ("sg", sgi)], writes=["acc"])
                        else:
                            p.add("dve", lambda e, pu=pu, sgi=sgi: e.tensor_tensor(out=tmpm[:], in0=sg[sgi][:], in1=psb[pu][:], op=ALU.mult),
                                  reads=[PK(pu), ("sg", sgi)], writes=["tmpm"])
                            if g < 3:
                                p.add("dve", lambda e: e.tensor_tensor(out=acc[:], in0=acc[:], in1=tmpm[:], op=ALU.add),
                                      reads=["tmpm", "acc"], writes=["acc"])
                            else:
                                p.add("dve", lambda e, dc=dc, bi=bi: e.tensor_tensor(out=mT[bi][:, dc, :], in0=acc[:], in1=tmpm[:], op=ALU.add),
                                      reads=["tmpm", "acc"], writes=[("mT", bi, dc)])
                for tt in range(4):
                    xt = xtl[xi % 2]
                    xk = ("xtl", xi % 2)
                    xi += 1
                    tok = slice(b * 512 + tt * 128, b * 512 + (tt + 1) * 128)
                    dma("sp", xt[:], xsrc_d[tok, :], writes=[xk])
                    for half in range(2):
                        po = 6 + half
                        for k in range(8):
                            p.add("pe", lambda e, k=k, tt=tt, half=half, po=po, bi=bi: e.matmul(
                                psb[po][:], lhsT=mT[bi][:, k, tt * 128:(tt + 1) * 128], rhs=wo[:, k, half * 512:(half + 1) * 512],
                                start=(k == 0), stop=(k == 7)), reads=[("mT", bi, k)], writes=[PK(po)])
                        p.add("dve", lambda e, xt=xt, half=half, po=po: e.tensor_tensor(out=xt[:, half * 512:(half + 1) * 512],
                                                                                        in0=xt[:, half * 512:(half + 1) * 512], in1=psb[po][:], op=ALU.add),
                              reads=[PK(po), xk], writes=[xk])
                    dma("pool", xres_d[tok, :], xt[:], reads=[xk], writes=[("xres", b, tt)])

        if "F" in phases:
            _phF()

        def _phG(l=l, xsrc_d=xsrc_d, lambda_init=lambda_init):
            phase_reset()
            wgt = sb.alloc("wgt", [128, 8, D_FF], BF16)
            wup = sb.alloc("wup", [128, 8, D_FF], BF16)
            wdn = sb.alloc("wdn", [128, 22, D], BF16)
            g2 = sb.alloc("g2", [128, D], F32)
            dma("sp", g2[:], n2g_d[l], writes=["g2"])
            mG = sb.mark()
            stg = [sb.alloc("stg", [128, 8, 512], F32) for _ in range(2)]
            load_cast(lambda c0, c1: wgt[:, :, c0:c1], lambda c0, c1: w_fg_d[l, :, c0:c1].rearrange("(c p) n -> p c n", p=128), 8, D_FF, stg, "wgt")
            load_cast(lambda c0, c1: wup[:, :, c0:c1], lambda c0, c1: w_fu_d[l, :, c0:c1].rearrange("(c p) n -> p c n", p=128), 8, D_FF, stg, "wup")
            for r in range(0, 22, 8):
                n = min(8, 22 - r)
                load_cast(lambda c0, c1, r=r, n=n: wdn[:, r:r + n, c0:c1],
                          lambda c0, c1, r=r, n=n: w_fd_d[l, r * 128:(r + n) * 128, c0:c1].rearrange("(c p) n -> p c n", p=128), n, D, stg, "wdn%d" % r)
            p.barrier()
            sb.reset(mG)
            xtl = [sb.alloc("xtl", [128, D], F32) for _ in range(4)]
            sqj = sb.alloc("sqj", [128, D], F32)
            ssb = [sb.alloc("ss", [128, 1], F32) for _ in range(2)]
            hbb = [sb.alloc("hb", [128, D], BF16) for _ in range(2)]
            h2T = sb.alloc("h2T", [128, 8, 512], BF16)
            aT = sb.alloc("aT", [128, 22, 512], BF16)
            sg = [sb.alloc("sg", [128, 512], F32) for _ in range(2)]
            bankc = 0
            for b in range(NB):
                for tt in range(4):
                    tok = slice(b * 512 + tt * 128, b * 512 + (tt + 1) * 128)
                    dma("sp", xtl[tt][:], xres_d[tok, :], reads=[("xres", b, tt)], writes=[("xtl", tt)])
                    norm_tile(xtl[tt][:], ("xtl", tt), g2[:], "g2", sqj[:], ssb[tt % 2][:], ("ss", tt % 2), hbb[tt % 2][:], ("hb", tt % 2),
                              6 + tt % 2, h2T[:, :, tt * 128:(tt + 1) * 128], ("h2T", tt))
                for fc in range(22):
                    pg = bankc % 6
                    pu = (bankc + 1) % 6
                    bankc += 2
                    for k in range(8):
                        p.add("pe", lambda e, k=k, fc=fc, pg=pg: e.matmul(psb[pg][:], lhsT=wgt[:, k, fc * 128:(fc + 1) * 128], rhs=h2T[:, k, :],
                                                                          start=(k == 0), stop=(k == 7)),
                              reads=[("h2T", 0), ("h2T", 1), ("h2T", 2), ("h2T", 3)], writes=[PK(pg)])
                    for k in range(8):
                        p.add("pe", lambda e, k=k, fc=fc, pu=pu: e.matmul(psb[pu][:], lhsT=wup[:, k, fc * 128:(fc + 1) * 128], rhs=h2T[:, k, :],
                                                                          start=(k == 0), stop=(k == 7)),
                              reads=[("h2T", 0), ("h2T", 1), ("h2T", 2), ("h2T", 3)], writes=[PK(pu)])
                    sgi = fc % 2
                    p.add("act", lambda e, pg=pg, sgi=sgi: e.activation(out=sg[sgi][:], in_=psb[pg][:], func=AF.Silu), reads=[PK(pg)], writes=[("sg", sgi)])
                    p.add("dve", lambda e, pu=pu, sgi=sgi, fc=fc: e.tensor_tensor(out=aT[:, fc, :], in0=sg[sgi][:], in1=psb[pu][:], op=ALU.mult),
                          reads=[PK(pu), ("sg", sgi)], writes=[("aT", fc)])
                for tt in range(4):
                    tok = slice(b * 512 + tt * 128, b * 512 + (tt + 1) * 128)
                    xt = xtl[tt]
                    for half in range(2):
                        po = 6 + half
                        for fc in range(22):
                            p.add("pe", lambda e, fc=fc, tt=tt, half=half, po=po: e.matmul(
                                psb[po][:], lhsT=aT[:, fc, tt * 128:(tt + 1) * 128], rhs=wdn[:, fc, half * 512:(half + 1) * 512],
                                start=(fc == 0), stop=(fc == 21)), reads=[("aT", fc)], writes=[PK(po)])
                        p.add("dve", lambda e, xt=xt, half=half, po=po: e.tensor_tensor(out=xt[:, half * 512:(half + 1) * 512],
                                                                                        in0=xt[:, half * 512:(half + 1) * 512], in1=psb[po][:], op=ALU.add),
                              reads=[PK(po), ("xtl", tt)], writes=[("xtl", tt)])
                    dma("pool", xres_d[tok, :], xt[:], reads=[("xtl", tt)], writes=[("xres", b, tt)])

        if "G" in phases:
            _phG()

    if "Z" in phases:
        phase_reset()
        gf = sb.alloc("gf", [128, D], F32)
        dma("sp", gf[:], fg_d.ap(), writes=["gf"])
        xb = [sb.alloc("xb", [128, D], F32) for _ in range(2)]
        ob_ = [sb.alloc("ob", [128, D], F32) for _ in range(2)]
        sqj = sb.alloc("sqj", [128, D], F32)
        ssb = [sb.alloc("ss", [128, 1], F32) for _ in range(2)]
        for t in range(NT):
            i = t % 2
            tok = slice(t * 128, (t + 1) * 128)
            dma("sp", xb[i][:], xres_d[tok, :], writes=[("xb", i)])
            p.add("dve", lambda e, i=i: e.memset(ssb[i][:], 0.0), writes=[("ss", i)])
            p.add("act", lambda e, i=i: e.activation(out=sqj[:], in_=xb[i][:], func=AF.Square, accum_out=ssb[i][:]),
                  reads=[("xb", i)], writes=[("ss", i), "sqj"])
            rstd_ops(ssb[i][:], ("ss", i), 1.0 / D)
            p.add("dve", lambda e, i=i: e.scalar_tensor_tensor(out=ob_[i][:], in0=xb[i][:], scalar=ssb[i][:], in1=gf[:], op0=ALU.mult, op1=ALU.mult),
                  reads=[("xb", i), ("ss", i), "gf"], writes=[("ob", i)])
            dma("pool", out_d[tok, :], ob_[i][:], reads=[("ob", i)], writes=[("out", t)])
    p.barrier()
    p.wait_all("pool", [])
    p.emit()
    return nc


def natten_tables(rpb, S):
    rows = S // GRID_W
    NT = S // 128
    wr, wc = 8, 16
    out = np.full((5, 128, 5, 8, 128), NEG, np.float32)
    reps = [0, 1, 2, NT - 2, NT - 1]
    for pi, i in enumerate(reps):
        kb0 = min(max(i - 2, 0), NT - 5)
        q = np.arange(i * 128, (i + 1) * 128)
        r = q // GRID_W
        c = q % GRID_W
        rs = np.clip(r - wr // 2, 0, rows - wr)
        cs = np.clip(c - wc // 2, 0, GRID_W - wc)
        keys = np.arange(kb0 * 128, (kb0 + 5) * 128)
        kr = keys // GRID_W
        kc = keys % GRID_W
        inwin = ((kr[None, :] >= rs[:, None]) & (kr[None, :] < rs[:, None] + wr) &
                 (kc[None, :] >= cs[:, None]) & (kc[None, :] < cs[:, None] + wc))
        offr = np.clip(kr[None, :] - r[:, None] + (wr - 1), 0, 2 * wr - 2)
        offc = np.clip(kc[None, :] - c[:, None] + (wc - 1), 0, 2 * wc - 2)
        g = rpb[:, offr, offc]
        g = np.where(inwin[None], g, np.float32(NEG)).astype(np.float32)
        g = g.reshape(8, 128, 5, 128).transpose(3, 2, 0, 1)
        out[pi] = g
    return out


def host_consts(S):
    t = np.arange(S)
    row = (t // GRID_W).astype(np.float32)
    col = (t % GRID_W).astype(np.float32)
    nf = 16
    inv = (10000.0 ** (-np.arange(nf, dtype=np.float32) / nf)).astype(np.float32)
    ar = row[:, None] * inv
    ac = col[:, None] * inv
    cos64 = np.concatenate([np.cos(ar), np.cos(ar), np.cos(ac), np.cos(ac)], axis=1).astype(np.float32)
    sin64 = np.concatenate([-np.sin(ar), np.sin(ar), -np.sin(ac), np.sin(ac)], axis=1).astype(np.float32)
    W = 2 * S - 128
    C0 = (S // 128 - 1) * 128
    pp = np.arange(128)[:, None]
    cc = np.arange(W)[None, :]
    tz = (-np.abs(cc - pp - C0)).astype(np.float32)
    s_ = np.arange(128)[:, None]
    t_ = np.arange(128)[None, :]
    maskf = (s_ <= t_).astype(np.float32)
    maskb = (s_ >= t_).astype(np.float32)
    ones = np.ones((128, 128), np.float32)
    sel = np.zeros((65, 64), np.float32)
    sel[64, :] = 1.0
    ident = np.eye(128, dtype=np.float32).astype(ml_dtypes.bfloat16)
    return dict(cos64=cos64, sin64=sin64, tz=tz, maskf=maskf, maskb=maskb, ones=ones, sel=sel, ident=ident)


def rep128(a):
    a = np.asarray(a, np.float32)
    return np.ascontiguousarray(np.broadcast_to(a[:, None, :], (a.shape[0], 128, a.shape[1])))


def prep_shared(inp, S):
    L = inp["w_in"].shape[0]
    f = lambda k: np.ascontiguousarray(np.asarray(inp[k], np.float32))
    sh = dict(
        w_in=f("w_in"), w_up_a=f("w_up_a"), w_up_b=f("w_up_b"), w_up_c=f("w_up_c"), w_up_d=f("w_up_d"),
        w_out=f("w_out"), w_ffn_gate=f("w_ffn_gate"), w_ffn_up=f("w_ffn_up"), w_ffn_down=f("w_ffn_down"),
        norm1_g_r=rep128(inp["norm1_g"]), norm2_g_r=rep128(inp["norm2_g"]),
        final_g_r=np.ascontiguousarray(np.broadcast_to(np.asarray(inp["final_g"], np.float32)[None, :], (128, D))),
        conv_w_r=np.ascontiguousarray(np.asarray(inp["a_conv_w"], np.float32).reshape(L, 3, 8, 128).transpose(0, 3, 2, 1)),
        gbias_r=rep128(inp["a_gate_bias"]), anorm_g_r=rep128(inp["a_norm_g"]),
        bqg_r=rep128(inp["b_qnorm_g"]), bkg_r=rep128(inp["b_knorm_g"]),
        dl_r=np.ascontiguousarray(np.broadcast_to(
            np.stack([np.asarray(inp[k], np.float32) for k in ("d_lambda_q1", "d_lambda_k1", "d_lambda_q2", "d_lambda_k2")], axis=1)[:, None],
            (L, 128, 4, 64))),
        dsub_r=rep128(inp["d_subln_g"]),
        natb=np.stack([natten_tables(np.asarray(inp["c_rpb"], np.float32)[l], S) for l in range(L)]),
    )
    sh.update(host_consts(S))
    return sh


_NC_CACHE = {}


def kernel(**inputs):
    x = np.asarray(inputs["x"], np.float32)
    B, S, _ = x.shape
    key = (S,)
    if key not in _NC_CACHE:
        _NC_CACHE[key] = build_program(S=S)
    nc = _NC_CACHE[key]
    sh = prep_shared(inputs, S)
    in_maps = []
    for b in range(B):
        m = dict(sh)
        m["x"] = np.ascontiguousarray(x[b])
        in_maps.append(m)
    res = run_bass_kernel_spmd(nc, in_maps, core_ids=list(range(B)))
    return np.stack([np.asarray(r["out"], np.float32) for r in res.results], axis=0)
```
